# Optimizing a Trainium2 kernel written in Bass

```python
import jax, jax.numpy as jnp
from jax import lax
import numpy as np

D_MODEL = 1024
BATCH = 2
SEQ = 8192
DEPTH = 4

GRID_W = 64
CTX_LEN = 256
N_MIXERS = 4
HEAD_DIM = 64
AT_HEADS = 16
AT_KV_HEADS = 4
AT_GROUP = AT_HEADS // AT_KV_HEADS
AT_WIDTH = AT_HEADS * HEAD_DIM
AT_KV_WIDTH = AT_KV_HEADS * HEAD_DIM
Q_BLOCK = 128
ROPE_THETA = 10000.0
NA_HEADS = 16
NA_WIDTH = NA_HEADS * HEAD_DIM
NA_WIN_ROWS = 8
NA_WIN_COLS = 16
CONV_WIDTH = 31
FT_GROUPS = 4
FT_GROUP_WIDTH = D_MODEL // FT_GROUPS
D_FF = 4 * D_MODEL
N_MOD = 6
EPS = 1e-6
N_AT_LAYERS = (DEPTH + 3) // 4
N_NA_LAYERS = (DEPTH + 2) // 4
N_CV_LAYERS = (DEPTH + 1) // 4
N_FT_LAYERS = DEPTH // 4

kernel_name = 'hybrid_interleaved_diffusion_trunk'


def rms_norm(x, g):
    xf = x.astype(jnp.float32)
    y = xf * lax.rsqrt(jnp.mean(xf * xf, axis=-1, keepdims=True) + EPS)
    return (y * g.astype(jnp.float32)).astype(x.dtype)


def layer_norm(x, g, b):
    xf = x.astype(jnp.float32)
    mu = jnp.mean(xf, axis=-1, keepdims=True)
    var = jnp.mean(jnp.square(xf - mu), axis=-1, keepdims=True)
    y = (xf - mu) * lax.rsqrt(var + EPS)
    return (y * g.astype(jnp.float32) + b.astype(jnp.float32)).astype(x.dtype)


def ada_mods(vec, w, b):
    m = jax.nn.silu(vec) @ w + b
    return [mi[..., None, :] for mi in jnp.split(m, N_MOD, axis=-1)]


def modulate(h, shift, scale):
    return h * (1.0 + scale) + shift


def rope_angles(n_tok):
    t = jnp.arange(n_tok)
    row = (t // GRID_W).astype(jnp.float32)
    col = (t % GRID_W).astype(jnp.float32)
    n_axis = HEAD_DIM // 4
    inv = ROPE_THETA ** (-jnp.arange(n_axis, dtype=jnp.float32) / n_axis)
    return jnp.concatenate([row[:, None] * inv, col[:, None] * inv], axis=-1)


def apply_rope(x, ang):
    xf = x.astype(jnp.float32).reshape(x.shape[:-1] + (HEAD_DIM // 2, 2))
    cos = jnp.cos(ang)[None, :, None, :]
    sin = jnp.sin(ang)[None, :, None, :]
    x1, x2 = xf[..., 0], xf[..., 1]
    out = jnp.stack([x1 * cos - x2 * sin, x1 * sin + x2 * cos], axis=-1)
    return out.reshape(x.shape).astype(x.dtype)


def gqa_attend(q, k, v):
    s = jnp.einsum('bqkgd,bnkd->bkgqn', q, k, preferred_element_type=jnp.float32)
    p = jax.nn.softmax(s, axis=-1).astype(v.dtype)
    return jnp.einsum('bkgqn,bnkd->bqkgd', p, v)


def gqa_mixer(hl, hc, ang, w_qkv, q_g, k_g, w_o, ctx_out):
    B, S, _ = hl.shape
    C = hc.shape[1]
    scale = HEAD_DIM ** -0.5
    ql, kl, vl = jnp.split(hl @ w_qkv, [AT_WIDTH, AT_WIDTH + AT_KV_WIDTH], axis=-1)
    ql = apply_rope(rms_norm(ql.reshape(B, S, AT_HEADS, HEAD_DIM), q_g), ang) * scale
    kl = apply_rope(rms_norm(kl.reshape(B, S, AT_KV_HEADS, HEAD_DIM), k_g), ang)
    vl = vl.reshape(B, S, AT_KV_HEADS, HEAD_DIM)
    kc, vc = jnp.split(hc @ w_qkv[:, AT_WIDTH:], 2, axis=-1)
    kc = rms_norm(kc.reshape(B, C, AT_KV_HEADS, HEAD_DIM), k_g)
    vc = vc.reshape(B, C, AT_KV_HEADS, HEAD_DIM)
    k_all = jnp.concatenate([kc, kl], axis=1)
    v_all = jnp.concatenate([vc, vl], axis=1)
    nb = S // Q_BLOCK
    qb = ql.reshape(B, nb, Q_BLOCK, AT_KV_HEADS, AT_GROUP, HEAD_DIM).transpose(1, 0, 2, 3, 4, 5)
    ob = lax.map(lambda q: gqa_attend(q, k_all, v_all), qb)
    yl = ob.transpose(1, 0, 2, 3, 4, 5).reshape(B, S, AT_WIDTH) @ w_o
    yc = None
    if ctx_out:
        qc = rms_norm((hc @ w_qkv[:, :AT_WIDTH]).reshape(B, C, AT_HEADS, HEAD_DIM), q_g) * scale
        oc = gqa_attend(qc.reshape(B, C, AT_KV_HEADS, AT_GROUP, HEAD_DIM), kc, vc)
        yc = oc.reshape(B, C, AT_WIDTH) @ w_o
    return yc, yl


def na_mixer(hl, hc, n_rows, w_qkv, rpb, w_o, ctx_out):
    B, S, _ = hl.shape
    C = hc.shape[1]
    scale = HEAD_DIM ** -0.5
    ql, kl, vl = jnp.split(hl @ w_qkv, 3, axis=-1)
    kc, vc = jnp.split(hc @ w_qkv[:, NA_WIDTH:], 2, axis=-1)
    kc = kc.reshape(B, C, NA_HEADS, HEAD_DIM)
    vc = vc.reshape(B, C, NA_HEADS, HEAD_DIM)
    wh = min(NA_WIN_ROWS, n_rows)
    ww = NA_WIN_COLS
    row_start = jnp.clip(jnp.arange(n_rows) - wh // 2, 0, n_rows - wh)
    cols = jnp.arange(GRID_W)
    col_idx = jnp.clip(cols - ww // 2, 0, GRID_W - ww)[:, None] + jnp.arange(ww)
    col_bias_idx = col_idx - cols[:, None] + (NA_WIN_COLS - 1)
    rpb_cols = rpb[:, :, col_bias_idx]
    qg = (ql * scale).reshape(B, n_rows, GRID_W, NA_HEADS, HEAD_DIM).transpose(1, 0, 2, 3, 4)
    kg = kl.reshape(B, n_rows, GRID_W, NA_HEADS, HEAD_DIM)
    vg = vl.reshape(B, n_rows, GRID_W, NA_HEADS, HEAD_DIM)

    def row_fn(args):
        r, q = args
        rs = row_start[r]
        k_nb = lax.dynamic_slice_in_dim(kg, rs, wh, axis=1)[:, :, col_idx]
        v_nb = lax.dynamic_slice_in_dim(vg, rs, wh, axis=1)[:, :, col_idx]
        s_nb = jnp.einsum('bqhd,biqjhd->bhqij', q, k_nb, preferred_element_type=jnp.float32)
        row_bias_idx = rs + jnp.arange(wh) - r + (NA_WIN_ROWS - 1)
        bias = rpb_cols[:, row_bias_idx].transpose(0, 2, 1, 3).astype(jnp.float32)
        s_nb = s_nb + bias[None]
        s_ctx = jnp.einsum('bqhd,bnhd->bhqn', q, kc, preferred_element_type=jnp.float32)
        s = jnp.concatenate([s_nb.reshape(B, NA_HEADS, GRID_W, wh * ww), s_ctx], axis=-1)
        p = jax.nn.softmax(s, axis=-1).astype(vg.dtype)
        p_nb = p[..., :wh * ww].reshape(B, NA_HEADS, GRID_W, wh, ww)
        p_ctx = p[..., wh * ww:]
        return (jnp.einsum('bhqij,biqjhd->bqhd', p_nb, v_nb)
                + jnp.einsum('bhqn,bnhd->bqhd', p_ctx, vc))

    o = lax.map(row_fn, (jnp.arange(n_rows), qg))
    yl = o.transpose(1, 0, 2, 3, 4).reshape(B, S, NA_WIDTH) @ w_o
    yc = None
    if ctx_out:
        qc = (hc @ w_qkv[:, :NA_WIDTH]).reshape(B, C, NA_HEADS, 1, HEAD_DIM) * scale
        yc = gqa_attend(qc, kc, vc).reshape(B, C, NA_WIDTH) @ w_o
    return yc, yl


def conv_module(h, w_pw1, b_pw1, w_dw, b_dw, ln_g, ln_b, w_pw2, b_pw2):
    a, g = jnp.split(h @ w_pw1 + b_pw1, 2, axis=-1)
    u = a * jax.nn.sigmoid(g)
    pad = CONV_WIDTH // 2
    u = lax.conv_general_dilated(u, w_dw[:, None, :], window_strides=(1,), padding=[(pad, pad)],
                                 dimension_numbers=('NWC', 'WIO', 'NWC'),
                                 feature_group_count=u.shape[-1]) + b_dw
    u = jax.nn.silu(layer_norm(u, ln_g, ln_b))
    return u @ w_pw2 + b_pw2


def fourier_mixer(h, w, b):
    B, n, D = h.shape
    hf = h.astype(jnp.float32).reshape(B, n, FT_GROUPS, FT_GROUP_WIDTH)
    z = jnp.fft.fft2(hf, axes=(1, 3), norm='ortho').real.astype(h.dtype).reshape(B, n, D)
    return z @ w + b


def sq_relu_mlp(h, w1, w2):
    return jnp.square(jax.nn.relu(h @ w1)) @ w2


def setup_inputs(seed: int = 0) -> dict:
    key = jax.random.key(seed)
    ks = jax.random.split(key, 32)
    f32 = jnp.float32
    D = D_MODEL

    def nrm(k, shape, s):
        return jax.random.normal(k, shape, f32) * s

    def gain(k, shape):
        return 1.0 + 0.02 * jax.random.normal(k, shape, f32)

    return {
        'x': nrm(ks[0], (BATCH, SEQ, D), 1.0),
        'c': nrm(ks[1], (BATCH, D), 1.0),
        'ctx': nrm(ks[2], (BATCH, CTX_LEN, D), 1.0),
        'c_ctx': nrm(ks[3], (D,), 1.0),
        'ada_w': nrm(ks[4], (DEPTH, D, N_MOD * D), 0.5 * D ** -0.5),
        'ada_b': nrm(ks[5], (DEPTH, N_MOD * D), 0.02),
        'norm1_g': gain(ks[6], (DEPTH, D)),
        'norm2_g': gain(ks[7], (DEPTH, D)),
        'mlp_w1': nrm(ks[8], (DEPTH, D, D_FF), D ** -0.5),
        'mlp_w2': nrm(ks[9], (DEPTH, D_FF, D), D_FF ** -0.5),
        'final_g': gain(ks[10], (D,)),
        'at_w_qkv': nrm(ks[11], (N_AT_LAYERS, D, AT_WIDTH + 2 * AT_KV_WIDTH), D ** -0.5),
        'at_q_g': gain(ks[12], (N_AT_LAYERS, HEAD_DIM)),
        'at_k_g': gain(ks[13], (N_AT_LAYERS, HEAD_DIM)),
        'at_w_o': nrm(ks[14], (N_AT_LAYERS, AT_WIDTH, D), AT_WIDTH ** -0.5),
        'na_w_qkv': nrm(ks[15], (N_NA_LAYERS, D, 3 * NA_WIDTH), D ** -0.5),
        'na_rpb': nrm(ks[16], (N_NA_LAYERS, NA_HEADS, 2 * NA_WIN_ROWS - 1, 2 * NA_WIN_COLS - 1), 0.1),
        'na_w_o': nrm(ks[17], (N_NA_LAYERS, NA_WIDTH, D), NA_WIDTH ** -0.5),
        'cv_w_pw1': nrm(ks[18], (N_CV_LAYERS, D, 2 * D), D ** -0.5),
        'cv_b_pw1': nrm(ks[19], (N_CV_LAYERS, 2 * D), 0.02),
        'cv_w_dw': nrm(ks[20], (N_CV_LAYERS, CONV_WIDTH, D), CONV_WIDTH ** -0.5),
        'cv_b_dw': nrm(ks[21], (N_CV_LAYERS, D), 0.02),
        'cv_ln_g': gain(ks[22], (N_CV_LAYERS, D)),
        'cv_ln_b': nrm(ks[23], (N_CV_LAYERS, D), 0.02),
        'cv_w_pw2': nrm(ks[24], (N_CV_LAYERS, D, D), D ** -0.5),
        'cv_b_pw2': nrm(ks[25], (N_CV_LAYERS, D), 0.02),
        'ft_w': nrm(ks[26], (N_FT_LAYERS, D, D), D ** -0.5),
        'ft_b': nrm(ks[27], (N_FT_LAYERS, D), 0.02),
    }


def reference(x, c, ctx, c_ctx, ada_w, ada_b, norm1_g, norm2_g, mlp_w1, mlp_w2, final_g,
              at_w_qkv, at_q_g, at_k_g, at_w_o, na_w_qkv, na_rpb, na_w_o,
              cv_w_pw1, cv_b_pw1, cv_w_dw, cv_b_dw, cv_ln_g, cv_ln_b, cv_w_pw2, cv_b_pw2,
              ft_w, ft_b):
    n_lat = x.shape[1]
    n_rows = n_lat // GRID_W
    ang = rope_angles(n_lat)
    h_ctx = ctx
    for i in range(DEPTH):
        kind = i % N_MIXERS
        occ = i // N_MIXERS
        ctx_later = any((j % N_MIXERS) in (0, 1) for j in range(i + 1, DEPTH))
        ctx_read = kind in (0, 1)
        ml = ada_mods(c, ada_w[i], ada_b[i])
        hl = modulate(rms_norm(x, norm1_g[i]), ml[0], ml[1])
        hc = None
        mc = None
        if ctx_read or ctx_later:
            mc = ada_mods(c_ctx, ada_w[i], ada_b[i])
            hc = modulate(rms_norm(h_ctx, norm1_g[i]), mc[0], mc[1])
        if kind == 0:
            yc, yl = gqa_mixer(hl, hc, ang, at_w_qkv[occ], at_q_g[occ], at_k_g[occ], at_w_o[occ], ctx_later)
        elif kind == 1:
            yc, yl = na_mixer(hl, hc, n_rows, na_w_qkv[occ], na_rpb[occ], na_w_o[occ], ctx_later)
        elif kind == 2:
            cv = (cv_w_pw1[occ], cv_b_pw1[occ], cv_w_dw[occ], cv_b_dw[occ], cv_ln_g[occ], cv_ln_b[occ],
                  cv_w_pw2[occ], cv_b_pw2[occ])
            yl = conv_module(hl, *cv)
            yc = conv_module(hc, *cv) if ctx_later else None
        else:
            yl = fourier_mixer(hl, ft_w[occ], ft_b[occ])
            yc = fourier_mixer(hc, ft_w[occ], ft_b[occ]) if ctx_later else None
        x = x + ml[2] * yl
        x = x + ml[5] * sq_relu_mlp(modulate(rms_norm(x, norm2_g[i]), ml[3], ml[4]), mlp_w1[i], mlp_w2[i])
        if ctx_later:
            h_ctx = h_ctx + mc[2] * yc
            h_ctx = h_ctx + mc[5] * sq_relu_mlp(modulate(rms_norm(h_ctx, norm2_g[i]), mc[3], mc[4]),
                                                mlp_w1[i], mlp_w2[i])
    return rms_norm(x, final_g)
```

```python
import contextlib
import numpy as np
from concourse.bass_utils import run_bass_kernel_spmd
import concourse.bass as bass
import concourse.mybir as mybir

F32 = mybir.dt.float32
BF16 = mybir.dt.bfloat16
AF = mybir.ActivationFunctionType
ALU = mybir.AluOpType
AX = mybir.AxisListType

ENGS = ('pe', 'act', 'dve', 'pool', 'sp')
NDMASLOT = 8


class Op:
    __slots__ = ('eng', 'fn', 'deps', 'sig', 'sigval', 'dma', 'dslot', 'dval', 'prev_slot_op', 'seq', 'dinc')

    def __init__(self, eng, fn, dma):
        self.eng = eng
        self.fn = fn
        self.deps = []
        self.sig = False
        self.sigval = 0
        self.dma = dma
        self.dslot = None
        self.dval = 0
        self.prev_slot_op = None
        self.dinc = 16


class Prog:
    def __init__(self, nc, same_engine_sync=True):
        self.nc = nc
        self.same = same_engine_sync
        self.E = {'pe': nc.tensor, 'act': nc.scalar, 'dve': nc.vector, 'pool': nc.gpsimd, 'sp': nc.sync}
        self.sem = {}
        self._ctx = []
        for e in ('pe', 'act', 'dve', 'pool'):
            self.sem[e] = self._enter(nc.semaphore('prog_' + e))
        self.dsem = {}
        for q in ('sp', 'pool', 'act'):
            self.dsem[q] = [self._enter(nc.semaphore('dma_%s_%d' % (q, i))) for i in range(NDMASLOT)]
        self.ccsem = self._enter(nc.semaphore('ccsem'))
        self.cccnt = 0
        self.ccscratch = self._enter(nc.sbuf_tensor('ccscratch', [128, 8], F32))
        self.cnt = {e: 0 for e in ENGS}
        self.dcnt = {q: 0 for q in ('sp', 'pool', 'act')}
        self.last_dma = {q: [None] * NDMASLOT for q in ('sp', 'pool', 'act')}
        self.nops = 0
        self._reset_phase()

    def _enter(self, cm):
        v = cm.__enter__()
        self._ctx.append(cm)
        return v

    def alloc(self, cm):
        return self._enter(cm)

    def close(self):
        for cm in reversed(self._ctx):
            cm.__exit__(None, None, None)
        self._ctx = []

    def _reset_phase(self):
        self.ops = {e: [] for e in ENGS}
        self.order = []
        self.last_writer = {}
        self.readers = {}

    def _record(self, eng, fn, reads, writes, dma):
        op = Op(eng, fn, dma)
        deps = set()
        for k in reads:
            w = self.last_writer.get(k)
            if w is not None:
                deps.add(w)
        for k in writes:
            w = self.last_writer.get(k)
            if w is not None:
                deps.add(w)
            for r in self.readers.get(k, ()):
                deps.add(r)
        deps.discard(op)
        for k in reads:
            self.readers.setdefault(k, []).append(op)
        for k in writes:
            self.last_writer[k] = op
            self.readers[k] = []
        best = {}
        out = []
        for d in deps:
            if d.dma:
                out.append(d)
                continue
            if d.eng == eng and not dma:
                if eng == 'pe' or not self.same:
                    continue
            b = best.get(d.eng)
            if b is None or d.seq > b.seq:
                best[d.eng] = d
        out.extend(best.values())
        op.deps = out
        for d in out:
            if not d.dma:
                d.sig = True
        op.seq = len(self.order)
        self.ops[eng].append(op)
        self.order.append(op)
        self.nops += 1
        return op

    def op(self, eng, fn, reads=(), writes=()):
        return self._record(eng, fn, reads, writes, False)

    def coll(self, fn, reads=(), writes=()):
        def wrapped(e):
            ins = fn(e)
            self.cccnt += 1
            ins.then_inc(self.ccsem)
            e.wait_ge(self.ccsem, self.cccnt)
            return e.memset(self.ccscratch[:], 0.0)
        return self._record('pool', wrapped, reads, writes, False)

    def dma(self, q, fn, reads=(), writes=()):
        op = self._record(q, fn, reads, writes, True)
        op.dinc = 16
        j = self.dcnt[q]
        self.dcnt[q] += 1
        slot = j % NDMASLOT
        op.dslot = self.dsem[q][slot]
        op.dval = 16 * (j // NDMASLOT + 1)
        op.prev_slot_op = self.last_dma[q][slot]
        self.last_dma[q][slot] = op
        return op

    def flush(self, final_wait=()):
        for e in ENGS:
            c = self.cnt[e]
            for op in self.ops[e]:
                if op.sig and not op.dma:
                    c += 1
                    op.sigval = c
            self.cnt[e] = c
        ops = self.ops
        sem = self.sem
        lastd = {q: list(v) for q, v in self.last_dma.items()}
        anyd = any(d is not None for v in lastd.values() for d in v)

        def run(engname):
            def body(eng):
                known = {e: 0 for e in ENGS}
                kd = {}
                for op in ops[engname]:
                    if op.dma and op.prev_slot_op is not None:
                        p = op.prev_slot_op
                        key = id(p.dslot)
                        if kd.get(key, 0) < p.dval:
                            eng.wait_ge(p.dslot, p.dval)
                            kd[key] = p.dval
                    for d in op.deps:
                        if d.dma:
                            key = id(d.dslot)
                            if kd.get(key, 0) < d.dval:
                                eng.wait_ge(d.dslot, d.dval)
                                kd[key] = d.dval
                        else:
                            if known[d.eng] < d.sigval:
                                eng.wait_ge(sem[d.eng], d.sigval)
                                known[d.eng] = d.sigval
                    ins = op.fn(eng)
                    if op.dma:
                        if op.dinc == 1:
                            ins.then_inc(op.dslot)
                        else:
                            ins.then_inc(op.dslot, 16)
                    elif op.sig:
                        ins.then_inc(sem[op.eng], 1)
                if engname == 'sp':
                    for q in lastd:
                        for d in lastd[q]:
                            if d is not None:
                                eng.wait_ge(d.dslot, d.dval)
            return body

        with self.nc.Block() as block:
            if ops['sp'] or anyd:
                block.sync(run('sp'))
            if ops['pe']:
                block.tensor(run('pe'))
            if ops['act']:
                block.scalar(run('act'))
            if ops['dve']:
                block.vector(run('dve'))
            if ops['pool']:
                block.gpsimd(run('pool'))
        for q in self.last_dma:
            self.last_dma[q] = [None] * NDMASLOT
        self._reset_phase()
D = 1024
KC = 8
DFF = 4096
EPS = 1e-6


class KB:
    def __init__(self, nt=2048):
        self.nc = nc = bass.Bass("TRN2", target_bir_lowering=False)
        self.P = P = Prog(nc)
        self.NT = nt
        self.stack = []
        self.uid = 0
        self.pfx = ''
        self.shared = {}
        self.ps = P.alloc(nc.psum_tensor("ps", [128, 4096], F32))
        self.ident = self.sb("ident_sb", [128, 128], F32)
        self.ones_bf = self.sb("ones_bf", [128, 128], BF16)
        self.eps_t = self.sb("eps_t", [128, 1], F32)
        self.rot = {}
        ident_d = self.nc.dram_tensor("ident", [128, 128], F32, kind="ExternalInput").ap()
        P.dma('sp', lambda e: e.dma_start(out=self.ident[:], in_=ident_d), writes=['ident'])
        P.op('pool', lambda e: e.memset(self.ones_bf[:], 1.0), writes=['ones_bf'])
        P.op('pool', lambda e: e.memset(self.eps_t[:], EPS), writes=['eps_t'])

    def sb(self, name, shape, dt):
        self.uid += 1
        name = "s%d_%s" % (self.uid, name)
        if self.stack:
            return self.stack[-1].enter_context(self.nc.sbuf_tensor(name, shape, dt))
        return self.P.alloc(self.nc.sbuf_tensor(name, shape, dt))

    @contextlib.contextmanager
    def phase(self):
        st = contextlib.ExitStack()
        self.stack.append(st)
        try:
            yield
            self.P.flush()
        finally:
            self.stack.pop()
            st.close()

    def din(self, name, shape, dt=F32):
        if name in ('cv',):
            if name not in self.shared:
                self.shared[name] = self.nc.dram_tensor(name, list(shape), dt, kind="ExternalInput").ap()
            return self.shared[name]
        return self.nc.dram_tensor(self.pfx + name, list(shape), dt, kind="ExternalInput").ap()

    def dint(self, name, shape, dt=F32):
        return self.nc.dram_tensor(name, list(shape), dt).ap()

    def dout(self, name, shape, dt=F32):
        return self.nc.dram_tensor(name, list(shape), dt, kind="ExternalOutput").ap()

    def bank(self, i, n=512, p0=0, p1=128):
        return self.ps[p0:p1, i * 512:i * 512 + n]

    def nextrot(self, name, n):
        v = self.rot.get(name, 0)
        self.rot[name] = v + 1
        return v % n

    def mods(self, cv_d, adaw_d, adab_d, modsT):
        P, nc = self.P, self.nc
        cv = self.sb("cv", [128, 8, 2], F32)
        sT = self.sb("sT", [128, 8, 2], F32)
        mrow = self.sb("mrow", [2, 6144], F32)
        adab = self.sb("adab", [2, 6144], F32)
        wst = [self.sb("wst%d" % i, [128, 8, 512], F32) for i in range(2)]
        P.dma('sp', lambda e: e.dma_start(out=cv[:], in_=cv_d), writes=['cv'])
        P.dma('sp', lambda e: e.dma_start(out=adab[:], in_=adab_d), writes=['adab'])
        P.op('act', lambda e: e.activation(out=sT[:], in_=cv[:], func=AF.Silu), reads=['cv'], writes=['sT'])
        for cg in range(12):
            b = cg % 2
            src = adaw_d[:, cg * 512:(cg + 1) * 512].rearrange("(kc p) c -> p kc c", p=128)
            for h in range(2):
                P.dma('sp', lambda e, b=b, h=h, src=src: e.dma_start(out=wst[b][:, h * 4:(h + 1) * 4, :], in_=src[:, h * 4:(h + 1) * 4, :]),
                      writes=[('wst', b, h)])
            pb = 6 + (cg % 2)
            for kc in range(KC):
                P.op('pe', lambda e, b=b, kc=kc, pb=pb: e.matmul(self.bank(pb, 512, 0, 2), lhsT=sT[:, kc, :], rhs=wst[b][:, kc, :],
                                                                 start=(kc == 0), stop=(kc == KC - 1)),
                     reads=['sT', ('wst', b, kc // 4)], writes=[('ps', pb)])
            P.op('dve', lambda e, cg=cg, pb=pb: e.tensor_tensor(out=mrow[:, cg * 512:(cg + 1) * 512], in0=self.bank(pb, 512, 0, 2),
                                                                in1=adab[:, cg * 512:(cg + 1) * 512], op=ALU.add),
                 reads=['adab'], writes=[('ps', pb), ('mrow', cg)])
        for ch in range(48):
            P.op('pe', lambda e, ch=ch: e.transpose(self.ps[:, 6 * 512 + ch * 2:6 * 512 + ch * 2 + 2], mrow[0:2, ch * 128:(ch + 1) * 128], self.ident[0:2, 0:2]),
                 reads=[('mrow', ch // 4), 'ident'], writes=[('ps', 6)])
        P.op('dve', lambda e: e.tensor_copy(out=modsT[:].rearrange("p a b -> p (a b)"), in_=self.ps[:, 6 * 512:6 * 512 + 96]),
             writes=[('ps', 6), 'modsT'])
        return modsT

    def mod_vectors(self, modsT, g1_d, g2_d, col, mv, tag):
        P = self.P
        gg = self.sb("gg" + tag, [128, 2, 8], F32)
        P.dma('sp', lambda e: e.dma_start(out=gg[:, 0, :], in_=g1_d), writes=['gg' + tag])
        P.dma('sp', lambda e: e.dma_start(out=gg[:, 1, :], in_=g2_d), writes=['gg' + tag])
        P.op('dve', lambda e: e.tensor_copy(out=mv[:].rearrange("p m k -> p (m k)"), in_=modsT[:, :, col]), reads=['modsT'], writes=['mv' + tag])
        for j, m in ((0, 1), (1, 4)):
            P.op('dve', lambda e, j=j, m=m: e.scalar_tensor_tensor(out=mv[:, m, :], in0=mv[:, m, :], scalar=1.0, in1=gg[:, j, :],
                                                                   op0=ALU.add, op1=ALU.mult),
                 reads=['gg' + tag], writes=['mv' + tag])
        key = 'mv' + tag
        return dict(A1=mv[:, 1, :], B1=mv[:, 0, :], G1=mv[:, 2, :], A2=mv[:, 4, :], B2=mv[:, 3, :], G2=mv[:, 5, :], key=key)

    def load_xT(self, x_d, ntok, xT, xkey, stage, t0=0):
        P = self.P
        for tt in range(ntok // 128):
            sb_ = self.nextrot('stage', 2)
            P.dma('sp', lambda e, tt=tt, sb_=sb_: e.dma_start(out=stage[sb_][:], in_=x_d[tt * 128:(tt + 1) * 128, :]), writes=[('stage', sb_)])
            for half in range(2):
                pb = 4 + self.nextrot('ldbank', 2)
                for j in range(4):
                    kc = half * 4 + j
                    P.op('pe', lambda e, sb_=sb_, kc=kc, pb=pb, j=j: e.transpose(self.bank(pb)[:, j * 128:(j + 1) * 128], stage[sb_][:, kc * 128:(kc + 1) * 128], self.ident[:]),
                         reads=[('stage', sb_), 'ident'], writes=[('ps', pb)])
                eng = 'act' if half == 0 else 'dve'
                dst = xT[:, half * 4:(half + 1) * 4, t0 + tt * 128:t0 + (tt + 1) * 128]
                src = self.bank(pb).rearrange("p (a b) -> p a b", a=4)
                if eng == 'act':
                    P.op('act', lambda e, dst=dst, src=src: e.activation(out=dst, in_=src, func=AF.Copy), writes=[('ps', pb), (xkey, (t0 + tt * 128) // 512)])
                else:
                    P.op('dve', lambda e, dst=dst, src=src: e.tensor_copy(out=dst, in_=src), writes=[('ps', pb), (xkey, (t0 + tt * 128) // 512)])

    def store_x(self, xT, xkey, ntok, out_d, ostage, scale_ap=None):
        P = self.P
        for tt in range(ntok // 128):
            ob = self.nextrot('ostage', 2)
            for half in range(2):
                pb = 4 + self.nextrot('ldbank', 2)
                for j in range(4):
                    kc = half * 4 + j
                    P.op('pe', lambda e, kc=kc, pb=pb, j=j, tt=tt: e.transpose(self.bank(pb)[:, j * 128:(j + 1) * 128], xT[:, kc, tt * 128:(tt + 1) * 128], self.ident[:]),
                         reads=[(xkey, tt // 4), 'ident'], writes=[('ps', pb)])
                dst = ostage[ob][:, half * 512:(half + 1) * 512]
                if half == 0:
                    P.op('act', lambda e, dst=dst, pb=pb: e.activation(out=dst, in_=self.bank(pb), func=AF.Copy), writes=[('ps', pb), ('ostage', ob, half)])
                else:
                    P.op('dve', lambda e, dst=dst, pb=pb: e.tensor_copy(out=dst, in_=self.bank(pb)), writes=[('ps', pb), ('ostage', ob, half)])
            P.dma('sp', lambda e, ob=ob, tt=tt: e.dma_start(out=out_d[tt * 128:(tt + 1) * 128, :], in_=ostage[ob][:]),
                  reads=[('ostage', ob, 0), ('ostage', ob, 1)])

    def norm_mod(self, xT, xkey, ntok, A, B, mkey, hT, hkey, scr, t0=0, ht0=0):
        P = self.P
        tgs = min(512, ntok)
        for g in range(ntok // tgs):
            c0 = t0 + g * tgs
            h0 = ht0 + g * tgs
            pb = 6 + self.nextrot('nbank', 2)
            for kc in range(KC):
                sq = self.nextrot('sq', 2)
                P.op('act', lambda e, kc=kc, sq=sq, c0=c0: e.activation(out=scr['sq'][sq][:, :tgs], in_=xT[:, kc, c0:c0 + tgs], func=AF.Square),
                     reads=[(xkey, c0 // 512)], writes=[('sq', sq)])
                P.op('pe', lambda e, kc=kc, sq=sq, pb=pb: e.matmul(self.bank(pb, tgs), lhsT=self.ones_bf[:], rhs=scr['sq'][sq][:, :tgs], start=(kc == 0), stop=(kc == KC - 1)),
                     reads=[('sq', sq), 'ones_bf'], writes=[('ps', pb)])
            rs = scr['rs']
            P.op('act', lambda e, pb=pb: e.activation(out=rs[:, :tgs], in_=self.bank(pb, tgs), func=AF.Ln, scale=1.0 / D, bias=self.eps_t[:]),
                 reads=['eps_t'], writes=[('ps', pb), 'rs'])
            P.op('act', lambda e: e.activation(out=rs[:, :tgs], in_=rs[:, :tgs], func=AF.Exp, scale=-0.5), writes=['rs'])
            for kc in range(KC):
                tb = self.nextrot('tmp', 2)
                P.op('dve', lambda e, kc=kc, tb=tb, c0=c0: e.scalar_tensor_tensor(out=scr['tmp'][tb][:, :tgs], in0=xT[:, kc, c0:c0 + tgs], scalar=A[:, kc:kc + 1],
                                                                                in1=rs[:, :tgs], op0=ALU.mult, op1=ALU.mult),
                     reads=[(xkey, c0 // 512), 'rs', mkey], writes=[('tmp', tb)])
                P.op('pool', lambda e, kc=kc, tb=tb, h0=h0: e.tensor_scalar(out=hT[:, kc, h0:h0 + tgs], in0=scr['tmp'][tb][:, :tgs], scalar1=B[:, kc:kc + 1], scalar2=None, op0=ALU.add),
                     reads=[('tmp', tb), mkey], writes=[(hkey, h0 // 512)])

    def load_w(self, dst, wkey, w_d, r0, nkc, c0, ncol):
        P = self.P
        for k in range(nkc):
            P.dma('pool', lambda e, k=k: e.dma_start(out=dst[:, k, 0:ncol], in_=w_d[r0 + k * 128:r0 + (k + 1) * 128, c0:c0 + ncol]),
                  writes=[(wkey, k)])

    def mlp(self, xT, xkey, ntok, hT, hkey, G, mkey, w1_d, w2_d, wbuf, aT, rbuf):
        P = self.P
        FB = 512
        nfb = DFF // FB
        tgs = min(512, ntok)
        ntg = ntok // tgs

        def ff1(j):
            wb = j % 2
            self.load_w(wbuf['w1'][wb], ('w1', wb), w1_d, 0, KC, j * FB, FB)
            self.load_w(wbuf['w2'][wb], ('w2', wb), w2_d, j * FB, FB // 128, 0, D)
            for fc in range(FB // 128):
                for g in range(ntg):
                    pb = self.nextrot('ff1bank', 3)
                    for kc in range(KC):
                        P.op('pe', lambda e, wb=wb, fc=fc, g=g, kc=kc, pb=pb: e.matmul(self.bank(pb, tgs), lhsT=wbuf['w1'][wb][:, kc, fc * 128:(fc + 1) * 128],
                                                                                    rhs=hT[:, kc, g * tgs:(g + 1) * tgs], start=(kc == 0), stop=(kc == KC - 1)),
                             reads=[(('w1', wb), kc), (hkey, g)], writes=[('ps', pb)])
                    rb = self.nextrot('rbuf', 2)
                    P.op('act', lambda e, pb=pb, rb=rb: e.activation(out=rbuf[rb][:, :tgs], in_=self.bank(pb, tgs), func=AF.Relu), writes=[('ps', pb), ('rbuf', rb)])
                    P.op('pool', lambda e, rb=rb, wb=wb, fc=fc, g=g: e.tensor_tensor(out=aT[wb][:, fc, g * tgs:(g + 1) * tgs], in0=rbuf[rb][:, :tgs], in1=rbuf[rb][:, :tgs], op=ALU.mult),
                         reads=[('rbuf', rb)], writes=[('aT', wb, fc, g)])

        def ff2(j):
            wb = j % 2
            for dc in range(KC):
                for g in range(ntg):
                    pb = 3 + self.nextrot('ff2bank', 3)
                    nf = FB // 128
                    for fc in range(nf):
                        P.op('pe', lambda e, wb=wb, fc=fc, g=g, dc=dc, pb=pb: e.matmul(self.bank(pb, tgs), lhsT=wbuf['w2'][wb][:, fc, dc * 128:(dc + 1) * 128],
                                                                                    rhs=aT[wb][:, fc, g * tgs:(g + 1) * tgs], start=(fc == 0), stop=(fc == nf - 1)),
                             reads=[(('w2', wb), fc), ('aT', wb, fc, g)], writes=[('ps', pb)])
                    P.op('dve', lambda e, dc=dc, g=g, pb=pb: e.scalar_tensor_tensor(out=xT[:, dc, g * tgs:(g + 1) * tgs], in0=self.bank(pb, tgs), scalar=G[:, dc:dc + 1],
                                                                                  in1=xT[:, dc, g * tgs:(g + 1) * tgs], op0=ALU.mult, op1=ALU.add),
                         reads=[mkey], writes=[('ps', pb), (xkey, g)])

        ff1(0)
        for j in range(nfb):
            if j + 1 < nfb:
                ff1(j + 1)
            ff2(j)


def load_cast(kb, dst, key, src_d):
    kb.P.dma('pool', lambda e: e.dma_start(out=dst, in_=src_d), writes=[key])


def proj_fm(kb, hT, hkey, t0, n, W, wkey, col0, pb):
    for kc in range(KC):
        kb.P.op('pe', lambda e, kc=kc: e.matmul(kb.bank(pb, n), lhsT=W[:, kc, col0:col0 + 128], rhs=hT[:, kc, t0:t0 + n],
                                               start=(kc == 0), stop=(kc == KC - 1)),
                reads=[(wkey, kc), (hkey, t0 // 512)], writes=[('ps', pb)])


def qk_norm_rope(kb, pb, n, gain, rope, qscale, out_ap, outkeys, C):
    P = kb.P
    r = kb.nextrot('qkr', 2)
    kg, k2, rs, t1 = C['kg'][r], C['k2'][r], C['rs2'][r], C['t1'][r]
    P.op('act', lambda e: e.activation(out=kg[:, :n], in_=kb.bank(pb, n), func=AF.Copy, scale=gain[:, 0:1]), reads=['gains'], writes=[('ps', pb), ('kg', r)])
    P.op('act', lambda e: e.activation(out=k2[:, :n], in_=kb.bank(pb, n), func=AF.Square), writes=[('ps', pb), ('k2', r)])
    P.op('pe', lambda e: e.matmul(kb.bank(2, n), lhsT=C['bones'][:], rhs=k2[:, :n], start=True, stop=True), reads=[('k2', r), 'bones'], writes=[('ps', 2)])
    if rope is not None:
        P.op('pe', lambda e: e.matmul(kb.bank(3, n), lhsT=C['rmat'][:], rhs=kg[:, :n], start=True, stop=True), reads=[('kg', r), 'rmat'], writes=[('ps', 3)])
    P.op('act', lambda e: e.activation(out=rs[:, :n], in_=kb.bank(2, n), func=AF.Ln, scale=1.0 / 64, bias=kb.eps_t[:]), reads=['eps_t'], writes=[('ps', 2), ('rs2', r)])
    if qscale:
        P.op('act', lambda e: e.activation(out=rs[:, :n], in_=rs[:, :n], func=AF.Exp, scale=-0.5, bias=C['lnq'][:]), reads=['lnq'], writes=[('rs2', r)])
    else:
        P.op('act', lambda e: e.activation(out=rs[:, :n], in_=rs[:, :n], func=AF.Exp, scale=-0.5), writes=[('rs2', r)])
    if rope is not None:
        cos_ap, sin_ap, rkey = rope
        P.op('dve', lambda e: e.tensor_tensor(out=t1[:, :n], in0=kg[:, :n], in1=cos_ap, op=ALU.mult), reads=[('kg', r), rkey], writes=[('t1', r)])
        P.op('dve', lambda e: e.tensor_tensor(out=kg[:, :n], in0=kb.bank(3, n), in1=sin_ap, op=ALU.mult), reads=[rkey], writes=[('ps', 3), ('kg', r)])
        P.op('dve', lambda e: e.tensor_tensor(out=t1[:, :n], in0=t1[:, :n], in1=kg[:, :n], op=ALU.add), reads=[('kg', r)], writes=[('t1', r)])
        P.op('dve', lambda e: e.tensor_tensor(out=out_ap, in0=t1[:, :n], in1=rs[:, :n], op=ALU.mult), reads=[('t1', r), ('rs2', r)], writes=outkeys)
    else:
        P.op('dve', lambda e: e.tensor_tensor(out=out_ap, in0=kg[:, :n], in1=rs[:, :n], op=ALU.mult), reads=[('kg', r), ('rs2', r)], writes=outkeys)


def attention(kb, q_ap, qkeys, NQ, tiles, dst_a, dst_b, dstkeys, C):
    P = kb.P
    oset = kb.nextrot('oset', 2)
    o0 = 4 + 2 * oset
    nt = len(tiles)
    ssets = []

    def qk(i):
        t = tiles[i]
        ss = kb.nextrot('sset', 2)
        ssets.append(ss)
        s0 = 2 * ss
        P.op('pe', lambda e: e.matmul(kb.bank(s0, NQ), lhsT=t['ka'], rhs=q_ap[0:64, :], start=True, stop=True), reads=t['keys'] + qkeys, writes=[('ps', s0)])
        P.op('pe', lambda e: e.matmul(kb.bank(s0 + 1, NQ), lhsT=t['kb'], rhs=q_ap[64:128, :], start=True, stop=True), reads=t['keys'] + qkeys, writes=[('ps', s0 + 1)])

    qk(0)
    for i in range(nt):
        if i + 1 < nt:
            qk(i + 1)
        t = tiles[i]
        s0 = 2 * ssets[i]
        pbuf = kb.nextrot('pbuf', 3)
        pb_ = C['pbuf'][pbuf]
        src = kb.ps[:, s0 * 512:(s0 + 2) * 512].rearrange("p (h n) -> p h n", h=2)[:, :, 0:NQ]
        if t.get('bias_a') is not None:
            sb_ = C['sbias'][kb.nextrot('sbias', 2)]
            P.op('dve', lambda e, sb_=sb_, t=t, s0=s0: e.tensor_tensor(out=sb_[:, 0, 0:NQ], in0=kb.bank(s0, NQ), in1=t['bias_a'], op=ALU.add), reads=t['bkeys'], writes=[('ps', s0), ('sbias', id(sb_), 0)])
            P.op('dve', lambda e, sb_=sb_, t=t, s0=s0: e.tensor_tensor(out=sb_[:, 1, 0:NQ], in0=kb.bank(s0 + 1, NQ), in1=t['bias_b'], op=ALU.add), reads=t['bkeys'], writes=[('ps', s0 + 1), ('sbias', id(sb_), 1)])
            P.op('act', lambda e, sb_=sb_, pb_=pb_: e.activation(out=pb_[:, :, 0:NQ], in_=sb_[:, :, 0:NQ], func=AF.Exp), reads=[('sbias', id(sb_), 0), ('sbias', id(sb_), 1)], writes=[('pbuf', pbuf)])
        else:
            P.op('act', lambda e, src=src, pb_=pb_: e.activation(out=pb_[:, :, 0:NQ], in_=src, func=AF.Exp), writes=[('ps', s0), ('ps', s0 + 1), ('pbuf', pbuf)])
        P.op('pe', lambda e, t=t, pb_=pb_, i=i: e.matmul(kb.bank(o0, NQ), lhsT=t['va'], rhs=pb_[:, 0, 0:NQ], start=(i == 0), stop=(i == nt - 1)), reads=t['keys'] + [('pbuf', pbuf)], writes=[('ps', o0)])
        P.op('pe', lambda e, t=t, pb_=pb_, i=i: e.matmul(kb.bank(o0 + 1, NQ), lhsT=t['vb'], rhs=pb_[:, 1, 0:NQ], start=(i == 0), stop=(i == nt - 1)), reads=t['keys'] + [('pbuf', pbuf)], writes=[('ps', o0 + 1)])
    rcr = kb.nextrot('rc', 2)
    rc = C['rc'][rcr]
    P.op('dve', lambda e: e.reciprocal(out=rc[64:128, 0:NQ], in_=kb.bank(o0, NQ)[64:128, :]), writes=[('ps', o0), ('rc', rcr, 0)])
    P.op('dve', lambda e: e.tensor_tensor(out=dst_a, in0=kb.bank(o0, NQ)[0:64, :], in1=rc[64:128, 0:NQ], op=ALU.mult), reads=[('rc', rcr, 0)], writes=[('ps', o0)] + dstkeys)
    P.op('dve', lambda e: e.reciprocal(out=rc[0:64, 0:NQ], in_=kb.bank(o0 + 1, NQ)[0:64, :]), writes=[('ps', o0 + 1), ('rc', rcr, 1)])
    P.op('dve', lambda e: e.tensor_tensor(out=dst_b, in0=kb.bank(o0 + 1, NQ)[64:128, :], in1=rc[0:64, 0:NQ], op=ALU.mult), reads=[('rc', rcr, 1)], writes=[('ps', o0 + 1)] + dstkeys)


def attn_scratch(kb, with_bias=False):
    C = dict(pbuf=[kb.sb("pbuf%d" % i, [128, 2, 512], BF16) for i in range(3)],
             rc=[kb.sb("rc%d" % i, [128, 512], F32) for i in range(2)])
    if with_bias:
        C['sbias'] = [kb.sb("sbias%d" % i, [128, 2, 512], F32) for i in range(2)]
    return C


def qk_scratch(kb, C):
    C['kg'] = [kb.sb("kg%d" % i, [128, 512], BF16) for i in range(2)]
    C['k2'] = [kb.sb("k2%d" % i, [128, 512], BF16) for i in range(2)]
    C['rs2'] = [kb.sb("rs2%d" % i, [128, 512], F32) for i in range(2)]
    C['t1'] = [kb.sb("t1%d" % i, [128, 512], F32) for i in range(2)]


def norm_scratch(kb):
    return dict(sq=[kb.sb("sq%d" % i, [128, 512], BF16) for i in range(2)], rs=kb.sb("rs", [128, 512], F32),
                tmp=[kb.sb("tmp%d" % i, [128, 512], F32) for i in range(2)])


def mlp_bufs(kb, ntok):
    wbuf = dict(w1=[kb.sb("w1b%d" % i, [128, 8, 512], BF16) for i in range(3)], w2=[kb.sb("w2b%d" % i, [128, 4, 1024], BF16) for i in range(3)])
    aT = [kb.sb("aT%d" % i, [128, 4, ntok], BF16) for i in range(2)]
    rbuf = [kb.sb("rbuf%d" % i, [128, 512], F32) for i in range(2)]
    return wbuf, aT, rbuf


def resid_proj(kb, srcT, skey, t0, n, W, wkey, xT, xkey, xt0, G, mkey, bG=None):
    P = kb.P
    for dc in range(KC):
        pb = kb.nextrot('projbank', 2)
        proj_fm(kb, srcT, skey, t0, n, W, wkey, dc * 128, pb)
        P.op('dve', lambda e, dc=dc, pb=pb: e.scalar_tensor_tensor(out=xT[:, dc, xt0:xt0 + n], in0=kb.bank(pb, n), scalar=G[:, dc:dc + 1],
                                                                 in1=xT[:, dc, xt0:xt0 + n], op0=ALU.mult, op1=ALU.add),
             reads=[mkey], writes=[('ps', pb), (xkey, xt0 // 512)])
        if bG is not None:
            P.op('pool', lambda e, dc=dc: e.tensor_scalar(out=xT[:, dc, xt0:xt0 + n], in0=xT[:, dc, xt0:xt0 + n], scalar1=bG[:, dc:dc + 1], scalar2=None, op0=ALU.add),
                 reads=['bG'], writes=[(xkey, xt0 // 512)])


def build_l0(kb=None, io=None):
    own = kb is None
    if own:
        kb = KB()
    io = io or {}
    kb.pfx = '' if own else 'l0_'
    P, nc = kb.P, kb.nc
    NTOK = 2048
    xb_d = io.get("xb") or kb.din("xb", [8192, D]); ctx_d = io.get("ctx") or kb.din("ctx", [256, D])
    cv_d = kb.din("cv", [128, 8, 2]); adaw_d = kb.din("adaw", [D, 6144]); adab_d = kb.din("adab", [2, 6144])
    g1_d = kb.din("g1", [128, 8]); g2_d = kb.din("g2", [128, 8])
    wqkv_d = kb.din("wqkv", [D, 1536]); gains_d = kb.din("gains", [128, 2])
    cos_d = kb.din("cos", [16, 128, 512]); sin_d = kb.din("sin", [16, 128, 512])
    wo_d = kb.din("wo", [D, D]); w1_d = kb.din("w1", [D, DFF]); w2_d = kb.din("w2", [DFF, D])
    rmat_d = kb.din("rmat", [128, 128]); bones_d = kb.din("bones", [128, 128])
    out_d = io.get("out") or kb.dout("out", [NTOK, D]); hctx_d = io.get("hctx") or kb.dout("hctx", [256, D])

    modsT = kb.sb("modsT", [128, 48, 2], F32)
    mvL = kb.sb("mvL", [128, 6, 8], F32); mvC = kb.sb("mvC", [128, 6, 8], F32)
    C = dict(rmat=kb.sb("rmat", [128, 128], BF16), bones=kb.sb("bones", [128, 128], BF16), lnq=kb.sb("lnq", [128, 1], F32))
    gains = kb.sb("gains", [128, 2], F32)
    with kb.phase():
        load_cast(kb, C['rmat'][:], 'rmat', rmat_d)
        load_cast(kb, C['bones'][:], 'bones', bones_d)
        P.op('pool', lambda e: e.memset(C['lnq'][:], float(np.log(0.125))), writes=['lnq'])
        P.dma('sp', lambda e: e.dma_start(out=gains[:], in_=gains_d), writes=['gains'])
        kb.mods(cv_d, adaw_d, adab_d, modsT)
        mL = kb.mod_vectors(modsT, g1_d, g2_d, 0, mvL, "L")
        mC = kb.mod_vectors(modsT, g1_d, g2_d, 1, mvC, "C")
    qgain, kgain = gains[:, 0:1], gains[:, 1:2]

    with kb.phase():
        KT = kb.sb("KT", [128, 2, 8448], BF16)
        Ve = kb.sb("Ve", [128, 66, 384], BF16)
        P.op('pool', lambda e: e.memset(Ve[:].rearrange("p t (a b c) -> p (t a) b c", a=2, b=3, c=64)[:, :, 1, :], 1.0), writes=['Ve_ones'])

        def kv_tiles(tile_ids, pr):
            out = []
            for kt in tile_ids:
                out.append(dict(ka=KT[0:64, pr, kt * 128:(kt + 1) * 128], kb=KT[64:128, pr, kt * 128:(kt + 1) * 128],
                                va=Ve[:, kt, pr * 192:pr * 192 + 128], vb=Ve[:, kt, pr * 192 + 64:pr * 192 + 192],
                                keys=[('KT', kt // 4), ('Ve', kt), 'Ve_ones']))
            return out

        def produce_kv(hT, hkey, n, g, wqkv, rope):
            for pr in range(2):
                pb = kb.nextrot('projbank', 2)
                proj_fm(kb, hT, hkey, 0, n, wqkv, 'wqkv', 1024 + pr * 128, pb)
                qk_norm_rope(kb, pb, n, kgain, rope, False, KT[:, pr, g * 512:g * 512 + n], [('KT', g)], C)
            for tt in range(n // 128):
                pb = kb.nextrot('projbank', 2)
                for kc in range(KC):
                    P.op('pe', lambda e, kc=kc, tt=tt, pb=pb: e.matmul(kb.bank(pb, 256), lhsT=hT[:, kc, tt * 128:(tt + 1) * 128], rhs=wqkv[:, kc, 1280:1536],
                                                                      start=(kc == 0), stop=(kc == KC - 1)),
                         reads=[('wqkv', kc), (hkey, 0)], writes=[('ps', pb)])
                kt = g * 4 + tt
                dst = Ve[:, kt, :].rearrange("p (a b c) -> p a b c", a=2, b=3, c=64)[:, :, ::2, :]
                src = kb.bank(pb, 256).rearrange("p (a b c) -> p a b c", a=2, b=2, c=64)
                P.op('dve', lambda e, dst=dst, src=src: e.tensor_copy(out=dst, in_=src), writes=[('ps', pb), ('Ve', kt)])

        with kb.phase():
            cT = kb.sb("cT", [128, 8, 256], F32)
            stage = [kb.sb("stage%d" % i, [128, 1024], F32) for i in range(2)]
            scr = norm_scratch(kb)
            with kb.phase():
                hcT = kb.sb("hcT", [128, 8, 256], BF16)
                QcT = kb.sb("QcT", [128, 8, 256], BF16)
                OcT = kb.sb("OcT", [128, 8, 256], BF16)
                qk_scratch(kb, C)
                C.update(attn_scratch(kb))
                wqkv = kb.sb("wqkv", [128, 8, 1536], BF16)
                wo = kb.sb("wo", [128, 8, 1024], BF16)
                kb.load_w(wqkv, 'wqkv', wqkv_d, 0, KC, 0, 1536)
                kb.load_w(wo, 'wo', wo_d, 0, KC, 0, 1024)
                kb.load_xT(ctx_d, 256, cT, 'cT', stage)
                kb.norm_mod(cT, 'cT', 256, mC['A1'], mC['B1'], mC['key'], hcT, 'hcT', scr)
                produce_kv(hcT, 'hcT', 256, 16, wqkv, None)
                for c in range(8):
                    pb = kb.nextrot('projbank', 2)
                    proj_fm(kb, hcT, 'hcT', 0, 256, wqkv, 'wqkv', c * 128, pb)
                    qk_norm_rope(kb, pb, 256, qgain, None, True, QcT[:, c, :], [('QcT', c)], C)
                for c in range(8):
                    attention(kb, QcT[:, c, :], [('QcT', c)], 256, kv_tiles([64, 65], c // 4), OcT[0:64, c, :], OcT[64:128, c, :], [('OcT', 0)], C)
                resid_proj(kb, OcT, 'OcT', 0, 256, wo, 'wo', cT, 'cT', 0, mC['G1'], mC['key'])
            with kb.phase():
                hc2 = kb.sb("hc2", [128, 8, 256], BF16)
                kb.norm_mod(cT, 'cT', 256, mC['A2'], mC['B2'], mC['key'], hc2, 'hc2', scr)
                wbuf, aT, rbuf = mlp_bufs(kb, 256)
                kb.mlp(cT, 'cT', 256, hc2, 'hc2', mC['G2'], mC['key'], w1_d, w2_d, wbuf, aT, rbuf)
                kb.store_x(cT, 'cT', 256, hctx_d, stage)

        with kb.phase():
            QT = kb.sb("QT", [128, 8, NTOK], BF16)
            with kb.phase():
                xtmp = kb.sb("xtmp", [128, 8, 512], F32)
                hTt = kb.sb("hTt", [128, 8, 512], BF16)
                stage = [kb.sb("stage%d" % i, [128, 1024], F32) for i in range(2)]
                scr = norm_scratch(kb)
                qk_scratch(kb, C)
                wqkv = kb.sb("wqkv", [128, 8, 1536], BF16)
                cs = [kb.sb("cs%d" % i, [128, 2, 512], F32) for i in range(2)]
                kb.load_w(wqkv, 'wqkv', wqkv_d, 0, KC, 0, 1536)
                for g in range(16):
                    kb.load_xT(xb_d[g * 512:(g + 1) * 512, :], 512, xtmp, 'xtmp', stage)
                    kb.norm_mod(xtmp, 'xtmp', 512, mL['A1'], mL['B1'], mL['key'], hTt, 'hTt', scr)
                    cb = g % 2
                    P.dma('sp', lambda e, g=g, cb=cb: e.dma_start(out=cs[cb][:, 0, :], in_=cos_d[g]), writes=[('cs', cb)])
                    P.dma('sp', lambda e, g=g, cb=cb: e.dma_start(out=cs[cb][:, 1, :], in_=sin_d[g]), writes=[('cs', cb)])
                    rope = (cs[cb][:, 0, :], cs[cb][:, 1, :], ('cs', cb))
                    produce_kv(hTt, 'hTt', 512, g, wqkv, rope)
                    if g < 4:
                        for c in range(8):
                            pb = kb.nextrot('projbank', 2)
                            proj_fm(kb, hTt, 'hTt', 0, 512, wqkv, 'wqkv', c * 128, pb)
                            qk_norm_rope(kb, pb, 512, qgain, rope, True, QT[:, c, g * 512:(g + 1) * 512], [('QT', c, g)], C)
            with kb.phase():
                OT = kb.sb("OT", [128, 8, NTOK], BF16)
                with kb.phase():
                    C.update(attn_scratch(kb))
                    for qg in range(4):
                        for c in range(8):
                            attention(kb, QT[:, c, qg * 512:(qg + 1) * 512], [('QT', c, qg)], 512, kv_tiles(list(range(66)), c // 4),
                                      OT[0:64, c, qg * 512:(qg + 1) * 512], OT[64:128, c, qg * 512:(qg + 1) * 512], [('OT', qg)], C)
                with kb.phase():
                    xtmp = kb.sb("xtmp", [128, 8, 512], F32)
                    stage = [kb.sb("stage%d" % i, [128, 1024], F32) for i in range(2)]
                    wo = kb.sb("wo", [128, 8, 1024], BF16)
                    ostage = [kb.sb("ostage%d" % i, [128, 1024], F32) for i in range(2)]
                    kb.load_w(wo, 'wo', wo_d, 0, KC, 0, 1024)
                    for g in range(4):
                        kb.load_xT(xb_d[g * 512:(g + 1) * 512, :], 512, xtmp, 'xtmp', stage)
                        resid_proj(kb, OT, 'OT', g * 512, 512, wo, 'wo', xtmp, 'xtmp', 0, mL['G1'], mL['key'])
                        kb.store_x(xtmp, 'xtmp', 512, out_d[g * 512:(g + 1) * 512, :], ostage)
    mlp_tail(kb, out_d, out_d, NTOK, mL, w1_d, w2_d)
    if own:
        P.close()
    return nc


def mlp_tail(kb, src_d, out_d, ntok, mL, w1_d, w2_d, final=None):
    with kb.phase():
        xT = kb.sb("xT", [128, 8, ntok], F32)
        with kb.phase():
            stage = [kb.sb("stage%d" % i, [128, 1024], F32) for i in range(2)]
            kb.load_xT(src_d, ntok, xT, 'xT', stage)
        with kb.phase():
            hT = kb.sb("hT", [128, 8, ntok], BF16)
            scr = norm_scratch(kb)
            kb.norm_mod(xT, 'xT', ntok, mL['A2'], mL['B2'], mL['key'], hT, 'hT', scr)
            wbuf, aT, rbuf = mlp_bufs(kb, ntok)
            kb.mlp(xT, 'xT', ntok, hT, 'hT', mL['G2'], mL['key'], w1_d, w2_d, wbuf, aT, rbuf)
        with kb.phase():
            ostage = [kb.sb("ostage%d" % i, [128, 1024], F32) for i in range(2)]
            if final is not None:
                yT = kb.sb("yT", [128, 8, ntok], F32)
                scr = norm_scratch(kb)
                kb.norm_mod(xT, 'xT', ntok, final[0], final[1], final[2], yT, 'yT', scr)
                kb.store_x(yT, 'yT', ntok, out_d, ostage)
            else:
                kb.store_x(xT, 'xT', ntok, out_d, ostage)


def fm(v):
    return np.ascontiguousarray(np.asarray(v, np.float32).reshape(8, 128).T)


def common_inputs(inp, layer, b):
    cv = np.stack([fm(inp['c'][b]), fm(inp['c_ctx'])], axis=-1)
    return dict(ident=np.eye(128, dtype=np.float32), cv=np.ascontiguousarray(cv), adaw=np.ascontiguousarray(inp['ada_w'][layer]),
                adab=np.ascontiguousarray(np.stack([inp['ada_b'][layer]] * 2)), g1=fm(inp['norm1_g'][layer]), g2=fm(inp['norm2_g'][layer]),
                w1=np.ascontiguousarray(inp['mlp_w1'][layer]), w2=np.ascontiguousarray(inp['mlp_w2'][layer]))


def rope_tables(order):
    t = np.asarray(order)
    row = (t // 64).astype(np.float32)
    col = (t % 64).astype(np.float32)
    inv = (10000.0 ** (-np.arange(16, dtype=np.float32) / 16)).astype(np.float32)
    ang = np.concatenate([row[:, None] * inv, col[:, None] * inv], axis=-1).astype(np.float32)
    idx = (np.arange(128) % 64) // 2
    a = ang[:, idx].T
    cos = np.cos(a).astype(np.float32).reshape(128, 16, 512).transpose(1, 0, 2)
    sin = np.sin(a).astype(np.float32).reshape(128, 16, 512).transpose(1, 0, 2)
    return np.ascontiguousarray(cos), np.ascontiguousarray(sin)


def gqa_chunk_heads():
    return [(c, 4 + c) if c < 4 else (8 + c - 4, 12 + c - 4) for c in range(8)]


def prep_l0(inp, b, q):
    d = common_inputs(inp, 0, b)
    order = np.concatenate([np.arange(q * 2048, 8192), np.arange(0, q * 2048)])
    d['xb'] = np.ascontiguousarray(inp['x'][b][order])
    d['ctx'] = np.ascontiguousarray(inp['ctx'][b])
    wqkv = inp['at_w_qkv'][0]
    qcols = np.concatenate([np.concatenate([np.arange(ha * 64, ha * 64 + 64), np.arange(hb * 64, hb * 64 + 64)]) for ha, hb in gqa_chunk_heads()])
    d['wqkv'] = np.ascontiguousarray(np.concatenate([wqkv[:, qcols], wqkv[:, 1024:]], axis=1))
    d['wo'] = np.ascontiguousarray(inp['at_w_o'][0][qcols, :])
    d['gains'] = np.ascontiguousarray(np.stack([np.tile(inp['at_q_g'][0], 2), np.tile(inp['at_k_g'][0], 2)], axis=1))
    d['cos'], d['sin'] = rope_tables(order)
    rmat = np.zeros((128, 128), np.float32)
    for i in range(64):
        rmat[2 * i + 1, 2 * i] = -1.0
        rmat[2 * i, 2 * i + 1] = 1.0
    d['rmat'] = rmat
    bones = np.zeros((128, 128), np.float32)
    bones[:64, :64] = 1.0
    bones[64:, 64:] = 1.0
    d['bones'] = bones
    return d


NA_CLASS = [0, 1] + [2] * 12 + [3, 4]
NA_ST = list(range(14)) + [12, 13]


def build_l1(kb=None, io=None):
    own = kb is None
    if own:
        kb = KB()
    io = io or {}
    kb.pfx = '' if own else 'l1_'
    P, nc = kb.P, kb.nc
    NTOK = 2048
    NH = 2560
    xh_d = io.get("xh") or kb.din("xh", [NH, D]); ctx_d = io.get("ctx") or kb.din("ctx", [256, D])
    cv_d = kb.din("cv", [128, 8, 2]); adaw_d = kb.din("adaw", [D, 6144]); adab_d = kb.din("adab", [2, 6144])
    g1_d = kb.din("g1", [128, 8]); g2_d = kb.din("g2", [128, 8])
    wqkv_d = kb.din("wqkv", [8, D, 384]); tab_d = kb.din("tab", [5, 8, 128, 2 * 7 * 128])
    wo_d = kb.din("wo", [D, D]); w1_d = kb.din("w1", [D, DFF]); w2_d = kb.din("w2", [DFF, D])
    out_d = io.get("out") or kb.dout("out", [NTOK, D])

    modsT = kb.sb("modsT", [128, 48, 2], F32)
    mvL = kb.sb("mvL", [128, 6, 8], F32); mvC = kb.sb("mvC", [128, 6, 8], F32)
    C = {}
    with kb.phase():
        kb.mods(cv_d, adaw_d, adab_d, modsT)
        mL = kb.mod_vectors(modsT, g1_d, g2_d, 0, mvL, "L")
        mC = kb.mod_vectors(modsT, g1_d, g2_d, 1, mvC, "C")
    with kb.phase():
        hT = kb.sb("hT", [128, 8, NH], BF16)
        hcT = kb.sb("hcT", [128, 8, 256], BF16)
        OT = kb.sb("OT", [128, 8, NTOK], BF16)
        with kb.phase():
            xtmp = kb.sb("xtmp", [128, 8, 512], F32)
            stage = [kb.sb("stage%d" % i, [128, 1024], F32) for i in range(2)]
            scr = norm_scratch(kb)
            for g in range(5):
                kb.load_xT(xh_d[g * 512:(g + 1) * 512, :], 512, xtmp, 'xtmp', stage)
                kb.norm_mod(xtmp, 'xtmp', 512, mL['A1'], mL['B1'], mL['key'], hT, 'hT', scr, t0=0, ht0=g * 512)
            kb.load_xT(ctx_d, 256, xtmp, 'xtmp', stage)
            kb.norm_mod(xtmp, 'xtmp', 256, mC['A1'], mC['B1'], mC['key'], hcT, 'hcT', scr)
        with kb.phase():
            C.update(attn_scratch(kb, with_bias=True))
            wc = [kb.sb("wc%d" % i, [128, 8, 384], BF16) for i in range(2)]
            QTc = [kb.sb("QTc%d" % i, [128, NTOK], BF16) for i in range(2)]
            KTc = [kb.sb("KTc%d" % i, [128, NH + 256], BF16) for i in range(2)]
            Vec = [kb.sb("Vec%d" % i, [128, 22, 192], BF16) for i in range(2)]
            tabI = [kb.sb("tabI%d" % i, [128, 2, 7, 128], F32) for i in range(2)]
            tabS = [kb.sb("tabS%d" % i, [128, 2, 7, 128], F32) for i in range(2)]
            for i in range(2):
                P.op('pool', lambda e, i=i: e.memset(Vec[i][:, :, 64:128], 1.0), writes=[('Vones', i)])
            for c in range(8):
                b = c % 2
                kb.load_w(wc[b], ('wc', b), wqkv_d[c], 0, KC, 0, 384)
                P.dma('sp', lambda e, c=c, b=b: e.dma_start(out=tabI[b][:].rearrange("p a j q -> p (a j q)"), in_=tab_d[2, c]), writes=[('tabI', b)])
                for g in range(4):
                    pb = kb.nextrot('projbank', 2)
                    proj_fm(kb, hT, 'hT', 256 + g * 512, 512, wc[b], ('wc', b), 0, pb)
                    P.op('act', lambda e, g=g, b=b, pb=pb: e.activation(out=QTc[b][:, g * 512:(g + 1) * 512], in_=kb.bank(pb), func=AF.Copy, scale=0.125),
                         writes=[('ps', pb), ('QTc', b, g)])
                for g in range(5):
                    pb = kb.nextrot('projbank', 2)
                    proj_fm(kb, hT, 'hT', g * 512, 512, wc[b], ('wc', b), 128, pb)
                    P.op('act', lambda e, g=g, b=b, pb=pb: e.activation(out=KTc[b][:, g * 512:(g + 1) * 512], in_=kb.bank(pb), func=AF.Copy),
                         writes=[('ps', pb), ('KTc', b, g)])
                pb = kb.nextrot('projbank', 2)
                proj_fm(kb, hcT, 'hcT', 0, 256, wc[b], ('wc', b), 128, pb)
                P.op('act', lambda e, b=b, pb=pb: e.activation(out=KTc[b][:, NH:NH + 256], in_=kb.bank(pb, 256), func=AF.Copy), writes=[('ps', pb), ('KTc', b, 5)])
                for kt in range(22):
                    src_h, hk, t0 = (hT, 'hT', kt * 128) if kt < 20 else (hcT, 'hcT', (kt - 20) * 128)
                    pb = kb.nextrot('projbank', 2)
                    for kc in range(KC):
                        P.op('pe', lambda e, kc=kc, b=b, pb=pb, src_h=src_h, t0=t0: e.matmul(kb.bank(pb, 128), lhsT=src_h[:, kc, t0:t0 + 128], rhs=wc[b][:, kc, 256:384],
                                                                                          start=(kc == 0), stop=(kc == KC - 1)),
                             reads=[(('wc', b), kc), (hk, t0 // 512)], writes=[('ps', pb)])
                    dst = Vec[b][:, kt, :].rearrange("p (t s) -> p t s", s=64)[:, ::2, :]
                    src = kb.bank(pb, 128).rearrange("p (t s) -> p t s", s=64)
                    P.op('dve', lambda e, dst=dst, src=src: e.tensor_copy(out=dst, in_=src), writes=[('ps', pb), ('Vec', b, kt)])
                for rp in range(16):
                    cls = NA_CLASS[rp]
                    st = NA_ST[rp]
                    if cls == 2:
                        tab, tkey = tabI[b], ('tabI', b)
                    else:
                        sbuf_i = kb.nextrot('tabS', 2)
                        tab, tkey = tabS[sbuf_i], ('tabS', sbuf_i)
                        P.dma('sp', lambda e, c=c, cls=cls, tab=tab: e.dma_start(out=tab[:].rearrange("p a j q -> p (a j q)"), in_=tab_d[cls, c]), writes=[tkey])
                    tiles = []
                    for j in range(9):
                        kt = st + j if j < 7 else 20 + (j - 7)
                        k0 = kt * 128
                        tl = dict(ka=KTc[b][0:64, k0:k0 + 128], kb=KTc[b][64:128, k0:k0 + 128], va=Vec[b][:, kt, 0:128], vb=Vec[b][:, kt, 64:192],
                                  keys=[('KTc', b, k0 // 512), ('Vec', b, kt), ('Vones', b)])
                        if j < 7:
                            tl['bias_a'] = tab[:, 0, j, :]
                            tl['bias_b'] = tab[:, 1, j, :]
                            tl['bkeys'] = [tkey]
                        tiles.append(tl)
                    attention(kb, QTc[b][:, rp * 128:(rp + 1) * 128], [('QTc', b, rp // 4)], 128, tiles,
                              OT[0:64, c, rp * 128:(rp + 1) * 128], OT[64:128, c, rp * 128:(rp + 1) * 128], [('OT', rp // 4)], C)
        with kb.phase():
            xtmp = kb.sb("xtmp", [128, 8, 512], F32)
            stage = [kb.sb("stage%d" % i, [128, 1024], F32) for i in range(2)]
            ostage = [kb.sb("ostage%d" % i, [128, 1024], F32) for i in range(2)]
            wo = kb.sb("wo", [128, 8, 1024], BF16)
            kb.load_w(wo, 'wo', wo_d, 0, KC, 0, 1024)
            for g in range(4):
                kb.load_xT(xh_d[256 + g * 512:256 + (g + 1) * 512, :], 512, xtmp, 'xtmp', stage)
                resid_proj(kb, OT, 'OT', g * 512, 512, wo, 'wo', xtmp, 'xtmp', 0, mL['G1'], mL['key'])
                kb.store_x(xtmp, 'xtmp', 512, out_d[g * 512:(g + 1) * 512, :], ostage)
    mlp_tail(kb, out_d, out_d, NTOK, mL, w1_d, w2_d)
    if own:
        P.close()
    return nc


def na_bias_tables(rpb, qq):
    NEG = np.float32(-30000.0)
    tab = np.full((5, 16, 2, 64, 7, 2, 64), NEG, np.float32)
    cq = np.arange(64)
    cs = np.clip(cq - 8, 0, 48)
    ck = np.arange(64)
    colvalid = (ck[:, None] >= cs[None, :]) & (ck[:, None] < cs[None, :] + 16)
    colidx = np.clip(ck[:, None] - cq[None, :] + 15, 0, 30)
    rep_rp = {0: 0, 1: 1, 2: 2, 3: 14, 4: 15}
    for cls in range(5):
        rp = rep_rp[cls]
        st = NA_ST[rp]
        for bq in range(2):
            r = 32 * qq + 2 * rp + bq
            rs = min(max(r - 4, 0), 120)
            for j in range(7):
                for a in range(2):
                    kr = 32 * qq - 4 + 2 * (st + j) + a
                    if kr < rs or kr >= rs + 8 or kr < 0 or kr > 127:
                        continue
                    vals = rpb[:, kr - r + 7, :][:, colidx]
                    tab[cls, :, a, :, j, bq, :] = np.where(colvalid[None], vals, NEG)
    tab = tab.reshape(5, 8, 2, 128, 7, 128)
    tab = tab.transpose(0, 1, 3, 2, 4, 5).reshape(5, 8, 128, 2 * 7 * 128)
    return np.ascontiguousarray(tab)


def prep_l1(inp, x1, hctx1, b, q):
    d = common_inputs(inp, 1, b)
    if x1 is not None:
        xh = np.zeros((2560, D), np.float32)
        lo = q * 2048 - 256
        hi = lo + 2560
        s0, s1 = max(lo, 0), min(hi, 8192)
        xh[s0 - lo:s1 - lo] = x1[b][s0:s1]
        d['xh'] = xh
        d['ctx'] = np.ascontiguousarray(hctx1[b])
    w = inp['na_w_qkv'][0]
    d['wqkv'] = np.ascontiguousarray(np.stack([np.concatenate([w[:, c * 128:(c + 1) * 128], w[:, 1024 + c * 128:1024 + (c + 1) * 128],
                                                              w[:, 2048 + c * 128:2048 + (c + 1) * 128]], axis=1) for c in range(8)]))
    d['tab'] = na_bias_tables(inp['na_rpb'][0], q)
    d['wo'] = np.ascontiguousarray(inp['na_w_o'][0])
    return d


def build_l2(kb=None, io=None):
    own = kb is None
    if own:
        kb = KB()
    io = io or {}
    kb.pfx = '' if own else 'l2_'
    P, nc = kb.P, kb.nc
    NTOK = 2048
    NH = 2304
    xh_d = io.get("xh") or kb.din("xh", [NH, D])
    cv_d = kb.din("cv", [128, 8, 2]); adaw_d = kb.din("adaw", [D, 6144]); adab_d = kb.din("adab", [2, 6144])
    g1_d = kb.din("g1", [128, 8]); g2_d = kb.din("g2", [128, 8])
    wpw1_d = kb.din("wpw1", [8, D, 256]); vecs_d = kb.din("vecs", [128, 6, 8]); wdw_d = kb.din("wdw", [128, 8, 31]); mask_d = kb.din("mask", [128, 2])
    wpw2_d = kb.din("wpw2", [D, D]); w1_d = kb.din("w1", [D, DFF]); w2_d = kb.din("w2", [DFF, D])
    out_d = io.get("out") or kb.dout("out", [NTOK, D])

    modsT = kb.sb("modsT", [128, 48, 2], F32)
    mvL = kb.sb("mvL", [128, 6, 8], F32)
    vecs = kb.sb("vecs", [128, 6, 8], F32)
    wdw = kb.sb("wdw", [128, 8, 31], F32)
    mask = kb.sb("mask", [128, 2], F32)
    bG = kb.sb("bG", [128, 8], F32)
    identb = kb.sb("identb", [128, 128], BF16)
    with kb.phase():
        kb.mods(cv_d, adaw_d, adab_d, modsT)
        mL = kb.mod_vectors(modsT, g1_d, g2_d, 0, mvL, "L")
        P.dma('sp', lambda e: e.dma_start(out=vecs[:], in_=vecs_d), writes=['vecs'])
        P.dma('sp', lambda e: e.dma_start(out=wdw[:], in_=wdw_d), writes=['wdw'])
        P.dma('sp', lambda e: e.dma_start(out=mask[:], in_=mask_d), writes=['mask'])
        P.op('dve', lambda e: e.tensor_tensor(out=bG[:], in0=vecs[:, 5, :], in1=mL['G1'], op=ALU.mult), reads=['vecs', mL['key']], writes=['bG'])
        P.op('dve', lambda e: e.tensor_copy(out=identb[:], in_=kb.ident[:]), reads=['ident'], writes=['identb'])
    with kb.phase():
        vT = kb.sb("vT", [128, 8, NTOK], BF16)
        with kb.phase():
            uT = kb.sb("uT", [128, 8, NH], BF16)
            with kb.phase():
                hT = kb.sb("hT", [128, 8, NH], BF16)
                with kb.phase():
                    xtmp = kb.sb("xtmp", [128, 8, 512], F32)
                    stage = [kb.sb("stage%d" % i, [128, 1024], F32) for i in range(2)]
                    scr = norm_scratch(kb)
                    for g in range(5):
                        n = 512 if g < 4 else 256
                        kb.load_xT(xh_d[g * 512:g * 512 + n, :], n, xtmp, 'xtmp', stage)
                        kb.norm_mod(xtmp, 'xtmp', n, mL['A1'], mL['B1'], mL['key'], hT, 'hT', scr, t0=0, ht0=g * 512)
                with kb.phase():
                    wp = [kb.sb("wp%d" % i, [128, 8, 256], BF16) for i in range(2)]
                    sig = [kb.sb("sig%d" % i, [128, 512], F32) for i in range(2)]
                    for fc in range(8):
                        b = fc % 2
                        kb.load_w(wp[b], ('wp', b), wpw1_d[fc], 0, KC, 0, 256)
                        for g in range(5):
                            n = 512 if g < 4 else 256
                            pa = kb.nextrot('projbank', 2)
                            proj_fm(kb, hT, 'hT', g * 512, n, wp[b], ('wp', b), 0, pa)
                            pg = 2 + kb.nextrot('projbank2', 2)
                            proj_fm(kb, hT, 'hT', g * 512, n, wp[b], ('wp', b), 128, pg)
                            sb_ = kb.nextrot('sig', 2)
                            P.op('act', lambda e, sb_=sb_, pg=pg, fc=fc, n=n: e.activation(out=sig[sb_][:, :n], in_=kb.bank(pg, n), func=AF.Sigmoid, bias=vecs[:, 1, fc:fc + 1]),
                                 reads=['vecs'], writes=[('ps', pg), ('sig', sb_)])
                            P.op('dve', lambda e, sb_=sb_, pa=pa, fc=fc, g=g, n=n: e.scalar_tensor_tensor(out=uT[:, fc, g * 512:g * 512 + n], in0=kb.bank(pa, n), scalar=vecs[:, 0, fc:fc + 1],
                                                                                                      in1=sig[sb_][:, :n], op0=ALU.add, op1=ALU.mult),
                                 reads=['vecs', ('sig', sb_)], writes=[('ps', pa), ('uT', fc, g)])
                        P.op('pool', lambda e, fc=fc: e.tensor_scalar(out=uT[:, fc, 0:128], in0=uT[:, fc, 0:128], scalar1=mask[:, 0:1], scalar2=None, op0=ALU.mult),
                             reads=['mask'], writes=[('uT', fc, 0)])
                        P.op('pool', lambda e, fc=fc: e.tensor_scalar(out=uT[:, fc, 2176:2304], in0=uT[:, fc, 2176:2304], scalar1=mask[:, 1:2], scalar2=None, op0=ALU.mult),
                             reads=['mask'], writes=[('uT', fc, 4)])
            with kb.phase():
                dg = kb.sb("dg", [128, 8, 31, 128], BF16)
                cT = kb.sb("cT", [128, 8, 512], F32)
                cbf = [kb.sb("cbf%d" % i, [128, 512], BF16) for i in range(2)]
                c2 = [kb.sb("c2%d" % i, [128, 512], BF16) for i in range(2)]
                mean = kb.sb("mean", [128, 512], F32); msq = kb.sb("msq", [128, 512], F32); rstd = kb.sb("rstd", [128, 512], F32)
                tt_ = [kb.sb("tt%d" % i, [128, 512], F32) for i in range(2)]
                for fc in range(8):
                    for j in range(31):
                        P.op('pool', lambda e, fc=fc, j=j: e.tensor_scalar(out=dg[:, fc, j, :], in0=identb[:], scalar1=wdw[:, fc, j:j + 1], scalar2=None, op0=ALU.mult),
                             reads=['identb', 'wdw'], writes=[('dg', fc)])
                for tg in range(4):
                    for fc in range(8):
                        pb = kb.nextrot('projbank', 2)
                        for j in range(31):
                            o = 128 + tg * 512 + j - 15
                            P.op('pe', lambda e, fc=fc, j=j, o=o, pb=pb: e.matmul(kb.bank(pb), lhsT=dg[:, fc, j, :], rhs=uT[:, fc, o:o + 512], start=(j == 0), stop=(j == 30)),
                                 reads=[('dg', fc)] + [('uT', fc, gg) for gg in range(5)], writes=[('ps', pb)])
                        P.op('act', lambda e, fc=fc, pb=pb: e.activation(out=cT[:, fc, :], in_=kb.bank(pb), func=AF.Identity, bias=vecs[:, 2, fc:fc + 1]),
                             reads=['vecs'], writes=[('ps', pb), ('cT', fc)])
                        r = kb.nextrot('cbf', 2)
                        P.op('dve', lambda e, fc=fc, r=r: e.tensor_copy(out=cbf[r][:], in_=cT[:, fc, :]), reads=[('cT', fc)], writes=[('cbf', r)])
                        P.op('act', lambda e, fc=fc, r=r: e.activation(out=c2[r][:], in_=cT[:, fc, :], func=AF.Square), reads=[('cT', fc)], writes=[('c2', r)])
                        P.op('pe', lambda e, fc=fc, r=r: e.matmul(kb.bank(6), lhsT=kb.ones_bf[:], rhs=cbf[r][:], start=(fc == 0), stop=(fc == 7)), reads=[('cbf', r), 'ones_bf'], writes=[('ps', 6)])
                        P.op('pe', lambda e, fc=fc, r=r: e.matmul(kb.bank(7), lhsT=kb.ones_bf[:], rhs=c2[r][:], start=(fc == 0), stop=(fc == 7)), reads=[('c2', r), 'ones_bf'], writes=[('ps', 7)])
                    P.op('act', lambda e: e.activation(out=mean[:], in_=kb.bank(6), func=AF.Copy, scale=1.0 / D), writes=[('ps', 6), 'mean'])
                    P.op('dve', lambda e: e.tensor_tensor(out=msq[:], in0=mean[:], in1=mean[:], op=ALU.mult), reads=['mean'], writes=['msq'])
                    P.op('dve', lambda e: e.scalar_tensor_tensor(out=msq[:], in0=kb.bank(7), scalar=1.0 / D, in1=msq[:], op0=ALU.mult, op1=ALU.subtract), writes=[('ps', 7), 'msq'])
                    P.op('act', lambda e: e.activation(out=rstd[:], in_=msq[:], func=AF.Ln, bias=kb.eps_t[:]), reads=['msq', 'eps_t'], writes=['rstd'])
                    P.op('act', lambda e: e.activation(out=rstd[:], in_=rstd[:], func=AF.Exp, scale=-0.5), writes=['rstd'])
                    for fc in range(8):
                        r = kb.nextrot('tt', 2)
                        P.op('dve', lambda e, fc=fc, r=r: e.tensor_tensor(out=tt_[r][:], in0=cT[:, fc, :], in1=mean[:], op=ALU.subtract), reads=[('cT', fc), 'mean'], writes=[('tt', r)])
                        P.op('dve', lambda e, r=r: e.tensor_tensor(out=tt_[r][:], in0=tt_[r][:], in1=rstd[:], op=ALU.mult), reads=['rstd'], writes=[('tt', r)])
                        P.op('act', lambda e, fc=fc, r=r, tg=tg: e.activation(out=vT[:, fc, tg * 512:(tg + 1) * 512], in_=tt_[r][:], func=AF.Silu, scale=vecs[:, 3, fc:fc + 1], bias=vecs[:, 4, fc:fc + 1]),
                             reads=[('tt', r), 'vecs'], writes=[('vT', tg)])
        with kb.phase():
            xtmp = kb.sb("xtmp", [128, 8, 512], F32)
            stage = [kb.sb("stage%d" % i, [128, 1024], F32) for i in range(2)]
            ostage = [kb.sb("ostage%d" % i, [128, 1024], F32) for i in range(2)]
            wo = kb.sb("wo", [128, 8, 1024], BF16)
            kb.load_w(wo, 'wo', wpw2_d, 0, KC, 0, 1024)
            for g in range(4):
                kb.load_xT(xh_d[128 + g * 512:128 + (g + 1) * 512, :], 512, xtmp, 'xtmp', stage)
                resid_proj(kb, vT, 'vT', g * 512, 512, wo, 'wo', xtmp, 'xtmp', 0, mL['G1'], mL['key'], bG=bG)
                kb.store_x(xtmp, 'xtmp', 512, out_d[g * 512:(g + 1) * 512, :], ostage)
    mlp_tail(kb, out_d, out_d, NTOK, mL, w1_d, w2_d)
    if own:
        P.close()
    return nc


def prep_l2(inp, x2, b, q):
    d = common_inputs(inp, 2, b)
    if x2 is not None:
        xh = np.zeros((2304, D), np.float32)
        lo = q * 2048 - 128
        hi = lo + 2304
        s0, s1 = max(lo, 0), min(hi, 8192)
        xh[s0 - lo:s1 - lo] = x2[b][s0:s1]
        d['xh'] = xh
    w = inp['cv_w_pw1'][0]
    d['wpw1'] = np.ascontiguousarray(np.stack([np.concatenate([w[:, c * 128:(c + 1) * 128], w[:, 1024 + c * 128:1024 + (c + 1) * 128]], axis=1) for c in range(8)]))
    bp = inp['cv_b_pw1'][0]
    d['vecs'] = np.ascontiguousarray(np.stack([fm(bp[:1024]), fm(bp[1024:]), fm(inp['cv_b_dw'][0]), fm(inp['cv_ln_g'][0]), fm(inp['cv_ln_b'][0]), fm(inp['cv_b_pw2'][0])], axis=1))
    d['wdw'] = np.ascontiguousarray(inp['cv_w_dw'][0].T.reshape(8, 128, 31).transpose(1, 0, 2))
    m = np.ones((128, 2), np.float32)
    if q == 0:
        m[:, 0] = 0.0
    if q == 3:
        m[:, 1] = 0.0
    d['mask'] = m
    d['wpw2'] = np.ascontiguousarray(inp['cv_w_pw2'][0])
    return d


def build_l3a(kb=None, io=None):
    own = kb is None
    if own:
        kb = KB()
    io = io or {}
    kb.pfx = '' if own else 'l3a_'
    P, nc = kb.P, kb.nc
    NTOK = 2048
    x_d = io.get("x") or kb.din("x", [NTOK, D])
    cv_d = kb.din("cv", [128, 8, 2]); adaw_d = kb.din("adaw", [D, 6144]); adab_d = kb.din("adab", [2, 6144])
    g1_d = kb.din("g1", [128, 8]); g2_d = kb.din("g2", [128, 8])
    csd_d = kb.din("csd", [256, 512])
    pq_d = io.get("pq") or kb.dout("pq", [NTOK, 2048], BF16)
    modsT = kb.sb("modsT", [128, 48, 2], F32)
    mvL = kb.sb("mvL", [128, 6, 8], F32)
    with kb.phase():
        kb.mods(cv_d, adaw_d, adab_d, modsT)
        mL = kb.mod_vectors(modsT, g1_d, g2_d, 0, mvL, "L")
    with kb.phase():
        hT = kb.sb("hT", [128, 8, NTOK], BF16)
        csd = kb.sb("csd", [128, 2, 512], BF16)
        kb.load_w(csd, 'csd', csd_d, 0, 2, 0, 512)
        with kb.phase():
            xtmp = kb.sb("xtmp", [128, 8, 512], F32)
            stage = [kb.sb("stage%d" % i, [128, 1024], F32) for i in range(2)]
            scr = norm_scratch(kb)
            for g in range(4):
                kb.load_xT(x_d[g * 512:(g + 1) * 512, :], 512, xtmp, 'xtmp', stage)
                kb.norm_mod(xtmp, 'xtmp', 512, mL['A1'], mL['B1'], mL['key'], hT, 'hT', scr, t0=0, ht0=g * 512)
        with kb.phase():
            pqs = [kb.sb("pqs%d" % i, [128, 2048], BF16) for i in range(2)]
            for tt in range(16):
                ob = tt % 2
                for grp in range(4):
                    pb = kb.nextrot('projbank', 4)
                    for kl in range(2):
                        kc = grp * 2 + kl
                        P.op('pe', lambda e, kc=kc, kl=kl, tt=tt, pb=pb: e.matmul(kb.bank(pb), lhsT=hT[:, kc, tt * 128:(tt + 1) * 128], rhs=csd[:, kl, :], start=(kl == 0), stop=(kl == 1)),
                             reads=[(('csd'), kl), ('hT', tt // 4)], writes=[('ps', pb)])
                    if grp % 2 == 0:
                        P.op('act', lambda e, ob=ob, grp=grp, pb=pb: e.activation(out=pqs[ob][:, grp * 512:(grp + 1) * 512], in_=kb.bank(pb), func=AF.Copy), writes=[('ps', pb), ('pqs', ob, grp)])
                    else:
                        P.op('dve', lambda e, ob=ob, grp=grp, pb=pb: e.tensor_copy(out=pqs[ob][:, grp * 512:(grp + 1) * 512], in_=kb.bank(pb)), writes=[('ps', pb), ('pqs', ob, grp)])
                P.dma('sp', lambda e, ob=ob, tt=tt: e.dma_start(out=pq_d[tt * 128:(tt + 1) * 128, :], in_=pqs[ob][:]), reads=[('pqs', ob, g_) for g_ in range(4)])
    if own:
        P.close()
    return nc


def build_l3b(kb=None, io=None):
    own = kb is None
    if own:
        kb = KB()
    io = io or {}
    kb.pfx = '' if own else 'l3b_'
    P, nc = kb.P, kb.nc
    NTOK = 2048
    x_d = io.get("x") or kb.din("x", [NTOK, D])
    cv_d = kb.din("cv", [128, 8, 2]); adaw_d = kb.din("adaw", [D, 6144]); adab_d = kb.din("adab", [2, 6144])
    g1_d = kb.din("g1", [128, 8]); g2_d = kb.din("g2", [128, 8])
    pq_d = io.get("pq") or kb.din("pq", [8192, 2048], BF16)
    cn_d = kb.din("cn", [8192, NTOK], BF16); sn_d = kb.din("sn", [8192, NTOK], BF16)
    ftw_d = kb.din("ftw", [D, D]); vecs_d = kb.din("vecs", [128, 2, 8])
    w1_d = kb.din("w1", [D, DFF]); w2_d = kb.din("w2", [DFF, D])
    out_d = io.get("out") or kb.dout("out", [NTOK, D])
    modsT = kb.sb("modsT", [128, 48, 2], F32)
    mvL = kb.sb("mvL", [128, 6, 8], F32)
    vecs = kb.sb("vecs", [128, 2, 8], F32)
    bG = kb.sb("bG", [128, 8], F32)
    zeros = kb.sb("zeros", [128, 8], F32)
    with kb.phase():
        kb.mods(cv_d, adaw_d, adab_d, modsT)
        mL = kb.mod_vectors(modsT, g1_d, g2_d, 0, mvL, "L")
        P.dma('sp', lambda e: e.dma_start(out=vecs[:], in_=vecs_d), writes=['vecs'])
        P.op('dve', lambda e: e.tensor_tensor(out=bG[:], in0=vecs[:, 0, :], in1=mL['G1'], op=ALU.mult), reads=['vecs', mL['key']], writes=['bG'])
        P.op('pool', lambda e: e.memset(zeros[:], 0.0), writes=['zeros'])
    with kb.phase():
        zT = kb.sb("zT", [128, 8, NTOK], BF16)
        with kb.phase():
            pqb = [kb.sb("pqb%d" % i, [128, 2048], BF16) for i in range(3)]
            tb = [kb.sb("tb%d" % i, [128, 2, 512], BF16) for i in range(3)]
            tokmap = io.get('tokmap') or (lambda nt: nt * 128)
            for kg in range(4):
                for nt in range(64):
                    tk = tokmap(nt)
                    r = kb.nextrot('pqb', 3)
                    P.dma('sp', lambda e, r=r, nt=nt: e.dma_start(out=pqb[r][:], in_=pq_d[nt * 128:(nt + 1) * 128, :]), writes=[('pqb', r)])
                    P.dma('sp', lambda e, r=r, tk=tk, kg=kg: e.dma_start(out=tb[r][:, 0, :], in_=cn_d[tk:tk + 128, kg * 512:(kg + 1) * 512]), writes=[('tb', r, 0)])
                    P.dma('sp', lambda e, r=r, tk=tk, kg=kg: e.dma_start(out=tb[r][:, 1, :], in_=sn_d[tk:tk + 128, kg * 512:(kg + 1) * 512]), writes=[('tb', r, 1)])
                    for fz in range(8):
                        grp, jh = fz // 2, fz % 2
                        P.op('pe', lambda e, r=r, fz=fz, grp=grp, jh=jh, nt=nt: e.matmul(kb.bank(fz), lhsT=pqb[r][:, grp * 512 + jh * 128:grp * 512 + jh * 128 + 128], rhs=tb[r][:, 0, :],
                                                                                      start=(nt == 0), stop=False),
                             reads=[('pqb', r), ('tb', r, 0)], writes=[('ps', fz)])
                        P.op('pe', lambda e, r=r, fz=fz, grp=grp, jh=jh, nt=nt: e.matmul(kb.bank(fz), lhsT=pqb[r][:, grp * 512 + 256 + jh * 128:grp * 512 + 256 + jh * 128 + 128], rhs=tb[r][:, 1, :],
                                                                                      start=False, stop=(nt == 63)),
                             reads=[('pqb', r), ('tb', r, 1)], writes=[('ps', fz)])
                for fz in range(8):
                    if fz % 2 == 0:
                        P.op('act', lambda e, fz=fz, kg=kg: e.activation(out=zT[:, fz, kg * 512:(kg + 1) * 512], in_=kb.bank(fz), func=AF.Copy), writes=[('ps', fz), ('zT', kg)])
                    else:
                        P.op('dve', lambda e, fz=fz, kg=kg: e.tensor_copy(out=zT[:, fz, kg * 512:(kg + 1) * 512], in_=kb.bank(fz)), writes=[('ps', fz), ('zT', kg)])
        with kb.phase():
            xtmp = kb.sb("xtmp", [128, 8, 512], F32)
            stage = [kb.sb("stage%d" % i, [128, 1024], F32) for i in range(2)]
            ostage = [kb.sb("ostage%d" % i, [128, 1024], F32) for i in range(2)]
            wo = kb.sb("wo", [128, 8, 1024], BF16)
            kb.load_w(wo, 'wo', ftw_d, 0, KC, 0, 1024)
            for g in range(4):
                kb.load_xT(x_d[g * 512:(g + 1) * 512, :], 512, xtmp, 'xtmp', stage)
                resid_proj(kb, zT, 'zT', g * 512, 512, wo, 'wo', xtmp, 'xtmp', 0, mL['G1'], mL['key'], bG=bG)
                kb.store_x(xtmp, 'xtmp', 512, out_d[g * 512:(g + 1) * 512, :], ostage)
    mlp_tail(kb, out_d, out_d, NTOK, mL, w1_d, w2_d, final=(vecs[:, 1, :], zeros[:], 'vecs'))
    if own:
        P.close()
    return nc


def prep_l3a(inp, x3, b, q):
    d = common_inputs(inp, 3, b)
    for k in ('w1', 'w2'):
        d.pop(k)
    if x3 is not None:
        d['x'] = np.ascontiguousarray(x3[b][q * 2048:(q + 1) * 2048])
    dd = np.arange(256)[:, None].astype(np.int64)
    jj = np.arange(256)[None, :].astype(np.int64)
    ang = 2.0 * np.pi * ((dd * jj) % 256).astype(np.float64) / 256.0
    d['csd'] = np.ascontiguousarray(np.concatenate([np.cos(ang) / 16.0, np.sin(ang) / 16.0], axis=1).astype(np.float32))
    return d


_DFT_CACHE = {}


def seq_dft_tables(q):
    if q not in _DFT_CACHE:
        import ml_dtypes
        n = np.arange(8192, dtype=np.int64)[:, None]
        k = np.arange(q * 2048, (q + 1) * 2048, dtype=np.int64)[None, :]
        ang = 2.0 * np.pi * ((n * k) % 8192).astype(np.float64) / 8192.0
        s = 1.0 / np.sqrt(8192.0)
        _DFT_CACHE[q] = (np.ascontiguousarray((np.cos(ang) * s).astype(np.float32).astype(ml_dtypes.bfloat16)),
                         np.ascontiguousarray((-np.sin(ang) * s).astype(np.float32).astype(ml_dtypes.bfloat16)))
    return _DFT_CACHE[q]


def prep_l3b(inp, x3, pq_b, b, q):
    d = common_inputs(inp, 3, b)
    if x3 is not None:
        d['x'] = np.ascontiguousarray(x3[b][q * 2048:(q + 1) * 2048])
        d['pq'] = pq_b
    d['cn'], d['sn'] = seq_dft_tables(q)
    d['ftw'] = np.ascontiguousarray(inp['ft_w'][0])
    d['vecs'] = np.ascontiguousarray(np.stack([fm(inp['ft_b'][0]), fm(inp['final_g'])], axis=1))
    return d


CORES = [(b, q) for b in range(2) for q in range(4)]


def _run(nc, maps):
    res = run_bass_kernel_spmd(nc, maps, core_ids=list(range(8)))
    return res.results


def kernel_unfused(**inputs):
    inp = {k: np.asarray(v) for k, v in inputs.items()}
    r = _run(build_l0(), [prep_l0(inp, b, q) for b, q in CORES])
    x1 = np.stack([np.concatenate([r[b * 4 + q]["out"] for q in range(4)], axis=0) for b in range(2)])
    hctx1 = np.stack([r[b * 4]["hctx"] for b in range(2)])
    r = _run(build_l1(), [prep_l1(inp, x1, hctx1, b, q) for b, q in CORES])
    x2 = np.stack([np.concatenate([r[b * 4 + q]["out"] for q in range(4)], axis=0) for b in range(2)])
    r = _run(build_l2(), [prep_l2(inp, x2, b, q) for b, q in CORES])
    x3 = np.stack([np.concatenate([r[b * 4 + q]["out"] for q in range(4)], axis=0) for b in range(2)])
    r = _run(build_l3a(), [prep_l3a(inp, x3, b, q) for b, q in CORES])
    pq = [np.ascontiguousarray(np.concatenate([r[b * 4 + q]["pq"] for q in range(4)], axis=0)) for b in range(2)]
    r = _run(build_l3b(), [prep_l3b(inp, x3, pq[b], b, q) for b, q in CORES])
    out = np.stack([np.concatenate([r[b * 4 + q]["out"] for q in range(4)], axis=0) for b in range(2)])
    return out.astype(np.float32)


RG = [[0, 1, 2, 3], [4, 5, 6, 7]]


def halo_exchange(kb, src, dst, H, sel, tag):
    P, nc = kb.P, kb.nc
    bF = kb.dint("bounceF" + tag, [H, D]); bL = kb.dint("bounceL" + tag, [H, D])
    gF = kb.dint("gathF" + tag, [4 * H, D]); gL = kb.dint("gathL" + tag, [4 * H, D])
    with kb.phase():
        P.dma('pool', lambda e: e.dma_start(out=bF, in_=src[0:H, :]), writes=['bF'])
        P.dma('pool', lambda e: e.dma_start(out=bL, in_=src[2048 - H:2048, :]), writes=['bL'])
        for g in range(4):
            P.dma('pool', lambda e, g=g: e.dma_start(out=dst[H + g * 512:H + (g + 1) * 512, :], in_=src[g * 512:(g + 1) * 512, :]), writes=[('dst', g)])
        P.coll(lambda e: e.collective_compute("AllGather", ALU.bypass, replica_groups=RG, ins=[bF.opt()], outs=[gF.opt()]), reads=['bF'], writes=['gF'])
        P.coll(lambda e: e.collective_compute("AllGather", ALU.bypass, replica_groups=RG, ins=[bL.opt()], outs=[gL.opt()]), reads=['bL'], writes=['gL'])
        cand = [kb.sb("cand%d" % i, [128, 4, 1024], F32) for i in range(2)]
        acc = [kb.sb("hacc%d" % i, [128, 1024], F32) for i in range(2)]
        for side in range(2):
            gsrc, gkey = (gL, 'gL') if side == 0 else (gF, 'gF')
            for t in range(H // 128):
                r_ = kb.nextrot('cand', 2)
                off = t * 128
                srcv = gsrc.rearrange("(r n) c -> n r c", r=4)[off:off + 128, :, :]
                P.dma('sp', lambda e, r_=r_, srcv=srcv: e.dma_start(out=cand[r_][:], in_=srcv), reads=[gkey], writes=[('cand', r_)])
                P.op('dve', lambda e, r_=r_, side=side: e.tensor_scalar(out=acc[r_][:], in0=cand[r_][:, 0, :], scalar1=sel[:, side * 4:side * 4 + 1], scalar2=None, op0=ALU.mult),
                     reads=[('cand', r_), 'sel'], writes=[('hacc', r_)])
                for r in range(1, 4):
                    P.op('dve', lambda e, r_=r_, side=side, r=r: e.scalar_tensor_tensor(out=acc[r_][:], in0=cand[r_][:, r, :], scalar=sel[:, side * 4 + r:side * 4 + r + 1], in1=acc[r_][:],
                                                                                 op0=ALU.mult, op1=ALU.add),
                         reads=[('cand', r_), 'sel'], writes=[('hacc', r_)])
                d0 = (0 if side == 0 else H + 2048) + t * 128
                P.dma('sp', lambda e, r_=r_, d0=d0: e.dma_start(out=dst[d0:d0 + 128, :], in_=acc[r_][:]), reads=[('hacc', r_)], writes=[('dsth', side, t)])


def pq_tokmap(nt):
    c, r, half = nt // 8, (nt % 8) // 2, nt % 2
    return r * 2048 + c * 256 + half * 128


def build_fused():
    kb = KB()
    P, nc = kb.P, kb.nc
    xb_d = kb.din("xb", [8192, D]); ctx_d = kb.din("ctx", [256, D]); sel_d = kb.din("sel", [128, 8])
    out_d = kb.dout("out", [2048, D])
    sel = kb.sb("sel", [128, 8], F32)
    P.dma('sp', lambda e: e.dma_start(out=sel[:], in_=sel_d), writes=['sel'])
    xa = kb.dint("xa", [2048, D]); hc1 = kb.dint("hc1", [256, D])
    build_l0(kb, dict(xb=xb_d, ctx=ctx_d, out=xa, hctx=hc1))
    xh1 = kb.dint("xh1", [2560, D])
    halo_exchange(kb, xa, xh1, 256, sel, "1")
    xb2 = kb.dint("xb2", [2048, D])
    build_l1(kb, dict(xh=xh1, ctx=hc1, out=xb2))
    xh2 = kb.dint("xh2", [2304, D])
    halo_exchange(kb, xb2, xh2, 128, sel, "2")
    xc = kb.dint("xc", [2048, D])
    build_l2(kb, dict(xh=xh2, out=xc))
    pqo = kb.dint("pqo", [2048, 2048], BF16); pqg = kb.dint("pqg", [8192, 2048], BF16)
    build_l3a(kb, dict(x=xc, pq=pqo))
    with kb.phase():
        for c in range(8):
            P.coll(lambda e, c=c: e.collective_compute("AllGather", ALU.bypass, replica_groups=RG, ins=[pqo[c * 256:(c + 1) * 256, :].opt()], outs=[pqg[c * 1024:(c + 1) * 1024, :].opt()]),
                   writes=[('pqg', c)])
    build_l3b(kb, dict(x=xc, pq=pqg, out=out_d, tokmap=pq_tokmap))
    P.close()
    return nc


def prep_fused(inp, b, q):
    d0 = prep_l0(inp, b, q)
    out = {k: d0[k] for k in ('ident', 'cv', 'xb', 'ctx')}
    sel = np.zeros((128, 8), np.float32)
    if q > 0:
        sel[:, q - 1] = 1.0
    if q < 3:
        sel[:, 4 + q + 1] = 1.0
    out['sel'] = sel
    for pfx, dd in (('l0_', d0), ('l1_', prep_l1(inp, None, None, b, q)), ('l2_', prep_l2(inp, None, b, q)),
                    ('l3a_', prep_l3a(inp, None, b, q)), ('l3b_', prep_l3b(inp, None, None, b, q))):
        for k, v in dd.items():
            if k in ('ident', 'cv', 'xb', 'ctx'):
                continue
            out[pfx + k] = v
    return out


def kernel(**inputs):
    inp = {k: np.asarray(v) for k, v in inputs.items()}
    r = _run(build_fused(), [prep_fused(inp, b, q) for b, q in CORES])
    out = np.stack([np.concatenate([r[b * 4 + q]["out"] for q in range(4)], axis=0) for b in range(2)])
    return out.astype(np.float32)
```

```python
import contextlib
import numpy as np
from concourse.bass_utils import run_bass_kernel_spmd
import concourse.bass as bass
import concourse.mybir as mybir

F32 = mybir.dt.float32
BF16 = mybir.dt.bfloat16
AF = mybir.ActivationFunctionType
ALU = mybir.AluOpType
AX = mybir.AxisListType

ENGS = ('pe', 'act', 'dve', 'pool', 'sp')
NDMASLOT = 8


class Op:
    __slots__ = ('eng', 'fn', 'deps', 'sig', 'sigval', 'dma', 'dslot', 'dval', 'prev_slot_op', 'seq', 'dinc')

    def __init__(self, eng, fn, dma):
        self.eng = eng
        self.fn = fn
        self.deps = []
        self.sig = False
        self.sigval = 0
        self.dma = dma
        self.dslot = None
        self.dval = 0
        self.prev_slot_op = None
        self.dinc = 16


class Prog:
    def __init__(self, nc, same_engine_sync=True):
        self.nc = nc
        self.same = same_engine_sync
        self.E = {'pe': nc.tensor, 'act': nc.scalar, 'dve': nc.vector, 'pool': nc.gpsimd, 'sp': nc.sync}
        self.sem = {}
        self._ctx = []
        for e in ('pe', 'act', 'dve', 'pool'):
            self.sem[e] = self._enter(nc.semaphore('prog_' + e))
        self.dsem = {}
        for q in ('sp', 'pool', 'act'):
            self.dsem[q] = [self._enter(nc.semaphore('dma_%s_%d' % (q, i))) for i in range(NDMASLOT)]
        self.ccsem = self._enter(nc.semaphore('ccsem'))
        self.cccnt = 0
        self.ccscratch = self._enter(nc.sbuf_tensor('ccscratch', [128, 8], F32))
        self.cnt = {e: 0 for e in ENGS}
        self.dcnt = {q: 0 for q in ('sp', 'pool', 'act')}
        self.last_dma = {q: [None] * NDMASLOT for q in ('sp', 'pool', 'act')}
        self.nops = 0
        self._reset_phase()

    def _enter(self, cm):
        v = cm.__enter__()
        self._ctx.append(cm)
        return v

    def alloc(self, cm):
        return self._enter(cm)

    def close(self):
        for cm in reversed(self._ctx):
            cm.__exit__(None, None, None)
        self._ctx = []

    def _reset_phase(self):
        self.ops = {e: [] for e in ENGS}
        self.order = []
        self.last_writer = {}
        self.readers = {}

    def _record(self, eng, fn, reads, writes, dma):
        op = Op(eng, fn, dma)
        deps = set()
        for k in reads:
            w = self.last_writer.get(k)
            if w is not None:
                deps.add(w)
        for k in writes:
            w = self.last_writer.get(k)
            if w is not None:
                deps.add(w)
            for r in self.readers.get(k, ()):
                deps.add(r)
        deps.discard(op)
        for k in reads:
            self.readers.setdefault(k, []).append(op)
        for k in writes:
            self.last_writer[k] = op
            self.readers[k] = []
        best = {}
        out = []
        for d in deps:
            if d.dma:
                out.append(d)
                continue
            if d.eng == eng and not dma:
                if eng == 'pe' or not self.same:
                    continue
            b = best.get(d.eng)
            if b is None or d.seq > b.seq:
                best[d.eng] = d
        out.extend(best.values())
        op.deps = out
        for d in out:
            if not d.dma:
                d.sig = True
        op.seq = len(self.order)
        self.ops[eng].append(op)
        self.order.append(op)
        self.nops += 1
        return op

    def op(self, eng, fn, reads=(), writes=()):
        return self._record(eng, fn, reads, writes, False)

    def coll(self, fn, reads=(), writes=()):
        def wrapped(e):
            ins = fn(e)
            self.cccnt += 1
            ins.then_inc(self.ccsem)
            e.wait_ge(self.ccsem, self.cccnt)
            return e.memset(self.ccscratch[:], 0.0)
        return self._record('pool', wrapped, reads, writes, False)

    def dma(self, q, fn, reads=(), writes=()):
        op = self._record(q, fn, reads, writes, True)
        op.dinc = 16
        j = self.dcnt[q]
        self.dcnt[q] += 1
        slot = j % NDMASLOT
        op.dslot = self.dsem[q][slot]
        op.dval = 16 * (j // NDMASLOT + 1)
        op.prev_slot_op = self.last_dma[q][slot]
        self.last_dma[q][slot] = op
        return op

    def flush(self, final_wait=()):
        for e in ENGS:
            c = self.cnt[e]
            for op in self.ops[e]:
                if op.sig and not op.dma:
                    c += 1
                    op.sigval = c
            self.cnt[e] = c
        ops = self.ops
        sem = self.sem
        lastd = {q: list(v) for q, v in self.last_dma.items()}
        anyd = any(d is not None for v in lastd.values() for d in v)

        def run(engname):
            def body(eng):
                known = {e: 0 for e in ENGS}
                kd = {}
                for op in ops[engname]:
                    if op.dma and op.prev_slot_op is not None:
                        p = op.prev_slot_op
                        key = id(p.dslot)
                        if kd.get(key, 0) < p.dval:
                            eng.wait_ge(p.dslot, p.dval)
                            kd[key] = p.dval
                    for d in op.deps:
                        if d.dma:
                            key = id(d.dslot)
                            if kd.get(key, 0) < d.dval:
                                eng.wait_ge(d.dslot, d.dval)
                                kd[key] = d.dval
                        else:
                            if known[d.eng] < d.sigval:
                                eng.wait_ge(sem[d.eng], d.sigval)
                                known[d.eng] = d.sigval
                    ins = op.fn(eng)
                    if op.dma:
                        if op.dinc == 1:
                            ins.then_inc(op.dslot)
                        else:
                            ins.then_inc(op.dslot, 16)
                    elif op.sig:
                        ins.then_inc(sem[op.eng], 1)
                if engname == 'sp':
                    for q in lastd:
                        for d in lastd[q]:
                            if d is not None:
                                eng.wait_ge(d.dslot, d.dval)
            return body

        with self.nc.Block() as block:
            if ops['sp'] or anyd:
                block.sync(run('sp'))
            if ops['pe']:
                block.tensor(run('pe'))
            if ops['act']:
                block.scalar(run('act'))
            if ops['dve']:
                block.vector(run('dve'))
            if ops['pool']:
                block.gpsimd(run('pool'))
        for q in self.last_dma:
            self.last_dma[q] = [None] * NDMASLOT
        self._reset_phase()
D = 1024
KC = 8
DFF = 4096
EPS = 1e-6


class KB:
    def __init__(self, nt=2048):
        self.nc = nc = bass.Bass("TRN2", target_bir_lowering=False)
        self.P = P = Prog(nc)
        self.NT = nt
        self.stack = []
        self.uid = 0
        self.pfx = ''
        self.shared = {}
        self.ps = P.alloc(nc.psum_tensor("ps", [128, 4096], F32))
        self.ident = self.sb("ident_sb", [128, 128], F32)
        self.ones_bf = self.sb("ones_bf", [128, 128], BF16)
        self.eps_t = self.sb("eps_t", [128, 1], F32)
        self.rot = {}
        ident_d = self.nc.dram_tensor("ident", [128, 128], F32, kind="ExternalInput").ap()
        P.dma('sp', lambda e: e.dma_start(out=self.ident[:], in_=ident_d), writes=['ident'])
        P.op('pool', lambda e: e.memset(self.ones_bf[:], 1.0), writes=['ones_bf'])
        P.op('pool', lambda e: e.memset(self.eps_t[:], EPS), writes=['eps_t'])

    def sb(self, name, shape, dt):
        self.uid += 1
        name = "s%d_%s" % (self.uid, name)
        if self.stack:
            return self.stack[-1].enter_context(self.nc.sbuf_tensor(name, shape, dt))
        return self.P.alloc(self.nc.sbuf_tensor(name, shape, dt))

    @contextlib.contextmanager
    def phase(self):
        st = contextlib.ExitStack()
        self.stack.append(st)
        try:
            yield
            self.P.flush()
        finally:
            self.stack.pop()
            st.close()

    def din(self, name, shape, dt=F32):
        if name in ('cv',):
            if name not in self.shared:
                self.shared[name] = self.nc.dram_tensor(name, list(shape), dt, kind="ExternalInput").ap()
            return self.shared[name]
        return self.nc.dram_tensor(self.pfx + name, list(shape), dt, kind="ExternalInput").ap()

    def dint(self, name, shape, dt=F32):
        return self.nc.dram_tensor(name, list(shape), dt).ap()

    def dout(self, name, shape, dt=F32):
        return self.nc.dram_tensor(name, list(shape), dt, kind="ExternalOutput").ap()

    def bank(self, i, n=512, p0=0, p1=128):
        return self.ps[p0:p1, i * 512:i * 512 + n]

    def nextrot(self, name, n):
        v = self.rot.get(name, 0)
        self.rot[name] = v + 1
        return v % n

    def mods(self, cv_d, adaw_d, adab_d, modsT):
        P, nc = self.P, self.nc
        cv = self.sb("cv", [128, 8, 2], F32)
        sT = self.sb("sT", [128, 8, 2], F32)
        mrow = self.sb("mrow", [2, 6144], F32)
        adab = self.sb("adab", [2, 6144], F32)
        wst = [self.sb("wst%d" % i, [128, 8, 512], F32) for i in range(2)]
        P.dma('sp', lambda e: e.dma_start(out=cv[:], in_=cv_d), writes=['cv'])
        P.dma('sp', lambda e: e.dma_start(out=adab[:], in_=adab_d), writes=['adab'])
        P.op('act', lambda e: e.activation(out=sT[:], in_=cv[:], func=AF.Silu), reads=['cv'], writes=['sT'])
        for cg in range(12):
            b = cg % 2
            src = adaw_d[:, cg * 512:(cg + 1) * 512].rearrange("(kc p) c -> p kc c", p=128)
            for h in range(2):
                P.dma('sp', lambda e, b=b, h=h, src=src: e.dma_start(out=wst[b][:, h * 4:(h + 1) * 4, :], in_=src[:, h * 4:(h + 1) * 4, :]),
                      writes=[('wst', b, h)])
            pb = 6 + (cg % 2)
            for kc in range(KC):
                P.op('pe', lambda e, b=b, kc=kc, pb=pb: e.matmul(self.bank(pb, 512, 0, 2), lhsT=sT[:, kc, :], rhs=wst[b][:, kc, :],
                                                                 start=(kc == 0), stop=(kc == KC - 1)),
                     reads=['sT', ('wst', b, kc // 4)], writes=[('ps', pb)])
            P.op('dve', lambda e, cg=cg, pb=pb: e.tensor_tensor(out=mrow[:, cg * 512:(cg + 1) * 512], in0=self.bank(pb, 512, 0, 2),
                                                                in1=adab[:, cg * 512:(cg + 1) * 512], op=ALU.add),
                 reads=['adab'], writes=[('ps', pb), ('mrow', cg)])
        for ch in range(48):
            P.op('pe', lambda e, ch=ch: e.transpose(self.ps[:, 6 * 512 + ch * 2:6 * 512 + ch * 2 + 2], mrow[0:2, ch * 128:(ch + 1) * 128], self.ident[0:2, 0:2]),
                 reads=[('mrow', ch // 4), 'ident'], writes=[('ps', 6)])
        P.op('dve', lambda e: e.tensor_copy(out=modsT[:].rearrange("p a b -> p (a b)"), in_=self.ps[:, 6 * 512:6 * 512 + 96]),
             writes=[('ps', 6), 'modsT'])
        return modsT

    def mod_vectors(self, modsT, g1_d, g2_d, col, mv, tag):
        P = self.P
        gg = self.sb("gg" + tag, [128, 2, 8], F32)
        P.dma('sp', lambda e: e.dma_start(out=gg[:, 0, :], in_=g1_d), writes=['gg' + tag])
        P.dma('sp', lambda e: e.dma_start(out=gg[:, 1, :], in_=g2_d), writes=['gg' + tag])
        P.op('dve', lambda e: e.tensor_copy(out=mv[:].rearrange("p m k -> p (m k)"), in_=modsT[:, :, col]), reads=['modsT'], writes=['mv' + tag])
        for j, m in ((0, 1), (1, 4)):
            P.op('dve', lambda e, j=j, m=m: e.scalar_tensor_tensor(out=mv[:, m, :], in0=mv[:, m, :], scalar=1.0, in1=gg[:, j, :],
                                                                   op0=ALU.add, op1=ALU.mult),
                 reads=['gg' + tag], writes=['mv' + tag])
        key = 'mv' + tag
        return dict(A1=mv[:, 1, :], B1=mv[:, 0, :], G1=mv[:, 2, :], A2=mv[:, 4, :], B2=mv[:, 3, :], G2=mv[:, 5, :], key=key)

    def load_xT(self, x_d, ntok, xT, xkey, stage, t0=0):
        P = self.P
        for tt in range(ntok // 128):
            sb_ = self.nextrot('stage', 2)
            P.dma('sp', lambda e, tt=tt, sb_=sb_: e.dma_start(out=stage[sb_][:], in_=x_d[tt * 128:(tt + 1) * 128, :]), writes=[('stage', sb_)])
            for half in range(2):
                pb = 4 + self.nextrot('ldbank', 2)
                for j in range(4):
                    kc = half * 4 + j
                    P.op('pe', lambda e, sb_=sb_, kc=kc, pb=pb, j=j: e.transpose(self.bank(pb)[:, j * 128:(j + 1) * 128], stage[sb_][:, kc * 128:(kc + 1) * 128], self.ident[:]),
                         reads=[('stage', sb_), 'ident'], writes=[('ps', pb)])
                eng = 'act' if half == 0 else 'dve'
                dst = xT[:, half * 4:(half + 1) * 4, t0 + tt * 128:t0 + (tt + 1) * 128]
                src = self.bank(pb).rearrange("p (a b) -> p a b", a=4)
                if eng == 'act':
                    P.op('act', lambda e, dst=dst, src=src: e.activation(out=dst, in_=src, func=AF.Copy), writes=[('ps', pb), (xkey, (t0 + tt * 128) // 512)])
                else:
                    P.op('dve', lambda e, dst=dst, src=src: e.tensor_copy(out=dst, in_=src), writes=[('ps', pb), (xkey, (t0 + tt * 128) // 512)])

    def store_x(self, xT, xkey, ntok, out_d, ostage, scale_ap=None):
        P = self.P
        for tt in range(ntok // 128):
            ob = self.nextrot('ostage', 2)
            for half in range(2):
                pb = 4 + self.nextrot('ldbank', 2)
                for j in range(4):
                    kc = half * 4 + j
                    P.op('pe', lambda e, kc=kc, pb=pb, j=j, tt=tt: e.transpose(self.bank(pb)[:, j * 128:(j + 1) * 128], xT[:, kc, tt * 128:(tt + 1) * 128], self.ident[:]),
                         reads=[(xkey, tt // 4), 'ident'], writes=[('ps', pb)])
                dst = ostage[ob][:, half * 512:(half + 1) * 512]
                if half == 0:
                    P.op('act', lambda e, dst=dst, pb=pb: e.activation(out=dst, in_=self.bank(pb), func=AF.Copy), writes=[('ps', pb), ('ostage', ob, half)])
                else:
                    P.op('dve', lambda e, dst=dst, pb=pb: e.tensor_copy(out=dst, in_=self.bank(pb)), writes=[('ps', pb), ('ostage', ob, half)])
            P.dma('sp', lambda e, ob=ob, tt=tt: e.dma_start(out=out_d[tt * 128:(tt + 1) * 128, :], in_=ostage[ob][:]),
                  reads=[('ostage', ob, 0), ('ostage', ob, 1)])

    def norm_mod(self, xT, xkey, ntok, A, B, mkey, hT, hkey, scr, t0=0, ht0=0):
        P = self.P
        tgs = min(512, ntok)
        for g in range(ntok // tgs):
            c0 = t0 + g * tgs
            h0 = ht0 + g * tgs
            pb = 6 + self.nextrot('nbank', 2)
            for kc in range(KC):
                sq = self.nextrot('sq', 2)
                P.op('pool', lambda e, kc=kc, sq=sq, c0=c0: e.tensor_tensor(out=scr['sq'][sq][:, :tgs], in0=xT[:, kc, c0:c0 + tgs], in1=xT[:, kc, c0:c0 + tgs], op=ALU.mult),
                     reads=[(xkey, c0 // 512)], writes=[('sq', sq)])
                P.op('pe', lambda e, kc=kc, sq=sq, pb=pb: e.matmul(self.bank(pb, tgs), lhsT=self.ones_bf[:], rhs=scr['sq'][sq][:, :tgs], start=(kc == 0), stop=(kc == KC - 1)),
                     reads=[('sq', sq), 'ones_bf'], writes=[('ps', pb)])
            rs = scr['rs']
            P.op('act', lambda e, pb=pb: e.activation(out=rs[:, :tgs], in_=self.bank(pb, tgs), func=AF.Ln, scale=1.0 / D, bias=self.eps_t[:]),
                 reads=['eps_t'], writes=[('ps', pb), 'rs'])
            P.op('act', lambda e: e.activation(out=rs[:, :tgs], in_=rs[:, :tgs], func=AF.Exp, scale=-0.5), writes=['rs'])
            for kc in range(KC):
                tb = self.nextrot('tmp', 2)
                P.op('dve', lambda e, kc=kc, tb=tb, c0=c0: e.scalar_tensor_tensor(out=scr['tmp'][tb][:, :tgs], in0=xT[:, kc, c0:c0 + tgs], scalar=A[:, kc:kc + 1],
                                                                                in1=rs[:, :tgs], op0=ALU.mult, op1=ALU.mult),
                     reads=[(xkey, c0 // 512), 'rs', mkey], writes=[('tmp', tb)])
                P.op('act', lambda e, kc=kc, tb=tb, h0=h0: e.activation(out=hT[:, kc, h0:h0 + tgs], in_=scr['tmp'][tb][:, :tgs], func=AF.Identity, bias=B[:, kc:kc + 1]),
                     reads=[('tmp', tb), mkey], writes=[(hkey, h0 // 512)])

    def load_w(self, dst, wkey, w_d, r0, nkc, c0, ncol):
        P = self.P
        for k in range(nkc):
            P.dma('pool', lambda e, k=k: e.dma_start(out=dst[:, k, 0:ncol], in_=w_d[r0 + k * 128:r0 + (k + 1) * 128, c0:c0 + ncol]),
                  writes=[(wkey, k)])

    def mlp(self, xT, xkey, ntok, hT, hkey, G, mkey, w1_d, w2_d, wbuf, aT, rbuf):
        P = self.P
        FB = 512
        nfb = DFF // FB
        tgs = min(512, ntok)
        ntg = ntok // tgs

        def ff1(j):
            wb = j % 2
            self.load_w(wbuf['w1'][wb], ('w1', wb), w1_d, 0, KC, j * FB, FB)
            self.load_w(wbuf['w2'][wb], ('w2', wb), w2_d, j * FB, FB // 128, 0, D)
            for fc in range(FB // 128):
                for g in range(ntg):
                    pb = self.nextrot('ff1bank', 3)
                    for kc in range(KC):
                        P.op('pe', lambda e, wb=wb, fc=fc, g=g, kc=kc, pb=pb: e.matmul(self.bank(pb, tgs), lhsT=wbuf['w1'][wb][:, kc, fc * 128:(fc + 1) * 128],
                                                                                    rhs=hT[:, kc, g * tgs:(g + 1) * tgs], start=(kc == 0), stop=(kc == KC - 1)),
                             reads=[(('w1', wb), kc), (hkey, g)], writes=[('ps', pb)])
                    rb = self.nextrot('rbuf', 2)
                    P.op('act', lambda e, pb=pb, rb=rb: e.activation(out=rbuf[rb][:, :tgs], in_=self.bank(pb, tgs), func=AF.Relu), writes=[('ps', pb), ('rbuf', rb)])
                    P.op('pool', lambda e, rb=rb, wb=wb, fc=fc, g=g: e.tensor_tensor(out=aT[wb][:, fc, g * tgs:(g + 1) * tgs], in0=rbuf[rb][:, :tgs], in1=rbuf[rb][:, :tgs], op=ALU.mult),
                         reads=[('rbuf', rb)], writes=[('aT', wb, fc, g)])

        def ff2(j):
            wb = j % 2
            for dc in range(KC):
                for g in range(ntg):
                    pb = 3 + self.nextrot('ff2bank', 3)
                    nf = FB // 128
                    for fc in range(nf):
                        P.op('pe', lambda e, wb=wb, fc=fc, g=g, dc=dc, pb=pb: e.matmul(self.bank(pb, tgs), lhsT=wbuf['w2'][wb][:, fc, dc * 128:(dc + 1) * 128],
                                                                                    rhs=aT[wb][:, fc, g * tgs:(g + 1) * tgs], start=(fc == 0), stop=(fc == nf - 1)),
                             reads=[(('w2', wb), fc), ('aT', wb, fc, g)], writes=[('ps', pb)])
                    P.op('dve', lambda e, dc=dc, g=g, pb=pb: e.scalar_tensor_tensor(out=xT[:, dc, g * tgs:(g + 1) * tgs], in0=self.bank(pb, tgs), scalar=G[:, dc:dc + 1],
                                                                                  in1=xT[:, dc, g * tgs:(g + 1) * tgs], op0=ALU.mult, op1=ALU.add),
                         reads=[mkey], writes=[('ps', pb), (xkey, g)])

        ff1(0)
        for j in range(nfb):
            if j + 1 < nfb:
                ff1(j + 1)
            ff2(j)


def load_cast(kb, dst, key, src_d):
    kb.P.dma('pool', lambda e: e.dma_start(out=dst, in_=src_d), writes=[key])


def proj_fm(kb, hT, hkey, t0, n, W, wkey, col0, pb):
    for kc in range(KC):
        kb.P.op('pe', lambda e, kc=kc: e.matmul(kb.bank(pb, n), lhsT=W[:, kc, col0:col0 + 128], rhs=hT[:, kc, t0:t0 + n],
                                               start=(kc == 0), stop=(kc == KC - 1)),
                reads=[(wkey, kc), (hkey, t0 // 512)], writes=[('ps', pb)])


def qk_norm_rope(kb, pb, n, gain, rope, qscale, out_ap, outkeys, C):
    P = kb.P
    r = kb.nextrot('qkr', 2)
    kg, k2, rs, t1 = C['kg'][r], C['k2'][r], C['rs2'][r], C['t1'][r]
    P.op('act', lambda e: e.activation(out=kg[:, :n], in_=kb.bank(pb, n), func=AF.Copy, scale=gain[:, 0:1]), reads=['gains'], writes=[('ps', pb), ('kg', r)])
    P.op('act', lambda e: e.activation(out=k2[:, :n], in_=kb.bank(pb, n), func=AF.Square), writes=[('ps', pb), ('k2', r)])
    P.op('pe', lambda e: e.matmul(kb.bank(2, n), lhsT=C['bones'][:], rhs=k2[:, :n], start=True, stop=True), reads=[('k2', r), 'bones'], writes=[('ps', 2)])
    if rope is not None:
        P.op('pe', lambda e: e.matmul(kb.bank(3, n), lhsT=C['rmat'][:], rhs=kg[:, :n], start=True, stop=True), reads=[('kg', r), 'rmat'], writes=[('ps', 3)])
    P.op('act', lambda e: e.activation(out=rs[:, :n], in_=kb.bank(2, n), func=AF.Ln, scale=1.0 / 64, bias=kb.eps_t[:]), reads=['eps_t'], writes=[('ps', 2), ('rs2', r)])
    if qscale:
        P.op('act', lambda e: e.activation(out=rs[:, :n], in_=rs[:, :n], func=AF.Exp, scale=-0.5, bias=C['lnq'][:]), reads=['lnq'], writes=[('rs2', r)])
    else:
        P.op('act', lambda e: e.activation(out=rs[:, :n], in_=rs[:, :n], func=AF.Exp, scale=-0.5), writes=[('rs2', r)])
    if rope is not None:
        cos_ap, sin_ap, rkey = rope
        P.op('dve', lambda e: e.tensor_tensor(out=t1[:, :n], in0=kg[:, :n], in1=cos_ap, op=ALU.mult), reads=[('kg', r), rkey], writes=[('t1', r)])
        P.op('dve', lambda e: e.tensor_tensor(out=kg[:, :n], in0=kb.bank(3, n), in1=sin_ap, op=ALU.mult), reads=[rkey], writes=[('ps', 3), ('kg', r)])
        P.op('dve', lambda e: e.tensor_tensor(out=t1[:, :n], in0=t1[:, :n], in1=kg[:, :n], op=ALU.add), reads=[('kg', r)], writes=[('t1', r)])
        P.op('dve', lambda e: e.tensor_tensor(out=out_ap, in0=t1[:, :n], in1=rs[:, :n], op=ALU.mult), reads=[('t1', r), ('rs2', r)], writes=outkeys)
    else:
        P.op('dve', lambda e: e.tensor_tensor(out=out_ap, in0=kg[:, :n], in1=rs[:, :n], op=ALU.mult), reads=[('kg', r), ('rs2', r)], writes=outkeys)


def attention(kb, q_ap, qkeys, NQ, tiles, dst_a, dst_b, dstkeys, C):
    P = kb.P
    oset = kb.nextrot('oset', 2)
    o0 = 4 + 2 * oset
    nt = len(tiles)
    ssets = []

    def qk(i):
        t = tiles[i]
        ss = kb.nextrot('sset', 2)
        ssets.append(ss)
        s0 = 2 * ss
        P.op('pe', lambda e: e.matmul(kb.bank(s0, NQ), lhsT=t['ka'], rhs=q_ap[0:64, :], start=True, stop=True), reads=t['keys'] + qkeys, writes=[('ps', s0)])
        P.op('pe', lambda e: e.matmul(kb.bank(s0 + 1, NQ), lhsT=t['kb'], rhs=q_ap[64:128, :], start=True, stop=True), reads=t['keys'] + qkeys, writes=[('ps', s0 + 1)])

    qk(0)
    for i in range(nt):
        if i + 1 < nt:
            qk(i + 1)
        t = tiles[i]
        s0 = 2 * ssets[i]
        pbuf = kb.nextrot('pbuf', 3)
        pb_ = C['pbuf'][pbuf]
        src = kb.ps[:, s0 * 512:(s0 + 2) * 512].rearrange("p (h n) -> p h n", h=2)[:, :, 0:NQ]
        if t.get('bias_a') is not None:
            sb_ = C['sbias'][kb.nextrot('sbias', 2)]
            P.op('dve', lambda e, sb_=sb_, t=t, s0=s0: e.tensor_tensor(out=sb_[:, 0, 0:NQ], in0=kb.bank(s0, NQ), in1=t['bias_a'], op=ALU.add), reads=t['bkeys'], writes=[('ps', s0), ('sbias', id(sb_), 0)])
            P.op('dve', lambda e, sb_=sb_, t=t, s0=s0: e.tensor_tensor(out=sb_[:, 1, 0:NQ], in0=kb.bank(s0 + 1, NQ), in1=t['bias_b'], op=ALU.add), reads=t['bkeys'], writes=[('ps', s0 + 1), ('sbias', id(sb_), 1)])
            P.op('act', lambda e, sb_=sb_, pb_=pb_: e.activation(out=pb_[:, :, 0:NQ], in_=sb_[:, :, 0:NQ], func=AF.Exp), reads=[('sbias', id(sb_), 0), ('sbias', id(sb_), 1)], writes=[('pbuf', pbuf)])
        else:
            P.op('act', lambda e, src=src, pb_=pb_: e.activation(out=pb_[:, :, 0:NQ], in_=src, func=AF.Exp), writes=[('ps', s0), ('ps', s0 + 1), ('pbuf', pbuf)])
        P.op('pe', lambda e, t=t, pb_=pb_, i=i: e.matmul(kb.bank(o0, NQ), lhsT=t['va'], rhs=pb_[:, 0, 0:NQ], start=(i == 0), stop=(i == nt - 1)), reads=t['keys'] + [('pbuf', pbuf)], writes=[('ps', o0)])
        P.op('pe', lambda e, t=t, pb_=pb_, i=i: e.matmul(kb.bank(o0 + 1, NQ), lhsT=t['vb'], rhs=pb_[:, 1, 0:NQ], start=(i == 0), stop=(i == nt - 1)), reads=t['keys'] + [('pbuf', pbuf)], writes=[('ps', o0 + 1)])
    rcr = kb.nextrot('rc', 2)
    rc = C['rc'][rcr]
    P.op('dve', lambda e: e.reciprocal(out=rc[64:128, 0:NQ], in_=kb.bank(o0, NQ)[64:128, :]), writes=[('ps', o0), ('rc', rcr, 0)])
    P.op('dve', lambda e: e.tensor_tensor(out=dst_a, in0=kb.bank(o0, NQ)[0:64, :], in1=rc[64:128, 0:NQ], op=ALU.mult), reads=[('rc', rcr, 0)], writes=[('ps', o0)] + dstkeys)
    P.op('dve', lambda e: e.reciprocal(out=rc[0:64, 0:NQ], in_=kb.bank(o0 + 1, NQ)[0:64, :]), writes=[('ps', o0 + 1), ('rc', rcr, 1)])
    P.op('dve', lambda e: e.tensor_tensor(out=dst_b, in0=kb.bank(o0 + 1, NQ)[64:128, :], in1=rc[0:64, 0:NQ], op=ALU.mult), reads=[('rc', rcr, 1)], writes=[('ps', o0 + 1)] + dstkeys)


def attn_scratch(kb, with_bias=False):
    C = dict(pbuf=[kb.sb("pbuf%d" % i, [128, 2, 512], BF16) for i in range(3)],
             rc=[kb.sb("rc%d" % i, [128, 512], F32) for i in range(2)])
    if with_bias:
        C['sbias'] = [kb.sb("sbias%d" % i, [128, 2, 512], F32) for i in range(2)]
    return C


def qk_scratch(kb, C):
    C['kg'] = [kb.sb("kg%d" % i, [128, 512], BF16) for i in range(2)]
    C['k2'] = [kb.sb("k2%d" % i, [128, 512], BF16) for i in range(2)]
    C['rs2'] = [kb.sb("rs2%d" % i, [128, 512], F32) for i in range(2)]
    C['t1'] = [kb.sb("t1%d" % i, [128, 512], F32) for i in range(2)]


def norm_scratch(kb):
    return dict(sq=[kb.sb("sq%d" % i, [128, 512], BF16) for i in range(2)], rs=kb.sb("rs", [128, 512], F32),
                tmp=[kb.sb("tmp%d" % i, [128, 512], F32) for i in range(2)])


def mlp_bufs(kb, ntok):
    wbuf = dict(w1=[kb.sb("w1b%d" % i, [128, 8, 512], BF16) for i in range(3)], w2=[kb.sb("w2b%d" % i, [128, 4, 1024], BF16) for i in range(3)])
    aT = [kb.sb("aT%d" % i, [128, 4, ntok], BF16) for i in range(2)]
    rbuf = [kb.sb("rbuf%d" % i, [128, 512], F32) for i in range(2)]
    return wbuf, aT, rbuf


def resid_proj(kb, srcT, skey, t0, n, W, wkey, xT, xkey, xt0, G, mkey, bG=None):
    P = kb.P
    for dc in range(KC):
        pb = kb.nextrot('projbank', 2)
        proj_fm(kb, srcT, skey, t0, n, W, wkey, dc * 128, pb)
        P.op('dve', lambda e, dc=dc, pb=pb: e.scalar_tensor_tensor(out=xT[:, dc, xt0:xt0 + n], in0=kb.bank(pb, n), scalar=G[:, dc:dc + 1],
                                                                 in1=xT[:, dc, xt0:xt0 + n], op0=ALU.mult, op1=ALU.add),
             reads=[mkey], writes=[('ps', pb), (xkey, xt0 // 512)])
        if bG is not None:
            P.op('act', lambda e, dc=dc: e.activation(out=xT[:, dc, xt0:xt0 + n], in_=xT[:, dc, xt0:xt0 + n], func=AF.Identity, bias=bG[:, dc:dc + 1]),
                 reads=['bG'], writes=[(xkey, xt0 // 512)])


def build_l0(kb=None, io=None):
    own = kb is None
    if own:
        kb = KB()
    io = io or {}
    kb.pfx = '' if own else 'l0_'
    P, nc = kb.P, kb.nc
    NTOK = 2048
    xb_d = io.get("xb") or kb.din("xb", [8192, D]); ctx_d = io.get("ctx") or kb.din("ctx", [256, D])
    cv_d = kb.din("cv", [128, 8, 2]); adaw_d = kb.din("adaw", [D, 6144]); adab_d = kb.din("adab", [2, 6144])
    g1_d = kb.din("g1", [128, 8]); g2_d = kb.din("g2", [128, 8])
    wqkv_d = kb.din("wqkv", [D, 1536]); gains_d = kb.din("gains", [128, 2])
    cos_d = kb.din("cos", [16, 128, 512]); sin_d = kb.din("sin", [16, 128, 512])
    wo_d = kb.din("wo", [D, D]); w1_d = kb.din("w1", [D, DFF]); w2_d = kb.din("w2", [DFF, D])
    rmat_d = kb.din("rmat", [128, 128]); bones_d = kb.din("bones", [128, 128])
    out_d = io.get("out") or kb.dout("out", [NTOK, D]); hctx_d = io.get("hctx") or kb.dout("hctx", [256, D])

    modsT = kb.sb("modsT", [128, 48, 2], F32)
    mvL = kb.sb("mvL", [128, 6, 8], F32); mvC = kb.sb("mvC", [128, 6, 8], F32)
    C = dict(rmat=kb.sb("rmat", [128, 128], BF16), bones=kb.sb("bones", [128, 128], BF16), lnq=kb.sb("lnq", [128, 1], F32))
    gains = kb.sb("gains", [128, 2], F32)
    with kb.phase():
        load_cast(kb, C['rmat'][:], 'rmat', rmat_d)
        load_cast(kb, C['bones'][:], 'bones', bones_d)
        P.op('pool', lambda e: e.memset(C['lnq'][:], float(np.log(0.125))), writes=['lnq'])
        P.dma('sp', lambda e: e.dma_start(out=gains[:], in_=gains_d), writes=['gains'])
        kb.mods(cv_d, adaw_d, adab_d, modsT)
        mL = kb.mod_vectors(modsT, g1_d, g2_d, 0, mvL, "L")
        mC = kb.mod_vectors(modsT, g1_d, g2_d, 1, mvC, "C")
    qgain, kgain = gains[:, 0:1], gains[:, 1:2]

    with kb.phase():
        KT = kb.sb("KT", [128, 2, 8448], BF16)
        Ve = kb.sb("Ve", [128, 66, 384], BF16)
        P.op('pool', lambda e: e.memset(Ve[:].rearrange("p t (a b c) -> p (t a) b c", a=2, b=3, c=64)[:, :, 1, :], 1.0), writes=['Ve_ones'])

        def kv_tiles(tile_ids, pr):
            out = []
            for kt in tile_ids:
                out.append(dict(ka=KT[0:64, pr, kt * 128:(kt + 1) * 128], kb=KT[64:128, pr, kt * 128:(kt + 1) * 128],
                                va=Ve[:, kt, pr * 192:pr * 192 + 128], vb=Ve[:, kt, pr * 192 + 64:pr * 192 + 192],
                                keys=[('KT', kt // 4), ('Ve', kt), 'Ve_ones']))
            return out

        def produce_kv(hT, hkey, n, g, wqkv, rope):
            for pr in range(2):
                pb = kb.nextrot('projbank', 2)
                proj_fm(kb, hT, hkey, 0, n, wqkv, 'wqkv', 1024 + pr * 128, pb)
                qk_norm_rope(kb, pb, n, kgain, rope, False, KT[:, pr, g * 512:g * 512 + n], [('KT', g)], C)
            for tt in range(n // 128):
                pb = kb.nextrot('projbank', 2)
                for kc in range(KC):
                    P.op('pe', lambda e, kc=kc, tt=tt, pb=pb: e.matmul(kb.bank(pb, 256), lhsT=hT[:, kc, tt * 128:(tt + 1) * 128], rhs=wqkv[:, kc, 1280:1536],
                                                                      start=(kc == 0), stop=(kc == KC - 1)),
                         reads=[('wqkv', kc), (hkey, 0)], writes=[('ps', pb)])
                kt = g * 4 + tt
                dst = Ve[:, kt, :].rearrange("p (a b c) -> p a b c", a=2, b=3, c=64)[:, :, ::2, :]
                src = kb.bank(pb, 256).rearrange("p (a b c) -> p a b c", a=2, b=2, c=64)
                P.op('dve', lambda e, dst=dst, src=src: e.tensor_copy(out=dst, in_=src), writes=[('ps', pb), ('Ve', kt)])

        with kb.phase():
            cT = kb.sb("cT", [128, 8, 256], F32)
            stage = [kb.sb("stage%d" % i, [128, 1024], F32) for i in range(2)]
            scr = norm_scratch(kb)
            with kb.phase():
                hcT = kb.sb("hcT", [128, 8, 256], BF16)
                QcT = kb.sb("QcT", [128, 8, 256], BF16)
                OcT = kb.sb("OcT", [128, 8, 256], BF16)
                qk_scratch(kb, C)
                C.update(attn_scratch(kb))
                wqkv = kb.sb("wqkv", [128, 8, 1536], BF16)
                wo = kb.sb("wo", [128, 8, 1024], BF16)
                kb.load_w(wqkv, 'wqkv', wqkv_d, 0, KC, 0, 1536)
                kb.load_w(wo, 'wo', wo_d, 0, KC, 0, 1024)
                kb.load_xT(ctx_d, 256, cT, 'cT', stage)
                kb.norm_mod(cT, 'cT', 256, mC['A1'], mC['B1'], mC['key'], hcT, 'hcT', scr)
                produce_kv(hcT, 'hcT', 256, 16, wqkv, None)
                for c in range(8):
                    pb = kb.nextrot('projbank', 2)
                    proj_fm(kb, hcT, 'hcT', 0, 256, wqkv, 'wqkv', c * 128, pb)
                    qk_norm_rope(kb, pb, 256, qgain, None, True, QcT[:, c, :], [('QcT', c)], C)
                for c in range(8):
                    attention(kb, QcT[:, c, :], [('QcT', c)], 256, kv_tiles([64, 65], c // 4), OcT[0:64, c, :], OcT[64:128, c, :], [('OcT', 0)], C)
                resid_proj(kb, OcT, 'OcT', 0, 256, wo, 'wo', cT, 'cT', 0, mC['G1'], mC['key'])
            with kb.phase():
                hc2 = kb.sb("hc2", [128, 8, 256], BF16)
                kb.norm_mod(cT, 'cT', 256, mC['A2'], mC['B2'], mC['key'], hc2, 'hc2', scr)
                wbuf, aT, rbuf = mlp_bufs(kb, 256)
                kb.mlp(cT, 'cT', 256, hc2, 'hc2', mC['G2'], mC['key'], w1_d, w2_d, wbuf, aT, rbuf)
                kb.store_x(cT, 'cT', 256, hctx_d, stage)

        with kb.phase():
            QT = kb.sb("QT", [128, 8, NTOK], BF16)
            with kb.phase():
                xtmp = kb.sb("xtmp", [128, 8, 512], F32)
                hTt = kb.sb("hTt", [128, 8, 512], BF16)
                stage = [kb.sb("stage%d" % i, [128, 1024], F32) for i in range(2)]
                scr = norm_scratch(kb)
                qk_scratch(kb, C)
                wqkv = kb.sb("wqkv", [128, 8, 1536], BF16)
                cs = [kb.sb("cs%d" % i, [128, 2, 512], F32) for i in range(2)]
                kb.load_w(wqkv, 'wqkv', wqkv_d, 0, KC, 0, 1536)
                for g in range(16):
                    kb.load_xT(xb_d[g * 512:(g + 1) * 512, :], 512, xtmp, 'xtmp', stage)
                    kb.norm_mod(xtmp, 'xtmp', 512, mL['A1'], mL['B1'], mL['key'], hTt, 'hTt', scr)
                    cb = g % 2
                    P.dma('sp', lambda e, g=g, cb=cb: e.dma_start(out=cs[cb][:, 0, :], in_=cos_d[g]), writes=[('cs', cb)])
                    P.dma('sp', lambda e, g=g, cb=cb: e.dma_start(out=cs[cb][:, 1, :], in_=sin_d[g]), writes=[('cs', cb)])
                    rope = (cs[cb][:, 0, :], cs[cb][:, 1, :], ('cs', cb))
                    produce_kv(hTt, 'hTt', 512, g, wqkv, rope)
                    if g < 4:
                        for c in range(8):
                            pb = kb.nextrot('projbank', 2)
                            proj_fm(kb, hTt, 'hTt', 0, 512, wqkv, 'wqkv', c * 128, pb)
                            qk_norm_rope(kb, pb, 512, qgain, rope, True, QT[:, c, g * 512:(g + 1) * 512], [('QT', c, g)], C)
            with kb.phase():
                OT = kb.sb("OT", [128, 8, NTOK], BF16)
                with kb.phase():
                    C.update(attn_scratch(kb))
                    for qg in range(4):
                        for c in range(8):
                            attention(kb, QT[:, c, qg * 512:(qg + 1) * 512], [('QT', c, qg)], 512, kv_tiles(list(range(66)), c // 4),
                                      OT[0:64, c, qg * 512:(qg + 1) * 512], OT[64:128, c, qg * 512:(qg + 1) * 512], [('OT', qg)], C)
                with kb.phase():
                    xtmp = kb.sb("xtmp", [128, 8, 512], F32)
                    stage = [kb.sb("stage%d" % i, [128, 1024], F32) for i in range(2)]
                    wo = kb.sb("wo", [128, 8, 1024], BF16)
                    ostage = [kb.sb("ostage%d" % i, [128, 1024], F32) for i in range(2)]
                    kb.load_w(wo, 'wo', wo_d, 0, KC, 0, 1024)
                    for g in range(4):
                        kb.load_xT(xb_d[g * 512:(g + 1) * 512, :], 512, xtmp, 'xtmp', stage)
                        resid_proj(kb, OT, 'OT', g * 512, 512, wo, 'wo', xtmp, 'xtmp', 0, mL['G1'], mL['key'])
                        kb.store_x(xtmp, 'xtmp', 512, out_d[g * 512:(g + 1) * 512, :], ostage)
    mlp_tail(kb, out_d, out_d, NTOK, mL, w1_d, w2_d)
    if own:
        P.close()
    return nc


def mlp_tail(kb, src_d, out_d, ntok, mL, w1_d, w2_d, final=None):
    with kb.phase():
        xT = kb.sb("xT", [128, 8, ntok], F32)
        with kb.phase():
            stage = [kb.sb("stage%d" % i, [128, 1024], F32) for i in range(2)]
            kb.load_xT(src_d, ntok, xT, 'xT', stage)
        with kb.phase():
            hT = kb.sb("hT", [128, 8, ntok], BF16)
            scr = norm_scratch(kb)
            kb.norm_mod(xT, 'xT', ntok, mL['A2'], mL['B2'], mL['key'], hT, 'hT', scr)
            wbuf, aT, rbuf = mlp_bufs(kb, ntok)
            kb.mlp(xT, 'xT', ntok, hT, 'hT', mL['G2'], mL['key'], w1_d, w2_d, wbuf, aT, rbuf)
        with kb.phase():
            ostage = [kb.sb("ostage%d" % i, [128, 1024], F32) for i in range(2)]
            if final is not None:
                yT = kb.sb("yT", [128, 8, ntok], F32)
                scr = norm_scratch(kb)
                kb.norm_mod(xT, 'xT', ntok, final[0], final[1], final[2], yT, 'yT', scr)
                kb.store_x(yT, 'yT', ntok, out_d, ostage)
            else:
                kb.store_x(xT, 'xT', ntok, out_d, ostage)


def fm(v):
    return np.ascontiguousarray(np.asarray(v, np.float32).reshape(8, 128).T)


def common_inputs(inp, layer, b):
    cv = np.stack([fm(inp['c'][b]), fm(inp['c_ctx'])], axis=-1)
    return dict(ident=np.eye(128, dtype=np.float32), cv=np.ascontiguousarray(cv), adaw=np.ascontiguousarray(inp['ada_w'][layer]),
                adab=np.ascontiguousarray(np.stack([inp['ada_b'][layer]] * 2)), g1=fm(inp['norm1_g'][layer]), g2=fm(inp['norm2_g'][layer]),
                w1=np.ascontiguousarray(inp['mlp_w1'][layer]), w2=np.ascontiguousarray(inp['mlp_w2'][layer]))


def rope_tables(order):
    t = np.asarray(order)
    row = (t // 64).astype(np.float32)
    col = (t % 64).astype(np.float32)
    inv = (10000.0 ** (-np.arange(16, dtype=np.float32) / 16)).astype(np.float32)
    ang = np.concatenate([row[:, None] * inv, col[:, None] * inv], axis=-1).astype(np.float32)
    idx = (np.arange(128) % 64) // 2
    a = ang[:, idx].T
    cos = np.cos(a).astype(np.float32).reshape(128, 16, 512).transpose(1, 0, 2)
    sin = np.sin(a).astype(np.float32).reshape(128, 16, 512).transpose(1, 0, 2)
    return np.ascontiguousarray(cos), np.ascontiguousarray(sin)


def gqa_chunk_heads():
    return [(c, 4 + c) if c < 4 else (8 + c - 4, 12 + c - 4) for c in range(8)]


def prep_l0(inp, b, q):
    d = common_inputs(inp, 0, b)
    order = np.concatenate([np.arange(q * 2048, 8192), np.arange(0, q * 2048)])
    d['xb'] = np.ascontiguousarray(inp['x'][b][order])
    d['ctx'] = np.ascontiguousarray(inp['ctx'][b])
    wqkv = inp['at_w_qkv'][0]
    qcols = np.concatenate([np.concatenate([np.arange(ha * 64, ha * 64 + 64), np.arange(hb * 64, hb * 64 + 64)]) for ha, hb in gqa_chunk_heads()])
    d['wqkv'] = np.ascontiguousarray(np.concatenate([wqkv[:, qcols], wqkv[:, 1024:]], axis=1))
    d['wo'] = np.ascontiguousarray(inp['at_w_o'][0][qcols, :])
    d['gains'] = np.ascontiguousarray(np.stack([np.tile(inp['at_q_g'][0], 2), np.tile(inp['at_k_g'][0], 2)], axis=1))
    d['cos'], d['sin'] = rope_tables(order)
    rmat = np.zeros((128, 128), np.float32)
    for i in range(64):
        rmat[2 * i + 1, 2 * i] = -1.0
        rmat[2 * i, 2 * i + 1] = 1.0
    d['rmat'] = rmat
    bones = np.zeros((128, 128), np.float32)
    bones[:64, :64] = 1.0
    bones[64:, 64:] = 1.0
    d['bones'] = bones
    return d


NA_CLASS = [0, 1] + [2] * 12 + [3, 4]
NA_ST = list(range(14)) + [12, 13]


def build_l1(kb=None, io=None):
    own = kb is None
    if own:
        kb = KB()
    io = io or {}
    kb.pfx = '' if own else 'l1_'
    P, nc = kb.P, kb.nc
    NTOK = 2048
    NH = 2560
    xh_d = io.get("xh") or kb.din("xh", [NH, D]); ctx_d = io.get("ctx") or kb.din("ctx", [256, D])
    cv_d = kb.din("cv", [128, 8, 2]); adaw_d = kb.din("adaw", [D, 6144]); adab_d = kb.din("adab", [2, 6144])
    g1_d = kb.din("g1", [128, 8]); g2_d = kb.din("g2", [128, 8])
    wqkv_d = kb.din("wqkv", [8, D, 384]); tab_d = kb.din("tab", [5, 8, 128, 2 * 7 * 128])
    wo_d = kb.din("wo", [D, D]); w1_d = kb.din("w1", [D, DFF]); w2_d = kb.din("w2", [DFF, D])
    out_d = io.get("out") or kb.dout("out", [NTOK, D])

    modsT = kb.sb("modsT", [128, 48, 2], F32)
    mvL = kb.sb("mvL", [128, 6, 8], F32); mvC = kb.sb("mvC", [128, 6, 8], F32)
    C = {}
    with kb.phase():
        kb.mods(cv_d, adaw_d, adab_d, modsT)
        mL = kb.mod_vectors(modsT, g1_d, g2_d, 0, mvL, "L")
        mC = kb.mod_vectors(modsT, g1_d, g2_d, 1, mvC, "C")
    with kb.phase():
        hT = kb.sb("hT", [128, 8, NH], BF16)
        hcT = kb.sb("hcT", [128, 8, 256], BF16)
        OT = kb.sb("OT", [128, 8, NTOK], BF16)
        with kb.phase():
            xtmp = kb.sb("xtmp", [128, 8, 512], F32)
            stage = [kb.sb("stage%d" % i, [128, 1024], F32) for i in range(2)]
            scr = norm_scratch(kb)
            for g in range(5):
                kb.load_xT(xh_d[g * 512:(g + 1) * 512, :], 512, xtmp, 'xtmp', stage)
                kb.norm_mod(xtmp, 'xtmp', 512, mL['A1'], mL['B1'], mL['key'], hT, 'hT', scr, t0=0, ht0=g * 512)
            kb.load_xT(ctx_d, 256, xtmp, 'xtmp', stage)
            kb.norm_mod(xtmp, 'xtmp', 256, mC['A1'], mC['B1'], mC['key'], hcT, 'hcT', scr)
        with kb.phase():
            C.update(attn_scratch(kb, with_bias=True))
            wc = [kb.sb("wc%d" % i, [128, 8, 384], BF16) for i in range(2)]
            QTc = [kb.sb("QTc%d" % i, [128, NTOK], BF16) for i in range(2)]
            KTc = [kb.sb("KTc%d" % i, [128, NH + 256], BF16) for i in range(2)]
            Vec = [kb.sb("Vec%d" % i, [128, 22, 192], BF16) for i in range(2)]
            tabI = [kb.sb("tabI%d" % i, [128, 2, 7, 128], F32) for i in range(2)]
            tabS = [kb.sb("tabS%d" % i, [128, 2, 7, 128], F32) for i in range(2)]
            for i in range(2):
                P.op('pool', lambda e, i=i: e.memset(Vec[i][:, :, 64:128], 1.0), writes=[('Vones', i)])
            for c in range(8):
                b = c % 2
                kb.load_w(wc[b], ('wc', b), wqkv_d[c], 0, KC, 0, 384)
                P.dma('sp', lambda e, c=c, b=b: e.dma_start(out=tabI[b][:].rearrange("p a j q -> p (a j q)"), in_=tab_d[2, c]), writes=[('tabI', b)])
                for g in range(4):
                    pb = kb.nextrot('projbank', 2)
                    proj_fm(kb, hT, 'hT', 256 + g * 512, 512, wc[b], ('wc', b), 0, pb)
                    P.op('act', lambda e, g=g, b=b, pb=pb: e.activation(out=QTc[b][:, g * 512:(g + 1) * 512], in_=kb.bank(pb), func=AF.Copy, scale=0.125),
                         writes=[('ps', pb), ('QTc', b, g)])
                for g in range(5):
                    pb = kb.nextrot('projbank', 2)
                    proj_fm(kb, hT, 'hT', g * 512, 512, wc[b], ('wc', b), 128, pb)
                    P.op('act', lambda e, g=g, b=b, pb=pb: e.activation(out=KTc[b][:, g * 512:(g + 1) * 512], in_=kb.bank(pb), func=AF.Copy),
                         writes=[('ps', pb), ('KTc', b, g)])
                pb = kb.nextrot('projbank', 2)
                proj_fm(kb, hcT, 'hcT', 0, 256, wc[b], ('wc', b), 128, pb)
                P.op('act', lambda e, b=b, pb=pb: e.activation(out=KTc[b][:, NH:NH + 256], in_=kb.bank(pb, 256), func=AF.Copy), writes=[('ps', pb), ('KTc', b, 5)])
                for kt in range(22):
                    src_h, hk, t0 = (hT, 'hT', kt * 128) if kt < 20 else (hcT, 'hcT', (kt - 20) * 128)
                    pb = kb.nextrot('projbank', 2)
                    for kc in range(KC):
                        P.op('pe', lambda e, kc=kc, b=b, pb=pb, src_h=src_h, t0=t0: e.matmul(kb.bank(pb, 128), lhsT=src_h[:, kc, t0:t0 + 128], rhs=wc[b][:, kc, 256:384],
                                                                                          start=(kc == 0), stop=(kc == KC - 1)),
                             reads=[(('wc', b), kc), (hk, t0 // 512)], writes=[('ps', pb)])
                    dst = Vec[b][:, kt, :].rearrange("p (t s) -> p t s", s=64)[:, ::2, :]
                    src = kb.bank(pb, 128).rearrange("p (t s) -> p t s", s=64)
                    P.op('dve', lambda e, dst=dst, src=src: e.tensor_copy(out=dst, in_=src), writes=[('ps', pb), ('Vec', b, kt)])
                for rp in range(16):
                    cls = NA_CLASS[rp]
                    st = NA_ST[rp]
                    if cls == 2:
                        tab, tkey = tabI[b], ('tabI', b)
                    else:
                        sbuf_i = kb.nextrot('tabS', 2)
                        tab, tkey = tabS[sbuf_i], ('tabS', sbuf_i)
                        P.dma('sp', lambda e, c=c, cls=cls, tab=tab: e.dma_start(out=tab[:].rearrange("p a j q -> p (a j q)"), in_=tab_d[cls, c]), writes=[tkey])
                    tiles = []
                    for j in range(9):
                        kt = st + j if j < 7 else 20 + (j - 7)
                        k0 = kt * 128
                        tl = dict(ka=KTc[b][0:64, k0:k0 + 128], kb=KTc[b][64:128, k0:k0 + 128], va=Vec[b][:, kt, 0:128], vb=Vec[b][:, kt, 64:192],
                                  keys=[('KTc', b, k0 // 512), ('Vec', b, kt), ('Vones', b)])
                        if j < 7:
                            tl['bias_a'] = tab[:, 0, j, :]
                            tl['bias_b'] = tab[:, 1, j, :]
                            tl['bkeys'] = [tkey]
                        tiles.append(tl)
                    attention(kb, QTc[b][:, rp * 128:(rp + 1) * 128], [('QTc', b, rp // 4)], 128, tiles,
                              OT[0:64, c, rp * 128:(rp + 1) * 128], OT[64:128, c, rp * 128:(rp + 1) * 128], [('OT', rp // 4)], C)
        with kb.phase():
            xtmp = kb.sb("xtmp", [128, 8, 512], F32)
            stage = [kb.sb("stage%d" % i, [128, 1024], F32) for i in range(2)]
            ostage = [kb.sb("ostage%d" % i, [128, 1024], F32) for i in range(2)]
            wo = kb.sb("wo", [128, 8, 1024], BF16)
            kb.load_w(wo, 'wo', wo_d, 0, KC, 0, 1024)
            for g in range(4):
                kb.load_xT(xh_d[256 + g * 512:256 + (g + 1) * 512, :], 512, xtmp, 'xtmp', stage)
                resid_proj(kb, OT, 'OT', g * 512, 512, wo, 'wo', xtmp, 'xtmp', 0, mL['G1'], mL['key'])
                kb.store_x(xtmp, 'xtmp', 512, out_d[g * 512:(g + 1) * 512, :], ostage)
    mlp_tail(kb, out_d, out_d, NTOK, mL, w1_d, w2_d)
    if own:
        P.close()
    return nc


def na_bias_tables(rpb, qq):
    NEG = np.float32(-30000.0)
    tab = np.full((5, 16, 2, 64, 7, 2, 64), NEG, np.float32)
    cq = np.arange(64)
    cs = np.clip(cq - 8, 0, 48)
    ck = np.arange(64)
    colvalid = (ck[:, None] >= cs[None, :]) & (ck[:, None] < cs[None, :] + 16)
    colidx = np.clip(ck[:, None] - cq[None, :] + 15, 0, 30)
    rep_rp = {0: 0, 1: 1, 2: 2, 3: 14, 4: 15}
    for cls in range(5):
        rp = rep_rp[cls]
        st = NA_ST[rp]
        for bq in range(2):
            r = 32 * qq + 2 * rp + bq
            rs = min(max(r - 4, 0), 120)
            for j in range(7):
                for a in range(2):
                    kr = 32 * qq - 4 + 2 * (st + j) + a
                    if kr < rs or kr >= rs + 8 or kr < 0 or kr > 127:
                        continue
                    vals = rpb[:, kr - r + 7, :][:, colidx]
                    tab[cls, :, a, :, j, bq, :] = np.where(colvalid[None], vals, NEG)
    tab = tab.reshape(5, 8, 2, 128, 7, 128)
    tab = tab.transpose(0, 1, 3, 2, 4, 5).reshape(5, 8, 128, 2 * 7 * 128)
    return np.ascontiguousarray(tab)


def prep_l1(inp, x1, hctx1, b, q):
    d = common_inputs(inp, 1, b)
    if x1 is not None:
        xh = np.zeros((2560, D), np.float32)
        lo = q * 2048 - 256
        hi = lo + 2560
        s0, s1 = max(lo, 0), min(hi, 8192)
        xh[s0 - lo:s1 - lo] = x1[b][s0:s1]
        d['xh'] = xh
        d['ctx'] = np.ascontiguousarray(hctx1[b])
    w = inp['na_w_qkv'][0]
    d['wqkv'] = np.ascontiguousarray(np.stack([np.concatenate([w[:, c * 128:(c + 1) * 128], w[:, 1024 + c * 128:1024 + (c + 1) * 128],
                                                              w[:, 2048 + c * 128:2048 + (c + 1) * 128]], axis=1) for c in range(8)]))
    d['tab'] = na_bias_tables(inp['na_rpb'][0], q)
    d['wo'] = np.ascontiguousarray(inp['na_w_o'][0])
    return d


def build_l2(kb=None, io=None):
    own = kb is None
    if own:
        kb = KB()
    io = io or {}
    kb.pfx = '' if own else 'l2_'
    P, nc = kb.P, kb.nc
    NTOK = 2048
    NH = 2304
    xh_d = io.get("xh") or kb.din("xh", [NH, D])
    cv_d = kb.din("cv", [128, 8, 2]); adaw_d = kb.din("adaw", [D, 6144]); adab_d = kb.din("adab", [2, 6144])
    g1_d = kb.din("g1", [128, 8]); g2_d = kb.din("g2", [128, 8])
    wpw1_d = kb.din("wpw1", [8, D, 256]); vecs_d = kb.din("vecs", [128, 6, 8]); wdw_d = kb.din("wdw", [128, 8, 31]); mask_d = kb.din("mask", [128, 2])
    wpw2_d = kb.din("wpw2", [D, D]); w1_d = kb.din("w1", [D, DFF]); w2_d = kb.din("w2", [DFF, D])
    out_d = io.get("out") or kb.dout("out", [NTOK, D])

    modsT = kb.sb("modsT", [128, 48, 2], F32)
    mvL = kb.sb("mvL", [128, 6, 8], F32)
    vecs = kb.sb("vecs", [128, 6, 8], F32)
    wdw = kb.sb("wdw", [128, 8, 31], F32)
    mask = kb.sb("mask", [128, 2], F32)
    bG = kb.sb("bG", [128, 8], F32)
    identb = kb.sb("identb", [128, 128], BF16)
    with kb.phase():
        kb.mods(cv_d, adaw_d, adab_d, modsT)
        mL = kb.mod_vectors(modsT, g1_d, g2_d, 0, mvL, "L")
        P.dma('sp', lambda e: e.dma_start(out=vecs[:], in_=vecs_d), writes=['vecs'])
        P.dma('sp', lambda e: e.dma_start(out=wdw[:], in_=wdw_d), writes=['wdw'])
        P.dma('sp', lambda e: e.dma_start(out=mask[:], in_=mask_d), writes=['mask'])
        P.op('dve', lambda e: e.tensor_tensor(out=bG[:], in0=vecs[:, 5, :], in1=mL['G1'], op=ALU.mult), reads=['vecs', mL['key']], writes=['bG'])
        P.op('dve', lambda e: e.tensor_copy(out=identb[:], in_=kb.ident[:]), reads=['ident'], writes=['identb'])
    with kb.phase():
        vT = kb.sb("vT", [128, 8, NTOK], BF16)
        with kb.phase():
            uT = kb.sb("uT", [128, 8, NH], BF16)
            with kb.phase():
                hT = kb.sb("hT", [128, 8, NH], BF16)
                with kb.phase():
                    xtmp = kb.sb("xtmp", [128, 8, 512], F32)
                    stage = [kb.sb("stage%d" % i, [128, 1024], F32) for i in range(2)]
                    scr = norm_scratch(kb)
                    for g in range(5):
                        n = 512 if g < 4 else 256
                        kb.load_xT(xh_d[g * 512:g * 512 + n, :], n, xtmp, 'xtmp', stage)
                        kb.norm_mod(xtmp, 'xtmp', n, mL['A1'], mL['B1'], mL['key'], hT, 'hT', scr, t0=0, ht0=g * 512)
                with kb.phase():
                    wp = [kb.sb("wp%d" % i, [128, 8, 256], BF16) for i in range(2)]
                    sig = [kb.sb("sig%d" % i, [128, 512], F32) for i in range(2)]
                    for fc in range(8):
                        b = fc % 2
                        kb.load_w(wp[b], ('wp', b), wpw1_d[fc], 0, KC, 0, 256)
                        for g in range(5):
                            n = 512 if g < 4 else 256
                            pa = kb.nextrot('projbank', 2)
                            proj_fm(kb, hT, 'hT', g * 512, n, wp[b], ('wp', b), 0, pa)
                            pg = 2 + kb.nextrot('projbank2', 2)
                            proj_fm(kb, hT, 'hT', g * 512, n, wp[b], ('wp', b), 128, pg)
                            sb_ = kb.nextrot('sig', 2)
                            P.op('act', lambda e, sb_=sb_, pg=pg, fc=fc, n=n: e.activation(out=sig[sb_][:, :n], in_=kb.bank(pg, n), func=AF.Sigmoid, bias=vecs[:, 1, fc:fc + 1]),
                                 reads=['vecs'], writes=[('ps', pg), ('sig', sb_)])
                            P.op('dve', lambda e, sb_=sb_, pa=pa, fc=fc, g=g, n=n: e.scalar_tensor_tensor(out=uT[:, fc, g * 512:g * 512 + n], in0=kb.bank(pa, n), scalar=vecs[:, 0, fc:fc + 1],
                                                                                                      in1=sig[sb_][:, :n], op0=ALU.add, op1=ALU.mult),
                                 reads=['vecs', ('sig', sb_)], writes=[('ps', pa), ('uT', fc, g)])
                        P.op('dve', lambda e, fc=fc: e.tensor_scalar(out=uT[:, fc, 0:128], in0=uT[:, fc, 0:128], scalar1=mask[:, 0:1], scalar2=None, op0=ALU.mult),
                             reads=['mask'], writes=[('uT', fc, 0)])
                        P.op('dve', lambda e, fc=fc: e.tensor_scalar(out=uT[:, fc, 2176:2304], in0=uT[:, fc, 2176:2304], scalar1=mask[:, 1:2], scalar2=None, op0=ALU.mult),
                             reads=['mask'], writes=[('uT', fc, 4)])
            with kb.phase():
                dg = kb.sb("dg", [128, 8, 31, 128], BF16)
                cT = kb.sb("cT", [128, 8, 512], F32)
                cbf = [kb.sb("cbf%d" % i, [128, 512], BF16) for i in range(2)]
                c2 = [kb.sb("c2%d" % i, [128, 512], BF16) for i in range(2)]
                mean = kb.sb("mean", [128, 512], F32); msq = kb.sb("msq", [128, 512], F32); rstd = kb.sb("rstd", [128, 512], F32)
                tt_ = [kb.sb("tt%d" % i, [128, 512], F32) for i in range(2)]
                for fc in range(8):
                    P.op('dve', lambda e, fc=fc: e.tensor_tensor(out=dg[:, fc, :, :], in0=identb[:].unsqueeze(1).broadcast_to([128, 31, 128]),
                                                                in1=wdw[:, fc, :].unsqueeze(2).broadcast_to([128, 31, 128]), op=ALU.mult),
                         reads=['identb', 'wdw'], writes=[('dg', fc)])
                for tg in range(4):
                    for fc in range(8):
                        pb = kb.nextrot('projbank', 2)
                        for j in range(31):
                            o = 128 + tg * 512 + j - 15
                            P.op('pe', lambda e, fc=fc, j=j, o=o, pb=pb: e.matmul(kb.bank(pb), lhsT=dg[:, fc, j, :], rhs=uT[:, fc, o:o + 512], start=(j == 0), stop=(j == 30)),
                                 reads=[('dg', fc)] + [('uT', fc, gg) for gg in range(5)], writes=[('ps', pb)])
                        P.op('act', lambda e, fc=fc, pb=pb: e.activation(out=cT[:, fc, :], in_=kb.bank(pb), func=AF.Identity, bias=vecs[:, 2, fc:fc + 1]),
                             reads=['vecs'], writes=[('ps', pb), ('cT', fc)])
                        r = kb.nextrot('cbf', 2)
                        P.op('dve', lambda e, fc=fc, r=r: e.tensor_copy(out=cbf[r][:], in_=cT[:, fc, :]), reads=[('cT', fc)], writes=[('cbf', r)])
                        P.op('act', lambda e, fc=fc, r=r: e.activation(out=c2[r][:], in_=cT[:, fc, :], func=AF.Square), reads=[('cT', fc)], writes=[('c2', r)])
                        P.op('pe', lambda e, fc=fc, r=r: e.matmul(kb.bank(6), lhsT=kb.ones_bf[:], rhs=cbf[r][:], start=(fc == 0), stop=(fc == 7)), reads=[('cbf', r), 'ones_bf'], writes=[('ps', 6)])
                        P.op('pe', lambda e, fc=fc, r=r: e.matmul(kb.bank(7), lhsT=kb.ones_bf[:], rhs=c2[r][:], start=(fc == 0), stop=(fc == 7)), reads=[('c2', r), 'ones_bf'], writes=[('ps', 7)])
                    P.op('act', lambda e: e.activation(out=mean[:], in_=kb.bank(6), func=AF.Copy, scale=1.0 / D), writes=[('ps', 6), 'mean'])
                    P.op('dve', lambda e: e.tensor_tensor(out=msq[:], in0=mean[:], in1=mean[:], op=ALU.mult), reads=['mean'], writes=['msq'])
                    P.op('dve', lambda e: e.scalar_tensor_tensor(out=msq[:], in0=kb.bank(7), scalar=1.0 / D, in1=msq[:], op0=ALU.mult, op1=ALU.subtract), writes=[('ps', 7), 'msq'])
                    P.op('act', lambda e: e.activation(out=rstd[:], in_=msq[:], func=AF.Ln, bias=kb.eps_t[:]), reads=['msq', 'eps_t'], writes=['rstd'])
                    P.op('act', lambda e: e.activation(out=rstd[:], in_=rstd[:], func=AF.Exp, scale=-0.5), writes=['rstd'])
                    for fc in range(8):
                        r = kb.nextrot('tt', 2)
                        P.op('dve', lambda e, fc=fc, r=r: e.tensor_tensor(out=tt_[r][:], in0=cT[:, fc, :], in1=mean[:], op=ALU.subtract), reads=[('cT', fc), 'mean'], writes=[('tt', r)])
                        P.op('dve', lambda e, r=r: e.tensor_tensor(out=tt_[r][:], in0=tt_[r][:], in1=rstd[:], op=ALU.mult), reads=['rstd'], writes=[('tt', r)])
                        P.op('act', lambda e, fc=fc, r=r, tg=tg: e.activation(out=vT[:, fc, tg * 512:(tg + 1) * 512], in_=tt_[r][:], func=AF.Silu, scale=vecs[:, 3, fc:fc + 1], bias=vecs[:, 4, fc:fc + 1]),
                             reads=[('tt', r), 'vecs'], writes=[('vT', tg)])
        with kb.phase():
            xtmp = kb.sb("xtmp", [128, 8, 512], F32)
            stage = [kb.sb("stage%d" % i, [128, 1024], F32) for i in range(2)]
            ostage = [kb.sb("ostage%d" % i, [128, 1024], F32) for i in range(2)]
            wo = kb.sb("wo", [128, 8, 1024], BF16)
            kb.load_w(wo, 'wo', wpw2_d, 0, KC, 0, 1024)
            for g in range(4):
                kb.load_xT(xh_d[128 + g * 512:128 + (g + 1) * 512, :], 512, xtmp, 'xtmp', stage)
                resid_proj(kb, vT, 'vT', g * 512, 512, wo, 'wo', xtmp, 'xtmp', 0, mL['G1'], mL['key'], bG=bG)
                kb.store_x(xtmp, 'xtmp', 512, out_d[g * 512:(g + 1) * 512, :], ostage)
    mlp_tail(kb, out_d, out_d, NTOK, mL, w1_d, w2_d)
    if own:
        P.close()
    return nc


def prep_l2(inp, x2, b, q):
    d = common_inputs(inp, 2, b)
    if x2 is not None:
        xh = np.zeros((2304, D), np.float32)
        lo = q * 2048 - 128
        hi = lo + 2304
        s0, s1 = max(lo, 0), min(hi, 8192)
        xh[s0 - lo:s1 - lo] = x2[b][s0:s1]
        d['xh'] = xh
    w = inp['cv_w_pw1'][0]
    d['wpw1'] = np.ascontiguousarray(np.stack([np.concatenate([w[:, c * 128:(c + 1) * 128], w[:, 1024 + c * 128:1024 + (c + 1) * 128]], axis=1) for c in range(8)]))
    bp = inp['cv_b_pw1'][0]
    d['vecs'] = np.ascontiguousarray(np.stack([fm(bp[:1024]), fm(bp[1024:]), fm(inp['cv_b_dw'][0]), fm(inp['cv_ln_g'][0]), fm(inp['cv_ln_b'][0]), fm(inp['cv_b_pw2'][0])], axis=1))
    d['wdw'] = np.ascontiguousarray(inp['cv_w_dw'][0].T.reshape(8, 128, 31).transpose(1, 0, 2))
    m = np.ones((128, 2), np.float32)
    if q == 0:
        m[:, 0] = 0.0
    if q == 3:
        m[:, 1] = 0.0
    d['mask'] = m
    d['wpw2'] = np.ascontiguousarray(inp['cv_w_pw2'][0])
    return d


def build_l3a(kb=None, io=None):
    own = kb is None
    if own:
        kb = KB()
    io = io or {}
    kb.pfx = '' if own else 'l3a_'
    P, nc = kb.P, kb.nc
    NTOK = 2048
    x_d = io.get("x") or kb.din("x", [NTOK, D])
    cv_d = kb.din("cv", [128, 8, 2]); adaw_d = kb.din("adaw", [D, 6144]); adab_d = kb.din("adab", [2, 6144])
    g1_d = kb.din("g1", [128, 8]); g2_d = kb.din("g2", [128, 8])
    csd_d = kb.din("csd", [256, 512])
    pq_d = io.get("pq") or kb.dout("pq", [NTOK, 2048], BF16)
    modsT = kb.sb("modsT", [128, 48, 2], F32)
    mvL = kb.sb("mvL", [128, 6, 8], F32)
    with kb.phase():
        kb.mods(cv_d, adaw_d, adab_d, modsT)
        mL = kb.mod_vectors(modsT, g1_d, g2_d, 0, mvL, "L")
    with kb.phase():
        hT = kb.sb("hT", [128, 8, NTOK], BF16)
        csd = kb.sb("csd", [128, 2, 512], BF16)
        kb.load_w(csd, 'csd', csd_d, 0, 2, 0, 512)
        with kb.phase():
            xtmp = kb.sb("xtmp", [128, 8, 512], F32)
            stage = [kb.sb("stage%d" % i, [128, 1024], F32) for i in range(2)]
            scr = norm_scratch(kb)
            for g in range(4):
                kb.load_xT(x_d[g * 512:(g + 1) * 512, :], 512, xtmp, 'xtmp', stage)
                kb.norm_mod(xtmp, 'xtmp', 512, mL['A1'], mL['B1'], mL['key'], hT, 'hT', scr, t0=0, ht0=g * 512)
        with kb.phase():
            pqs = [kb.sb("pqs%d" % i, [128, 2048], BF16) for i in range(2)]
            for tt in range(16):
                ob = tt % 2
                for grp in range(4):
                    pb = kb.nextrot('projbank', 4)
                    for kl in range(2):
                        kc = grp * 2 + kl
                        P.op('pe', lambda e, kc=kc, kl=kl, tt=tt, pb=pb: e.matmul(kb.bank(pb), lhsT=hT[:, kc, tt * 128:(tt + 1) * 128], rhs=csd[:, kl, :], start=(kl == 0), stop=(kl == 1)),
                             reads=[(('csd'), kl), ('hT', tt // 4)], writes=[('ps', pb)])
                    if grp % 2 == 0:
                        P.op('act', lambda e, ob=ob, grp=grp, pb=pb: e.activation(out=pqs[ob][:, grp * 512:(grp + 1) * 512], in_=kb.bank(pb), func=AF.Copy), writes=[('ps', pb), ('pqs', ob, grp)])
                    else:
                        P.op('dve', lambda e, ob=ob, grp=grp, pb=pb: e.tensor_copy(out=pqs[ob][:, grp * 512:(grp + 1) * 512], in_=kb.bank(pb)), writes=[('ps', pb), ('pqs', ob, grp)])
                P.dma('sp', lambda e, ob=ob, tt=tt: e.dma_start(out=pq_d[tt * 128:(tt + 1) * 128, :], in_=pqs[ob][:]), reads=[('pqs', ob, g_) for g_ in range(4)])
    if own:
        P.close()
    return nc


def build_l3b(kb=None, io=None):
    own = kb is None
    if own:
        kb = KB()
    io = io or {}
    kb.pfx = '' if own else 'l3b_'
    P, nc = kb.P, kb.nc
    NTOK = 2048
    x_d = io.get("x") or kb.din("x", [NTOK, D])
    cv_d = kb.din("cv", [128, 8, 2]); adaw_d = kb.din("adaw", [D, 6144]); adab_d = kb.din("adab", [2, 6144])
    g1_d = kb.din("g1", [128, 8]); g2_d = kb.din("g2", [128, 8])
    pq_d = io.get("pq") or kb.din("pq", [8192, 2048], BF16)
    cn_d = kb.din("cn", [8192, NTOK], BF16); sn_d = kb.din("sn", [8192, NTOK], BF16)
    ftw_d = kb.din("ftw", [D, D]); vecs_d = kb.din("vecs", [128, 2, 8])
    w1_d = kb.din("w1", [D, DFF]); w2_d = kb.din("w2", [DFF, D])
    out_d = io.get("out") or kb.dout("out", [NTOK, D])
    modsT = kb.sb("modsT", [128, 48, 2], F32)
    mvL = kb.sb("mvL", [128, 6, 8], F32)
    vecs = kb.sb("vecs", [128, 2, 8], F32)
    bG = kb.sb("bG", [128, 8], F32)
    zeros = kb.sb("zeros", [128, 8], F32)
    with kb.phase():
        kb.mods(cv_d, adaw_d, adab_d, modsT)
        mL = kb.mod_vectors(modsT, g1_d, g2_d, 0, mvL, "L")
        P.dma('sp', lambda e: e.dma_start(out=vecs[:], in_=vecs_d), writes=['vecs'])
        P.op('dve', lambda e: e.tensor_tensor(out=bG[:], in0=vecs[:, 0, :], in1=mL['G1'], op=ALU.mult), reads=['vecs', mL['key']], writes=['bG'])
        P.op('pool', lambda e: e.memset(zeros[:], 0.0), writes=['zeros'])
    with kb.phase():
        zT = kb.sb("zT", [128, 8, NTOK], BF16)
        with kb.phase():
            pqb = [kb.sb("pqb%d" % i, [128, 2048], BF16) for i in range(3)]
            tb = [kb.sb("tb%d" % i, [128, 2, 512], BF16) for i in range(3)]
            tokmap = io.get('tokmap') or (lambda nt: nt * 128)
            for kg in range(4):
                for nt in range(64):
                    tk = tokmap(nt)
                    r = kb.nextrot('pqb', 3)
                    P.dma('sp', lambda e, r=r, nt=nt: e.dma_start(out=pqb[r][:], in_=pq_d[nt * 128:(nt + 1) * 128, :]), writes=[('pqb', r)])
                    P.dma('sp', lambda e, r=r, tk=tk, kg=kg: e.dma_start(out=tb[r][:, 0, :], in_=cn_d[tk:tk + 128, kg * 512:(kg + 1) * 512]), writes=[('tb', r, 0)])
                    P.dma('sp', lambda e, r=r, tk=tk, kg=kg: e.dma_start(out=tb[r][:, 1, :], in_=sn_d[tk:tk + 128, kg * 512:(kg + 1) * 512]), writes=[('tb', r, 1)])
                    for fz in range(8):
                        grp, jh = fz // 2, fz % 2
                        P.op('pe', lambda e, r=r, fz=fz, grp=grp, jh=jh, nt=nt: e.matmul(kb.bank(fz), lhsT=pqb[r][:, grp * 512 + jh * 128:grp * 512 + jh * 128 + 128], rhs=tb[r][:, 0, :],
                                                                                      start=(nt == 0), stop=False),
                             reads=[('pqb', r), ('tb', r, 0)], writes=[('ps', fz)])
                        P.op('pe', lambda e, r=r, fz=fz, grp=grp, jh=jh, nt=nt: e.matmul(kb.bank(fz), lhsT=pqb[r][:, grp * 512 + 256 + jh * 128:grp * 512 + 256 + jh * 128 + 128], rhs=tb[r][:, 1, :],
                                                                                      start=False, stop=(nt == 63)),
                             reads=[('pqb', r), ('tb', r, 1)], writes=[('ps', fz)])
                for fz in range(8):
                    if fz % 2 == 0:
                        P.op('act', lambda e, fz=fz, kg=kg: e.activation(out=zT[:, fz, kg * 512:(kg + 1) * 512], in_=kb.bank(fz), func=AF.Copy), writes=[('ps', fz), ('zT', kg)])
                    else:
                        P.op('dve', lambda e, fz=fz, kg=kg: e.tensor_copy(out=zT[:, fz, kg * 512:(kg + 1) * 512], in_=kb.bank(fz)), writes=[('ps', fz), ('zT', kg)])
        with kb.phase():
            xtmp = kb.sb("xtmp", [128, 8, 512], F32)
            stage = [kb.sb("stage%d" % i, [128, 1024], F32) for i in range(2)]
            ostage = [kb.sb("ostage%d" % i, [128, 1024], F32) for i in range(2)]
            wo = kb.sb("wo", [128, 8, 1024], BF16)
            kb.load_w(wo, 'wo', ftw_d, 0, KC, 0, 1024)
            for g in range(4):
                kb.load_xT(x_d[g * 512:(g + 1) * 512, :], 512, xtmp, 'xtmp', stage)
                resid_proj(kb, zT, 'zT', g * 512, 512, wo, 'wo', xtmp, 'xtmp', 0, mL['G1'], mL['key'], bG=bG)
                kb.store_x(xtmp, 'xtmp', 512, out_d[g * 512:(g + 1) * 512, :], ostage)
    mlp_tail(kb, out_d, out_d, NTOK, mL, w1_d, w2_d, final=(vecs[:, 1, :], zeros[:], 'vecs'))
    if own:
        P.close()
    return nc


def prep_l3a(inp, x3, b, q):
    d = common_inputs(inp, 3, b)
    for k in ('w1', 'w2'):
        d.pop(k)
    if x3 is not None:
        d['x'] = np.ascontiguousarray(x3[b][q * 2048:(q + 1) * 2048])
    dd = np.arange(256)[:, None].astype(np.int64)
    jj = np.arange(256)[None, :].astype(np.int64)
    ang = 2.0 * np.pi * ((dd * jj) % 256).astype(np.float64) / 256.0
    d['csd'] = np.ascontiguousarray(np.concatenate([np.cos(ang) / 16.0, np.sin(ang) / 16.0], axis=1).astype(np.float32))
    return d


_DFT_CACHE = {}


def seq_dft_tables(q):
    if q not in _DFT_CACHE:
        import ml_dtypes
        n = np.arange(8192, dtype=np.int64)[:, None]
        k = np.arange(q * 2048, (q + 1) * 2048, dtype=np.int64)[None, :]
        ang = 2.0 * np.pi * ((n * k) % 8192).astype(np.float64) / 8192.0
        s = 1.0 / np.sqrt(8192.0)
        _DFT_CACHE[q] = (np.ascontiguousarray((np.cos(ang) * s).astype(np.float32).astype(ml_dtypes.bfloat16)),
                         np.ascontiguousarray((-np.sin(ang) * s).astype(np.float32).astype(ml_dtypes.bfloat16)))
    return _DFT_CACHE[q]


def prep_l3b(inp, x3, pq_b, b, q):
    d = common_inputs(inp, 3, b)
    if x3 is not None:
        d['x'] = np.ascontiguousarray(x3[b][q * 2048:(q + 1) * 2048])
        d['pq'] = pq_b
    d['cn'], d['sn'] = seq_dft_tables(q)
    d['ftw'] = np.ascontiguousarray(inp['ft_w'][0])
    d['vecs'] = np.ascontiguousarray(np.stack([fm(inp['ft_b'][0]), fm(inp['final_g'])], axis=1))
    return d


CORES = [(b, q) for b in range(2) for q in range(4)]


def _run(nc, maps):
    res = run_bass_kernel_spmd(nc, maps, core_ids=list(range(8)))
    return res.results


def kernel_unfused(**inputs):
    inp = {k: np.asarray(v) for k, v in inputs.items()}
    r = _run(build_l0(), [prep_l0(inp, b, q) for b, q in CORES])
    x1 = np.stack([np.concatenate([r[b * 4 + q]["out"] for q in range(4)], axis=0) for b in range(2)])
    hctx1 = np.stack([r[b * 4]["hctx"] for b in range(2)])
    r = _run(build_l1(), [prep_l1(inp, x1, hctx1, b, q) for b, q in CORES])
    x2 = np.stack([np.concatenate([r[b * 4 + q]["out"] for q in range(4)], axis=0) for b in range(2)])
    r = _run(build_l2(), [prep_l2(inp, x2, b, q) for b, q in CORES])
    x3 = np.stack([np.concatenate([r[b * 4 + q]["out"] for q in range(4)], axis=0) for b in range(2)])
    r = _run(build_l3a(), [prep_l3a(inp, x3, b, q) for b, q in CORES])
    pq = [np.ascontiguousarray(np.concatenate([r[b * 4 + q]["pq"] for q in range(4)], axis=0)) for b in range(2)]
    r = _run(build_l3b(), [prep_l3b(inp, x3, pq[b], b, q) for b, q in CORES])
    out = np.stack([np.concatenate([r[b * 4 + q]["out"] for q in range(4)], axis=0) for b in range(2)])
    return out.astype(np.float32)


RG = [[0, 1, 2, 3], [4, 5, 6, 7]]


def halo_exchange(kb, src, dst, H, sel, tag):
    P, nc = kb.P, kb.nc
    bF = kb.dint("bounceF" + tag, [H, D]); bL = kb.dint("bounceL" + tag, [H, D])
    gF = kb.dint("gathF" + tag, [4 * H, D]); gL = kb.dint("gathL" + tag, [4 * H, D])
    with kb.phase():
        P.dma('pool', lambda e: e.dma_start(out=bF, in_=src[0:H, :]), writes=['bF'])
        P.dma('pool', lambda e: e.dma_start(out=bL, in_=src[2048 - H:2048, :]), writes=['bL'])
        for g in range(4):
            P.dma('pool', lambda e, g=g: e.dma_start(out=dst[H + g * 512:H + (g + 1) * 512, :], in_=src[g * 512:(g + 1) * 512, :]), writes=[('dst', g)])
        P.coll(lambda e: e.collective_compute("AllGather", ALU.bypass, replica_groups=RG, ins=[bF.opt()], outs=[gF.opt()]), reads=['bF'], writes=['gF'])
        P.coll(lambda e: e.collective_compute("AllGather", ALU.bypass, replica_groups=RG, ins=[bL.opt()], outs=[gL.opt()]), reads=['bL'], writes=['gL'])
        cand = [kb.sb("cand%d" % i, [128, 4, 1024], F32) for i in range(2)]
        acc = [kb.sb("hacc%d" % i, [128, 1024], F32) for i in range(2)]
        for side in range(2):
            gsrc, gkey = (gL, 'gL') if side == 0 else (gF, 'gF')
            for t in range(H // 128):
                r_ = kb.nextrot('cand', 2)
                off = t * 128
                srcv = gsrc.rearrange("(r n) c -> n r c", r=4)[off:off + 128, :, :]
                P.dma('sp', lambda e, r_=r_, srcv=srcv: e.dma_start(out=cand[r_][:], in_=srcv), reads=[gkey], writes=[('cand', r_)])
                P.op('dve', lambda e, r_=r_, side=side: e.tensor_scalar(out=acc[r_][:], in0=cand[r_][:, 0, :], scalar1=sel[:, side * 4:side * 4 + 1], scalar2=None, op0=ALU.mult),
                     reads=[('cand', r_), 'sel'], writes=[('hacc', r_)])
                for r in range(1, 4):
                    P.op('dve', lambda e, r_=r_, side=side, r=r: e.scalar_tensor_tensor(out=acc[r_][:], in0=cand[r_][:, r, :], scalar=sel[:, side * 4 + r:side * 4 + r + 1], in1=acc[r_][:],
                                                                                 op0=ALU.mult, op1=ALU.add),
                         reads=[('cand', r_), 'sel'], writes=[('hacc', r_)])
                d0 = (0 if side == 0 else H + 2048) + t * 128
                P.dma('sp', lambda e, r_=r_, d0=d0: e.dma_start(out=dst[d0:d0 + 128, :], in_=acc[r_][:]), reads=[('hacc', r_)], writes=[('dsth', side, t)])


def pq_tokmap(nt):
    c, r, half = nt // 8, (nt % 8) // 2, nt % 2
    return r * 2048 + c * 256 + half * 128


def build_fused():
    kb = KB()
    P, nc = kb.P, kb.nc
    xb_d = kb.din("xb", [8192, D]); ctx_d = kb.din("ctx", [256, D]); sel_d = kb.din("sel", [128, 8])
    out_d = kb.dout("out", [2048, D])
    sel = kb.sb("sel", [128, 8], F32)
    P.dma('sp', lambda e: e.dma_start(out=sel[:], in_=sel_d), writes=['sel'])
    xa = kb.dint("xa", [2048, D]); hc1 = kb.dint("hc1", [256, D])
    build_l0(kb, dict(xb=xb_d, ctx=ctx_d, out=xa, hctx=hc1))
    xh1 = kb.dint("xh1", [2560, D])
    halo_exchange(kb, xa, xh1, 256, sel, "1")
    xb2 = kb.dint("xb2", [2048, D])
    build_l1(kb, dict(xh=xh1, ctx=hc1, out=xb2))
    xh2 = kb.dint("xh2", [2304, D])
    halo_exchange(kb, xb2, xh2, 128, sel, "2")
    xc = kb.dint("xc", [2048, D])
    build_l2(kb, dict(xh=xh2, out=xc))
    pqo = kb.dint("pqo", [2048, 2048], BF16); pqg = kb.dint("pqg", [8192, 2048], BF16)
    build_l3a(kb, dict(x=xc, pq=pqo))
    with kb.phase():
        for c in range(8):
            P.coll(lambda e, c=c: e.collective_compute("AllGather", ALU.bypass, replica_groups=RG, ins=[pqo[c * 256:(c + 1) * 256, :].opt()], outs=[pqg[c * 1024:(c + 1) * 1024, :].opt()]),
                   writes=[('pqg', c)])
    build_l3b(kb, dict(x=xc, pq=pqg, out=out_d, tokmap=pq_tokmap))
    P.close()
    return nc


def prep_fused(inp, b, q):
    d0 = prep_l0(inp, b, q)
    out = {k: d0[k] for k in ('ident', 'cv', 'xb', 'ctx')}
    sel = np.zeros((128, 8), np.float32)
    if q > 0:
        sel[:, q - 1] = 1.0
    if q < 3:
        sel[:, 4 + q + 1] = 1.0
    out['sel'] = sel
    for pfx, dd in (('l0_', d0), ('l1_', prep_l1(inp, None, None, b, q)), ('l2_', prep_l2(inp, None, b, q)),
                    ('l3a_', prep_l3a(inp, None, b, q)), ('l3b_', prep_l3b(inp, None, None, b, q))):
        for k, v in dd.items():
            if k in ('ident', 'cv', 'xb', 'ctx'):
                continue
            out[pfx + k] = v
    return out


def kernel(**inputs):
    inp = {k: np.asarray(v) for k, v in inputs.items()}
    r = _run(build_fused(), [prep_fused(inp, b, q) for b, q in CORES])
    out = np.stack([np.concatenate([r[b * 4 + q]["out"] for q in range(4)], axis=0) for b in range(2)])
    return out.astype(np.float32)
```

```python
import contextlib
import numpy as np
from concourse.bass_utils import run_bass_kernel_spmd
import concourse.bass as bass
import concourse.mybir as mybir

F32 = mybir.dt.float32
BF16 = mybir.dt.bfloat16
AF = mybir.ActivationFunctionType
ALU = mybir.AluOpType
AX = mybir.AxisListType

ENGS = ('pe', 'act', 'dve', 'pool', 'sp')
NDMASLOT = 8


class Op:
    __slots__ = ('eng', 'fn', 'deps', 'sig', 'sigval', 'dma', 'dslot', 'dval', 'prev_slot_op', 'seq', 'dinc')

    def __init__(self, eng, fn, dma):
        self.eng = eng
        self.fn = fn
        self.deps = []
        self.sig = False
        self.sigval = 0
        self.dma = dma
        self.dslot = None
        self.dval = 0
        self.prev_slot_op = None
        self.dinc = 16


class Prog:
    def __init__(self, nc, same_engine_sync=True):
        self.nc = nc
        self.same = same_engine_sync
        self.E = {'pe': nc.tensor, 'act': nc.scalar, 'dve': nc.vector, 'pool': nc.gpsimd, 'sp': nc.sync}
        self.sem = {}
        self._ctx = []
        for e in ('pe', 'act', 'dve', 'pool'):
            self.sem[e] = self._enter(nc.semaphore('prog_' + e))
        self.dsem = {}
        for q in ('sp', 'pool', 'act'):
            self.dsem[q] = [self._enter(nc.semaphore('dma_%s_%d' % (q, i))) for i in range(NDMASLOT)]
        self.ccsem = self._enter(nc.semaphore('ccsem'))
        self.cccnt = 0
        self.ccscratch = self._enter(nc.sbuf_tensor('ccscratch', [128, 8], F32))
        self.cnt = {e: 0 for e in ENGS}
        self.dcnt = {q: 0 for q in ('sp', 'pool', 'act')}
        self.last_dma = {q: [None] * NDMASLOT for q in ('sp', 'pool', 'act')}
        self.nops = 0
        self._reset_phase()

    def _enter(self, cm):
        v = cm.__enter__()
        self._ctx.append(cm)
        return v

    def alloc(self, cm):
        return self._enter(cm)

    def close(self):
        for cm in reversed(self._ctx):
            cm.__exit__(None, None, None)
        self._ctx = []

    def _reset_phase(self):
        self.ops = {e: [] for e in ENGS}
        self.order = []
        self.last_writer = {}
        self.readers = {}

    def _record(self, eng, fn, reads, writes, dma):
        op = Op(eng, fn, dma)
        deps = set()
        for k in reads:
            w = self.last_writer.get(k)
            if w is not None:
                deps.add(w)
        for k in writes:
            w = self.last_writer.get(k)
            if w is not None:
                deps.add(w)
            for r in self.readers.get(k, ()):
                deps.add(r)
        deps.discard(op)
        for k in reads:
            self.readers.setdefault(k, []).append(op)
        for k in writes:
            self.last_writer[k] = op
            self.readers[k] = []
        best = {}
        out = []
        for d in deps:
            if d.dma:
                out.append(d)
                continue
            if d.eng == eng and not dma:
                if eng == 'pe' or not self.same:
                    continue
            b = best.get(d.eng)
            if b is None or d.seq > b.seq:
                best[d.eng] = d
        out.extend(best.values())
        op.deps = out
        for d in out:
            if not d.dma:
                d.sig = True
        op.seq = len(self.order)
        self.ops[eng].append(op)
        self.order.append(op)
        self.nops += 1
        return op

    def op(self, eng, fn, reads=(), writes=()):
        return self._record(eng, fn, reads, writes, False)

    def coll(self, fn, reads=(), writes=()):
        def wrapped(e):
            ins = fn(e)
            self.cccnt += 1
            ins.then_inc(self.ccsem)
            e.wait_ge(self.ccsem, self.cccnt)
            return e.memset(self.ccscratch[:], 0.0)
        return self._record('pool', wrapped, reads, writes, False)

    def dma(self, q, fn, reads=(), writes=()):
        op = self._record(q, fn, reads, writes, True)
        op.dinc = 16
        j = self.dcnt[q]
        self.dcnt[q] += 1
        slot = j % NDMASLOT
        op.dslot = self.dsem[q][slot]
        op.dval = 16 * (j // NDMASLOT + 1)
        op.prev_slot_op = self.last_dma[q][slot]
        self.last_dma[q][slot] = op
        return op

    def flush(self, final_wait=()):
        for e in ENGS:
            c = self.cnt[e]
            for op in self.ops[e]:
                if op.sig and not op.dma:
                    c += 1
                    op.sigval = c
            self.cnt[e] = c
        ops = self.ops
        sem = self.sem
        lastd = {q: list(v) for q, v in self.last_dma.items()}
        anyd = any(d is not None for v in lastd.values() for d in v)

        def run(engname):
            def body(eng):
                known = {e: 0 for e in ENGS}
                kd = {}
                for op in ops[engname]:
                    if op.dma and op.prev_slot_op is not None:
                        p = op.prev_slot_op
                        key = id(p.dslot)
                        if kd.get(key, 0) < p.dval:
                            eng.wait_ge(p.dslot, p.dval)
                            kd[key] = p.dval
                    for d in op.deps:
                        if d.dma:
                            key = id(d.dslot)
                            if kd.get(key, 0) < d.dval:
                                eng.wait_ge(d.dslot, d.dval)
                                kd[key] = d.dval
                        else:
                            if known[d.eng] < d.sigval:
                                eng.wait_ge(sem[d.eng], d.sigval)
                                known[d.eng] = d.sigval
                    ins = op.fn(eng)
                    if op.dma:
                        if op.dinc == 1:
                            ins.then_inc(op.dslot)
                        else:
                            ins.then_inc(op.dslot, 16)
                    elif op.sig:
                        ins.then_inc(sem[op.eng], 1)
                if engname == 'sp':
                    for q in lastd:
                        for d in lastd[q]:
                            if d is not None:
                                eng.wait_ge(d.dslot, d.dval)
            return body

        with self.nc.Block() as block:
            if ops['sp'] or anyd:
                block.sync(run('sp'))
            if ops['pe']:
                block.tensor(run('pe'))
            if ops['act']:
                block.scalar(run('act'))
            if ops['dve']:
                block.vector(run('dve'))
            if ops['pool']:
                block.gpsimd(run('pool'))
        for q in self.last_dma:
            self.last_dma[q] = [None] * NDMASLOT
        self._reset_phase()
D = 1024
KC = 8
DFF = 4096
EPS = 1e-6


class FM:
    def __init__(self, ap, t0=0):
        self.ap = ap
        self.t0 = t0


def tsl(x, a, b):
    if isinstance(x, FM):
        return FM(x.ap, x.t0 + a)
    return x[a:b, :]


class KB:
    def __init__(self, nt=2048):
        self.nc = nc = bass.Bass("TRN2", target_bir_lowering=False)
        self.P = P = Prog(nc)
        self.NT = nt
        self.stack = []
        self.uid = 0
        self.pfx = ''
        self.shared = {}
        self.ps = P.alloc(nc.psum_tensor("ps", [128, 4096], F32))
        self.ident = self.sb("ident_sb", [128, 128], F32)
        self.ones_bf = self.sb("ones_bf", [128, 128], BF16)
        self.eps_t = self.sb("eps_t", [128, 1], F32)
        self.rot = {}
        ident_d = self.nc.dram_tensor("ident", [128, 128], F32, kind="ExternalInput").ap()
        P.dma('sp', lambda e: e.dma_start(out=self.ident[:], in_=ident_d), writes=['ident'])
        P.op('pool', lambda e: e.memset(self.ones_bf[:], 1.0), writes=['ones_bf'])
        P.op('pool', lambda e: e.memset(self.eps_t[:], EPS), writes=['eps_t'])

    def sb(self, name, shape, dt):
        self.uid += 1
        name = "s%d_%s" % (self.uid, name)
        if self.stack:
            return self.stack[-1].enter_context(self.nc.sbuf_tensor(name, shape, dt))
        return self.P.alloc(self.nc.sbuf_tensor(name, shape, dt))

    @contextlib.contextmanager
    def phase(self):
        st = contextlib.ExitStack()
        self.stack.append(st)
        try:
            yield
            self.P.flush()
        finally:
            self.stack.pop()
            st.close()

    def din(self, name, shape, dt=F32):
        if name in ('cv',):
            if name not in self.shared:
                self.shared[name] = self.nc.dram_tensor(name, list(shape), dt, kind="ExternalInput").ap()
            return self.shared[name]
        return self.nc.dram_tensor(self.pfx + name, list(shape), dt, kind="ExternalInput").ap()

    def dint(self, name, shape, dt=F32):
        return self.nc.dram_tensor(name, list(shape), dt).ap()

    def dout(self, name, shape, dt=F32):
        return self.nc.dram_tensor(name, list(shape), dt, kind="ExternalOutput").ap()

    def bank(self, i, n=512, p0=0, p1=128):
        return self.ps[p0:p1, i * 512:i * 512 + n]

    def nextrot(self, name, n):
        v = self.rot.get(name, 0)
        self.rot[name] = v + 1
        return v % n

    def mods(self, cv_d, adaw_d, adab_d, modsT):
        P, nc = self.P, self.nc
        cv = self.sb("cv", [128, 8, 2], F32)
        sT = self.sb("sT", [128, 8, 2], F32)
        mrow = self.sb("mrow", [2, 6144], F32)
        adab = self.sb("adab", [2, 6144], F32)
        wst = [self.sb("wst%d" % i, [128, 8, 512], F32) for i in range(2)]
        P.dma('sp', lambda e: e.dma_start(out=cv[:], in_=cv_d), writes=['cv'])
        P.dma('sp', lambda e: e.dma_start(out=adab[:], in_=adab_d), writes=['adab'])
        P.op('act', lambda e: e.activation(out=sT[:], in_=cv[:], func=AF.Silu), reads=['cv'], writes=['sT'])
        for cg in range(12):
            b = cg % 2
            src = adaw_d[:, cg * 512:(cg + 1) * 512].rearrange("(kc p) c -> p kc c", p=128)
            for h in range(2):
                P.dma('sp', lambda e, b=b, h=h, src=src: e.dma_start(out=wst[b][:, h * 4:(h + 1) * 4, :], in_=src[:, h * 4:(h + 1) * 4, :]),
                      writes=[('wst', b, h)])
            pb = 6 + (cg % 2)
            for kc in range(KC):
                P.op('pe', lambda e, b=b, kc=kc, pb=pb: e.matmul(self.bank(pb, 512, 0, 2), lhsT=sT[:, kc, :], rhs=wst[b][:, kc, :],
                                                                 start=(kc == 0), stop=(kc == KC - 1)),
                     reads=['sT', ('wst', b, kc // 4)], writes=[('ps', pb)])
            P.op('dve', lambda e, cg=cg, pb=pb: e.tensor_tensor(out=mrow[:, cg * 512:(cg + 1) * 512], in0=self.bank(pb, 512, 0, 2),
                                                                in1=adab[:, cg * 512:(cg + 1) * 512], op=ALU.add),
                 reads=['adab'], writes=[('ps', pb), ('mrow', cg)])
        for ch in range(48):
            P.op('pe', lambda e, ch=ch: e.transpose(self.ps[:, 6 * 512 + ch * 2:6 * 512 + ch * 2 + 2], mrow[0:2, ch * 128:(ch + 1) * 128], self.ident[0:2, 0:2]),
                 reads=[('mrow', ch // 4), 'ident'], writes=[('ps', 6)])
        P.op('dve', lambda e: e.tensor_copy(out=modsT[:].rearrange("p a b -> p (a b)"), in_=self.ps[:, 6 * 512:6 * 512 + 96]),
             writes=[('ps', 6), 'modsT'])
        return modsT

    def mod_vectors(self, modsT, g1_d, g2_d, col, mv, tag):
        P = self.P
        gg = self.sb("gg" + tag, [128, 2, 8], F32)
        P.dma('sp', lambda e: e.dma_start(out=gg[:, 0, :], in_=g1_d), writes=['gg' + tag])
        P.dma('sp', lambda e: e.dma_start(out=gg[:, 1, :], in_=g2_d), writes=['gg' + tag])
        P.op('dve', lambda e: e.tensor_copy(out=mv[:].rearrange("p m k -> p (m k)"), in_=modsT[:, :, col]), reads=['modsT'], writes=['mv' + tag])
        for j, m in ((0, 1), (1, 4)):
            P.op('dve', lambda e, j=j, m=m: e.scalar_tensor_tensor(out=mv[:, m, :], in0=mv[:, m, :], scalar=1.0, in1=gg[:, j, :],
                                                                   op0=ALU.add, op1=ALU.mult),
                 reads=['gg' + tag], writes=['mv' + tag])
        key = 'mv' + tag
        return dict(A1=mv[:, 1, :], B1=mv[:, 0, :], G1=mv[:, 2, :], A2=mv[:, 4, :], B2=mv[:, 3, :], G2=mv[:, 5, :], key=key)

    def load_xT(self, x_d, ntok, xT, xkey, stage, t0=0):
        P = self.P
        if isinstance(x_d, FM):
            keys = [(xkey, g) for g in range(t0 // 512, (t0 + ntok - 1) // 512 + 1)]
            for half in range(2):
                P.dma('sp', lambda e, half=half: e.dma_start(out=xT[:, half * 4:(half + 1) * 4, t0:t0 + ntok], in_=x_d.ap[:, half * 4:(half + 1) * 4, x_d.t0:x_d.t0 + ntok]), writes=keys)
            return
        for tt in range(ntok // 128):
            sb_ = self.nextrot('stage', 2)
            P.dma('sp', lambda e, tt=tt, sb_=sb_: e.dma_start(out=stage[sb_][:], in_=x_d[tt * 128:(tt + 1) * 128, :]), writes=[('stage', sb_)])
            for half in range(2):
                pb = 4 + self.nextrot('ldbank', 2)
                for j in range(4):
                    kc = half * 4 + j
                    P.op('pe', lambda e, sb_=sb_, kc=kc, pb=pb, j=j: e.transpose(self.bank(pb)[:, j * 128:(j + 1) * 128], stage[sb_][:, kc * 128:(kc + 1) * 128], self.ident[:]),
                         reads=[('stage', sb_), 'ident'], writes=[('ps', pb)])
                eng = 'act' if half == 0 else 'dve'
                dst = xT[:, half * 4:(half + 1) * 4, t0 + tt * 128:t0 + (tt + 1) * 128]
                src = self.bank(pb).rearrange("p (a b) -> p a b", a=4)
                if eng == 'act':
                    P.op('act', lambda e, dst=dst, src=src: e.activation(out=dst, in_=src, func=AF.Copy), writes=[('ps', pb), (xkey, (t0 + tt * 128) // 512)])
                else:
                    P.op('dve', lambda e, dst=dst, src=src: e.tensor_copy(out=dst, in_=src), writes=[('ps', pb), (xkey, (t0 + tt * 128) // 512)])

    def store_x(self, xT, xkey, ntok, out_d, ostage, scale_ap=None):
        P = self.P
        if isinstance(out_d, FM):
            keys = [(xkey, g) for g in range(0, (ntok - 1) // 512 + 1)]
            for half in range(2):
                P.dma('sp', lambda e, half=half: e.dma_start(out=out_d.ap[:, half * 4:(half + 1) * 4, out_d.t0:out_d.t0 + ntok], in_=xT[:, half * 4:(half + 1) * 4, 0:ntok]), reads=keys)
            return
        for tt in range(ntok // 128):
            ob = self.nextrot('ostage', 2)
            for half in range(2):
                pb = 4 + self.nextrot('ldbank', 2)
                for j in range(4):
                    kc = half * 4 + j
                    P.op('pe', lambda e, kc=kc, pb=pb, j=j, tt=tt: e.transpose(self.bank(pb)[:, j * 128:(j + 1) * 128], xT[:, kc, tt * 128:(tt + 1) * 128], self.ident[:]),
                         reads=[(xkey, tt // 4), 'ident'], writes=[('ps', pb)])
                dst = ostage[ob][:, half * 512:(half + 1) * 512]
                if half == 0:
                    P.op('act', lambda e, dst=dst, pb=pb: e.activation(out=dst, in_=self.bank(pb), func=AF.Copy), writes=[('ps', pb), ('ostage', ob, half)])
                else:
                    P.op('dve', lambda e, dst=dst, pb=pb: e.tensor_copy(out=dst, in_=self.bank(pb)), writes=[('ps', pb), ('ostage', ob, half)])
            P.dma('sp', lambda e, ob=ob, tt=tt: e.dma_start(out=out_d[tt * 128:(tt + 1) * 128, :], in_=ostage[ob][:]),
                  reads=[('ostage', ob, 0), ('ostage', ob, 1)])

    def norm_mod(self, xT, xkey, ntok, A, B, mkey, hT, hkey, scr, t0=0, ht0=0):
        P = self.P
        tgs = min(512, ntok)
        for g in range(ntok // tgs):
            c0 = t0 + g * tgs
            h0 = ht0 + g * tgs
            pb = 6 + self.nextrot('nbank', 2)
            for kc in range(KC):
                sq = self.nextrot('sq', 2)
                P.op('pool', lambda e, kc=kc, sq=sq, c0=c0: e.tensor_tensor(out=scr['sq'][sq][:, :tgs], in0=xT[:, kc, c0:c0 + tgs], in1=xT[:, kc, c0:c0 + tgs], op=ALU.mult),
                     reads=[(xkey, c0 // 512)], writes=[('sq', sq)])
                P.op('pe', lambda e, kc=kc, sq=sq, pb=pb: e.matmul(self.bank(pb, tgs), lhsT=self.ones_bf[:], rhs=scr['sq'][sq][:, :tgs], start=(kc == 0), stop=(kc == KC - 1)),
                     reads=[('sq', sq), 'ones_bf'], writes=[('ps', pb)])
            rs = scr['rs']
            P.op('act', lambda e, pb=pb: e.activation(out=rs[:, :tgs], in_=self.bank(pb, tgs), func=AF.Ln, scale=1.0 / D, bias=self.eps_t[:]),
                 reads=['eps_t'], writes=[('ps', pb), 'rs'])
            P.op('act', lambda e: e.activation(out=rs[:, :tgs], in_=rs[:, :tgs], func=AF.Exp, scale=-0.5), writes=['rs'])
            for kc in range(KC):
                tb = self.nextrot('tmp', 2)
                P.op('dve', lambda e, kc=kc, tb=tb, c0=c0: e.scalar_tensor_tensor(out=scr['tmp'][tb][:, :tgs], in0=xT[:, kc, c0:c0 + tgs], scalar=A[:, kc:kc + 1],
                                                                                in1=rs[:, :tgs], op0=ALU.mult, op1=ALU.mult),
                     reads=[(xkey, c0 // 512), 'rs', mkey], writes=[('tmp', tb)])
                P.op('act', lambda e, kc=kc, tb=tb, h0=h0: e.activation(out=hT[:, kc, h0:h0 + tgs], in_=scr['tmp'][tb][:, :tgs], func=AF.Identity, bias=B[:, kc:kc + 1]),
                     reads=[('tmp', tb), mkey], writes=[(hkey, h0 // 512)])

    def load_w(self, dst, wkey, w_d, r0, nkc, c0, ncol):
        P = self.P
        for k in range(nkc):
            P.dma('pool', lambda e, k=k: e.dma_start(out=dst[:, k, 0:ncol], in_=w_d[r0 + k * 128:r0 + (k + 1) * 128, c0:c0 + ncol]),
                  writes=[(wkey, k)])

    def mlp(self, xT, xkey, ntok, hT, hkey, G, mkey, w1_d, w2_d, wbuf, aT, rbuf):
        P = self.P
        FB = 512
        nfb = DFF // FB
        tgs = min(512, ntok)
        ntg = ntok // tgs

        def ff1(j):
            wb = j % 2
            self.load_w(wbuf['w1'][wb], ('w1', wb), w1_d, 0, KC, j * FB, FB)
            self.load_w(wbuf['w2'][wb], ('w2', wb), w2_d, j * FB, FB // 128, 0, D)
            for fc in range(FB // 128):
                for g in range(ntg):
                    pb = self.nextrot('ff1bank', 3)
                    for kc in range(KC):
                        P.op('pe', lambda e, wb=wb, fc=fc, g=g, kc=kc, pb=pb: e.matmul(self.bank(pb, tgs), lhsT=wbuf['w1'][wb][:, kc, fc * 128:(fc + 1) * 128],
                                                                                    rhs=hT[:, kc, g * tgs:(g + 1) * tgs], start=(kc == 0), stop=(kc == KC - 1)),
                             reads=[(('w1', wb), kc), (hkey, g)], writes=[('ps', pb)])
                    rb = self.nextrot('rbuf', 2)
                    P.op('act', lambda e, pb=pb, rb=rb: e.activation(out=rbuf[rb][:, :tgs], in_=self.bank(pb, tgs), func=AF.Relu), writes=[('ps', pb), ('rbuf', rb)])
                    P.op('pool', lambda e, rb=rb, wb=wb, fc=fc, g=g: e.tensor_tensor(out=aT[wb][:, fc, g * tgs:(g + 1) * tgs], in0=rbuf[rb][:, :tgs], in1=rbuf[rb][:, :tgs], op=ALU.mult),
                         reads=[('rbuf', rb)], writes=[('aT', wb, fc, g)])

        def ff2(j):
            wb = j % 2
            for dc in range(KC):
                for g in range(ntg):
                    pb = 3 + self.nextrot('ff2bank', 3)
                    nf = FB // 128
                    for fc in range(nf):
                        P.op('pe', lambda e, wb=wb, fc=fc, g=g, dc=dc, pb=pb: e.matmul(self.bank(pb, tgs), lhsT=wbuf['w2'][wb][:, fc, dc * 128:(dc + 1) * 128],
                                                                                    rhs=aT[wb][:, fc, g * tgs:(g + 1) * tgs], start=(fc == 0), stop=(fc == nf - 1)),
                             reads=[(('w2', wb), fc), ('aT', wb, fc, g)], writes=[('ps', pb)])
                    P.op('dve', lambda e, dc=dc, g=g, pb=pb: e.scalar_tensor_tensor(out=xT[:, dc, g * tgs:(g + 1) * tgs], in0=self.bank(pb, tgs), scalar=G[:, dc:dc + 1],
                                                                                  in1=xT[:, dc, g * tgs:(g + 1) * tgs], op0=ALU.mult, op1=ALU.add),
                         reads=[mkey], writes=[('ps', pb), (xkey, g)])

        ff1(0)
        for j in range(nfb):
            if j + 1 < nfb:
                ff1(j + 1)
            ff2(j)


def load_cast(kb, dst, key, src_d):
    kb.P.dma('pool', lambda e: e.dma_start(out=dst, in_=src_d), writes=[key])


def proj_fm(kb, hT, hkey, t0, n, W, wkey, col0, pb):
    for kc in range(KC):
        kb.P.op('pe', lambda e, kc=kc: e.matmul(kb.bank(pb, n), lhsT=W[:, kc, col0:col0 + 128], rhs=hT[:, kc, t0:t0 + n],
                                               start=(kc == 0), stop=(kc == KC - 1)),
                reads=[(wkey, kc), (hkey, t0 // 512)], writes=[('ps', pb)])


def qk_norm_rope(kb, pb, n, gain, rope, qscale, out_ap, outkeys, C):
    P = kb.P
    r = kb.nextrot('qkr', 2)
    kg, k2, rs, t1 = C['kg'][r], C['k2'][r], C['rs2'][r], C['t1'][r]
    P.op('act', lambda e: e.activation(out=kg[:, :n], in_=kb.bank(pb, n), func=AF.Copy, scale=gain[:, 0:1]), reads=['gains'], writes=[('ps', pb), ('kg', r)])
    P.op('act', lambda e: e.activation(out=k2[:, :n], in_=kb.bank(pb, n), func=AF.Square), writes=[('ps', pb), ('k2', r)])
    P.op('pe', lambda e: e.matmul(kb.bank(2, n), lhsT=C['bones'][:], rhs=k2[:, :n], start=True, stop=True), reads=[('k2', r), 'bones'], writes=[('ps', 2)])
    if rope is not None:
        P.op('pe', lambda e: e.matmul(kb.bank(3, n), lhsT=C['rmat'][:], rhs=kg[:, :n], start=True, stop=True), reads=[('kg', r), 'rmat'], writes=[('ps', 3)])
    P.op('act', lambda e: e.activation(out=rs[:, :n], in_=kb.bank(2, n), func=AF.Ln, scale=1.0 / 64, bias=kb.eps_t[:]), reads=['eps_t'], writes=[('ps', 2), ('rs2', r)])
    if qscale:
        P.op('act', lambda e: e.activation(out=rs[:, :n], in_=rs[:, :n], func=AF.Exp, scale=-0.5, bias=C['lnq'][:]), reads=['lnq'], writes=[('rs2', r)])
    else:
        P.op('act', lambda e: e.activation(out=rs[:, :n], in_=rs[:, :n], func=AF.Exp, scale=-0.5), writes=[('rs2', r)])
    if rope is not None:
        cos_ap, sin_ap, rkey = rope
        P.op('dve', lambda e: e.tensor_tensor(out=t1[:, :n], in0=kg[:, :n], in1=cos_ap, op=ALU.mult), reads=[('kg', r), rkey], writes=[('t1', r)])
        P.op('dve', lambda e: e.tensor_tensor(out=kg[:, :n], in0=kb.bank(3, n), in1=sin_ap, op=ALU.mult), reads=[rkey], writes=[('ps', 3), ('kg', r)])
        P.op('dve', lambda e: e.tensor_tensor(out=t1[:, :n], in0=t1[:, :n], in1=kg[:, :n], op=ALU.add), reads=[('kg', r)], writes=[('t1', r)])
        P.op('dve', lambda e: e.tensor_tensor(out=out_ap, in0=t1[:, :n], in1=rs[:, :n], op=ALU.mult), reads=[('t1', r), ('rs2', r)], writes=outkeys)
    else:
        P.op('dve', lambda e: e.tensor_tensor(out=out_ap, in0=kg[:, :n], in1=rs[:, :n], op=ALU.mult), reads=[('kg', r), ('rs2', r)], writes=outkeys)


def attention(kb, q_ap, qkeys, NQ, tiles, dst_a, dst_b, dstkeys, C):
    P = kb.P
    oset = kb.nextrot('oset', 2)
    o0 = 4 + 2 * oset
    nt = len(tiles)
    ssets = []

    def qk(i):
        t = tiles[i]
        ss = kb.nextrot('sset', 2)
        ssets.append(ss)
        s0 = 2 * ss
        P.op('pe', lambda e: e.matmul(kb.bank(s0, NQ), lhsT=t['ka'], rhs=q_ap[0:64, :], start=True, stop=True), reads=t['keys'] + qkeys, writes=[('ps', s0)])
        P.op('pe', lambda e: e.matmul(kb.bank(s0 + 1, NQ), lhsT=t['kb'], rhs=q_ap[64:128, :], start=True, stop=True), reads=t['keys'] + qkeys, writes=[('ps', s0 + 1)])

    qk(0)
    for i in range(nt):
        if i + 1 < nt:
            qk(i + 1)
        t = tiles[i]
        s0 = 2 * ssets[i]
        pbuf = kb.nextrot('pbuf', 3)
        pb_ = C['pbuf'][pbuf]
        src = kb.ps[:, s0 * 512:(s0 + 2) * 512].rearrange("p (h n) -> p h n", h=2)[:, :, 0:NQ]
        if t.get('bias_a') is not None:
            sb_ = C['sbias'][kb.nextrot('sbias', 2)]
            P.op('dve', lambda e, sb_=sb_, t=t, s0=s0: e.tensor_tensor(out=sb_[:, 0, 0:NQ], in0=kb.bank(s0, NQ), in1=t['bias_a'], op=ALU.add), reads=t['bkeys'], writes=[('ps', s0), ('sbias', id(sb_), 0)])
            P.op('dve', lambda e, sb_=sb_, t=t, s0=s0: e.tensor_tensor(out=sb_[:, 1, 0:NQ], in0=kb.bank(s0 + 1, NQ), in1=t['bias_b'], op=ALU.add), reads=t['bkeys'], writes=[('ps', s0 + 1), ('sbias', id(sb_), 1)])
            P.op('act', lambda e, sb_=sb_, pb_=pb_: e.activation(out=pb_[:, :, 0:NQ], in_=sb_[:, :, 0:NQ], func=AF.Exp), reads=[('sbias', id(sb_), 0), ('sbias', id(sb_), 1)], writes=[('pbuf', pbuf)])
        else:
            P.op('act', lambda e, src=src, pb_=pb_: e.activation(out=pb_[:, :, 0:NQ], in_=src, func=AF.Exp), writes=[('ps', s0), ('ps', s0 + 1), ('pbuf', pbuf)])
        P.op('pe', lambda e, t=t, pb_=pb_, i=i: e.matmul(kb.bank(o0, NQ), lhsT=t['va'], rhs=pb_[:, 0, 0:NQ], start=(i == 0), stop=(i == nt - 1)), reads=t['keys'] + [('pbuf', pbuf)], writes=[('ps', o0)])
        P.op('pe', lambda e, t=t, pb_=pb_, i=i: e.matmul(kb.bank(o0 + 1, NQ), lhsT=t['vb'], rhs=pb_[:, 1, 0:NQ], start=(i == 0), stop=(i == nt - 1)), reads=t['keys'] + [('pbuf', pbuf)], writes=[('ps', o0 + 1)])
    rcr = kb.nextrot('rc', 2)
    rc = C['rc'][rcr]
    P.op('dve', lambda e: e.reciprocal(out=rc[64:128, 0:NQ], in_=kb.bank(o0, NQ)[64:128, :]), writes=[('ps', o0), ('rc', rcr, 0)])
    P.op('dve', lambda e: e.tensor_tensor(out=dst_a, in0=kb.bank(o0, NQ)[0:64, :], in1=rc[64:128, 0:NQ], op=ALU.mult), reads=[('rc', rcr, 0)], writes=[('ps', o0)] + dstkeys)
    P.op('dve', lambda e: e.reciprocal(out=rc[0:64, 0:NQ], in_=kb.bank(o0 + 1, NQ)[0:64, :]), writes=[('ps', o0 + 1), ('rc', rcr, 1)])
    P.op('dve', lambda e: e.tensor_tensor(out=dst_b, in0=kb.bank(o0 + 1, NQ)[64:128, :], in1=rc[0:64, 0:NQ], op=ALU.mult), reads=[('rc', rcr, 1)], writes=[('ps', o0 + 1)] + dstkeys)


def attn_scratch(kb, with_bias=False):
    C = dict(pbuf=[kb.sb("pbuf%d" % i, [128, 2, 512], BF16) for i in range(3)],
             rc=[kb.sb("rc%d" % i, [128, 512], F32) for i in range(2)])
    if with_bias:
        C['sbias'] = [kb.sb("sbias%d" % i, [128, 2, 512], F32) for i in range(2)]
    return C


def qk_scratch(kb, C):
    C['kg'] = [kb.sb("kg%d" % i, [128, 512], BF16) for i in range(2)]
    C['k2'] = [kb.sb("k2%d" % i, [128, 512], BF16) for i in range(2)]
    C['rs2'] = [kb.sb("rs2%d" % i, [128, 512], F32) for i in range(2)]
    C['t1'] = [kb.sb("t1%d" % i, [128, 512], F32) for i in range(2)]


def norm_scratch(kb):
    return dict(sq=[kb.sb("sq%d" % i, [128, 512], BF16) for i in range(2)], rs=kb.sb("rs", [128, 512], F32),
                tmp=[kb.sb("tmp%d" % i, [128, 512], F32) for i in range(2)])


def mlp_bufs(kb, ntok):
    wbuf = dict(w1=[kb.sb("w1b%d" % i, [128, 8, 512], BF16) for i in range(3)], w2=[kb.sb("w2b%d" % i, [128, 4, 1024], BF16) for i in range(3)])
    aT = [kb.sb("aT%d" % i, [128, 4, ntok], BF16) for i in range(2)]
    rbuf = [kb.sb("rbuf%d" % i, [128, 512], F32) for i in range(2)]
    return wbuf, aT, rbuf


def resid_proj(kb, srcT, skey, t0, n, W, wkey, xT, xkey, xt0, G, mkey, bG=None):
    P = kb.P
    for dc in range(KC):
        pb = kb.nextrot('projbank', 2)
        proj_fm(kb, srcT, skey, t0, n, W, wkey, dc * 128, pb)
        P.op('dve', lambda e, dc=dc, pb=pb: e.scalar_tensor_tensor(out=xT[:, dc, xt0:xt0 + n], in0=kb.bank(pb, n), scalar=G[:, dc:dc + 1],
                                                                 in1=xT[:, dc, xt0:xt0 + n], op0=ALU.mult, op1=ALU.add),
             reads=[mkey], writes=[('ps', pb), (xkey, xt0 // 512)])
        if bG is not None:
            P.op('act', lambda e, dc=dc: e.activation(out=xT[:, dc, xt0:xt0 + n], in_=xT[:, dc, xt0:xt0 + n], func=AF.Identity, bias=bG[:, dc:dc + 1]),
                 reads=['bG'], writes=[(xkey, xt0 // 512)])


def build_l0(kb=None, io=None):
    own = kb is None
    if own:
        kb = KB()
    io = io or {}
    kb.pfx = '' if own else 'l0_'
    P, nc = kb.P, kb.nc
    NTOK = 2048
    xb_d = io.get("xb") or kb.din("xb", [8192, D]); ctx_d = io.get("ctx") or kb.din("ctx", [256, D])
    cv_d = kb.din("cv", [128, 8, 2]); adaw_d = kb.din("adaw", [D, 6144]); adab_d = kb.din("adab", [2, 6144])
    g1_d = kb.din("g1", [128, 8]); g2_d = kb.din("g2", [128, 8])
    wqkv_d = kb.din("wqkv", [D, 1536]); gains_d = kb.din("gains", [128, 2])
    cos_d = kb.din("cos", [16, 128, 512]); sin_d = kb.din("sin", [16, 128, 512])
    wo_d = kb.din("wo", [D, D]); w1_d = kb.din("w1", [D, DFF]); w2_d = kb.din("w2", [DFF, D])
    rmat_d = kb.din("rmat", [128, 128]); bones_d = kb.din("bones", [128, 128])
    out_d = io.get("out") or kb.dout("out", [NTOK, D]); hctx_d = io.get("hctx") or kb.dout("hctx", [256, D])
    mid_d = io.get("mid") or out_d

    modsT = kb.sb("modsT", [128, 48, 2], F32)
    mvL = kb.sb("mvL", [128, 6, 8], F32); mvC = kb.sb("mvC", [128, 6, 8], F32)
    C = dict(rmat=kb.sb("rmat", [128, 128], BF16), bones=kb.sb("bones", [128, 128], BF16), lnq=kb.sb("lnq", [128, 1], F32))
    gains = kb.sb("gains", [128, 2], F32)
    with kb.phase():
        load_cast(kb, C['rmat'][:], 'rmat', rmat_d)
        load_cast(kb, C['bones'][:], 'bones', bones_d)
        P.op('pool', lambda e: e.memset(C['lnq'][:], float(np.log(0.125))), writes=['lnq'])
        P.dma('sp', lambda e: e.dma_start(out=gains[:], in_=gains_d), writes=['gains'])
        kb.mods(cv_d, adaw_d, adab_d, modsT)
        mL = kb.mod_vectors(modsT, g1_d, g2_d, 0, mvL, "L")
        mC = kb.mod_vectors(modsT, g1_d, g2_d, 1, mvC, "C")
    qgain, kgain = gains[:, 0:1], gains[:, 1:2]

    with kb.phase():
        KT = kb.sb("KT", [128, 2, 8448], BF16)
        Ve = kb.sb("Ve", [128, 66, 384], BF16)
        P.op('pool', lambda e: e.memset(Ve[:].rearrange("p t (a b c) -> p (t a) b c", a=2, b=3, c=64)[:, :, 1, :], 1.0), writes=['Ve_ones'])

        def kv_tiles(tile_ids, pr):
            out = []
            for kt in tile_ids:
                out.append(dict(ka=KT[0:64, pr, kt * 128:(kt + 1) * 128], kb=KT[64:128, pr, kt * 128:(kt + 1) * 128],
                                va=Ve[:, kt, pr * 192:pr * 192 + 128], vb=Ve[:, kt, pr * 192 + 64:pr * 192 + 192],
                                keys=[('KT', kt // 4), ('Ve', kt), 'Ve_ones']))
            return out

        def produce_kv(hT, hkey, n, g, wqkv, rope):
            for pr in range(2):
                pb = kb.nextrot('projbank', 2)
                proj_fm(kb, hT, hkey, 0, n, wqkv, 'wqkv', 1024 + pr * 128, pb)
                qk_norm_rope(kb, pb, n, kgain, rope, False, KT[:, pr, g * 512:g * 512 + n], [('KT', g)], C)
            for tt in range(n // 128):
                pb = kb.nextrot('projbank', 2)
                for kc in range(KC):
                    P.op('pe', lambda e, kc=kc, tt=tt, pb=pb: e.matmul(kb.bank(pb, 256), lhsT=hT[:, kc, tt * 128:(tt + 1) * 128], rhs=wqkv[:, kc, 1280:1536],
                                                                      start=(kc == 0), stop=(kc == KC - 1)),
                         reads=[('wqkv', kc), (hkey, 0)], writes=[('ps', pb)])
                kt = g * 4 + tt
                dst = Ve[:, kt, :].rearrange("p (a b c) -> p a b c", a=2, b=3, c=64)[:, :, ::2, :]
                src = kb.bank(pb, 256).rearrange("p (a b c) -> p a b c", a=2, b=2, c=64)
                P.op('dve', lambda e, dst=dst, src=src: e.tensor_copy(out=dst, in_=src), writes=[('ps', pb), ('Ve', kt)])

        with kb.phase():
            cT = kb.sb("cT", [128, 8, 256], F32)
            stage = [kb.sb("stage%d" % i, [128, 1024], F32) for i in range(2)]
            scr = norm_scratch(kb)
            with kb.phase():
                hcT = kb.sb("hcT", [128, 8, 256], BF16)
                QcT = kb.sb("QcT", [128, 8, 256], BF16)
                OcT = kb.sb("OcT", [128, 8, 256], BF16)
                qk_scratch(kb, C)
                C.update(attn_scratch(kb))
                wqkv = kb.sb("wqkv", [128, 8, 1536], BF16)
                wo = kb.sb("wo", [128, 8, 1024], BF16)
                kb.load_w(wqkv, 'wqkv', wqkv_d, 0, KC, 0, 1536)
                kb.load_w(wo, 'wo', wo_d, 0, KC, 0, 1024)
                kb.load_xT(ctx_d, 256, cT, 'cT', stage)
                kb.norm_mod(cT, 'cT', 256, mC['A1'], mC['B1'], mC['key'], hcT, 'hcT', scr)
                produce_kv(hcT, 'hcT', 256, 16, wqkv, None)
                for c in range(8):
                    pb = kb.nextrot('projbank', 2)
                    proj_fm(kb, hcT, 'hcT', 0, 256, wqkv, 'wqkv', c * 128, pb)
                    qk_norm_rope(kb, pb, 256, qgain, None, True, QcT[:, c, :], [('QcT', c)], C)
                for c in range(8):
                    attention(kb, QcT[:, c, :], [('QcT', c)], 256, kv_tiles([64, 65], c // 4), OcT[0:64, c, :], OcT[64:128, c, :], [('OcT', 0)], C)
                resid_proj(kb, OcT, 'OcT', 0, 256, wo, 'wo', cT, 'cT', 0, mC['G1'], mC['key'])
            with kb.phase():
                hc2 = kb.sb("hc2", [128, 8, 256], BF16)
                kb.norm_mod(cT, 'cT', 256, mC['A2'], mC['B2'], mC['key'], hc2, 'hc2', scr)
                wbuf, aT, rbuf = mlp_bufs(kb, 256)
                kb.mlp(cT, 'cT', 256, hc2, 'hc2', mC['G2'], mC['key'], w1_d, w2_d, wbuf, aT, rbuf)
                kb.store_x(cT, 'cT', 256, hctx_d, stage)

        with kb.phase():
            QT = kb.sb("QT", [128, 8, NTOK], BF16)
            with kb.phase():
                xtmp = kb.sb("xtmp", [128, 8, 512], F32)
                hTt = kb.sb("hTt", [128, 8, 512], BF16)
                stage = [kb.sb("stage%d" % i, [128, 1024], F32) for i in range(2)]
                scr = norm_scratch(kb)
                qk_scratch(kb, C)
                wqkv = kb.sb("wqkv", [128, 8, 1536], BF16)
                cs = [kb.sb("cs%d" % i, [128, 2, 512], F32) for i in range(2)]
                kb.load_w(wqkv, 'wqkv', wqkv_d, 0, KC, 0, 1536)
                for g in range(16):
                    kb.load_xT(tsl(xb_d, g * 512, (g + 1) * 512), 512, xtmp, 'xtmp', stage)
                    kb.norm_mod(xtmp, 'xtmp', 512, mL['A1'], mL['B1'], mL['key'], hTt, 'hTt', scr)
                    cb = g % 2
                    P.dma('sp', lambda e, g=g, cb=cb: e.dma_start(out=cs[cb][:, 0, :], in_=cos_d[g]), writes=[('cs', cb)])
                    P.dma('sp', lambda e, g=g, cb=cb: e.dma_start(out=cs[cb][:, 1, :], in_=sin_d[g]), writes=[('cs', cb)])
                    rope = (cs[cb][:, 0, :], cs[cb][:, 1, :], ('cs', cb))
                    produce_kv(hTt, 'hTt', 512, g, wqkv, rope)
                    if g < 4:
                        for c in range(8):
                            pb = kb.nextrot('projbank', 2)
                            proj_fm(kb, hTt, 'hTt', 0, 512, wqkv, 'wqkv', c * 128, pb)
                            qk_norm_rope(kb, pb, 512, qgain, rope, True, QT[:, c, g * 512:(g + 1) * 512], [('QT', c, g)], C)
            with kb.phase():
                OT = kb.sb("OT", [128, 8, NTOK], BF16)
                with kb.phase():
                    C.update(attn_scratch(kb))
                    for qg in range(4):
                        for c in range(8):
                            attention(kb, QT[:, c, qg * 512:(qg + 1) * 512], [('QT', c, qg)], 512, kv_tiles(list(range(66)), c // 4),
                                      OT[0:64, c, qg * 512:(qg + 1) * 512], OT[64:128, c, qg * 512:(qg + 1) * 512], [('OT', qg)], C)
                with kb.phase():
                    xtmp = kb.sb("xtmp", [128, 8, 512], F32)
                    stage = [kb.sb("stage%d" % i, [128, 1024], F32) for i in range(2)]
                    wo = kb.sb("wo", [128, 8, 1024], BF16)
                    ostage = [kb.sb("ostage%d" % i, [128, 1024], F32) for i in range(2)]
                    kb.load_w(wo, 'wo', wo_d, 0, KC, 0, 1024)
                    for g in range(4):
                        kb.load_xT(tsl(xb_d, g * 512, (g + 1) * 512), 512, xtmp, 'xtmp', stage)
                        resid_proj(kb, OT, 'OT', g * 512, 512, wo, 'wo', xtmp, 'xtmp', 0, mL['G1'], mL['key'])
                        kb.store_x(xtmp, 'xtmp', 512, tsl(mid_d, g * 512, (g + 1) * 512), ostage)
    mlp_tail(kb, mid_d, out_d, NTOK, mL, w1_d, w2_d)
    if own:
        P.close()
    return nc


def mlp_tail(kb, src_d, out_d, ntok, mL, w1_d, w2_d, final=None):
    with kb.phase():
        xT = kb.sb("xT", [128, 8, ntok], F32)
        with kb.phase():
            stage = [kb.sb("stage%d" % i, [128, 1024], F32) for i in range(2)]
            kb.load_xT(src_d, ntok, xT, 'xT', stage)
        with kb.phase():
            hT = kb.sb("hT", [128, 8, ntok], BF16)
            scr = norm_scratch(kb)
            kb.norm_mod(xT, 'xT', ntok, mL['A2'], mL['B2'], mL['key'], hT, 'hT', scr)
            wbuf, aT, rbuf = mlp_bufs(kb, ntok)
            kb.mlp(xT, 'xT', ntok, hT, 'hT', mL['G2'], mL['key'], w1_d, w2_d, wbuf, aT, rbuf)
        with kb.phase():
            ostage = [kb.sb("ostage%d" % i, [128, 1024], F32) for i in range(2)]
            if final is not None:
                yT = kb.sb("yT", [128, 8, ntok], F32)
                scr = norm_scratch(kb)
                kb.norm_mod(xT, 'xT', ntok, final[0], final[1], final[2], yT, 'yT', scr)
                kb.store_x(yT, 'yT', ntok, out_d, ostage)
            else:
                kb.store_x(xT, 'xT', ntok, out_d, ostage)


def fm(v):
    return np.ascontiguousarray(np.asarray(v, np.float32).reshape(8, 128).T)


def common_inputs(inp, layer, b):
    cv = np.stack([fm(inp['c'][b]), fm(inp['c_ctx'])], axis=-1)
    return dict(ident=np.eye(128, dtype=np.float32), cv=np.ascontiguousarray(cv), adaw=np.ascontiguousarray(inp['ada_w'][layer]),
                adab=np.ascontiguousarray(np.stack([inp['ada_b'][layer]] * 2)), g1=fm(inp['norm1_g'][layer]), g2=fm(inp['norm2_g'][layer]),
                w1=np.ascontiguousarray(inp['mlp_w1'][layer]), w2=np.ascontiguousarray(inp['mlp_w2'][layer]))


def rope_tables(order):
    t = np.asarray(order)
    row = (t // 64).astype(np.float32)
    col = (t % 64).astype(np.float32)
    inv = (10000.0 ** (-np.arange(16, dtype=np.float32) / 16)).astype(np.float32)
    ang = np.concatenate([row[:, None] * inv, col[:, None] * inv], axis=-1).astype(np.float32)
    idx = (np.arange(128) % 64) // 2
    a = ang[:, idx].T
    cos = np.cos(a).astype(np.float32).reshape(128, 16, 512).transpose(1, 0, 2)
    sin = np.sin(a).astype(np.float32).reshape(128, 16, 512).transpose(1, 0, 2)
    return np.ascontiguousarray(cos), np.ascontiguousarray(sin)


def gqa_chunk_heads():
    return [(c, 4 + c) if c < 4 else (8 + c - 4, 12 + c - 4) for c in range(8)]


def prep_l0(inp, b, q):
    d = common_inputs(inp, 0, b)
    order = np.concatenate([np.arange(q * 2048, 8192), np.arange(0, q * 2048)])
    d['xb'] = np.ascontiguousarray(inp['x'][b][order])
    d['ctx'] = np.ascontiguousarray(inp['ctx'][b])
    wqkv = inp['at_w_qkv'][0]
    qcols = np.concatenate([np.concatenate([np.arange(ha * 64, ha * 64 + 64), np.arange(hb * 64, hb * 64 + 64)]) for ha, hb in gqa_chunk_heads()])
    d['wqkv'] = np.ascontiguousarray(np.concatenate([wqkv[:, qcols], wqkv[:, 1024:]], axis=1))
    d['wo'] = np.ascontiguousarray(inp['at_w_o'][0][qcols, :])
    d['gains'] = np.ascontiguousarray(np.stack([np.tile(inp['at_q_g'][0], 2), np.tile(inp['at_k_g'][0], 2)], axis=1))
    d['cos'], d['sin'] = rope_tables(order)
    rmat = np.zeros((128, 128), np.float32)
    for i in range(64):
        rmat[2 * i + 1, 2 * i] = -1.0
        rmat[2 * i, 2 * i + 1] = 1.0
    d['rmat'] = rmat
    bones = np.zeros((128, 128), np.float32)
    bones[:64, :64] = 1.0
    bones[64:, 64:] = 1.0
    d['bones'] = bones
    return d


NA_CLASS = [0, 1] + [2] * 12 + [3, 4]
NA_ST = list(range(14)) + [12, 13]


def build_l1(kb=None, io=None):
    own = kb is None
    if own:
        kb = KB()
    io = io or {}
    kb.pfx = '' if own else 'l1_'
    P, nc = kb.P, kb.nc
    NTOK = 2048
    NH = 2560
    xh_d = io.get("xh") or kb.din("xh", [NH, D]); ctx_d = io.get("ctx") or kb.din("ctx", [256, D])
    cv_d = kb.din("cv", [128, 8, 2]); adaw_d = kb.din("adaw", [D, 6144]); adab_d = kb.din("adab", [2, 6144])
    g1_d = kb.din("g1", [128, 8]); g2_d = kb.din("g2", [128, 8])
    wqkv_d = kb.din("wqkv", [8, D, 384]); tab_d = kb.din("tab", [5, 8, 128, 2 * 7 * 128])
    wo_d = kb.din("wo", [D, D]); w1_d = kb.din("w1", [D, DFF]); w2_d = kb.din("w2", [DFF, D])
    out_d = io.get("out") or kb.dout("out", [NTOK, D])
    mid_d = io.get("mid") or out_d

    modsT = kb.sb("modsT", [128, 48, 2], F32)
    mvL = kb.sb("mvL", [128, 6, 8], F32); mvC = kb.sb("mvC", [128, 6, 8], F32)
    C = {}
    with kb.phase():
        kb.mods(cv_d, adaw_d, adab_d, modsT)
        mL = kb.mod_vectors(modsT, g1_d, g2_d, 0, mvL, "L")
        mC = kb.mod_vectors(modsT, g1_d, g2_d, 1, mvC, "C")
    with kb.phase():
        hT = kb.sb("hT", [128, 8, NH], BF16)
        hcT = kb.sb("hcT", [128, 8, 256], BF16)
        OT = kb.sb("OT", [128, 8, NTOK], BF16)
        with kb.phase():
            xtmp = kb.sb("xtmp", [128, 8, 512], F32)
            stage = [kb.sb("stage%d" % i, [128, 1024], F32) for i in range(2)]
            scr = norm_scratch(kb)
            for g in range(5):
                kb.load_xT(tsl(xh_d, g * 512, (g + 1) * 512), 512, xtmp, 'xtmp', stage)
                kb.norm_mod(xtmp, 'xtmp', 512, mL['A1'], mL['B1'], mL['key'], hT, 'hT', scr, t0=0, ht0=g * 512)
            kb.load_xT(ctx_d, 256, xtmp, 'xtmp', stage)
            kb.norm_mod(xtmp, 'xtmp', 256, mC['A1'], mC['B1'], mC['key'], hcT, 'hcT', scr)
        with kb.phase():
            C.update(attn_scratch(kb, with_bias=True))
            wc = [kb.sb("wc%d" % i, [128, 8, 384], BF16) for i in range(2)]
            QTc = [kb.sb("QTc%d" % i, [128, NTOK], BF16) for i in range(2)]
            KTc = [kb.sb("KTc%d" % i, [128, NH + 256], BF16) for i in range(2)]
            Vec = [kb.sb("Vec%d" % i, [128, 22, 192], BF16) for i in range(2)]
            tabI = [kb.sb("tabI%d" % i, [128, 2, 7, 128], F32) for i in range(2)]
            tabS = [kb.sb("tabS%d" % i, [128, 2, 7, 128], F32) for i in range(2)]
            for i in range(2):
                P.op('pool', lambda e, i=i: e.memset(Vec[i][:, :, 64:128], 1.0), writes=[('Vones', i)])
            for c in range(8):
                b = c % 2
                kb.load_w(wc[b], ('wc', b), wqkv_d[c], 0, KC, 0, 384)
                P.dma('sp', lambda e, c=c, b=b: e.dma_start(out=tabI[b][:].rearrange("p a j q -> p (a j q)"), in_=tab_d[2, c]), writes=[('tabI', b)])
                for g in range(4):
                    pb = kb.nextrot('projbank', 2)
                    proj_fm(kb, hT, 'hT', 256 + g * 512, 512, wc[b], ('wc', b), 0, pb)
                    P.op('act', lambda e, g=g, b=b, pb=pb: e.activation(out=QTc[b][:, g * 512:(g + 1) * 512], in_=kb.bank(pb), func=AF.Copy, scale=0.125),
                         writes=[('ps', pb), ('QTc', b, g)])
                for g in range(5):
                    pb = kb.nextrot('projbank', 2)
                    proj_fm(kb, hT, 'hT', g * 512, 512, wc[b], ('wc', b), 128, pb)
                    P.op('act', lambda e, g=g, b=b, pb=pb: e.activation(out=KTc[b][:, g * 512:(g + 1) * 512], in_=kb.bank(pb), func=AF.Copy),
                         writes=[('ps', pb), ('KTc', b, g)])
                pb = kb.nextrot('projbank', 2)
                proj_fm(kb, hcT, 'hcT', 0, 256, wc[b], ('wc', b), 128, pb)
                P.op('act', lambda e, b=b, pb=pb: e.activation(out=KTc[b][:, NH:NH + 256], in_=kb.bank(pb, 256), func=AF.Copy), writes=[('ps', pb), ('KTc', b, 5)])
                for kt in range(22):
                    src_h, hk, t0 = (hT, 'hT', kt * 128) if kt < 20 else (hcT, 'hcT', (kt - 20) * 128)
                    pb = kb.nextrot('projbank', 2)
                    for kc in range(KC):
                        P.op('pe', lambda e, kc=kc, b=b, pb=pb, src_h=src_h, t0=t0: e.matmul(kb.bank(pb, 128), lhsT=src_h[:, kc, t0:t0 + 128], rhs=wc[b][:, kc, 256:384],
                                                                                          start=(kc == 0), stop=(kc == KC - 1)),
                             reads=[(('wc', b), kc), (hk, t0 // 512)], writes=[('ps', pb)])
                    dst = Vec[b][:, kt, :].rearrange("p (t s) -> p t s", s=64)[:, ::2, :]
                    src = kb.bank(pb, 128).rearrange("p (t s) -> p t s", s=64)
                    P.op('dve', lambda e, dst=dst, src=src: e.tensor_copy(out=dst, in_=src), writes=[('ps', pb), ('Vec', b, kt)])
                for rp in range(16):
                    cls = NA_CLASS[rp]
                    st = NA_ST[rp]
                    if cls == 2:
                        tab, tkey = tabI[b], ('tabI', b)
                    else:
                        sbuf_i = kb.nextrot('tabS', 2)
                        tab, tkey = tabS[sbuf_i], ('tabS', sbuf_i)
                        P.dma('sp', lambda e, c=c, cls=cls, tab=tab: e.dma_start(out=tab[:].rearrange("p a j q -> p (a j q)"), in_=tab_d[cls, c]), writes=[tkey])
                    tiles = []
                    for j in range(9):
                        kt = st + j if j < 7 else 20 + (j - 7)
                        k0 = kt * 128
                        tl = dict(ka=KTc[b][0:64, k0:k0 + 128], kb=KTc[b][64:128, k0:k0 + 128], va=Vec[b][:, kt, 0:128], vb=Vec[b][:, kt, 64:192],
                                  keys=[('KTc', b, k0 // 512), ('Vec', b, kt), ('Vones', b)])
                        if j < 7:
                            tl['bias_a'] = tab[:, 0, j, :]
                            tl['bias_b'] = tab[:, 1, j, :]
                            tl['bkeys'] = [tkey]
                        tiles.append(tl)
                    attention(kb, QTc[b][:, rp * 128:(rp + 1) * 128], [('QTc', b, rp // 4)], 128, tiles,
                              OT[0:64, c, rp * 128:(rp + 1) * 128], OT[64:128, c, rp * 128:(rp + 1) * 128], [('OT', rp // 4)], C)
        with kb.phase():
            xtmp = kb.sb("xtmp", [128, 8, 512], F32)
            stage = [kb.sb("stage%d" % i, [128, 1024], F32) for i in range(2)]
            ostage = [kb.sb("ostage%d" % i, [128, 1024], F32) for i in range(2)]
            wo = kb.sb("wo", [128, 8, 1024], BF16)
            kb.load_w(wo, 'wo', wo_d, 0, KC, 0, 1024)
            for g in range(4):
                kb.load_xT(tsl(xh_d, 256 + g * 512, 256 + (g + 1) * 512), 512, xtmp, 'xtmp', stage)
                resid_proj(kb, OT, 'OT', g * 512, 512, wo, 'wo', xtmp, 'xtmp', 0, mL['G1'], mL['key'])
                kb.store_x(xtmp, 'xtmp', 512, tsl(mid_d, g * 512, (g + 1) * 512), ostage)
    mlp_tail(kb, mid_d, out_d, NTOK, mL, w1_d, w2_d)
    if own:
        P.close()
    return nc


def na_bias_tables(rpb, qq):
    NEG = np.float32(-30000.0)
    tab = np.full((5, 16, 2, 64, 7, 2, 64), NEG, np.float32)
    cq = np.arange(64)
    cs = np.clip(cq - 8, 0, 48)
    ck = np.arange(64)
    colvalid = (ck[:, None] >= cs[None, :]) & (ck[:, None] < cs[None, :] + 16)
    colidx = np.clip(ck[:, None] - cq[None, :] + 15, 0, 30)
    rep_rp = {0: 0, 1: 1, 2: 2, 3: 14, 4: 15}
    for cls in range(5):
        rp = rep_rp[cls]
        st = NA_ST[rp]
        for bq in range(2):
            r = 32 * qq + 2 * rp + bq
            rs = min(max(r - 4, 0), 120)
            for j in range(7):
                for a in range(2):
                    kr = 32 * qq - 4 + 2 * (st + j) + a
                    if kr < rs or kr >= rs + 8 or kr < 0 or kr > 127:
                        continue
                    vals = rpb[:, kr - r + 7, :][:, colidx]
                    tab[cls, :, a, :, j, bq, :] = np.where(colvalid[None], vals, NEG)
    tab = tab.reshape(5, 8, 2, 128, 7, 128)
    tab = tab.transpose(0, 1, 3, 2, 4, 5).reshape(5, 8, 128, 2 * 7 * 128)
    return np.ascontiguousarray(tab)


def prep_l1(inp, x1, hctx1, b, q):
    d = common_inputs(inp, 1, b)
    if x1 is not None:
        xh = np.zeros((2560, D), np.float32)
        lo = q * 2048 - 256
        hi = lo + 2560
        s0, s1 = max(lo, 0), min(hi, 8192)
        xh[s0 - lo:s1 - lo] = x1[b][s0:s1]
        d['xh'] = xh
        d['ctx'] = np.ascontiguousarray(hctx1[b])
    w = inp['na_w_qkv'][0]
    d['wqkv'] = np.ascontiguousarray(np.stack([np.concatenate([w[:, c * 128:(c + 1) * 128], w[:, 1024 + c * 128:1024 + (c + 1) * 128],
                                                              w[:, 2048 + c * 128:2048 + (c + 1) * 128]], axis=1) for c in range(8)]))
    d['tab'] = na_bias_tables(inp['na_rpb'][0], q)
    d['wo'] = np.ascontiguousarray(inp['na_w_o'][0])
    return d


def build_l2(kb=None, io=None):
    own = kb is None
    if own:
        kb = KB()
    io = io or {}
    kb.pfx = '' if own else 'l2_'
    P, nc = kb.P, kb.nc
    NTOK = 2048
    NH = 2304
    xh_d = io.get("xh") or kb.din("xh", [NH, D])
    cv_d = kb.din("cv", [128, 8, 2]); adaw_d = kb.din("adaw", [D, 6144]); adab_d = kb.din("adab", [2, 6144])
    g1_d = kb.din("g1", [128, 8]); g2_d = kb.din("g2", [128, 8])
    wpw1_d = kb.din("wpw1", [8, D, 256]); vecs_d = kb.din("vecs", [128, 6, 8]); wdw_d = kb.din("wdw", [128, 8, 31]); mask_d = kb.din("mask", [128, 2])
    wpw2_d = kb.din("wpw2", [D, D]); w1_d = kb.din("w1", [D, DFF]); w2_d = kb.din("w2", [DFF, D])
    out_d = io.get("out") or kb.dout("out", [NTOK, D])
    mid_d = io.get("mid") or out_d

    modsT = kb.sb("modsT", [128, 48, 2], F32)
    mvL = kb.sb("mvL", [128, 6, 8], F32)
    vecs = kb.sb("vecs", [128, 6, 8], F32)
    wdw = kb.sb("wdw", [128, 8, 31], F32)
    mask = kb.sb("mask", [128, 2], F32)
    bG = kb.sb("bG", [128, 8], F32)
    identb = kb.sb("identb", [128, 128], BF16)
    with kb.phase():
        kb.mods(cv_d, adaw_d, adab_d, modsT)
        mL = kb.mod_vectors(modsT, g1_d, g2_d, 0, mvL, "L")
        P.dma('sp', lambda e: e.dma_start(out=vecs[:], in_=vecs_d), writes=['vecs'])
        P.dma('sp', lambda e: e.dma_start(out=wdw[:], in_=wdw_d), writes=['wdw'])
        P.dma('sp', lambda e: e.dma_start(out=mask[:], in_=mask_d), writes=['mask'])
        P.op('dve', lambda e: e.tensor_tensor(out=bG[:], in0=vecs[:, 5, :], in1=mL['G1'], op=ALU.mult), reads=['vecs', mL['key']], writes=['bG'])
        P.op('dve', lambda e: e.tensor_copy(out=identb[:], in_=kb.ident[:]), reads=['ident'], writes=['identb'])
    with kb.phase():
        vT = kb.sb("vT", [128, 8, NTOK], BF16)
        with kb.phase():
            uT = kb.sb("uT", [128, 8, NH], BF16)
            with kb.phase():
                hT = kb.sb("hT", [128, 8, NH], BF16)
                with kb.phase():
                    xtmp = kb.sb("xtmp", [128, 8, 512], F32)
                    stage = [kb.sb("stage%d" % i, [128, 1024], F32) for i in range(2)]
                    scr = norm_scratch(kb)
                    for g in range(5):
                        n = 512 if g < 4 else 256
                        kb.load_xT(tsl(xh_d, g * 512, g * 512 + n), n, xtmp, 'xtmp', stage)
                        kb.norm_mod(xtmp, 'xtmp', n, mL['A1'], mL['B1'], mL['key'], hT, 'hT', scr, t0=0, ht0=g * 512)
                with kb.phase():
                    wp = [kb.sb("wp%d" % i, [128, 8, 256], BF16) for i in range(2)]
                    sig = [kb.sb("sig%d" % i, [128, 512], F32) for i in range(2)]
                    for fc in range(8):
                        b = fc % 2
                        kb.load_w(wp[b], ('wp', b), wpw1_d[fc], 0, KC, 0, 256)
                        for g in range(5):
                            n = 512 if g < 4 else 256
                            pa = kb.nextrot('projbank', 2)
                            proj_fm(kb, hT, 'hT', g * 512, n, wp[b], ('wp', b), 0, pa)
                            pg = 2 + kb.nextrot('projbank2', 2)
                            proj_fm(kb, hT, 'hT', g * 512, n, wp[b], ('wp', b), 128, pg)
                            sb_ = kb.nextrot('sig', 2)
                            P.op('act', lambda e, sb_=sb_, pg=pg, fc=fc, n=n: e.activation(out=sig[sb_][:, :n], in_=kb.bank(pg, n), func=AF.Sigmoid, bias=vecs[:, 1, fc:fc + 1]),
                                 reads=['vecs'], writes=[('ps', pg), ('sig', sb_)])
                            P.op('dve', lambda e, sb_=sb_, pa=pa, fc=fc, g=g, n=n: e.scalar_tensor_tensor(out=uT[:, fc, g * 512:g * 512 + n], in0=kb.bank(pa, n), scalar=vecs[:, 0, fc:fc + 1],
                                                                                                      in1=sig[sb_][:, :n], op0=ALU.add, op1=ALU.mult),
                                 reads=['vecs', ('sig', sb_)], writes=[('ps', pa), ('uT', fc, g)])
                        P.op('dve', lambda e, fc=fc: e.tensor_scalar(out=uT[:, fc, 0:128], in0=uT[:, fc, 0:128], scalar1=mask[:, 0:1], scalar2=None, op0=ALU.mult),
                             reads=['mask'], writes=[('uT', fc, 0)])
                        P.op('dve', lambda e, fc=fc: e.tensor_scalar(out=uT[:, fc, 2176:2304], in0=uT[:, fc, 2176:2304], scalar1=mask[:, 1:2], scalar2=None, op0=ALU.mult),
                             reads=['mask'], writes=[('uT', fc, 4)])
            with kb.phase():
                dg = kb.sb("dg", [128, 8, 31, 128], BF16)
                cT = kb.sb("cT", [128, 8, 512], F32)
                cbf = [kb.sb("cbf%d" % i, [128, 512], BF16) for i in range(2)]
                c2 = [kb.sb("c2%d" % i, [128, 512], BF16) for i in range(2)]
                mean = kb.sb("mean", [128, 512], F32); msq = kb.sb("msq", [128, 512], F32); rstd = kb.sb("rstd", [128, 512], F32)
                tt_ = [kb.sb("tt%d" % i, [128, 512], F32) for i in range(2)]
                for fc in range(8):
                    P.op('dve', lambda e, fc=fc: e.tensor_tensor(out=dg[:, fc, :, :], in0=identb[:].unsqueeze(1).broadcast_to([128, 31, 128]),
                                                                in1=wdw[:, fc, :].unsqueeze(2).broadcast_to([128, 31, 128]), op=ALU.mult),
                         reads=['identb', 'wdw'], writes=[('dg', fc)])
                for tg in range(4):
                    for fc in range(8):
                        pb = kb.nextrot('projbank', 2)
                        for j in range(31):
                            o = 128 + tg * 512 + j - 15
                            P.op('pe', lambda e, fc=fc, j=j, o=o, pb=pb: e.matmul(kb.bank(pb), lhsT=dg[:, fc, j, :], rhs=uT[:, fc, o:o + 512], start=(j == 0), stop=(j == 30)),
                                 reads=[('dg', fc)] + [('uT', fc, gg) for gg in range(5)], writes=[('ps', pb)])
                        P.op('act', lambda e, fc=fc, pb=pb: e.activation(out=cT[:, fc, :], in_=kb.bank(pb), func=AF.Identity, bias=vecs[:, 2, fc:fc + 1]),
                             reads=['vecs'], writes=[('ps', pb), ('cT', fc)])
                        r = kb.nextrot('cbf', 2)
                        P.op('dve', lambda e, fc=fc, r=r: e.tensor_copy(out=cbf[r][:], in_=cT[:, fc, :]), reads=[('cT', fc)], writes=[('cbf', r)])
                        P.op('act', lambda e, fc=fc, r=r: e.activation(out=c2[r][:], in_=cT[:, fc, :], func=AF.Square), reads=[('cT', fc)], writes=[('c2', r)])
                        P.op('pe', lambda e, fc=fc, r=r: e.matmul(kb.bank(6), lhsT=kb.ones_bf[:], rhs=cbf[r][:], start=(fc == 0), stop=(fc == 7)), reads=[('cbf', r), 'ones_bf'], writes=[('ps', 6)])
                        P.op('pe', lambda e, fc=fc, r=r: e.matmul(kb.bank(7), lhsT=kb.ones_bf[:], rhs=c2[r][:], start=(fc == 0), stop=(fc == 7)), reads=[('c2', r), 'ones_bf'], writes=[('ps', 7)])
                    P.op('act', lambda e: e.activation(out=mean[:], in_=kb.bank(6), func=AF.Copy, scale=1.0 / D), writes=[('ps', 6), 'mean'])
                    P.op('dve', lambda e: e.tensor_tensor(out=msq[:], in0=mean[:], in1=mean[:], op=ALU.mult), reads=['mean'], writes=['msq'])
                    P.op('dve', lambda e: e.scalar_tensor_tensor(out=msq[:], in0=kb.bank(7), scalar=1.0 / D, in1=msq[:], op0=ALU.mult, op1=ALU.subtract), writes=[('ps', 7), 'msq'])
                    P.op('act', lambda e: e.activation(out=rstd[:], in_=msq[:], func=AF.Ln, bias=kb.eps_t[:]), reads=['msq', 'eps_t'], writes=['rstd'])
                    P.op('act', lambda e: e.activation(out=rstd[:], in_=rstd[:], func=AF.Exp, scale=-0.5), writes=['rstd'])
                    for fc in range(8):
                        r = kb.nextrot('tt', 2)
                        P.op('dve', lambda e, fc=fc, r=r: e.tensor_tensor(out=tt_[r][:], in0=cT[:, fc, :], in1=mean[:], op=ALU.subtract), reads=[('cT', fc), 'mean'], writes=[('tt', r)])
                        P.op('dve', lambda e, r=r: e.tensor_tensor(out=tt_[r][:], in0=tt_[r][:], in1=rstd[:], op=ALU.mult), reads=['rstd'], writes=[('tt', r)])
                        P.op('act', lambda e, fc=fc, r=r, tg=tg: e.activation(out=vT[:, fc, tg * 512:(tg + 1) * 512], in_=tt_[r][:], func=AF.Silu, scale=vecs[:, 3, fc:fc + 1], bias=vecs[:, 4, fc:fc + 1]),
                             reads=[('tt', r), 'vecs'], writes=[('vT', tg)])
        with kb.phase():
            xtmp = kb.sb("xtmp", [128, 8, 512], F32)
            stage = [kb.sb("stage%d" % i, [128, 1024], F32) for i in range(2)]
            ostage = [kb.sb("ostage%d" % i, [128, 1024], F32) for i in range(2)]
            wo = kb.sb("wo", [128, 8, 1024], BF16)
            kb.load_w(wo, 'wo', wpw2_d, 0, KC, 0, 1024)
            for g in range(4):
                kb.load_xT(tsl(xh_d, 128 + g * 512, 128 + (g + 1) * 512), 512, xtmp, 'xtmp', stage)
                resid_proj(kb, vT, 'vT', g * 512, 512, wo, 'wo', xtmp, 'xtmp', 0, mL['G1'], mL['key'], bG=bG)
                kb.store_x(xtmp, 'xtmp', 512, tsl(mid_d, g * 512, (g + 1) * 512), ostage)
    mlp_tail(kb, mid_d, out_d, NTOK, mL, w1_d, w2_d)
    if own:
        P.close()
    return nc


def prep_l2(inp, x2, b, q):
    d = common_inputs(inp, 2, b)
    if x2 is not None:
        xh = np.zeros((2304, D), np.float32)
        lo = q * 2048 - 128
        hi = lo + 2304
        s0, s1 = max(lo, 0), min(hi, 8192)
        xh[s0 - lo:s1 - lo] = x2[b][s0:s1]
        d['xh'] = xh
    w = inp['cv_w_pw1'][0]
    d['wpw1'] = np.ascontiguousarray(np.stack([np.concatenate([w[:, c * 128:(c + 1) * 128], w[:, 1024 + c * 128:1024 + (c + 1) * 128]], axis=1) for c in range(8)]))
    bp = inp['cv_b_pw1'][0]
    d['vecs'] = np.ascontiguousarray(np.stack([fm(bp[:1024]), fm(bp[1024:]), fm(inp['cv_b_dw'][0]), fm(inp['cv_ln_g'][0]), fm(inp['cv_ln_b'][0]), fm(inp['cv_b_pw2'][0])], axis=1))
    d['wdw'] = np.ascontiguousarray(inp['cv_w_dw'][0].T.reshape(8, 128, 31).transpose(1, 0, 2))
    m = np.ones((128, 2), np.float32)
    if q == 0:
        m[:, 0] = 0.0
    if q == 3:
        m[:, 1] = 0.0
    d['mask'] = m
    d['wpw2'] = np.ascontiguousarray(inp['cv_w_pw2'][0])
    return d


def build_l3a(kb=None, io=None):
    own = kb is None
    if own:
        kb = KB()
    io = io or {}
    kb.pfx = '' if own else 'l3a_'
    P, nc = kb.P, kb.nc
    NTOK = 2048
    x_d = io.get("x") or kb.din("x", [NTOK, D])
    cv_d = kb.din("cv", [128, 8, 2]); adaw_d = kb.din("adaw", [D, 6144]); adab_d = kb.din("adab", [2, 6144])
    g1_d = kb.din("g1", [128, 8]); g2_d = kb.din("g2", [128, 8])
    csd_d = kb.din("csd", [256, 512])
    pq_d = io.get("pq") or kb.dout("pq", [NTOK, 2048], BF16)
    modsT = kb.sb("modsT", [128, 48, 2], F32)
    mvL = kb.sb("mvL", [128, 6, 8], F32)
    with kb.phase():
        kb.mods(cv_d, adaw_d, adab_d, modsT)
        mL = kb.mod_vectors(modsT, g1_d, g2_d, 0, mvL, "L")
    io['mL_out'] = mL
    with kb.phase():
        hT = kb.sb("hT", [128, 8, NTOK], BF16)
        csd = kb.sb("csd", [128, 2, 512], BF16)
        kb.load_w(csd, 'csd', csd_d, 0, 2, 0, 512)
        with kb.phase():
            xtmp = kb.sb("xtmp", [128, 8, 512], F32)
            stage = [kb.sb("stage%d" % i, [128, 1024], F32) for i in range(2)]
            scr = norm_scratch(kb)
            for g in range(4):
                kb.load_xT(tsl(x_d, g * 512, (g + 1) * 512), 512, xtmp, 'xtmp', stage)
                kb.norm_mod(xtmp, 'xtmp', 512, mL['A1'], mL['B1'], mL['key'], hT, 'hT', scr, t0=0, ht0=g * 512)
        with kb.phase():
            pqs = [kb.sb("pqs%d" % i, [128, 2048], BF16) for i in range(2)]
            for tt in range(16):
                ob = tt % 2
                for grp in range(4):
                    pb = kb.nextrot('projbank', 4)
                    for kl in range(2):
                        kc = grp * 2 + kl
                        P.op('pe', lambda e, kc=kc, kl=kl, tt=tt, pb=pb: e.matmul(kb.bank(pb), lhsT=hT[:, kc, tt * 128:(tt + 1) * 128], rhs=csd[:, kl, :], start=(kl == 0), stop=(kl == 1)),
                             reads=[(('csd'), kl), ('hT', tt // 4)], writes=[('ps', pb)])
                    if grp % 2 == 0:
                        P.op('act', lambda e, ob=ob, grp=grp, pb=pb: e.activation(out=pqs[ob][:, grp * 512:(grp + 1) * 512], in_=kb.bank(pb), func=AF.Copy), writes=[('ps', pb), ('pqs', ob, grp)])
                    else:
                        P.op('dve', lambda e, ob=ob, grp=grp, pb=pb: e.tensor_copy(out=pqs[ob][:, grp * 512:(grp + 1) * 512], in_=kb.bank(pb)), writes=[('ps', pb), ('pqs', ob, grp)])
                P.dma('sp', lambda e, ob=ob, tt=tt: e.dma_start(out=pq_d[tt * 128:(tt + 1) * 128, :], in_=pqs[ob][:]), reads=[('pqs', ob, g_) for g_ in range(4)])
    if own:
        P.close()
    return nc


def build_l3b(kb=None, io=None):
    own = kb is None
    if own:
        kb = KB()
    io = io or {}
    kb.pfx = '' if own else 'l3b_'
    P, nc = kb.P, kb.nc
    NTOK = 2048
    x_d = io.get("x") or kb.din("x", [NTOK, D])
    cv_d = kb.din("cv", [128, 8, 2]); adaw_d = kb.din("adaw", [D, 6144]); adab_d = kb.din("adab", [2, 6144])
    g1_d = kb.din("g1", [128, 8]); g2_d = kb.din("g2", [128, 8])
    pq_d = io.get("pq") or kb.din("pq", [8192, 2048], BF16)
    cn_d = kb.din("cn", [8192, NTOK], BF16); sn_d = kb.din("sn", [8192, NTOK], BF16)
    ftw_d = kb.din("ftw", [D, D]); vecs_d = kb.din("vecs", [128, 2, 8])
    w1_d = kb.din("w1", [D, DFF]); w2_d = kb.din("w2", [DFF, D])
    out_d = io.get("out") or kb.dout("out", [NTOK, D])
    mid_d = io.get("mid") or out_d
    modsT = kb.sb("modsT", [128, 48, 2], F32)
    mvL = kb.sb("mvL", [128, 6, 8], F32)
    vecs = kb.sb("vecs", [128, 2, 8], F32)
    bG = kb.sb("bG", [128, 8], F32)
    zeros = kb.sb("zeros", [128, 8], F32)
    with kb.phase():
        if io.get('mL') is not None:
            mL = io['mL']
        else:
            kb.mods(cv_d, adaw_d, adab_d, modsT)
            mL = kb.mod_vectors(modsT, g1_d, g2_d, 0, mvL, "L")
        P.dma('sp', lambda e: e.dma_start(out=vecs[:], in_=vecs_d), writes=['vecs'])
        P.op('dve', lambda e: e.tensor_tensor(out=bG[:], in0=vecs[:, 0, :], in1=mL['G1'], op=ALU.mult), reads=['vecs', mL['key']], writes=['bG'])
        P.op('pool', lambda e: e.memset(zeros[:], 0.0), writes=['zeros'])
    with kb.phase():
        zT = kb.sb("zT", [128, 8, NTOK], BF16)
        with kb.phase():
            pqb = [kb.sb("pqb%d" % i, [128, 2048], BF16) for i in range(3)]
            tb = [kb.sb("tb%d" % i, [128, 2, 512], BF16) for i in range(3)]
            tokmap = io.get('tokmap') or (lambda nt: nt * 128)
            for kg in range(4):
                for nt in range(64):
                    tk = tokmap(nt)
                    r = kb.nextrot('pqb', 3)
                    P.dma('sp', lambda e, r=r, nt=nt: e.dma_start(out=pqb[r][:], in_=pq_d[nt * 128:(nt + 1) * 128, :]), writes=[('pqb', r)])
                    P.dma('sp', lambda e, r=r, tk=tk, kg=kg: e.dma_start(out=tb[r][:, 0, :], in_=cn_d[tk:tk + 128, kg * 512:(kg + 1) * 512]), writes=[('tb', r, 0)])
                    P.dma('sp', lambda e, r=r, tk=tk, kg=kg: e.dma_start(out=tb[r][:, 1, :], in_=sn_d[tk:tk + 128, kg * 512:(kg + 1) * 512]), writes=[('tb', r, 1)])
                    for fz in range(8):
                        grp, jh = fz // 2, fz % 2
                        P.op('pe', lambda e, r=r, fz=fz, grp=grp, jh=jh, nt=nt: e.matmul(kb.bank(fz), lhsT=pqb[r][:, grp * 512 + jh * 128:grp * 512 + jh * 128 + 128], rhs=tb[r][:, 0, :],
                                                                                      start=(nt == 0), stop=False),
                             reads=[('pqb', r), ('tb', r, 0)], writes=[('ps', fz)])
                        P.op('pe', lambda e, r=r, fz=fz, grp=grp, jh=jh, nt=nt: e.matmul(kb.bank(fz), lhsT=pqb[r][:, grp * 512 + 256 + jh * 128:grp * 512 + 256 + jh * 128 + 128], rhs=tb[r][:, 1, :],
                                                                                      start=False, stop=(nt == 63)),
                             reads=[('pqb', r), ('tb', r, 1)], writes=[('ps', fz)])
                for fz in range(8):
                    if fz % 2 == 0:
                        P.op('act', lambda e, fz=fz, kg=kg: e.activation(out=zT[:, fz, kg * 512:(kg + 1) * 512], in_=kb.bank(fz), func=AF.Copy), writes=[('ps', fz), ('zT', kg)])
                    else:
                        P.op('dve', lambda e, fz=fz, kg=kg: e.tensor_copy(out=zT[:, fz, kg * 512:(kg + 1) * 512], in_=kb.bank(fz)), writes=[('ps', fz), ('zT', kg)])
        with kb.phase():
            xtmp = kb.sb("xtmp", [128, 8, 512], F32)
            stage = [kb.sb("stage%d" % i, [128, 1024], F32) for i in range(2)]
            ostage = [kb.sb("ostage%d" % i, [128, 1024], F32) for i in range(2)]
            wo = kb.sb("wo", [128, 8, 1024], BF16)
            kb.load_w(wo, 'wo', ftw_d, 0, KC, 0, 1024)
            for g in range(4):
                kb.load_xT(tsl(x_d, g * 512, (g + 1) * 512), 512, xtmp, 'xtmp', stage)
                resid_proj(kb, zT, 'zT', g * 512, 512, wo, 'wo', xtmp, 'xtmp', 0, mL['G1'], mL['key'], bG=bG)
                kb.store_x(xtmp, 'xtmp', 512, tsl(mid_d, g * 512, (g + 1) * 512), ostage)
    mlp_tail(kb, mid_d, out_d, NTOK, mL, w1_d, w2_d, final=(vecs[:, 1, :], zeros[:], 'vecs'))
    if own:
        P.close()
    return nc


def prep_l3a(inp, x3, b, q):
    d = common_inputs(inp, 3, b)
    for k in ('w1', 'w2'):
        d.pop(k)
    if x3 is not None:
        d['x'] = np.ascontiguousarray(x3[b][q * 2048:(q + 1) * 2048])
    dd = np.arange(256)[:, None].astype(np.int64)
    jj = np.arange(256)[None, :].astype(np.int64)
    ang = 2.0 * np.pi * ((dd * jj) % 256).astype(np.float64) / 256.0
    d['csd'] = np.ascontiguousarray(np.concatenate([np.cos(ang) / 16.0, np.sin(ang) / 16.0], axis=1).astype(np.float32))
    return d


_DFT_CACHE = {}


def seq_dft_tables(q):
    if q not in _DFT_CACHE:
        import ml_dtypes
        n = np.arange(8192, dtype=np.int64)[:, None]
        k = np.arange(q * 2048, (q + 1) * 2048, dtype=np.int64)[None, :]
        ang = 2.0 * np.pi * ((n * k) % 8192).astype(np.float64) / 8192.0
        s = 1.0 / np.sqrt(8192.0)
        _DFT_CACHE[q] = (np.ascontiguousarray((np.cos(ang) * s).astype(np.float32).astype(ml_dtypes.bfloat16)),
                         np.ascontiguousarray((-np.sin(ang) * s).astype(np.float32).astype(ml_dtypes.bfloat16)))
    return _DFT_CACHE[q]


def prep_l3b(inp, x3, pq_b, b, q):
    d = common_inputs(inp, 3, b)
    if x3 is not None:
        d['x'] = np.ascontiguousarray(x3[b][q * 2048:(q + 1) * 2048])
        d['pq'] = pq_b
    d['cn'], d['sn'] = seq_dft_tables(q)
    d['ftw'] = np.ascontiguousarray(inp['ft_w'][0])
    d['vecs'] = np.ascontiguousarray(np.stack([fm(inp['ft_b'][0]), fm(inp['final_g'])], axis=1))
    return d


CORES = [(b, q) for b in range(2) for q in range(4)]


def _run(nc, maps):
    res = run_bass_kernel_spmd(nc, maps, core_ids=list(range(8)))
    return res.results


def kernel_unfused(**inputs):
    inp = {k: np.asarray(v) for k, v in inputs.items()}
    r = _run(build_l0(), [prep_l0(inp, b, q) for b, q in CORES])
    x1 = np.stack([np.concatenate([r[b * 4 + q]["out"] for q in range(4)], axis=0) for b in range(2)])
    hctx1 = np.stack([r[b * 4]["hctx"] for b in range(2)])
    r = _run(build_l1(), [prep_l1(inp, x1, hctx1, b, q) for b, q in CORES])
    x2 = np.stack([np.concatenate([r[b * 4 + q]["out"] for q in range(4)], axis=0) for b in range(2)])
    r = _run(build_l2(), [prep_l2(inp, x2, b, q) for b, q in CORES])
    x3 = np.stack([np.concatenate([r[b * 4 + q]["out"] for q in range(4)], axis=0) for b in range(2)])
    r = _run(build_l3a(), [prep_l3a(inp, x3, b, q) for b, q in CORES])
    pq = [np.ascontiguousarray(np.concatenate([r[b * 4 + q]["pq"] for q in range(4)], axis=0)) for b in range(2)]
    r = _run(build_l3b(), [prep_l3b(inp, x3, pq[b], b, q) for b, q in CORES])
    out = np.stack([np.concatenate([r[b * 4 + q]["out"] for q in range(4)], axis=0) for b in range(2)])
    return out.astype(np.float32)


RG = [[0, 1, 2, 3], [4, 5, 6, 7]]


def halo_exchange(kb, src, dst, H, sel, tag):
    P, nc = kb.P, kb.nc
    bF = kb.dint("bounceF" + tag, [1024, H]); bL = kb.dint("bounceL" + tag, [1024, H])
    gF = kb.dint("gathF" + tag, [4096, H]); gL = kb.dint("gathL" + tag, [4096, H])
    with kb.phase():
        P.dma('pool', lambda e: e.dma_start(out=bF.rearrange("(p k) h -> p k h", k=8), in_=src.ap[:, :, 0:H]), writes=['bF'])
        P.dma('pool', lambda e: e.dma_start(out=bL.rearrange("(p k) h -> p k h", k=8), in_=src.ap[:, :, 2048 - H:2048]), writes=['bL'])
        for hf in range(2):
            P.dma('pool', lambda e, hf=hf: e.dma_start(out=dst.ap[:, hf * 4:(hf + 1) * 4, H:H + 2048], in_=src.ap[:, hf * 4:(hf + 1) * 4, :]), writes=[('dst', hf)])
        P.coll(lambda e: e.collective_compute("AllGather", ALU.bypass, replica_groups=RG, ins=[bF.opt()], outs=[gF.opt()]), reads=['bF'], writes=['gF'])
        P.coll(lambda e: e.collective_compute("AllGather", ALU.bypass, replica_groups=RG, ins=[bL.opt()], outs=[gL.opt()]), reads=['bL'], writes=['gL'])
        cand = [kb.sb("cand%d" % i, [128, 4, 8, H], F32) for i in range(2)]
        acc = [kb.sb("hacc%d" % i, [128, 8, H], F32) for i in range(2)]
        for side in range(2):
            gsrc, gkey = (gL, 'gL') if side == 0 else (gF, 'gF')
            srcv = gsrc.rearrange("(r p k) h -> p r k h", r=4, k=8)
            for r in range(4):
                P.dma('sp', lambda e, side=side, srcv=srcv, r=r: e.dma_start(out=cand[side][:, r, :, :], in_=srcv[:, r, :, :]), reads=[gkey], writes=[('cand', side, r)])
            P.op('dve', lambda e, side=side: e.tensor_scalar(out=acc[side][:], in0=cand[side][:, 0, :, :], scalar1=sel[:, side * 4:side * 4 + 1], scalar2=None, op0=ALU.mult),
                 reads=[('cand', side, 0), 'sel'], writes=[('hacc', side)])
            for r in range(1, 4):
                P.op('dve', lambda e, side=side, r=r: e.scalar_tensor_tensor(out=acc[side][:], in0=cand[side][:, r, :, :], scalar=sel[:, side * 4 + r:side * 4 + r + 1], in1=acc[side][:],
                                                                          op0=ALU.mult, op1=ALU.add),
                     reads=[('cand', side, r), 'sel'], writes=[('hacc', side)])
            d0 = 0 if side == 0 else H + 2048
            P.dma('sp', lambda e, side=side, d0=d0: e.dma_start(out=dst.ap[:, :, d0:d0 + H], in_=acc[side][:]), reads=[('hacc', side)], writes=[('dsth', side)])


def pq_tokmap(nt):
    c, r, half = nt // 8, (nt % 8) // 2, nt % 2
    return r * 2048 + c * 256 + half * 128


def build_fused():
    kb = KB()
    P, nc = kb.P, kb.nc
    xb_d = kb.din("xb", [8192, D]); ctx_d = kb.din("ctx", [256, D]); sel_d = kb.din("sel", [128, 8])
    out_d = kb.dout("out", [2048, D])
    sel = kb.sb("sel", [128, 8], F32)
    P.dma('sp', lambda e: e.dma_start(out=sel[:], in_=sel_d), writes=['sel'])

    def fmt(name, n):
        return FM(kb.dint(name, [128, 8, n]))
    mid = fmt("mid", 2048)
    xa = fmt("xa", 2048); hc1 = fmt("hc1", 256)
    build_l0(kb, dict(xb=xb_d, ctx=ctx_d, out=xa, hctx=hc1, mid=mid))
    xh1 = fmt("xh1", 2560)
    halo_exchange(kb, xa, xh1, 256, sel, "1")
    xb2 = fmt("xb2", 2048)
    build_l1(kb, dict(xh=xh1, ctx=hc1, out=xb2, mid=mid))
    xh2 = fmt("xh2", 2304)
    halo_exchange(kb, xb2, xh2, 128, sel, "2")
    xc = fmt("xc", 2048)
    build_l2(kb, dict(xh=xh2, out=xc, mid=mid))
    pqo = kb.dint("pqo", [2048, 2048], BF16); pqg = kb.dint("pqg", [8192, 2048], BF16)
    io3 = dict(x=xc, pq=pqo)
    build_l3a(kb, io3)
    with kb.phase():
        for c in range(8):
            P.coll(lambda e, c=c: e.collective_compute("AllGather", ALU.bypass, replica_groups=RG, ins=[pqo[c * 256:(c + 1) * 256, :].opt()], outs=[pqg[c * 1024:(c + 1) * 1024, :].opt()]),
                   writes=[('pqg', c)])
    build_l3b(kb, dict(x=xc, pq=pqg, out=out_d, mid=mid, tokmap=pq_tokmap, mL=io3['mL_out']))
    P.close()
    return nc


def prep_fused(inp, b, q):
    d0 = prep_l0(inp, b, q)
    out = {k: d0[k] for k in ('ident', 'cv', 'xb', 'ctx')}
    sel = np.zeros((128, 8), np.float32)
    if q > 0:
        sel[:, q - 1] = 1.0
    if q < 3:
        sel[:, 4 + q + 1] = 1.0
    out['sel'] = sel
    for pfx, dd in (('l0_', d0), ('l1_', prep_l1(inp, None, None, b, q)), ('l2_', prep_l2(inp, None, b, q)),
                    ('l3a_', prep_l3a(inp, None, b, q)), ('l3b_', prep_l3b(inp, None, None, b, q))):
        for k, v in dd.items():
            if k in ('ident', 'cv', 'xb', 'ctx'):
                continue
            out[pfx + k] = v
    return out


def kernel(**inputs):
    inp = {k: np.asarray(v) for k, v in inputs.items()}
    r = _run(build_fused(), [prep_fused(inp, b, q) for b, q in CORES])
    out = np.stack([np.concatenate([r[b * 4 + q]["out"] for q in range(4)], axis=0) for b in range(2)])
    return out.astype(np.float32)
```

```python
import contextlib
import numpy as np
from concourse.bass_utils import run_bass_kernel_spmd
import concourse.bass as bass
import concourse.mybir as mybir

F32 = mybir.dt.float32
BF16 = mybir.dt.bfloat16
AF = mybir.ActivationFunctionType
ALU = mybir.AluOpType
AX = mybir.AxisListType

ENGS = ('pe', 'act', 'dve', 'pool', 'sp')
RG = [[0, 1, 2, 3], [4, 5, 6, 7]]
NDMASLOT = 8


class Op:
    __slots__ = ('eng', 'fn', 'deps', 'sig', 'sigval', 'dma', 'dslot', 'dval', 'prev_slot_op', 'seq', 'dinc')

    def __init__(self, eng, fn, dma):
        self.eng = eng
        self.fn = fn
        self.deps = []
        self.sig = False
        self.sigval = 0
        self.dma = dma
        self.dslot = None
        self.dval = 0
        self.prev_slot_op = None
        self.dinc = 16


class Prog:
    def __init__(self, nc, same_engine_sync=True):
        self.nc = nc
        self.same = same_engine_sync
        self.E = {'pe': nc.tensor, 'act': nc.scalar, 'dve': nc.vector, 'pool': nc.gpsimd, 'sp': nc.sync}
        self.sem = {}
        self._ctx = []
        for e in ('pe', 'act', 'dve', 'pool'):
            self.sem[e] = self._enter(nc.semaphore('prog_' + e))
        self.dsem = {}
        for q in ('sp', 'pool', 'act'):
            self.dsem[q] = [self._enter(nc.semaphore('dma_%s_%d' % (q, i))) for i in range(NDMASLOT)]
        self.ccsem = self._enter(nc.semaphore('ccsem'))
        self.cccnt = 0
        self.ccscratch = self._enter(nc.sbuf_tensor('ccscratch', [128, 8], F32))
        self.cnt = {e: 0 for e in ENGS}
        self.dcnt = {q: 0 for q in ('sp', 'pool', 'act')}
        self.last_dma = {q: [None] * NDMASLOT for q in ('sp', 'pool', 'act')}
        self.nops = 0
        self._reset_phase()

    def _enter(self, cm):
        v = cm.__enter__()
        self._ctx.append(cm)
        return v

    def alloc(self, cm):
        return self._enter(cm)

    def close(self):
        for cm in reversed(self._ctx):
            cm.__exit__(None, None, None)
        self._ctx = []

    def _reset_phase(self):
        self.ops = {e: [] for e in ENGS}
        self.order = []
        self.last_writer = {}
        self.readers = {}

    def _record(self, eng, fn, reads, writes, dma):
        op = Op(eng, fn, dma)
        deps = set()
        for k in reads:
            w = self.last_writer.get(k)
            if w is not None:
                deps.add(w)
        for k in writes:
            w = self.last_writer.get(k)
            if w is not None:
                deps.add(w)
            for r in self.readers.get(k, ()):
                deps.add(r)
        deps.discard(op)
        for k in reads:
            self.readers.setdefault(k, []).append(op)
        for k in writes:
            self.last_writer[k] = op
            self.readers[k] = []
        best = {}
        out = []
        for d in deps:
            if d.dma:
                out.append(d)
                continue
            if d.eng == eng and not dma:
                if eng == 'pe' or not self.same:
                    continue
            b = best.get(d.eng)
            if b is None or d.seq > b.seq:
                best[d.eng] = d
        out.extend(best.values())
        op.deps = out
        for d in out:
            if not d.dma:
                d.sig = True
        op.seq = len(self.order)
        self.ops[eng].append(op)
        self.order.append(op)
        self.nops += 1
        return op

    def op(self, eng, fn, reads=(), writes=()):
        return self._record(eng, fn, reads, writes, False)

    def coll(self, fn, reads=(), writes=()):
        def wrapped(e):
            ins = fn(e)
            self.cccnt += 1
            ins.then_inc(self.ccsem)
            e.wait_ge(self.ccsem, self.cccnt)
            return e.memset(self.ccscratch[:], 0.0)
        return self._record('pool', wrapped, reads, writes, False)

    def dma(self, q, fn, reads=(), writes=()):
        op = self._record(q, fn, reads, writes, True)
        op.dinc = 16
        j = self.dcnt[q]
        self.dcnt[q] += 1
        slot = j % NDMASLOT
        op.dslot = self.dsem[q][slot]
        op.dval = 16 * (j // NDMASLOT + 1)
        op.prev_slot_op = self.last_dma[q][slot]
        self.last_dma[q][slot] = op
        return op

    def flush(self, final_wait=()):
        for e in ENGS:
            c = self.cnt[e]
            for op in self.ops[e]:
                if op.sig and not op.dma:
                    c += 1
                    op.sigval = c
            self.cnt[e] = c
        ops = self.ops
        sem = self.sem
        lastd = {q: list(v) for q, v in self.last_dma.items()}
        anyd = any(d is not None for v in lastd.values() for d in v)

        def run(engname):
            def body(eng):
                known = {e: 0 for e in ENGS}
                kd = {}
                for op in ops[engname]:
                    if op.dma and op.prev_slot_op is not None:
                        p = op.prev_slot_op
                        key = id(p.dslot)
                        if kd.get(key, 0) < p.dval:
                            eng.wait_ge(p.dslot, p.dval)
                            kd[key] = p.dval
                    for d in op.deps:
                        if d.dma:
                            key = id(d.dslot)
                            if kd.get(key, 0) < d.dval:
                                eng.wait_ge(d.dslot, d.dval)
                                kd[key] = d.dval
                        else:
                            if known[d.eng] < d.sigval:
                                eng.wait_ge(sem[d.eng], d.sigval)
                                known[d.eng] = d.sigval
                    ins = op.fn(eng)
                    if op.dma:
                        if op.dinc == 1:
                            ins.then_inc(op.dslot)
                        else:
                            ins.then_inc(op.dslot, 16)
                    elif op.sig:
                        ins.then_inc(sem[op.eng], 1)
                if engname == 'sp':
                    for q in lastd:
                        for d in lastd[q]:
                            if d is not None:
                                eng.wait_ge(d.dslot, d.dval)
            return body

        with self.nc.Block() as block:
            if ops['sp'] or anyd:
                block.sync(run('sp'))
            if ops['pe']:
                block.tensor(run('pe'))
            if ops['act']:
                block.scalar(run('act'))
            if ops['dve']:
                block.vector(run('dve'))
            if ops['pool']:
                block.gpsimd(run('pool'))
        for q in self.last_dma:
            self.last_dma[q] = [None] * NDMASLOT
        self._reset_phase()
D = 1024
KC = 8
DFF = 4096
EPS = 1e-6


class FM:
    def __init__(self, ap, t0=0):
        self.ap = ap
        self.t0 = t0


def tsl(x, a, b):
    if isinstance(x, FM):
        return FM(x.ap, x.t0 + a)
    return x[a:b, :]


class KB:
    def __init__(self, nt=2048):
        self.nc = nc = bass.Bass("TRN2", target_bir_lowering=False)
        self.P = P = Prog(nc)
        self.NT = nt
        self.stack = []
        self.uid = 0
        self.pfx = ''
        self.shared = {}
        self.split_mods = False
        self.ps = P.alloc(nc.psum_tensor("ps", [128, 4096], F32))
        self.ident = self.sb("ident_sb", [128, 128], F32)
        self.ones_bf = self.sb("ones_bf", [128, 128], BF16)
        self.eps_t = self.sb("eps_t", [128, 1], F32)
        self.rot = {}
        ident_d = self.nc.dram_tensor("ident", [128, 128], F32, kind="ExternalInput").ap()
        P.dma('sp', lambda e: e.dma_start(out=self.ident[:], in_=ident_d), writes=['ident'])
        P.op('pool', lambda e: e.memset(self.ones_bf[:], 1.0), writes=['ones_bf'])
        P.op('pool', lambda e: e.memset(self.eps_t[:], EPS), writes=['eps_t'])

    def sb(self, name, shape, dt):
        self.uid += 1
        name = "s%d_%s" % (self.uid, name)
        if self.stack:
            return self.stack[-1].enter_context(self.nc.sbuf_tensor(name, shape, dt))
        return self.P.alloc(self.nc.sbuf_tensor(name, shape, dt))

    @contextlib.contextmanager
    def phase(self):
        st = contextlib.ExitStack()
        self.stack.append(st)
        try:
            yield
            self.P.flush()
        finally:
            self.stack.pop()
            st.close()

    def din(self, name, shape, dt=F32):
        if name in ('cv',):
            if name not in self.shared:
                self.shared[name] = self.nc.dram_tensor(name, list(shape), dt, kind="ExternalInput").ap()
            return self.shared[name]
        return self.nc.dram_tensor(self.pfx + name, list(shape), dt, kind="ExternalInput").ap()

    def dint(self, name, shape, dt=F32):
        return self.nc.dram_tensor(name, list(shape), dt).ap()

    def dout(self, name, shape, dt=F32):
        return self.nc.dram_tensor(name, list(shape), dt, kind="ExternalOutput").ap()

    def ncol_ada(self):
        return 1536 if self.split_mods else 6144

    def bank(self, i, n=512, p0=0, p1=128):
        return self.ps[p0:p1, i * 512:i * 512 + n]

    def nextrot(self, name, n):
        v = self.rot.get(name, 0)
        self.rot[name] = v + 1
        return v % n

    def mods(self, cv_d, adaw_d, adab_d, modsT):
        P, nc = self.P, self.nc
        cv = self.sb("cv", [128, 8, 2], F32)
        sT = self.sb("sT", [128, 8, 2], F32)
        mrow = self.sb("mrow", [2, 6144], F32)
        adab = self.sb("adab", [2, 1536 if self.split_mods else 6144], F32)
        wst = [self.sb("wst%d" % i, [128, 8, 512], F32) for i in range(2)]
        P.dma('sp', lambda e: e.dma_start(out=cv[:], in_=cv_d), writes=['cv'])
        P.dma('sp', lambda e: e.dma_start(out=adab[:], in_=adab_d), writes=['adab'])
        P.op('act', lambda e: e.activation(out=sT[:], in_=cv[:], func=AF.Silu), reads=['cv'], writes=['sT'])
        ncg = 3 if self.split_mods else 12
        mpart = self.sb("mpart", [2, 1536], F32) if self.split_mods else None
        for cg in range(ncg):
            b = cg % 2
            src = adaw_d[:, cg * 512:(cg + 1) * 512].rearrange("(kc p) c -> p kc c", p=128)
            for h in range(2):
                P.dma('sp', lambda e, b=b, h=h, src=src: e.dma_start(out=wst[b][:, h * 4:(h + 1) * 4, :], in_=src[:, h * 4:(h + 1) * 4, :]),
                      writes=[('wst', b, h)])
            pb = 6 + (cg % 2)
            for kc in range(KC):
                P.op('pe', lambda e, b=b, kc=kc, pb=pb: e.matmul(self.bank(pb, 512, 0, 2), lhsT=sT[:, kc, :], rhs=wst[b][:, kc, :],
                                                                 start=(kc == 0), stop=(kc == KC - 1)),
                     reads=['sT', ('wst', b, kc // 4)], writes=[('ps', pb)])
            dstrow = mpart if self.split_mods else mrow
            P.op('dve', lambda e, cg=cg, pb=pb, dstrow=dstrow: e.tensor_tensor(out=dstrow[:, cg * 512:(cg + 1) * 512], in0=self.bank(pb, 512, 0, 2),
                                                                              in1=adab[:, cg * 512:(cg + 1) * 512], op=ALU.add),
                 reads=['adab'], writes=[('ps', pb), ('mrow', cg)])
        if self.split_mods:
            self.uid += 1
            mb = self.dint("modb%d" % self.uid, [2, 1536]); mg = self.dint("modg%d" % self.uid, [8, 1536])
            P.dma('sp', lambda e: e.dma_start(out=mb, in_=mpart[:]), reads=[('mrow', g_) for g_ in range(3)], writes=['modb'])
            P.coll(lambda e: e.collective_compute("AllGather", ALU.bypass, replica_groups=RG, ins=[mb.opt()], outs=[mg.opt()]), reads=['modb'], writes=['modg'])
            P.dma('sp', lambda e: e.dma_start(out=mrow[:].rearrange("t (r j) -> t r j", r=4), in_=mg.rearrange("(r t) j -> t r j", t=2)), reads=['modg'],
                  writes=[('mrow', g_) for g_ in range(12)])
        for ch in range(48):
            P.op('pe', lambda e, ch=ch: e.transpose(self.ps[:, 6 * 512 + ch * 2:6 * 512 + ch * 2 + 2], mrow[0:2, ch * 128:(ch + 1) * 128], self.ident[0:2, 0:2]),
                 reads=[('mrow', ch // 4), 'ident'], writes=[('ps', 6)])
        P.op('dve', lambda e: e.tensor_copy(out=modsT[:].rearrange("p a b -> p (a b)"), in_=self.ps[:, 6 * 512:6 * 512 + 96]),
             writes=[('ps', 6), 'modsT'])
        return modsT

    def mod_vectors(self, modsT, g1_d, g2_d, col, mv, tag):
        P = self.P
        gg = self.sb("gg" + tag, [128, 2, 8], F32)
        P.dma('sp', lambda e: e.dma_start(out=gg[:, 0, :], in_=g1_d), writes=['gg' + tag])
        P.dma('sp', lambda e: e.dma_start(out=gg[:, 1, :], in_=g2_d), writes=['gg' + tag])
        P.op('dve', lambda e: e.tensor_copy(out=mv[:].rearrange("p m k -> p (m k)"), in_=modsT[:, :, col]), reads=['modsT'], writes=['mv' + tag])
        for j, m in ((0, 1), (1, 4)):
            P.op('dve', lambda e, j=j, m=m: e.scalar_tensor_tensor(out=mv[:, m, :], in0=mv[:, m, :], scalar=1.0, in1=gg[:, j, :],
                                                                   op0=ALU.add, op1=ALU.mult),
                 reads=['gg' + tag], writes=['mv' + tag])
        key = 'mv' + tag
        return dict(A1=mv[:, 1, :], B1=mv[:, 0, :], G1=mv[:, 2, :], A2=mv[:, 4, :], B2=mv[:, 3, :], G2=mv[:, 5, :], key=key)

    def load_xT(self, x_d, ntok, xT, xkey, stage, t0=0):
        P = self.P
        if isinstance(x_d, FM):
            keys = [(xkey, g) for g in range(t0 // 512, (t0 + ntok - 1) // 512 + 1)]
            for half in range(2):
                P.dma('sp', lambda e, half=half: e.dma_start(out=xT[:, half * 4:(half + 1) * 4, t0:t0 + ntok], in_=x_d.ap[:, half * 4:(half + 1) * 4, x_d.t0:x_d.t0 + ntok]), writes=keys)
            return
        for tt in range(ntok // 128):
            sb_ = self.nextrot('stage', 2)
            P.dma('sp', lambda e, tt=tt, sb_=sb_: e.dma_start(out=stage[sb_][:], in_=x_d[tt * 128:(tt + 1) * 128, :]), writes=[('stage', sb_)])
            for half in range(2):
                pb = 4 + self.nextrot('ldbank', 2)
                for j in range(4):
                    kc = half * 4 + j
                    P.op('pe', lambda e, sb_=sb_, kc=kc, pb=pb, j=j: e.transpose(self.bank(pb)[:, j * 128:(j + 1) * 128], stage[sb_][:, kc * 128:(kc + 1) * 128], self.ident[:]),
                         reads=[('stage', sb_), 'ident'], writes=[('ps', pb)])
                eng = 'act' if half == 0 else 'dve'
                dst = xT[:, half * 4:(half + 1) * 4, t0 + tt * 128:t0 + (tt + 1) * 128]
                src = self.bank(pb).rearrange("p (a b) -> p a b", a=4)
                if eng == 'act':
                    P.op('act', lambda e, dst=dst, src=src: e.activation(out=dst, in_=src, func=AF.Copy), writes=[('ps', pb), (xkey, (t0 + tt * 128) // 512)])
                else:
                    P.op('dve', lambda e, dst=dst, src=src: e.tensor_copy(out=dst, in_=src), writes=[('ps', pb), (xkey, (t0 + tt * 128) // 512)])

    def store_x(self, xT, xkey, ntok, out_d, ostage, scale_ap=None):
        P = self.P
        if isinstance(out_d, FM):
            keys = [(xkey, g) for g in range(0, (ntok - 1) // 512 + 1)]
            for half in range(2):
                P.dma('sp', lambda e, half=half: e.dma_start(out=out_d.ap[:, half * 4:(half + 1) * 4, out_d.t0:out_d.t0 + ntok], in_=xT[:, half * 4:(half + 1) * 4, 0:ntok]), reads=keys)
            return
        for tt in range(ntok // 128):
            ob = self.nextrot('ostage', 2)
            for half in range(2):
                pb = 4 + self.nextrot('ldbank', 2)
                for j in range(4):
                    kc = half * 4 + j
                    P.op('pe', lambda e, kc=kc, pb=pb, j=j, tt=tt: e.transpose(self.bank(pb)[:, j * 128:(j + 1) * 128], xT[:, kc, tt * 128:(tt + 1) * 128], self.ident[:]),
                         reads=[(xkey, tt // 4), 'ident'], writes=[('ps', pb)])
                dst = ostage[ob][:, half * 512:(half + 1) * 512]
                if half == 0:
                    P.op('act', lambda e, dst=dst, pb=pb: e.activation(out=dst, in_=self.bank(pb), func=AF.Copy), writes=[('ps', pb), ('ostage', ob, half)])
                else:
                    P.op('dve', lambda e, dst=dst, pb=pb: e.tensor_copy(out=dst, in_=self.bank(pb)), writes=[('ps', pb), ('ostage', ob, half)])
            P.dma('sp', lambda e, ob=ob, tt=tt: e.dma_start(out=out_d[tt * 128:(tt + 1) * 128, :], in_=ostage[ob][:]),
                  reads=[('ostage', ob, 0), ('ostage', ob, 1)])

    def norm_mod(self, xT, xkey, ntok, A, B, mkey, hT, hkey, scr, t0=0, ht0=0):
        P = self.P
        tgs = min(512, ntok)
        for g in range(ntok // tgs):
            c0 = t0 + g * tgs
            h0 = ht0 + g * tgs
            pb = 6 + self.nextrot('nbank', 2)
            for kc in range(KC):
                sq = self.nextrot('sq', 2)
                P.op('pool', lambda e, kc=kc, sq=sq, c0=c0: e.tensor_tensor(out=scr['sq'][sq][:, :tgs], in0=xT[:, kc, c0:c0 + tgs], in1=xT[:, kc, c0:c0 + tgs], op=ALU.mult),
                     reads=[(xkey, c0 // 512)], writes=[('sq', sq)])
                P.op('pe', lambda e, kc=kc, sq=sq, pb=pb: e.matmul(self.bank(pb, tgs), lhsT=self.ones_bf[:], rhs=scr['sq'][sq][:, :tgs], start=(kc == 0), stop=(kc == KC - 1)),
                     reads=[('sq', sq), 'ones_bf'], writes=[('ps', pb)])
            rs = scr['rs']
            P.op('act', lambda e, pb=pb: e.activation(out=rs[:, :tgs], in_=self.bank(pb, tgs), func=AF.Ln, scale=1.0 / D, bias=self.eps_t[:]),
                 reads=['eps_t'], writes=[('ps', pb), 'rs'])
            P.op('act', lambda e: e.activation(out=rs[:, :tgs], in_=rs[:, :tgs], func=AF.Exp, scale=-0.5), writes=['rs'])
            for kc in range(KC):
                tb = self.nextrot('tmp', 2)
                P.op('dve', lambda e, kc=kc, tb=tb, c0=c0: e.scalar_tensor_tensor(out=scr['tmp'][tb][:, :tgs], in0=xT[:, kc, c0:c0 + tgs], scalar=A[:, kc:kc + 1],
                                                                                in1=rs[:, :tgs], op0=ALU.mult, op1=ALU.mult),
                     reads=[(xkey, c0 // 512), 'rs', mkey], writes=[('tmp', tb)])
                P.op('act', lambda e, kc=kc, tb=tb, h0=h0: e.activation(out=hT[:, kc, h0:h0 + tgs], in_=scr['tmp'][tb][:, :tgs], func=AF.Identity, bias=B[:, kc:kc + 1]),
                     reads=[('tmp', tb), mkey], writes=[(hkey, h0 // 512)])

    def load_w(self, dst, wkey, w_d, r0, nkc, c0, ncol):
        P = self.P
        for k in range(nkc):
            P.dma('pool', lambda e, k=k: e.dma_start(out=dst[:, k, 0:ncol], in_=w_d[r0 + k * 128:r0 + (k + 1) * 128, c0:c0 + ncol]),
                  writes=[(wkey, k)])

    def mlp(self, xT, xkey, ntok, hT, hkey, G, mkey, w1_d, w2_d, wbuf, aT, rbuf):
        P = self.P
        FB = 512
        nfb = DFF // FB
        tgs = min(512, ntok)
        ntg = ntok // tgs

        def ff1(j):
            wb = j % 2
            self.load_w(wbuf['w1'][wb], ('w1', wb), w1_d, 0, KC, j * FB, FB)
            self.load_w(wbuf['w2'][wb], ('w2', wb), w2_d, j * FB, FB // 128, 0, D)
            for fc in range(FB // 128):
                for g in range(ntg):
                    pb = self.nextrot('ff1bank', 3)
                    for kc in range(KC):
                        P.op('pe', lambda e, wb=wb, fc=fc, g=g, kc=kc, pb=pb: e.matmul(self.bank(pb, tgs), lhsT=wbuf['w1'][wb][:, kc, fc * 128:(fc + 1) * 128],
                                                                                    rhs=hT[:, kc, g * tgs:(g + 1) * tgs], start=(kc == 0), stop=(kc == KC - 1)),
                             reads=[(('w1', wb), kc), (hkey, g)], writes=[('ps', pb)])
                    rb = self.nextrot('rbuf', 2)
                    P.op('act', lambda e, pb=pb, rb=rb: e.activation(out=rbuf[rb][:, :tgs], in_=self.bank(pb, tgs), func=AF.Relu), writes=[('ps', pb), ('rbuf', rb)])
                    P.op('pool', lambda e, rb=rb, wb=wb, fc=fc, g=g: e.tensor_tensor(out=aT[wb][:, fc, g * tgs:(g + 1) * tgs], in0=rbuf[rb][:, :tgs], in1=rbuf[rb][:, :tgs], op=ALU.mult),
                         reads=[('rbuf', rb)], writes=[('aT', wb, fc, g)])

        def ff2(j):
            wb = j % 2
            for dc in range(KC):
                for g in range(ntg):
                    pb = 3 + self.nextrot('ff2bank', 3)
                    nf = FB // 128
                    for fc in range(nf):
                        P.op('pe', lambda e, wb=wb, fc=fc, g=g, dc=dc, pb=pb: e.matmul(self.bank(pb, tgs), lhsT=wbuf['w2'][wb][:, fc, dc * 128:(dc + 1) * 128],
                                                                                    rhs=aT[wb][:, fc, g * tgs:(g + 1) * tgs], start=(fc == 0), stop=(fc == nf - 1)),
                             reads=[(('w2', wb), fc), ('aT', wb, fc, g)], writes=[('ps', pb)])
                    P.op('dve', lambda e, dc=dc, g=g, pb=pb: e.scalar_tensor_tensor(out=xT[:, dc, g * tgs:(g + 1) * tgs], in0=self.bank(pb, tgs), scalar=G[:, dc:dc + 1],
                                                                                  in1=xT[:, dc, g * tgs:(g + 1) * tgs], op0=ALU.mult, op1=ALU.add),
                         reads=[mkey], writes=[('ps', pb), (xkey, g)])

        ff1(0)
        for j in range(nfb):
            if j + 1 < nfb:
                ff1(j + 1)
            ff2(j)


def load_cast(kb, dst, key, src_d):
    kb.P.dma('pool', lambda e: e.dma_start(out=dst, in_=src_d), writes=[key])


def proj_fm(kb, hT, hkey, t0, n, W, wkey, col0, pb):
    for kc in range(KC):
        kb.P.op('pe', lambda e, kc=kc: e.matmul(kb.bank(pb, n), lhsT=W[:, kc, col0:col0 + 128], rhs=hT[:, kc, t0:t0 + n],
                                               start=(kc == 0), stop=(kc == KC - 1)),
                reads=[(wkey, kc), (hkey, t0 // 512)], writes=[('ps', pb)])


def qk_norm_rope(kb, pb, n, gain, rope, qscale, out_ap, outkeys, C):
    P = kb.P
    r = kb.nextrot('qkr', 2)
    kg, k2, rs, t1 = C['kg'][r], C['k2'][r], C['rs2'][r], C['t1'][r]
    P.op('act', lambda e: e.activation(out=kg[:, :n], in_=kb.bank(pb, n), func=AF.Copy, scale=gain[:, 0:1]), reads=['gains'], writes=[('ps', pb), ('kg', r)])
    P.op('act', lambda e: e.activation(out=k2[:, :n], in_=kb.bank(pb, n), func=AF.Square), writes=[('ps', pb), ('k2', r)])
    P.op('pe', lambda e: e.matmul(kb.bank(2, n), lhsT=C['bones'][:], rhs=k2[:, :n], start=True, stop=True), reads=[('k2', r), 'bones'], writes=[('ps', 2)])
    if rope is not None:
        P.op('pe', lambda e: e.matmul(kb.bank(3, n), lhsT=C['rmat'][:], rhs=kg[:, :n], start=True, stop=True), reads=[('kg', r), 'rmat'], writes=[('ps', 3)])
    P.op('act', lambda e: e.activation(out=rs[:, :n], in_=kb.bank(2, n), func=AF.Ln, scale=1.0 / 64, bias=kb.eps_t[:]), reads=['eps_t'], writes=[('ps', 2), ('rs2', r)])
    if qscale:
        P.op('act', lambda e: e.activation(out=rs[:, :n], in_=rs[:, :n], func=AF.Exp, scale=-0.5, bias=C['lnq'][:]), reads=['lnq'], writes=[('rs2', r)])
    else:
        P.op('act', lambda e: e.activation(out=rs[:, :n], in_=rs[:, :n], func=AF.Exp, scale=-0.5), writes=[('rs2', r)])
    if rope is not None:
        cos_ap, sin_ap, rkey = rope
        P.op('dve', lambda e: e.tensor_tensor(out=t1[:, :n], in0=kg[:, :n], in1=cos_ap, op=ALU.mult), reads=[('kg', r), rkey], writes=[('t1', r)])
        P.op('dve', lambda e: e.tensor_tensor(out=kg[:, :n], in0=kb.bank(3, n), in1=sin_ap, op=ALU.mult), reads=[rkey], writes=[('ps', 3), ('kg', r)])
        P.op('dve', lambda e: e.tensor_tensor(out=t1[:, :n], in0=t1[:, :n], in1=kg[:, :n], op=ALU.add), reads=[('kg', r)], writes=[('t1', r)])
        P.op('dve', lambda e: e.tensor_tensor(out=out_ap, in0=t1[:, :n], in1=rs[:, :n], op=ALU.mult), reads=[('t1', r), ('rs2', r)], writes=outkeys)
    else:
        P.op('dve', lambda e: e.tensor_tensor(out=out_ap, in0=kg[:, :n], in1=rs[:, :n], op=ALU.mult), reads=[('kg', r), ('rs2', r)], writes=outkeys)


def attention(kb, q_ap, qkeys, NQ, tiles, dst_a, dst_b, dstkeys, C):
    P = kb.P
    oset = kb.nextrot('oset', 2)
    o0 = 4 + 2 * oset
    nt = len(tiles)
    ssets = []

    def qk(i):
        t = tiles[i]
        ss = kb.nextrot('sset', 2)
        ssets.append(ss)
        s0 = 2 * ss
        P.op('pe', lambda e: e.matmul(kb.bank(s0, NQ), lhsT=t['ka'], rhs=q_ap[0:64, :], start=True, stop=True), reads=t['keys'] + qkeys, writes=[('ps', s0)])
        P.op('pe', lambda e: e.matmul(kb.bank(s0 + 1, NQ), lhsT=t['kb'], rhs=q_ap[64:128, :], start=True, stop=True), reads=t['keys'] + qkeys, writes=[('ps', s0 + 1)])

    qk(0)
    for i in range(nt):
        if i + 1 < nt:
            qk(i + 1)
        t = tiles[i]
        s0 = 2 * ssets[i]
        pbuf = kb.nextrot('pbuf', 3)
        pb_ = C['pbuf'][pbuf]
        src = kb.ps[:, s0 * 512:(s0 + 2) * 512].rearrange("p (h n) -> p h n", h=2)[:, :, 0:NQ]
        if t.get('bias_a') is not None:
            sb_ = C['sbias'][kb.nextrot('sbias', 2)]
            P.op('dve', lambda e, sb_=sb_, t=t, s0=s0: e.tensor_tensor(out=sb_[:, 0, 0:NQ], in0=kb.bank(s0, NQ), in1=t['bias_a'], op=ALU.add), reads=t['bkeys'], writes=[('ps', s0), ('sbias', id(sb_), 0)])
            P.op('dve', lambda e, sb_=sb_, t=t, s0=s0: e.tensor_tensor(out=sb_[:, 1, 0:NQ], in0=kb.bank(s0 + 1, NQ), in1=t['bias_b'], op=ALU.add), reads=t['bkeys'], writes=[('ps', s0 + 1), ('sbias', id(sb_), 1)])
            P.op('act', lambda e, sb_=sb_, pb_=pb_: e.activation(out=pb_[:, :, 0:NQ], in_=sb_[:, :, 0:NQ], func=AF.Exp), reads=[('sbias', id(sb_), 0), ('sbias', id(sb_), 1)], writes=[('pbuf', pbuf)])
        else:
            P.op('act', lambda e, src=src, pb_=pb_: e.activation(out=pb_[:, :, 0:NQ], in_=src, func=AF.Exp), writes=[('ps', s0), ('ps', s0 + 1), ('pbuf', pbuf)])
        P.op('pe', lambda e, t=t, pb_=pb_, i=i: e.matmul(kb.bank(o0, NQ), lhsT=t['va'], rhs=pb_[:, 0, 0:NQ], start=(i == 0), stop=(i == nt - 1)), reads=t['keys'] + [('pbuf', pbuf)], writes=[('ps', o0)])
        P.op('pe', lambda e, t=t, pb_=pb_, i=i: e.matmul(kb.bank(o0 + 1, NQ), lhsT=t['vb'], rhs=pb_[:, 1, 0:NQ], start=(i == 0), stop=(i == nt - 1)), reads=t['keys'] + [('pbuf', pbuf)], writes=[('ps', o0 + 1)])
    rcr = kb.nextrot('rc', 2)
    rc = C['rc'][rcr]
    P.op('dve', lambda e: e.reciprocal(out=rc[64:128, 0:NQ], in_=kb.bank(o0, NQ)[64:128, :]), writes=[('ps', o0), ('rc', rcr, 0)])
    P.op('dve', lambda e: e.tensor_tensor(out=dst_a, in0=kb.bank(o0, NQ)[0:64, :], in1=rc[64:128, 0:NQ], op=ALU.mult), reads=[('rc', rcr, 0)], writes=[('ps', o0)] + dstkeys)
    P.op('dve', lambda e: e.reciprocal(out=rc[0:64, 0:NQ], in_=kb.bank(o0 + 1, NQ)[0:64, :]), writes=[('ps', o0 + 1), ('rc', rcr, 1)])
    P.op('dve', lambda e: e.tensor_tensor(out=dst_b, in0=kb.bank(o0 + 1, NQ)[64:128, :], in1=rc[0:64, 0:NQ], op=ALU.mult), reads=[('rc', rcr, 1)], writes=[('ps', o0 + 1)] + dstkeys)


def attn_scratch(kb, with_bias=False):
    C = dict(pbuf=[kb.sb("pbuf%d" % i, [128, 2, 512], BF16) for i in range(3)],
             rc=[kb.sb("rc%d" % i, [128, 512], F32) for i in range(2)])
    if with_bias:
        C['sbias'] = [kb.sb("sbias%d" % i, [128, 2, 512], F32) for i in range(2)]
    return C


def qk_scratch(kb, C):
    C['kg'] = [kb.sb("kg%d" % i, [128, 512], BF16) for i in range(2)]
    C['k2'] = [kb.sb("k2%d" % i, [128, 512], BF16) for i in range(2)]
    C['rs2'] = [kb.sb("rs2%d" % i, [128, 512], F32) for i in range(2)]
    C['t1'] = [kb.sb("t1%d" % i, [128, 512], F32) for i in range(2)]


def norm_scratch(kb):
    return dict(sq=[kb.sb("sq%d" % i, [128, 512], BF16) for i in range(2)], rs=kb.sb("rs", [128, 512], F32),
                tmp=[kb.sb("tmp%d" % i, [128, 512], F32) for i in range(2)])


def mlp_bufs(kb, ntok):
    wbuf = dict(w1=[kb.sb("w1b%d" % i, [128, 8, 512], BF16) for i in range(3)], w2=[kb.sb("w2b%d" % i, [128, 4, 1024], BF16) for i in range(3)])
    aT = [kb.sb("aT%d" % i, [128, 4, ntok], BF16) for i in range(2)]
    rbuf = [kb.sb("rbuf%d" % i, [128, 512], F32) for i in range(2)]
    return wbuf, aT, rbuf


def resid_proj(kb, srcT, skey, t0, n, W, wkey, xT, xkey, xt0, G, mkey, bG=None):
    P = kb.P
    for dc in range(KC):
        pb = kb.nextrot('projbank', 2)
        proj_fm(kb, srcT, skey, t0, n, W, wkey, dc * 128, pb)
        P.op('dve', lambda e, dc=dc, pb=pb: e.scalar_tensor_tensor(out=xT[:, dc, xt0:xt0 + n], in0=kb.bank(pb, n), scalar=G[:, dc:dc + 1],
                                                                 in1=xT[:, dc, xt0:xt0 + n], op0=ALU.mult, op1=ALU.add),
             reads=[mkey], writes=[('ps', pb), (xkey, xt0 // 512)])
        if bG is not None:
            P.op('act', lambda e, dc=dc: e.activation(out=xT[:, dc, xt0:xt0 + n], in_=xT[:, dc, xt0:xt0 + n], func=AF.Identity, bias=bG[:, dc:dc + 1]),
                 reads=['bG'], writes=[(xkey, xt0 // 512)])


def build_l0(kb=None, io=None):
    own = kb is None
    if own:
        kb = KB()
    io = io or {}
    kb.pfx = '' if own else 'l0_'
    P, nc = kb.P, kb.nc
    NTOK = 2048
    NG0 = 4 if io.get("kv_gather") else 16
    xb_d = io.get("xb") or kb.din("xb", [NG0 * 512, D]); ctx_d = io.get("ctx") or kb.din("ctx", [256, D])
    cv_d = kb.din("cv", [128, 8, 2]); adaw_d = kb.din("adaw", [D, kb.ncol_ada()]); adab_d = kb.din("adab", [2, kb.ncol_ada()])
    g1_d = kb.din("g1", [128, 8]); g2_d = kb.din("g2", [128, 8])
    wqkv_d = kb.din("wqkv", [D, 1536]); gains_d = kb.din("gains", [128, 2])
    cos_d = kb.din("cos", [NG0, 128, 512]); sin_d = kb.din("sin", [NG0, 128, 512])
    wo_d = kb.din("wo", [D, D]); w1_d = kb.din("w1", [D, DFF]); w2_d = kb.din("w2", [DFF, D])
    rmat_d = kb.din("rmat", [128, 128]); bones_d = kb.din("bones", [128, 128])
    out_d = io.get("out") or kb.dout("out", [NTOK, D]); hctx_d = io.get("hctx") or kb.dout("hctx", [256, D])
    mid_d = io.get("mid") or out_d

    modsT = kb.sb("modsT", [128, 48, 2], F32)
    mvL = kb.sb("mvL", [128, 6, 8], F32); mvC = kb.sb("mvC", [128, 6, 8], F32)
    C = dict(rmat=kb.sb("rmat", [128, 128], BF16), bones=kb.sb("bones", [128, 128], BF16), lnq=kb.sb("lnq", [128, 1], F32))
    gains = kb.sb("gains", [128, 2], F32)
    with kb.phase():
        load_cast(kb, C['rmat'][:], 'rmat', rmat_d)
        load_cast(kb, C['bones'][:], 'bones', bones_d)
        P.op('pool', lambda e: e.memset(C['lnq'][:], float(np.log(0.125))), writes=['lnq'])
        P.dma('sp', lambda e: e.dma_start(out=gains[:], in_=gains_d), writes=['gains'])
        kb.mods(cv_d, adaw_d, adab_d, modsT)
        mL = kb.mod_vectors(modsT, g1_d, g2_d, 0, mvL, "L")
        mC = kb.mod_vectors(modsT, g1_d, g2_d, 1, mvC, "C")
    qgain, kgain = gains[:, 0:1], gains[:, 1:2]

    with kb.phase():
        KT = kb.sb("KT", [128, 2, 8448], BF16)
        Ve = kb.sb("Ve", [128, 66, 384], BF16)
        P.op('pool', lambda e: e.memset(Ve[:].rearrange("p t (a b c) -> p (t a) b c", a=2, b=3, c=64)[:, :, 1, :], 1.0), writes=['Ve_ones'])

        def kv_tiles(tile_ids, pr):
            out = []
            for kt in tile_ids:
                out.append(dict(ka=KT[0:64, pr, kt * 128:(kt + 1) * 128], kb=KT[64:128, pr, kt * 128:(kt + 1) * 128],
                                va=Ve[:, kt, pr * 192:pr * 192 + 128], vb=Ve[:, kt, pr * 192 + 64:pr * 192 + 192],
                                keys=[('KT', kt // 4), ('Ve', kt), 'Ve_ones']))
            return out

        def produce_kv(hT, hkey, n, g, wqkv, rope):
            for pr in range(2):
                pb = kb.nextrot('projbank', 2)
                proj_fm(kb, hT, hkey, 0, n, wqkv, 'wqkv', 1024 + pr * 128, pb)
                qk_norm_rope(kb, pb, n, kgain, rope, False, KT[:, pr, g * 512:g * 512 + n], [('KT', g)], C)
            for tt in range(n // 128):
                pb = kb.nextrot('projbank', 2)
                for kc in range(KC):
                    P.op('pe', lambda e, kc=kc, tt=tt, pb=pb: e.matmul(kb.bank(pb, 256), lhsT=hT[:, kc, tt * 128:(tt + 1) * 128], rhs=wqkv[:, kc, 1280:1536],
                                                                      start=(kc == 0), stop=(kc == KC - 1)),
                         reads=[('wqkv', kc), (hkey, 0)], writes=[('ps', pb)])
                kt = g * 4 + tt
                dst = Ve[:, kt, :].rearrange("p (a b c) -> p a b c", a=2, b=3, c=64)[:, :, ::2, :]
                src = kb.bank(pb, 256).rearrange("p (a b c) -> p a b c", a=2, b=2, c=64)
                P.op('dve', lambda e, dst=dst, src=src: e.tensor_copy(out=dst, in_=src), writes=[('ps', pb), ('Ve', kt)])

        with kb.phase():
            cT = kb.sb("cT", [128, 8, 256], F32)
            stage = [kb.sb("stage%d" % i, [128, 1024], F32) for i in range(2)]
            scr = norm_scratch(kb)
            with kb.phase():
                hcT = kb.sb("hcT", [128, 8, 256], BF16)
                QcT = kb.sb("QcT", [128, 8, 256], BF16)
                OcT = kb.sb("OcT", [128, 8, 256], BF16)
                qk_scratch(kb, C)
                C.update(attn_scratch(kb))
                wqkv = kb.sb("wqkv", [128, 8, 1536], BF16)
                wo = kb.sb("wo", [128, 8, 1024], BF16)
                kb.load_w(wqkv, 'wqkv', wqkv_d, 0, KC, 0, 1536)
                kb.load_w(wo, 'wo', wo_d, 0, KC, 0, 1024)
                kb.load_xT(ctx_d, 256, cT, 'cT', stage)
                kb.norm_mod(cT, 'cT', 256, mC['A1'], mC['B1'], mC['key'], hcT, 'hcT', scr)
                produce_kv(hcT, 'hcT', 256, 16, wqkv, None)
                for c in range(8):
                    pb = kb.nextrot('projbank', 2)
                    proj_fm(kb, hcT, 'hcT', 0, 256, wqkv, 'wqkv', c * 128, pb)
                    qk_norm_rope(kb, pb, 256, qgain, None, True, QcT[:, c, :], [('QcT', c)], C)
                for c in range(8):
                    attention(kb, QcT[:, c, :], [('QcT', c)], 256, kv_tiles([64, 65], c // 4), OcT[0:64, c, :], OcT[64:128, c, :], [('OcT', 0)], C)
                resid_proj(kb, OcT, 'OcT', 0, 256, wo, 'wo', cT, 'cT', 0, mC['G1'], mC['key'])
            with kb.phase():
                hc2 = kb.sb("hc2", [128, 8, 256], BF16)
                kb.norm_mod(cT, 'cT', 256, mC['A2'], mC['B2'], mC['key'], hc2, 'hc2', scr)
                wbuf, aT, rbuf = mlp_bufs(kb, 256)
                kb.mlp(cT, 'cT', 256, hc2, 'hc2', mC['G2'], mC['key'], w1_d, w2_d, wbuf, aT, rbuf)
                kb.store_x(cT, 'cT', 256, hctx_d, stage)

        with kb.phase():
            QT = kb.sb("QT", [128, 8, NTOK], BF16)
            with kb.phase():
                xtmp = kb.sb("xtmp", [128, 8, 512], F32)
                hTt = kb.sb("hTt", [128, 8, 512], BF16)
                stage = [kb.sb("stage%d" % i, [128, 1024], F32) for i in range(2)]
                scr = norm_scratch(kb)
                qk_scratch(kb, C)
                wqkv = kb.sb("wqkv", [128, 8, 1536], BF16)
                cs = [kb.sb("cs%d" % i, [128, 2, 512], F32) for i in range(2)]
                kb.load_w(wqkv, 'wqkv', wqkv_d, 0, KC, 0, 1536)
                for g in range(NG0):
                    kb.load_xT(tsl(xb_d, g * 512, (g + 1) * 512), 512, xtmp, 'xtmp', stage)
                    kb.norm_mod(xtmp, 'xtmp', 512, mL['A1'], mL['B1'], mL['key'], hTt, 'hTt', scr)
                    cb = g % 2
                    P.dma('sp', lambda e, g=g, cb=cb: e.dma_start(out=cs[cb][:, 0, :], in_=cos_d[g]), writes=[('cs', cb)])
                    P.dma('sp', lambda e, g=g, cb=cb: e.dma_start(out=cs[cb][:, 1, :], in_=sin_d[g]), writes=[('cs', cb)])
                    rope = (cs[cb][:, 0, :], cs[cb][:, 1, :], ('cs', cb))
                    produce_kv(hTt, 'hTt', 512, g, wqkv, rope)
                    if g < 4:
                        for c in range(8):
                            pb = kb.nextrot('projbank', 2)
                            proj_fm(kb, hTt, 'hTt', 0, 512, wqkv, 'wqkv', c * 128, pb)
                            qk_norm_rope(kb, pb, 512, qgain, rope, True, QT[:, c, g * 512:(g + 1) * 512], [('QT', c, g)], C)
            if io.get("kv_gather"):
                ktb = kb.dint("ktb", [256, 2048], BF16); ktg = kb.dint("ktg", [1024, 2048], BF16)
                veb = [kb.dint("veb%d" % h, [1024, 384], BF16) for h in range(2)]
                veg = [kb.dint("veg%d" % h, [4096, 384], BF16) for h in range(2)]
                with kb.phase():
                    P.dma('sp', lambda e: e.dma_start(out=ktb.rearrange("(p a) n -> p a n", a=2), in_=KT[:, :, 0:2048]), reads=[('KT', g_) for g_ in range(4)], writes=['ktb'])
                    P.coll(lambda e: e.collective_compute("AllGather", ALU.bypass, replica_groups=RG, ins=[ktb.opt()], outs=[ktg.opt()]), reads=['ktb'], writes=['ktg'])
                    for h in range(2):
                        P.dma('sp', lambda e, h=h: e.dma_start(out=veb[h].rearrange("(p t) c -> p t c", t=8), in_=Ve[:, h * 8:(h + 1) * 8, :]),
                              reads=[('Ve', kt_) for kt_ in range(16)] + ['Ve_ones'], writes=[('veb', h)])
                        P.coll(lambda e, h=h: e.collective_compute("AllGather", ALU.bypass, replica_groups=RG, ins=[veb[h].opt()], outs=[veg[h].opt()]), reads=[('veb', h)], writes=[('veg', h)])
                    for r in range(4):
                        P.dma('sp', lambda e, r=r: e.dma_start(out=KT[:, :, r * 2048:(r + 1) * 2048], in_=ktg[r * 256:(r + 1) * 256, :].rearrange("(p a) n -> p a n", a=2)),
                              reads=['ktg'], writes=[('KT', g_) for g_ in range(r * 4, r * 4 + 4)])
                        for h in range(2):
                            P.dma('sp', lambda e, r=r, h=h: e.dma_start(out=Ve[:, r * 16 + h * 8:r * 16 + (h + 1) * 8, :], in_=veg[h][r * 1024:(r + 1) * 1024, :].rearrange("(p t) c -> p t c", t=8)),
                                  reads=[('veg', h)], writes=[('Ve', kt_) for kt_ in range(r * 16 + h * 8, r * 16 + (h + 1) * 8)])
            with kb.phase():
                OT = kb.sb("OT", [128, 8, NTOK], BF16)
                with kb.phase():
                    C.update(attn_scratch(kb))
                    for qg in range(4):
                        for c in range(8):
                            attention(kb, QT[:, c, qg * 512:(qg + 1) * 512], [('QT', c, qg)], 512, kv_tiles(list(range(66)), c // 4),
                                      OT[0:64, c, qg * 512:(qg + 1) * 512], OT[64:128, c, qg * 512:(qg + 1) * 512], [('OT', qg)], C)
                with kb.phase():
                    xtmp = kb.sb("xtmp", [128, 8, 512], F32)
                    stage = [kb.sb("stage%d" % i, [128, 1024], F32) for i in range(2)]
                    wo = kb.sb("wo", [128, 8, 1024], BF16)
                    ostage = [kb.sb("ostage%d" % i, [128, 1024], F32) for i in range(2)]
                    kb.load_w(wo, 'wo', wo_d, 0, KC, 0, 1024)
                    for g in range(4):
                        kb.load_xT(tsl(xb_d, g * 512, (g + 1) * 512), 512, xtmp, 'xtmp', stage)
                        resid_proj(kb, OT, 'OT', g * 512, 512, wo, 'wo', xtmp, 'xtmp', 0, mL['G1'], mL['key'])
                        kb.store_x(xtmp, 'xtmp', 512, tsl(mid_d, g * 512, (g + 1) * 512), ostage)
    mlp_tail(kb, mid_d, out_d, NTOK, mL, w1_d, w2_d)
    if own:
        P.close()
    return nc


def mlp_tail(kb, src_d, out_d, ntok, mL, w1_d, w2_d, final=None):
    with kb.phase():
        xT = kb.sb("xT", [128, 8, ntok], F32)
        with kb.phase():
            stage = [kb.sb("stage%d" % i, [128, 1024], F32) for i in range(2)]
            kb.load_xT(src_d, ntok, xT, 'xT', stage)
        with kb.phase():
            hT = kb.sb("hT", [128, 8, ntok], BF16)
            scr = norm_scratch(kb)
            kb.norm_mod(xT, 'xT', ntok, mL['A2'], mL['B2'], mL['key'], hT, 'hT', scr)
            wbuf, aT, rbuf = mlp_bufs(kb, ntok)
            kb.mlp(xT, 'xT', ntok, hT, 'hT', mL['G2'], mL['key'], w1_d, w2_d, wbuf, aT, rbuf)
        with kb.phase():
            ostage = [kb.sb("ostage%d" % i, [128, 1024], F32) for i in range(2)]
            if final is not None:
                yT = kb.sb("yT", [128, 8, ntok], F32)
                scr = norm_scratch(kb)
                kb.norm_mod(xT, 'xT', ntok, final[0], final[1], final[2], yT, 'yT', scr)
                kb.store_x(yT, 'yT', ntok, out_d, ostage)
            else:
                kb.store_x(xT, 'xT', ntok, out_d, ostage)


def fm(v):
    return np.ascontiguousarray(np.asarray(v, np.float32).reshape(8, 128).T)


def common_inputs(inp, layer, b):
    cv = np.stack([fm(inp['c'][b]), fm(inp['c_ctx'])], axis=-1)
    return dict(ident=np.eye(128, dtype=np.float32), cv=np.ascontiguousarray(cv), adaw=np.ascontiguousarray(inp['ada_w'][layer]),
                adab=np.ascontiguousarray(np.stack([inp['ada_b'][layer]] * 2)), g1=fm(inp['norm1_g'][layer]), g2=fm(inp['norm2_g'][layer]),
                w1=np.ascontiguousarray(inp['mlp_w1'][layer]), w2=np.ascontiguousarray(inp['mlp_w2'][layer]))


def rope_tables(order):
    t = np.asarray(order)
    row = (t // 64).astype(np.float32)
    col = (t % 64).astype(np.float32)
    inv = (10000.0 ** (-np.arange(16, dtype=np.float32) / 16)).astype(np.float32)
    ang = np.concatenate([row[:, None] * inv, col[:, None] * inv], axis=-1).astype(np.float32)
    idx = (np.arange(128) % 64) // 2
    a = ang[:, idx].T
    cos = np.cos(a).astype(np.float32).reshape(128, 16, 512).transpose(1, 0, 2)
    sin = np.sin(a).astype(np.float32).reshape(128, 16, 512).transpose(1, 0, 2)
    return np.ascontiguousarray(cos), np.ascontiguousarray(sin)


def gqa_chunk_heads():
    return [(c, 4 + c) if c < 4 else (8 + c - 4, 12 + c - 4) for c in range(8)]


def prep_l0(inp, b, q):
    d = common_inputs(inp, 0, b)
    order = np.concatenate([np.arange(q * 2048, 8192), np.arange(0, q * 2048)])
    d['xb'] = np.ascontiguousarray(inp['x'][b][order])
    d['ctx'] = np.ascontiguousarray(inp['ctx'][b])
    wqkv = inp['at_w_qkv'][0]
    qcols = np.concatenate([np.concatenate([np.arange(ha * 64, ha * 64 + 64), np.arange(hb * 64, hb * 64 + 64)]) for ha, hb in gqa_chunk_heads()])
    d['wqkv'] = np.ascontiguousarray(np.concatenate([wqkv[:, qcols], wqkv[:, 1024:]], axis=1))
    d['wo'] = np.ascontiguousarray(inp['at_w_o'][0][qcols, :])
    d['gains'] = np.ascontiguousarray(np.stack([np.tile(inp['at_q_g'][0], 2), np.tile(inp['at_k_g'][0], 2)], axis=1))
    d['cos'], d['sin'] = rope_tables(order)
    rmat = np.zeros((128, 128), np.float32)
    for i in range(64):
        rmat[2 * i + 1, 2 * i] = -1.0
        rmat[2 * i, 2 * i + 1] = 1.0
    d['rmat'] = rmat
    bones = np.zeros((128, 128), np.float32)
    bones[:64, :64] = 1.0
    bones[64:, 64:] = 1.0
    d['bones'] = bones
    return d


NA_CLASS = [0, 1] + [2] * 12 + [3, 4]
NA_ST = list(range(14)) + [12, 13]


def na_unit(kb, q_ap, qkeys, kt_ap, vt, tab_ap, tkey, st, dst, dstkeys, kvkeys, C):
    P = kb.P
    u = kb.nextrot('naunit', 2)
    b0 = 3 * u
    ob = 6 + u
    for j in range(7):
        kt = st + j
        P.op('pe', lambda e, j=j, kt=kt: e.matmul(kb.ps[:, b0 * 512 + j * 128:b0 * 512 + (j + 1) * 128], lhsT=kt_ap(kt), rhs=q_ap, start=True, stop=True),
             reads=kvkeys + qkeys, writes=[('ps', b0 + j // 4)])
    for j in range(2):
        kt = 20 + j
        P.op('pe', lambda e, j=j, kt=kt: e.matmul(kb.ps[:, (b0 + 2) * 512 + j * 128:(b0 + 2) * 512 + (j + 1) * 128], lhsT=kt_ap(kt), rhs=q_ap, start=True, stop=True),
             reads=kvkeys + qkeys, writes=[('ps', b0 + 2)])
    sbr = kb.nextrot('nasb', 2)
    sb_ = C['nasb'][sbr]
    P.op('dve', lambda e: e.tensor_tensor(out=sb_[:], in0=kb.ps[:, b0 * 512:b0 * 512 + 896], in1=tab_ap, op=ALU.add), reads=[tkey], writes=[('ps', b0), ('ps', b0 + 1), ('nasb', sbr)])
    pr = kb.nextrot('napw', 3)
    pw, pc = C['napw'][pr], C['napc'][pr]
    P.op('act', lambda e: e.activation(out=pw[:], in_=sb_[:], func=AF.Exp), reads=[('nasb', sbr)], writes=[('napw', pr)])
    P.op('act', lambda e: e.activation(out=pc[:], in_=kb.ps[:, (b0 + 2) * 512:(b0 + 2) * 512 + 256], func=AF.Exp), writes=[('ps', b0 + 2), ('napc', pr)])
    for j in range(9):
        kt = st + j if j < 7 else 20 + (j - 7)
        rhs = pw[:, j * 128:(j + 1) * 128] if j < 7 else pc[:, (j - 7) * 128:(j - 6) * 128]
        P.op('pe', lambda e, j=j, kt=kt, rhs=rhs: e.matmul(kb.bank(ob, 128), lhsT=vt(kt), rhs=rhs, start=(j == 0), stop=(j == 8)),
             reads=kvkeys + [('napw', pr), ('napc', pr)], writes=[('ps', ob)])
    return ob


def build_l1(kb=None, io=None):
    own = kb is None
    if own:
        kb = KB()
    io = io or {}
    kb.pfx = '' if own else 'l1_'
    P, nc = kb.P, kb.nc
    NTOK = 2048
    NH = 2560
    xh_d = io.get("xh") or kb.din("xh", [NH, D]); ctx_d = io.get("ctx") or kb.din("ctx", [256, D])
    cv_d = kb.din("cv", [128, 8, 2]); adaw_d = kb.din("adaw", [D, kb.ncol_ada()]); adab_d = kb.din("adab", [2, kb.ncol_ada()])
    g1_d = kb.din("g1", [128, 8]); g2_d = kb.din("g2", [128, 8])
    wqkv_d = kb.din("wqkv", [8, D, 384]); tab_d = kb.din("tab", [5, 8, 128, 2 * 7 * 128])
    wo_d = kb.din("wo", [D, D]); w1_d = kb.din("w1", [D, DFF]); w2_d = kb.din("w2", [DFF, D])
    out_d = io.get("out") or kb.dout("out", [NTOK, D])
    mid_d = io.get("mid") or out_d

    modsT = kb.sb("modsT", [128, 48, 2], F32)
    mvL = kb.sb("mvL", [128, 6, 8], F32); mvC = kb.sb("mvC", [128, 6, 8], F32)
    C = {}
    with kb.phase():
        kb.mods(cv_d, adaw_d, adab_d, modsT)
        mL = kb.mod_vectors(modsT, g1_d, g2_d, 0, mvL, "L")
        mC = kb.mod_vectors(modsT, g1_d, g2_d, 1, mvC, "C")
    with kb.phase():
        hT = kb.sb("hT", [128, 8, NH], BF16)
        hcT = kb.sb("hcT", [128, 8, 256], BF16)
        OT = kb.sb("OT", [128, 8, NTOK], BF16)
        with kb.phase():
            xtmp = kb.sb("xtmp", [128, 8, 512], F32)
            stage = [kb.sb("stage%d" % i, [128, 1024], F32) for i in range(2)]
            scr = norm_scratch(kb)
            for g in range(5):
                kb.load_xT(tsl(xh_d, g * 512, (g + 1) * 512), 512, xtmp, 'xtmp', stage)
                kb.norm_mod(xtmp, 'xtmp', 512, mL['A1'], mL['B1'], mL['key'], hT, 'hT', scr, t0=0, ht0=g * 512)
            kb.load_xT(ctx_d, 256, xtmp, 'xtmp', stage)
            kb.norm_mod(xtmp, 'xtmp', 256, mC['A1'], mC['B1'], mC['key'], hcT, 'hcT', scr)
        with kb.phase():
            C['rc'] = [kb.sb("rc%d" % i, [128, 512], F32) for i in range(2)]
            C['nasb'] = [kb.sb("nasb%d" % i, [128, 896], F32) for i in range(2)]
            C['napw'] = [kb.sb("napw%d" % i, [128, 896], BF16) for i in range(3)]
            C['napc'] = [kb.sb("napc%d" % i, [128, 256], BF16) for i in range(3)]
            wc = [kb.sb("wc%d" % i, [128, 8, 384], BF16) for i in range(2)]
            QTc = [kb.sb("QTc%d" % i, [128, NTOK], BF16) for i in range(2)]
            KTc = [kb.sb("KTc%d" % i, [128, NH + 256], BF16) for i in range(2)]
            Vec = [kb.sb("Vec%d" % i, [128, 22, 192], BF16) for i in range(2)]
            tabI = [kb.sb("tabI%d" % i, [128, 2, 7, 128], F32) for i in range(2)]
            tabS = [kb.sb("tabS%d" % i, [128, 2, 7, 128], F32) for i in range(2)]
            for i in range(2):
                P.op('pool', lambda e, i=i: e.memset(Vec[i][:, :, 64:128], 1.0), writes=[('Vones', i)])
            for c in range(8):
                b = c % 2
                kb.load_w(wc[b], ('wc', b), wqkv_d[c], 0, KC, 0, 384)
                P.dma('sp', lambda e, c=c, b=b: e.dma_start(out=tabI[b][:].rearrange("p a j q -> p (a j q)"), in_=tab_d[2, c]), writes=[('tabI', b)])
                for g in range(4):
                    pb = kb.nextrot('projbank', 2)
                    proj_fm(kb, hT, 'hT', 256 + g * 512, 512, wc[b], ('wc', b), 0, pb)
                    P.op('act', lambda e, g=g, b=b, pb=pb: e.activation(out=QTc[b][:, g * 512:(g + 1) * 512], in_=kb.bank(pb), func=AF.Copy, scale=0.125),
                         writes=[('ps', pb), ('QTc', b, g)])
                for g in range(5):
                    pb = kb.nextrot('projbank', 2)
                    proj_fm(kb, hT, 'hT', g * 512, 512, wc[b], ('wc', b), 128, pb)
                    P.op('act', lambda e, g=g, b=b, pb=pb: e.activation(out=KTc[b][:, g * 512:(g + 1) * 512], in_=kb.bank(pb), func=AF.Copy),
                         writes=[('ps', pb), ('KTc', b, g)])
                pb = kb.nextrot('projbank', 2)
                proj_fm(kb, hcT, 'hcT', 0, 256, wc[b], ('wc', b), 128, pb)
                P.op('act', lambda e, b=b, pb=pb: e.activation(out=KTc[b][:, NH:NH + 256], in_=kb.bank(pb, 256), func=AF.Copy), writes=[('ps', pb), ('KTc', b, 5)])
                for kt in range(22):
                    src_h, hk, t0 = (hT, 'hT', kt * 128) if kt < 20 else (hcT, 'hcT', (kt - 20) * 128)
                    pb = kb.nextrot('projbank', 2)
                    for kc in range(KC):
                        P.op('pe', lambda e, kc=kc, b=b, pb=pb, src_h=src_h, t0=t0: e.matmul(kb.bank(pb, 128), lhsT=src_h[:, kc, t0:t0 + 128], rhs=wc[b][:, kc, 256:384],
                                                                                          start=(kc == 0), stop=(kc == KC - 1)),
                             reads=[(('wc', b), kc), (hk, t0 // 512)], writes=[('ps', pb)])
                    dst = Vec[b][:, kt, :].rearrange("p (t s) -> p t s", s=64)[:, ::2, :]
                    src = kb.bank(pb, 128).rearrange("p (t s) -> p t s", s=64)
                    P.op('dve', lambda e, dst=dst, src=src: e.tensor_copy(out=dst, in_=src), writes=[('ps', pb), ('Vec', b, kt)])
                for rp in range(16):
                    cls = NA_CLASS[rp]
                    st = NA_ST[rp]
                    if cls == 2:
                        tab, tkey = tabI[b], ('tabI', b)
                    else:
                        sbuf_i = kb.nextrot('tabS', 2)
                        tab, tkey = tabS[sbuf_i], ('tabS', sbuf_i)
                        P.dma('sp', lambda e, c=c, cls=cls, tab=tab: e.dma_start(out=tab[:].rearrange("p a j q -> p (a j q)"), in_=tab_d[cls, c]), writes=[tkey])
                    kvkeys = [('KTc', b, g_) for g_ in range(6)] + [('Vec', b, kt_) for kt_ in range(22)] + [('Vones', b)]
                    for hh in range(2):
                        p0 = hh * 64
                        q_ap = QTc[b][p0:p0 + 64, rp * 128:(rp + 1) * 128]
                        kt_ap = (lambda kt, b=b, p0=p0: KTc[b][p0:p0 + 64, kt * 128:(kt + 1) * 128])
                        vt = (lambda kt, b=b, hh=hh: Vec[b][:, kt, hh * 64:hh * 64 + 128])
                        tab_ap = tab[:, hh, :, :].rearrange("p j q -> p (j q)")
                        ob = na_unit(kb, q_ap, [('QTc', b, rp // 4)], kt_ap, vt, tab_ap, tkey, st, None, None, kvkeys, C)
                        rcr = kb.nextrot('rc', 2)
                        rc = C['rc'][rcr]
                        if hh == 0:
                            P.op('dve', lambda e, rc=rc, ob=ob: e.reciprocal(out=rc[64:128, 0:128], in_=kb.bank(ob, 128)[64:128, :]), writes=[('ps', ob), ('rc', rcr)])
                            P.op('dve', lambda e, rc=rc, ob=ob, c=c, rp=rp: e.tensor_tensor(out=OT[0:64, c, rp * 128:(rp + 1) * 128], in0=kb.bank(ob, 128)[0:64, :], in1=rc[64:128, 0:128], op=ALU.mult),
                                 reads=[('rc', rcr)], writes=[('ps', ob), ('OT', rp // 4)])
                        else:
                            P.op('dve', lambda e, rc=rc, ob=ob: e.reciprocal(out=rc[0:64, 0:128], in_=kb.bank(ob, 128)[0:64, :]), writes=[('ps', ob), ('rc', rcr)])
                            P.op('dve', lambda e, rc=rc, ob=ob, c=c, rp=rp: e.tensor_tensor(out=OT[64:128, c, rp * 128:(rp + 1) * 128], in0=kb.bank(ob, 128)[64:128, :], in1=rc[0:64, 0:128], op=ALU.mult),
                                 reads=[('rc', rcr)], writes=[('ps', ob), ('OT', rp // 4)])
        with kb.phase():
            xtmp = kb.sb("xtmp", [128, 8, 512], F32)
            stage = [kb.sb("stage%d" % i, [128, 1024], F32) for i in range(2)]
            ostage = [kb.sb("ostage%d" % i, [128, 1024], F32) for i in range(2)]
            wo = kb.sb("wo", [128, 8, 1024], BF16)
            kb.load_w(wo, 'wo', wo_d, 0, KC, 0, 1024)
            for g in range(4):
                kb.load_xT(tsl(xh_d, 256 + g * 512, 256 + (g + 1) * 512), 512, xtmp, 'xtmp', stage)
                resid_proj(kb, OT, 'OT', g * 512, 512, wo, 'wo', xtmp, 'xtmp', 0, mL['G1'], mL['key'])
                kb.store_x(xtmp, 'xtmp', 512, tsl(mid_d, g * 512, (g + 1) * 512), ostage)
    mlp_tail(kb, mid_d, out_d, NTOK, mL, w1_d, w2_d)
    if own:
        P.close()
    return nc


def na_bias_tables(rpb, qq):
    NEG = np.float32(-30000.0)
    tab = np.full((5, 16, 2, 64, 7, 2, 64), NEG, np.float32)
    cq = np.arange(64)
    cs = np.clip(cq - 8, 0, 48)
    ck = np.arange(64)
    colvalid = (ck[:, None] >= cs[None, :]) & (ck[:, None] < cs[None, :] + 16)
    colidx = np.clip(ck[:, None] - cq[None, :] + 15, 0, 30)
    rep_rp = {0: 0, 1: 1, 2: 2, 3: 14, 4: 15}
    for cls in range(5):
        rp = rep_rp[cls]
        st = NA_ST[rp]
        for bq in range(2):
            r = 32 * qq + 2 * rp + bq
            rs = min(max(r - 4, 0), 120)
            for j in range(7):
                for a in range(2):
                    kr = 32 * qq - 4 + 2 * (st + j) + a
                    if kr < rs or kr >= rs + 8 or kr < 0 or kr > 127:
                        continue
                    vals = rpb[:, kr - r + 7, :][:, colidx]
                    tab[cls, :, a, :, j, bq, :] = np.where(colvalid[None], vals, NEG)
    tab = tab.reshape(5, 8, 2, 128, 7, 128)
    tab = tab.transpose(0, 1, 3, 2, 4, 5).reshape(5, 8, 128, 2 * 7 * 128)
    return np.ascontiguousarray(tab)


def prep_l1(inp, x1, hctx1, b, q):
    d = common_inputs(inp, 1, b)
    if x1 is not None:
        xh = np.zeros((2560, D), np.float32)
        lo = q * 2048 - 256
        hi = lo + 2560
        s0, s1 = max(lo, 0), min(hi, 8192)
        xh[s0 - lo:s1 - lo] = x1[b][s0:s1]
        d['xh'] = xh
        d['ctx'] = np.ascontiguousarray(hctx1[b])
    w = inp['na_w_qkv'][0]
    d['wqkv'] = np.ascontiguousarray(np.stack([np.concatenate([w[:, c * 128:(c + 1) * 128], w[:, 1024 + c * 128:1024 + (c + 1) * 128],
                                                              w[:, 2048 + c * 128:2048 + (c + 1) * 128]], axis=1) for c in range(8)]))
    d['tab'] = na_bias_tables(inp['na_rpb'][0], q)
    d['wo'] = np.ascontiguousarray(inp['na_w_o'][0])
    return d


def build_l2(kb=None, io=None):
    own = kb is None
    if own:
        kb = KB()
    io = io or {}
    kb.pfx = '' if own else 'l2_'
    P, nc = kb.P, kb.nc
    NTOK = 2048
    NH = 2304
    xh_d = io.get("xh") or kb.din("xh", [NH, D])
    cv_d = kb.din("cv", [128, 8, 2]); adaw_d = kb.din("adaw", [D, kb.ncol_ada()]); adab_d = kb.din("adab", [2, kb.ncol_ada()])
    g1_d = kb.din("g1", [128, 8]); g2_d = kb.din("g2", [128, 8])
    wpw1_d = kb.din("wpw1", [8, D, 256]); vecs_d = kb.din("vecs", [128, 6, 8]); wdw_d = kb.din("wdw", [128, 8, 31]); mask_d = kb.din("mask", [128, 2])
    wpw2_d = kb.din("wpw2", [D, D]); w1_d = kb.din("w1", [D, DFF]); w2_d = kb.din("w2", [DFF, D])
    out_d = io.get("out") or kb.dout("out", [NTOK, D])
    mid_d = io.get("mid") or out_d

    modsT = kb.sb("modsT", [128, 48, 2], F32)
    mvL = kb.sb("mvL", [128, 6, 8], F32)
    vecs = kb.sb("vecs", [128, 6, 8], F32)
    wdw = kb.sb("wdw", [128, 8, 31], F32)
    mask = kb.sb("mask", [128, 2], F32)
    bG = kb.sb("bG", [128, 8], F32)
    identb = kb.sb("identb", [128, 128], BF16)
    with kb.phase():
        kb.mods(cv_d, adaw_d, adab_d, modsT)
        mL = kb.mod_vectors(modsT, g1_d, g2_d, 0, mvL, "L")
        P.dma('sp', lambda e: e.dma_start(out=vecs[:], in_=vecs_d), writes=['vecs'])
        P.dma('sp', lambda e: e.dma_start(out=wdw[:], in_=wdw_d), writes=['wdw'])
        P.dma('sp', lambda e: e.dma_start(out=mask[:], in_=mask_d), writes=['mask'])
        P.op('dve', lambda e: e.tensor_tensor(out=bG[:], in0=vecs[:, 5, :], in1=mL['G1'], op=ALU.mult), reads=['vecs', mL['key']], writes=['bG'])
        P.op('dve', lambda e: e.tensor_copy(out=identb[:], in_=kb.ident[:]), reads=['ident'], writes=['identb'])
    with kb.phase():
        vT = kb.sb("vT", [128, 8, NTOK], BF16)
        with kb.phase():
            uT = kb.sb("uT", [128, 8, NH], BF16)
            with kb.phase():
                hT = kb.sb("hT", [128, 8, NH], BF16)
                with kb.phase():
                    xtmp = kb.sb("xtmp", [128, 8, 512], F32)
                    stage = [kb.sb("stage%d" % i, [128, 1024], F32) for i in range(2)]
                    scr = norm_scratch(kb)
                    for g in range(5):
                        n = 512 if g < 4 else 256
                        kb.load_xT(tsl(xh_d, g * 512, g * 512 + n), n, xtmp, 'xtmp', stage)
                        kb.norm_mod(xtmp, 'xtmp', n, mL['A1'], mL['B1'], mL['key'], hT, 'hT', scr, t0=0, ht0=g * 512)
                with kb.phase():
                    wp = [kb.sb("wp%d" % i, [128, 8, 256], BF16) for i in range(2)]
                    sig = [kb.sb("sig%d" % i, [128, 512], F32) for i in range(2)]
                    for fc in range(8):
                        b = fc % 2
                        kb.load_w(wp[b], ('wp', b), wpw1_d[fc], 0, KC, 0, 256)
                        for g in range(5):
                            n = 512 if g < 4 else 256
                            pa = kb.nextrot('projbank', 2)
                            proj_fm(kb, hT, 'hT', g * 512, n, wp[b], ('wp', b), 0, pa)
                            pg = 2 + kb.nextrot('projbank2', 2)
                            proj_fm(kb, hT, 'hT', g * 512, n, wp[b], ('wp', b), 128, pg)
                            sb_ = kb.nextrot('sig', 2)
                            P.op('act', lambda e, sb_=sb_, pg=pg, fc=fc, n=n: e.activation(out=sig[sb_][:, :n], in_=kb.bank(pg, n), func=AF.Sigmoid, bias=vecs[:, 1, fc:fc + 1]),
                                 reads=['vecs'], writes=[('ps', pg), ('sig', sb_)])
                            P.op('dve', lambda e, sb_=sb_, pa=pa, fc=fc, g=g, n=n: e.scalar_tensor_tensor(out=uT[:, fc, g * 512:g * 512 + n], in0=kb.bank(pa, n), scalar=vecs[:, 0, fc:fc + 1],
                                                                                                      in1=sig[sb_][:, :n], op0=ALU.add, op1=ALU.mult),
                                 reads=['vecs', ('sig', sb_)], writes=[('ps', pa), ('uT', fc, g)])
                        P.op('dve', lambda e, fc=fc: e.tensor_scalar(out=uT[:, fc, 0:128], in0=uT[:, fc, 0:128], scalar1=mask[:, 0:1], scalar2=None, op0=ALU.mult),
                             reads=['mask'], writes=[('uT', fc, 0)])
                        P.op('dve', lambda e, fc=fc: e.tensor_scalar(out=uT[:, fc, 2176:2304], in0=uT[:, fc, 2176:2304], scalar1=mask[:, 1:2], scalar2=None, op0=ALU.mult),
                             reads=['mask'], writes=[('uT', fc, 4)])
            with kb.phase():
                dg = kb.sb("dg", [128, 8, 31, 128], BF16)
                cT = kb.sb("cT", [128, 8, 512], F32)
                cbf = [kb.sb("cbf%d" % i, [128, 512], BF16) for i in range(2)]
                c2 = [kb.sb("c2%d" % i, [128, 512], BF16) for i in range(2)]
                mean = kb.sb("mean", [128, 512], F32); msq = kb.sb("msq", [128, 512], F32); rstd = kb.sb("rstd", [128, 512], F32)
                tt_ = [kb.sb("tt%d" % i, [128, 512], F32) for i in range(2)]
                for fc in range(8):
                    P.op('dve', lambda e, fc=fc: e.tensor_tensor(out=dg[:, fc, :, :], in0=identb[:].unsqueeze(1).broadcast_to([128, 31, 128]),
                                                                in1=wdw[:, fc, :].unsqueeze(2).broadcast_to([128, 31, 128]), op=ALU.mult),
                         reads=['identb', 'wdw'], writes=[('dg', fc)])
                for tg in range(4):
                    for fc in range(8):
                        pb = kb.nextrot('projbank', 2)
                        for j in range(31):
                            o = 128 + tg * 512 + j - 15
                            P.op('pe', lambda e, fc=fc, j=j, o=o, pb=pb: e.matmul(kb.bank(pb), lhsT=dg[:, fc, j, :], rhs=uT[:, fc, o:o + 512], start=(j == 0), stop=(j == 30)),
                                 reads=[('dg', fc)] + [('uT', fc, gg) for gg in range(5)], writes=[('ps', pb)])
                        P.op('act', lambda e, fc=fc, pb=pb: e.activation(out=cT[:, fc, :], in_=kb.bank(pb), func=AF.Identity, bias=vecs[:, 2, fc:fc + 1]),
                             reads=['vecs'], writes=[('ps', pb), ('cT', fc)])
                        r = kb.nextrot('cbf', 2)
                        P.op('dve', lambda e, fc=fc, r=r: e.tensor_copy(out=cbf[r][:], in_=cT[:, fc, :]), reads=[('cT', fc)], writes=[('cbf', r)])
                        P.op('act', lambda e, fc=fc, r=r: e.activation(out=c2[r][:], in_=cT[:, fc, :], func=AF.Square), reads=[('cT', fc)], writes=[('c2', r)])
                        P.op('pe', lambda e, fc=fc, r=r: e.matmul(kb.bank(6), lhsT=kb.ones_bf[:], rhs=cbf[r][:], start=(fc == 0), stop=(fc == 7)), reads=[('cbf', r), 'ones_bf'], writes=[('ps', 6)])
                        P.op('pe', lambda e, fc=fc, r=r: e.matmul(kb.bank(7), lhsT=kb.ones_bf[:], rhs=c2[r][:], start=(fc == 0), stop=(fc == 7)), reads=[('c2', r), 'ones_bf'], writes=[('ps', 7)])
                    P.op('act', lambda e: e.activation(out=mean[:], in_=kb.bank(6), func=AF.Copy, scale=1.0 / D), writes=[('ps', 6), 'mean'])
                    P.op('dve', lambda e: e.tensor_tensor(out=msq[:], in0=mean[:], in1=mean[:], op=ALU.mult), reads=['mean'], writes=['msq'])
                    P.op('dve', lambda e: e.scalar_tensor_tensor(out=msq[:], in0=kb.bank(7), scalar=1.0 / D, in1=msq[:], op0=ALU.mult, op1=ALU.subtract), writes=[('ps', 7), 'msq'])
                    P.op('act', lambda e: e.activation(out=rstd[:], in_=msq[:], func=AF.Ln, bias=kb.eps_t[:]), reads=['msq', 'eps_t'], writes=['rstd'])
                    P.op('act', lambda e: e.activation(out=rstd[:], in_=rstd[:], func=AF.Exp, scale=-0.5), writes=['rstd'])
                    for fc in range(8):
                        r = kb.nextrot('tt', 2)
                        P.op('dve', lambda e, fc=fc, r=r: e.tensor_tensor(out=tt_[r][:], in0=cT[:, fc, :], in1=mean[:], op=ALU.subtract), reads=[('cT', fc), 'mean'], writes=[('tt', r)])
                        P.op('dve', lambda e, r=r: e.tensor_tensor(out=tt_[r][:], in0=tt_[r][:], in1=rstd[:], op=ALU.mult), reads=['rstd'], writes=[('tt', r)])
                        P.op('act', lambda e, fc=fc, r=r, tg=tg: e.activation(out=vT[:, fc, tg * 512:(tg + 1) * 512], in_=tt_[r][:], func=AF.Silu, scale=vecs[:, 3, fc:fc + 1], bias=vecs[:, 4, fc:fc + 1]),
                             reads=[('tt', r), 'vecs'], writes=[('vT', tg)])
        with kb.phase():
            xtmp = kb.sb("xtmp", [128, 8, 512], F32)
            stage = [kb.sb("stage%d" % i, [128, 1024], F32) for i in range(2)]
            ostage = [kb.sb("ostage%d" % i, [128, 1024], F32) for i in range(2)]
            wo = kb.sb("wo", [128, 8, 1024], BF16)
            kb.load_w(wo, 'wo', wpw2_d, 0, KC, 0, 1024)
            for g in range(4):
                kb.load_xT(tsl(xh_d, 128 + g * 512, 128 + (g + 1) * 512), 512, xtmp, 'xtmp', stage)
                resid_proj(kb, vT, 'vT', g * 512, 512, wo, 'wo', xtmp, 'xtmp', 0, mL['G1'], mL['key'], bG=bG)
                kb.store_x(xtmp, 'xtmp', 512, tsl(mid_d, g * 512, (g + 1) * 512), ostage)
    mlp_tail(kb, mid_d, out_d, NTOK, mL, w1_d, w2_d)
    if own:
        P.close()
    return nc


def prep_l2(inp, x2, b, q):
    d = common_inputs(inp, 2, b)
    if x2 is not None:
        xh = np.zeros((2304, D), np.float32)
        lo = q * 2048 - 128
        hi = lo + 2304
        s0, s1 = max(lo, 0), min(hi, 8192)
        xh[s0 - lo:s1 - lo] = x2[b][s0:s1]
        d['xh'] = xh
    w = inp['cv_w_pw1'][0]
    d['wpw1'] = np.ascontiguousarray(np.stack([np.concatenate([w[:, c * 128:(c + 1) * 128], w[:, 1024 + c * 128:1024 + (c + 1) * 128]], axis=1) for c in range(8)]))
    bp = inp['cv_b_pw1'][0]
    d['vecs'] = np.ascontiguousarray(np.stack([fm(bp[:1024]), fm(bp[1024:]), fm(inp['cv_b_dw'][0]), fm(inp['cv_ln_g'][0]), fm(inp['cv_ln_b'][0]), fm(inp['cv_b_pw2'][0])], axis=1))
    d['wdw'] = np.ascontiguousarray(inp['cv_w_dw'][0].T.reshape(8, 128, 31).transpose(1, 0, 2))
    m = np.ones((128, 2), np.float32)
    if q == 0:
        m[:, 0] = 0.0
    if q == 3:
        m[:, 1] = 0.0
    d['mask'] = m
    d['wpw2'] = np.ascontiguousarray(inp['cv_w_pw2'][0])
    return d


def build_l3a(kb=None, io=None):
    own = kb is None
    if own:
        kb = KB()
    io = io or {}
    kb.pfx = '' if own else 'l3a_'
    P, nc = kb.P, kb.nc
    NTOK = 2048
    x_d = io.get("x") or kb.din("x", [NTOK, D])
    cv_d = kb.din("cv", [128, 8, 2]); adaw_d = kb.din("adaw", [D, kb.ncol_ada()]); adab_d = kb.din("adab", [2, kb.ncol_ada()])
    g1_d = kb.din("g1", [128, 8]); g2_d = kb.din("g2", [128, 8])
    csd_d = kb.din("csd", [256, 512])
    pq_d = io.get("pq") or kb.dout("pq", [NTOK, 2048], BF16)
    modsT = kb.sb("modsT", [128, 48, 2], F32)
    mvL = kb.sb("mvL", [128, 6, 8], F32)
    with kb.phase():
        kb.mods(cv_d, adaw_d, adab_d, modsT)
        mL = kb.mod_vectors(modsT, g1_d, g2_d, 0, mvL, "L")
    io['mL_out'] = mL
    with kb.phase():
        hT = kb.sb("hT", [128, 8, NTOK], BF16)
        csd = kb.sb("csd", [128, 2, 512], BF16)
        kb.load_w(csd, 'csd', csd_d, 0, 2, 0, 512)
        with kb.phase():
            xtmp = kb.sb("xtmp", [128, 8, 512], F32)
            stage = [kb.sb("stage%d" % i, [128, 1024], F32) for i in range(2)]
            scr = norm_scratch(kb)
            for g in range(4):
                kb.load_xT(tsl(x_d, g * 512, (g + 1) * 512), 512, xtmp, 'xtmp', stage)
                kb.norm_mod(xtmp, 'xtmp', 512, mL['A1'], mL['B1'], mL['key'], hT, 'hT', scr, t0=0, ht0=g * 512)
        with kb.phase():
            pqs = [kb.sb("pqs%d" % i, [128, 2048], BF16) for i in range(2)]
            for tt in range(16):
                ob = tt % 2
                for grp in range(4):
                    pb = kb.nextrot('projbank', 4)
                    for kl in range(2):
                        kc = grp * 2 + kl
                        P.op('pe', lambda e, kc=kc, kl=kl, tt=tt, pb=pb: e.matmul(kb.bank(pb), lhsT=hT[:, kc, tt * 128:(tt + 1) * 128], rhs=csd[:, kl, :], start=(kl == 0), stop=(kl == 1)),
                             reads=[(('csd'), kl), ('hT', tt // 4)], writes=[('ps', pb)])
                    if grp % 2 == 0:
                        P.op('act', lambda e, ob=ob, grp=grp, pb=pb: e.activation(out=pqs[ob][:, grp * 512:(grp + 1) * 512], in_=kb.bank(pb), func=AF.Copy), writes=[('ps', pb), ('pqs', ob, grp)])
                    else:
                        P.op('dve', lambda e, ob=ob, grp=grp, pb=pb: e.tensor_copy(out=pqs[ob][:, grp * 512:(grp + 1) * 512], in_=kb.bank(pb)), writes=[('ps', pb), ('pqs', ob, grp)])
                P.dma('sp', lambda e, ob=ob, tt=tt: e.dma_start(out=pq_d[tt * 128:(tt + 1) * 128, :], in_=pqs[ob][:]), reads=[('pqs', ob, g_) for g_ in range(4)])
    if own:
        P.close()
    return nc


def build_l3b(kb=None, io=None):
    own = kb is None
    if own:
        kb = KB()
    io = io or {}
    kb.pfx = '' if own else 'l3b_'
    P, nc = kb.P, kb.nc
    NTOK = 2048
    x_d = io.get("x") or kb.din("x", [NTOK, D])
    cv_d = kb.din("cv", [128, 8, 2]); adaw_d = kb.din("adaw", [D, kb.ncol_ada()]); adab_d = kb.din("adab", [2, kb.ncol_ada()])
    g1_d = kb.din("g1", [128, 8]); g2_d = kb.din("g2", [128, 8])
    pq_d = io.get("pq") or kb.din("pq", [8192, 2048], BF16)
    cn_d = kb.din("cn", [8192, NTOK], BF16); sn_d = kb.din("sn", [8192, NTOK], BF16)
    ftw_d = kb.din("ftw", [D, D]); vecs_d = kb.din("vecs", [128, 2, 8])
    w1_d = kb.din("w1", [D, DFF]); w2_d = kb.din("w2", [DFF, D])
    out_d = io.get("out") or kb.dout("out", [NTOK, D])
    mid_d = io.get("mid") or out_d
    modsT = kb.sb("modsT", [128, 48, 2], F32)
    mvL = kb.sb("mvL", [128, 6, 8], F32)
    vecs = kb.sb("vecs", [128, 2, 8], F32)
    bG = kb.sb("bG", [128, 8], F32)
    zeros = kb.sb("zeros", [128, 8], F32)
    with kb.phase():
        if io.get('mL') is not None:
            mL = io['mL']
        else:
            kb.mods(cv_d, adaw_d, adab_d, modsT)
            mL = kb.mod_vectors(modsT, g1_d, g2_d, 0, mvL, "L")
        P.dma('sp', lambda e: e.dma_start(out=vecs[:], in_=vecs_d), writes=['vecs'])
        P.op('dve', lambda e: e.tensor_tensor(out=bG[:], in0=vecs[:, 0, :], in1=mL['G1'], op=ALU.mult), reads=['vecs', mL['key']], writes=['bG'])
        P.op('pool', lambda e: e.memset(zeros[:], 0.0), writes=['zeros'])
    with kb.phase():
        zT = kb.sb("zT", [128, 8, NTOK], BF16)
        with kb.phase():
            pqb = [kb.sb("pqb%d" % i, [128, 2048], BF16) for i in range(3)]
            tb = [kb.sb("tb%d" % i, [128, 2, 512], BF16) for i in range(3)]
            tokmap = io.get('tokmap') or (lambda nt: nt * 128)
            for kg in range(4):
                for nt in range(64):
                    tk = tokmap(nt)
                    r = kb.nextrot('pqb', 3)
                    P.dma('sp', lambda e, r=r, nt=nt: e.dma_start(out=pqb[r][:], in_=pq_d[nt * 128:(nt + 1) * 128, :]), writes=[('pqb', r)])
                    P.dma('sp', lambda e, r=r, tk=tk, kg=kg: e.dma_start(out=tb[r][:, 0, :], in_=cn_d[tk:tk + 128, kg * 512:(kg + 1) * 512]), writes=[('tb', r, 0)])
                    P.dma('sp', lambda e, r=r, tk=tk, kg=kg: e.dma_start(out=tb[r][:, 1, :], in_=sn_d[tk:tk + 128, kg * 512:(kg + 1) * 512]), writes=[('tb', r, 1)])
                    for fz in range(8):
                        grp, jh = fz // 2, fz % 2
                        P.op('pe', lambda e, r=r, fz=fz, grp=grp, jh=jh, nt=nt: e.matmul(kb.bank(fz), lhsT=pqb[r][:, grp * 512 + jh * 128:grp * 512 + jh * 128 + 128], rhs=tb[r][:, 0, :],
                                                                                      start=(nt == 0), stop=False),
                             reads=[('pqb', r), ('tb', r, 0)], writes=[('ps', fz)])
                        P.op('pe', lambda e, r=r, fz=fz, grp=grp, jh=jh, nt=nt: e.matmul(kb.bank(fz), lhsT=pqb[r][:, grp * 512 + 256 + jh * 128:grp * 512 + 256 + jh * 128 + 128], rhs=tb[r][:, 1, :],
                                                                                      start=False, stop=(nt == 63)),
                             reads=[('pqb', r), ('tb', r, 1)], writes=[('ps', fz)])
                for fz in range(8):
                    if fz % 2 == 0:
                        P.op('act', lambda e, fz=fz, kg=kg: e.activation(out=zT[:, fz, kg * 512:(kg + 1) * 512], in_=kb.bank(fz), func=AF.Copy), writes=[('ps', fz), ('zT', kg)])
                    else:
                        P.op('dve', lambda e, fz=fz, kg=kg: e.tensor_copy(out=zT[:, fz, kg * 512:(kg + 1) * 512], in_=kb.bank(fz)), writes=[('ps', fz), ('zT', kg)])
        with kb.phase():
            xtmp = kb.sb("xtmp", [128, 8, 512], F32)
            stage = [kb.sb("stage%d" % i, [128, 1024], F32) for i in range(2)]
            ostage = [kb.sb("ostage%d" % i, [128, 1024], F32) for i in range(2)]
            wo = kb.sb("wo", [128, 8, 1024], BF16)
            kb.load_w(wo, 'wo', ftw_d, 0, KC, 0, 1024)
            for g in range(4):
                kb.load_xT(tsl(x_d, g * 512, (g + 1) * 512), 512, xtmp, 'xtmp', stage)
                resid_proj(kb, zT, 'zT', g * 512, 512, wo, 'wo', xtmp, 'xtmp', 0, mL['G1'], mL['key'], bG=bG)
                kb.store_x(xtmp, 'xtmp', 512, tsl(mid_d, g * 512, (g + 1) * 512), ostage)
    mlp_tail(kb, mid_d, out_d, NTOK, mL, w1_d, w2_d, final=(vecs[:, 1, :], zeros[:], 'vecs'))
    if own:
        P.close()
    return nc


def prep_l3a(inp, x3, b, q):
    d = common_inputs(inp, 3, b)
    for k in ('w1', 'w2'):
        d.pop(k)
    if x3 is not None:
        d['x'] = np.ascontiguousarray(x3[b][q * 2048:(q + 1) * 2048])
    dd = np.arange(256)[:, None].astype(np.int64)
    jj = np.arange(256)[None, :].astype(np.int64)
    ang = 2.0 * np.pi * ((dd * jj) % 256).astype(np.float64) / 256.0
    d['csd'] = np.ascontiguousarray(np.concatenate([np.cos(ang) / 16.0, np.sin(ang) / 16.0], axis=1).astype(np.float32))
    return d


_DFT_CACHE = {}


def seq_dft_tables(q):
    if q not in _DFT_CACHE:
        import ml_dtypes
        n = np.arange(8192, dtype=np.int64)[:, None]
        k = np.arange(q * 2048, (q + 1) * 2048, dtype=np.int64)[None, :]
        ang = 2.0 * np.pi * ((n * k) % 8192).astype(np.float64) / 8192.0
        s = 1.0 / np.sqrt(8192.0)
        _DFT_CACHE[q] = (np.ascontiguousarray((np.cos(ang) * s).astype(np.float32).astype(ml_dtypes.bfloat16)),
                         np.ascontiguousarray((-np.sin(ang) * s).astype(np.float32).astype(ml_dtypes.bfloat16)))
    return _DFT_CACHE[q]


def prep_l3b(inp, x3, pq_b, b, q):
    d = common_inputs(inp, 3, b)
    if x3 is not None:
        d['x'] = np.ascontiguousarray(x3[b][q * 2048:(q + 1) * 2048])
        d['pq'] = pq_b
    d['cn'], d['sn'] = seq_dft_tables(q)
    d['ftw'] = np.ascontiguousarray(inp['ft_w'][0])
    d['vecs'] = np.ascontiguousarray(np.stack([fm(inp['ft_b'][0]), fm(inp['final_g'])], axis=1))
    return d


CORES = [(b, q) for b in range(2) for q in range(4)]


def _run(nc, maps):
    res = run_bass_kernel_spmd(nc, maps, core_ids=list(range(8)))
    return res.results


def kernel_unfused(**inputs):
    inp = {k: np.asarray(v) for k, v in inputs.items()}
    r = _run(build_l0(), [prep_l0(inp, b, q) for b, q in CORES])
    x1 = np.stack([np.concatenate([r[b * 4 + q]["out"] for q in range(4)], axis=0) for b in range(2)])
    hctx1 = np.stack([r[b * 4]["hctx"] for b in range(2)])
    r = _run(build_l1(), [prep_l1(inp, x1, hctx1, b, q) for b, q in CORES])
    x2 = np.stack([np.concatenate([r[b * 4 + q]["out"] for q in range(4)], axis=0) for b in range(2)])
    r = _run(build_l2(), [prep_l2(inp, x2, b, q) for b, q in CORES])
    x3 = np.stack([np.concatenate([r[b * 4 + q]["out"] for q in range(4)], axis=0) for b in range(2)])
    r = _run(build_l3a(), [prep_l3a(inp, x3, b, q) for b, q in CORES])
    pq = [np.ascontiguousarray(np.concatenate([r[b * 4 + q]["pq"] for q in range(4)], axis=0)) for b in range(2)]
    r = _run(build_l3b(), [prep_l3b(inp, x3, pq[b], b, q) for b, q in CORES])
    out = np.stack([np.concatenate([r[b * 4 + q]["out"] for q in range(4)], axis=0) for b in range(2)])
    return out.astype(np.float32)


def halo_exchange(kb, src, dst, H, sel, tag):
    P, nc = kb.P, kb.nc
    bF = kb.dint("bounceF" + tag, [1024, H]); bL = kb.dint("bounceL" + tag, [1024, H])
    gF = kb.dint("gathF" + tag, [4096, H]); gL = kb.dint("gathL" + tag, [4096, H])
    with kb.phase():
        P.dma('pool', lambda e: e.dma_start(out=bF.rearrange("(p k) h -> p k h", k=8), in_=src.ap[:, :, 0:H]), writes=['bF'])
        P.dma('pool', lambda e: e.dma_start(out=bL.rearrange("(p k) h -> p k h", k=8), in_=src.ap[:, :, 2048 - H:2048]), writes=['bL'])
        for hf in range(2):
            P.dma('pool', lambda e, hf=hf: e.dma_start(out=dst.ap[:, hf * 4:(hf + 1) * 4, H:H + 2048], in_=src.ap[:, hf * 4:(hf + 1) * 4, :]), writes=[('dst', hf)])
        P.coll(lambda e: e.collective_compute("AllGather", ALU.bypass, replica_groups=RG, ins=[bF.opt()], outs=[gF.opt()]), reads=['bF'], writes=['gF'])
        P.coll(lambda e: e.collective_compute("AllGather", ALU.bypass, replica_groups=RG, ins=[bL.opt()], outs=[gL.opt()]), reads=['bL'], writes=['gL'])
        cand = [kb.sb("cand%d" % i, [128, 4, 8, H], F32) for i in range(2)]
        acc = [kb.sb("hacc%d" % i, [128, 8, H], F32) for i in range(2)]
        for side in range(2):
            gsrc, gkey = (gL, 'gL') if side == 0 else (gF, 'gF')
            srcv = gsrc.rearrange("(r p k) h -> p r k h", r=4, k=8)
            for r in range(4):
                P.dma('sp', lambda e, side=side, srcv=srcv, r=r: e.dma_start(out=cand[side][:, r, :, :], in_=srcv[:, r, :, :]), reads=[gkey], writes=[('cand', side, r)])
            P.op('dve', lambda e, side=side: e.tensor_scalar(out=acc[side][:], in0=cand[side][:, 0, :, :], scalar1=sel[:, side * 4:side * 4 + 1], scalar2=None, op0=ALU.mult),
                 reads=[('cand', side, 0), 'sel'], writes=[('hacc', side)])
            for r in range(1, 4):
                P.op('dve', lambda e, side=side, r=r: e.scalar_tensor_tensor(out=acc[side][:], in0=cand[side][:, r, :, :], scalar=sel[:, side * 4 + r:side * 4 + r + 1], in1=acc[side][:],
                                                                          op0=ALU.mult, op1=ALU.add),
                     reads=[('cand', side, r), 'sel'], writes=[('hacc', side)])
            d0 = 0 if side == 0 else H + 2048
            P.dma('sp', lambda e, side=side, d0=d0: e.dma_start(out=dst.ap[:, :, d0:d0 + H], in_=acc[side][:]), reads=[('hacc', side)], writes=[('dsth', side)])


def pq_tokmap(nt):
    c, r, half = nt // 8, (nt % 8) // 2, nt % 2
    return r * 2048 + c * 256 + half * 128


def build_fused():
    kb = KB()
    P, nc = kb.P, kb.nc
    kb.split_mods = True
    xb_d = kb.din("xb", [2048, D]); ctx_d = kb.din("ctx", [256, D]); sel_d = kb.din("sel", [128, 8])
    out_d = kb.dout("out", [2048, D])
    sel = kb.sb("sel", [128, 8], F32)
    P.dma('sp', lambda e: e.dma_start(out=sel[:], in_=sel_d), writes=['sel'])

    def fmt(name, n):
        return FM(kb.dint(name, [128, 8, n]))
    mid = fmt("mid", 2048)
    xa = fmt("xa", 2048); hc1 = fmt("hc1", 256)
    build_l0(kb, dict(xb=xb_d, ctx=ctx_d, out=xa, hctx=hc1, mid=mid, kv_gather=True))
    xh1 = fmt("xh1", 2560)
    halo_exchange(kb, xa, xh1, 256, sel, "1")
    xb2 = fmt("xb2", 2048)
    build_l1(kb, dict(xh=xh1, ctx=hc1, out=xb2, mid=mid))
    xh2 = fmt("xh2", 2304)
    halo_exchange(kb, xb2, xh2, 128, sel, "2")
    xc = fmt("xc", 2048)
    build_l2(kb, dict(xh=xh2, out=xc, mid=mid))
    pqo = kb.dint("pqo", [2048, 2048], BF16); pqg = kb.dint("pqg", [8192, 2048], BF16)
    io3 = dict(x=xc, pq=pqo)
    build_l3a(kb, io3)
    with kb.phase():
        for c in range(8):
            P.coll(lambda e, c=c: e.collective_compute("AllGather", ALU.bypass, replica_groups=RG, ins=[pqo[c * 256:(c + 1) * 256, :].opt()], outs=[pqg[c * 1024:(c + 1) * 1024, :].opt()]),
                   writes=[('pqg', c)])
    build_l3b(kb, dict(x=xc, pq=pqg, out=out_d, mid=mid, tokmap=pq_tokmap, mL=io3['mL_out']))
    P.close()
    return nc


def prep_fused(inp, b, q):
    d0 = prep_l0(inp, b, q)
    out = {k: d0[k] for k in ('ident', 'cv', 'xb', 'ctx')}
    sel = np.zeros((128, 8), np.float32)
    if q > 0:
        sel[:, q - 1] = 1.0
    if q < 3:
        sel[:, 4 + q + 1] = 1.0
    out['sel'] = sel
    out['xb'] = np.ascontiguousarray(out['xb'][:2048])
    for pfx, dd in (('l0_', d0), ('l1_', prep_l1(inp, None, None, b, q)), ('l2_', prep_l2(inp, None, b, q)),
                    ('l3a_', prep_l3a(inp, None, b, q)), ('l3b_', prep_l3b(inp, None, None, b, q))):
        for k, v in dd.items():
            if k in ('ident', 'cv', 'xb', 'ctx'):
                continue
            if k in ('adaw', 'adab'):
                v = np.ascontiguousarray(v[:, q * 1536:(q + 1) * 1536])
            if pfx == 'l0_' and k in ('cos', 'sin'):
                v = np.ascontiguousarray(v[:4])
            out[pfx + k] = v
    return out


def kernel(**inputs):
    inp = {k: np.asarray(v) for k, v in inputs.items()}
    r = _run(build_fused(), [prep_fused(inp, b, q) for b, q in CORES])
    out = np.stack([np.concatenate([r[b * 4 + q]["out"] for q in range(4)], axis=0) for b in range(2)])
    return out.astype(np.float32)
```

```python
import contextlib
import numpy as np
from concourse.bass_utils import run_bass_kernel_spmd
import concourse.bass as bass
import concourse.mybir as mybir

F32 = mybir.dt.float32
BF16 = mybir.dt.bfloat16
AF = mybir.ActivationFunctionType
ALU = mybir.AluOpType
AX = mybir.AxisListType

ENGS = ('pe', 'act', 'dve', 'pool', 'sp')
RG = [[0, 1, 2, 3], [4, 5, 6, 7]]
NDMASLOT = 8


class Op:
    __slots__ = ('eng', 'fn', 'deps', 'sig', 'sigval', 'dma', 'dslot', 'dval', 'prev_slot_op', 'seq', 'dinc')

    def __init__(self, eng, fn, dma):
        self.eng = eng
        self.fn = fn
        self.deps = []
        self.sig = False
        self.sigval = 0
        self.dma = dma
        self.dslot = None
        self.dval = 0
        self.prev_slot_op = None
        self.dinc = 16


class Prog:
    def __init__(self, nc, same_engine_sync=True):
        self.nc = nc
        self.same = same_engine_sync
        self.E = {'pe': nc.tensor, 'act': nc.scalar, 'dve': nc.vector, 'pool': nc.gpsimd, 'sp': nc.sync}
        self.sem = {}
        self._ctx = []
        for e in ('pe', 'act', 'dve', 'pool'):
            self.sem[e] = self._enter(nc.semaphore('prog_' + e))
        self.dsem = {}
        for q in ('sp', 'pool', 'act'):
            self.dsem[q] = [self._enter(nc.semaphore('dma_%s_%d' % (q, i))) for i in range(NDMASLOT)]
        self.ccsem = self._enter(nc.semaphore('ccsem'))
        self.cccnt = 0
        self.ccscratch = self._enter(nc.sbuf_tensor('ccscratch', [128, 8], F32))
        self.cnt = {e: 0 for e in ENGS}
        self.dcnt = {q: 0 for q in ('sp', 'pool', 'act')}
        self.last_dma = {q: [None] * NDMASLOT for q in ('sp', 'pool', 'act')}
        self.nops = 0
        self._reset_phase()

    def _enter(self, cm):
        v = cm.__enter__()
        self._ctx.append(cm)
        return v

    def alloc(self, cm):
        return self._enter(cm)

    def close(self):
        for cm in reversed(self._ctx):
            cm.__exit__(None, None, None)
        self._ctx = []

    def _reset_phase(self):
        self.ops = {e: [] for e in ENGS}
        self.order = []
        self.last_writer = {}
        self.readers = {}

    def _record(self, eng, fn, reads, writes, dma):
        op = Op(eng, fn, dma)
        deps = set()
        for k in reads:
            w = self.last_writer.get(k)
            if w is not None:
                deps.add(w)
        for k in writes:
            w = self.last_writer.get(k)
            if w is not None:
                deps.add(w)
            for r in self.readers.get(k, ()):
                deps.add(r)
        deps.discard(op)
        for k in reads:
            self.readers.setdefault(k, []).append(op)
        for k in writes:
            self.last_writer[k] = op
            self.readers[k] = []
        best = {}
        out = []
        for d in deps:
            if d.dma:
                out.append(d)
                continue
            if d.eng == eng and not dma:
                if eng == 'pe' or not self.same:
                    continue
            b = best.get(d.eng)
            if b is None or d.seq > b.seq:
                best[d.eng] = d
        out.extend(best.values())
        op.deps = out
        for d in out:
            if not d.dma:
                d.sig = True
        op.seq = len(self.order)
        self.ops[eng].append(op)
        self.order.append(op)
        self.nops += 1
        return op

    def op(self, eng, fn, reads=(), writes=()):
        return self._record(eng, fn, reads, writes, False)

    def coll(self, fn, reads=(), writes=()):
        fns = fn if isinstance(fn, (list, tuple)) else [fn]

        def wrapped(e):
            for f in fns:
                ins = f(e)
                self.cccnt += 1
                ins.then_inc(self.ccsem)
            e.wait_ge(self.ccsem, self.cccnt)
            return e.memset(self.ccscratch[:], 0.0)
        return self._record('pool', wrapped, reads, writes, False)

    def dma(self, q, fn, reads=(), writes=()):
        op = self._record(q, fn, reads, writes, True)
        op.dinc = 16
        j = self.dcnt[q]
        self.dcnt[q] += 1
        slot = j % NDMASLOT
        op.dslot = self.dsem[q][slot]
        op.dval = 16 * (j // NDMASLOT + 1)
        op.prev_slot_op = self.last_dma[q][slot]
        self.last_dma[q][slot] = op
        return op

    def flush(self, final_wait=()):
        for e in ENGS:
            c = self.cnt[e]
            for op in self.ops[e]:
                if op.sig and not op.dma:
                    c += 1
                    op.sigval = c
            self.cnt[e] = c
        ops = self.ops
        sem = self.sem
        lastd = {q: list(v) for q, v in self.last_dma.items()}
        anyd = any(d is not None for v in lastd.values() for d in v)

        def run(engname):
            def body(eng):
                known = {e: 0 for e in ENGS}
                kd = {}
                for op in ops[engname]:
                    if op.dma and op.prev_slot_op is not None:
                        p = op.prev_slot_op
                        key = id(p.dslot)
                        if kd.get(key, 0) < p.dval:
                            eng.wait_ge(p.dslot, p.dval)
                            kd[key] = p.dval
                    for d in op.deps:
                        if d.dma:
                            key = id(d.dslot)
                            if kd.get(key, 0) < d.dval:
                                eng.wait_ge(d.dslot, d.dval)
                                kd[key] = d.dval
                        else:
                            if known[d.eng] < d.sigval:
                                eng.wait_ge(sem[d.eng], d.sigval)
                                known[d.eng] = d.sigval
                    ins = op.fn(eng)
                    if op.dma:
                        if op.dinc == 1:
                            ins.then_inc(op.dslot)
                        else:
                            ins.then_inc(op.dslot, 16)
                    elif op.sig:
                        ins.then_inc(sem[op.eng], 1)
                if engname == 'sp':
                    for q in lastd:
                        for d in lastd[q]:
                            if d is not None:
                                eng.wait_ge(d.dslot, d.dval)
            return body

        with self.nc.Block(no_gpsimd_drain=True) as block:
            if ops['sp'] or anyd:
                block.sync(run('sp'))
            if ops['pe']:
                block.tensor(run('pe'))
            if ops['act']:
                block.scalar(run('act'))
            if ops['dve']:
                block.vector(run('dve'))
            if ops['pool']:
                block.gpsimd(run('pool'))
        for q in self.last_dma:
            self.last_dma[q] = [None] * NDMASLOT
        self._reset_phase()
D = 1024
KC = 8
DFF = 4096
EPS = 1e-6


class FM:
    def __init__(self, ap, t0=0):
        self.ap = ap
        self.t0 = t0


def tsl(x, a, b):
    if isinstance(x, FM):
        return FM(x.ap, x.t0 + a)
    return x[a:b, :]


class KB:
    def __init__(self, nt=2048):
        self.nc = nc = bass.Bass("TRN2", target_bir_lowering=False)
        self.P = P = Prog(nc)
        self.NT = nt
        self.stack = []
        self.uid = 0
        self.pfx = ''
        self.shared = {}
        self.split_mods = False
        self.ps = P.alloc(nc.psum_tensor("ps", [128, 4096], F32))
        self.ident = self.sb("ident_sb", [128, 128], F32)
        self.ones_bf = self.sb("ones_bf", [128, 128], BF16)
        self.eps_t = self.sb("eps_t", [128, 1], F32)
        self.rot = {}
        ident_d = self.nc.dram_tensor("ident", [128, 128], F32, kind="ExternalInput").ap()
        P.dma('sp', lambda e: e.dma_start(out=self.ident[:], in_=ident_d), writes=['ident'])
        P.op('pool', lambda e: e.memset(self.ones_bf[:], 1.0), writes=['ones_bf'])
        P.op('pool', lambda e: e.memset(self.eps_t[:], EPS), writes=['eps_t'])

    def sb(self, name, shape, dt):
        self.uid += 1
        name = "s%d_%s" % (self.uid, name)
        if self.stack:
            return self.stack[-1].enter_context(self.nc.sbuf_tensor(name, shape, dt))
        return self.P.alloc(self.nc.sbuf_tensor(name, shape, dt))

    @contextlib.contextmanager
    def phase(self):
        st = contextlib.ExitStack()
        self.stack.append(st)
        try:
            yield
            self.P.flush()
        finally:
            self.stack.pop()
            st.close()

    def din(self, name, shape, dt=F32):
        if name in ('cv',):
            if name not in self.shared:
                self.shared[name] = self.nc.dram_tensor(name, list(shape), dt, kind="ExternalInput").ap()
            return self.shared[name]
        return self.nc.dram_tensor(self.pfx + name, list(shape), dt, kind="ExternalInput").ap()

    def dint(self, name, shape, dt=F32):
        return self.nc.dram_tensor(name, list(shape), dt).ap()

    def dout(self, name, shape, dt=F32):
        return self.nc.dram_tensor(name, list(shape), dt, kind="ExternalOutput").ap()

    def ncol_ada(self):
        return 1536 if self.split_mods else 6144

    def bank(self, i, n=512, p0=0, p1=128):
        return self.ps[p0:p1, i * 512:i * 512 + n]

    def nextrot(self, name, n):
        v = self.rot.get(name, 0)
        self.rot[name] = v + 1
        return v % n

    def mods(self, cv_d, adaw_d, adab_d, modsT):
        P, nc = self.P, self.nc
        cv = self.sb("cv", [128, 8, 2], F32)
        sT = self.sb("sT", [128, 8, 2], F32)
        mrow = self.sb("mrow", [2, 6144], F32)
        adab = self.sb("adab", [2, 1536 if self.split_mods else 6144], F32)
        wst = [self.sb("wst%d" % i, [128, 8, 512], F32) for i in range(2)]
        P.dma('sp', lambda e: e.dma_start(out=cv[:], in_=cv_d), writes=['cv'])
        P.dma('sp', lambda e: e.dma_start(out=adab[:], in_=adab_d), writes=['adab'])
        P.op('act', lambda e: e.activation(out=sT[:], in_=cv[:], func=AF.Silu), reads=['cv'], writes=['sT'])
        ncg = 3 if self.split_mods else 12
        mpart = self.sb("mpart", [2, 1536], F32) if self.split_mods else None
        for cg in range(ncg):
            b = cg % 2
            src = adaw_d[:, cg * 512:(cg + 1) * 512].rearrange("(kc p) c -> p kc c", p=128)
            for h in range(2):
                P.dma('sp', lambda e, b=b, h=h, src=src: e.dma_start(out=wst[b][:, h * 4:(h + 1) * 4, :], in_=src[:, h * 4:(h + 1) * 4, :]),
                      writes=[('wst', b, h)])
            pb = 6 + (cg % 2)
            for kc in range(KC):
                P.op('pe', lambda e, b=b, kc=kc, pb=pb: e.matmul(self.bank(pb, 512, 0, 2), lhsT=sT[:, kc, :], rhs=wst[b][:, kc, :],
                                                                 start=(kc == 0), stop=(kc == KC - 1)),
                     reads=['sT', ('wst', b, kc // 4)], writes=[('ps', pb)])
            dstrow = mpart if self.split_mods else mrow
            P.op('dve', lambda e, cg=cg, pb=pb, dstrow=dstrow: e.tensor_tensor(out=dstrow[:, cg * 512:(cg + 1) * 512], in0=self.bank(pb, 512, 0, 2),
                                                                              in1=adab[:, cg * 512:(cg + 1) * 512], op=ALU.add),
                 reads=['adab'], writes=[('ps', pb), ('mrow', cg)])
        if self.split_mods:
            self.uid += 1
            mb = self.dint("modb%d" % self.uid, [2, 1536]); mg = self.dint("modg%d" % self.uid, [8, 1536])
            P.dma('sp', lambda e: e.dma_start(out=mb, in_=mpart[:]), reads=[('mrow', g_) for g_ in range(3)], writes=['modb'])
            P.coll(lambda e: e.collective_compute("AllGather", ALU.bypass, replica_groups=RG, ins=[mb.opt()], outs=[mg.opt()]), reads=['modb'], writes=['modg'])
            P.dma('sp', lambda e: e.dma_start(out=mrow[:].rearrange("t (r j) -> t r j", r=4), in_=mg.rearrange("(r t) j -> t r j", t=2)), reads=['modg'],
                  writes=[('mrow', g_) for g_ in range(12)])
        for ch in range(48):
            P.op('pe', lambda e, ch=ch: e.transpose(self.ps[:, 6 * 512 + ch * 2:6 * 512 + ch * 2 + 2], mrow[0:2, ch * 128:(ch + 1) * 128], self.ident[0:2, 0:2]),
                 reads=[('mrow', ch // 4), 'ident'], writes=[('ps', 6)])
        P.op('dve', lambda e: e.tensor_copy(out=modsT[:].rearrange("p a b -> p (a b)"), in_=self.ps[:, 6 * 512:6 * 512 + 96]),
             writes=[('ps', 6), 'modsT'])
        return modsT

    def mod_vectors(self, modsT, g1_d, g2_d, col, mv, tag):
        P = self.P
        gg = self.sb("gg" + tag, [128, 2, 8], F32)
        P.dma('sp', lambda e: e.dma_start(out=gg[:, 0, :], in_=g1_d), writes=['gg' + tag])
        P.dma('sp', lambda e: e.dma_start(out=gg[:, 1, :], in_=g2_d), writes=['gg' + tag])
        P.op('dve', lambda e: e.tensor_copy(out=mv[:].rearrange("p m k -> p (m k)"), in_=modsT[:, :, col]), reads=['modsT'], writes=['mv' + tag])
        for j, m in ((0, 1), (1, 4)):
            P.op('dve', lambda e, j=j, m=m: e.scalar_tensor_tensor(out=mv[:, m, :], in0=mv[:, m, :], scalar=1.0, in1=gg[:, j, :],
                                                                   op0=ALU.add, op1=ALU.mult),
                 reads=['gg' + tag], writes=['mv' + tag])
        key = 'mv' + tag
        return dict(A1=mv[:, 1, :], B1=mv[:, 0, :], G1=mv[:, 2, :], A2=mv[:, 4, :], B2=mv[:, 3, :], G2=mv[:, 5, :], key=key)

    def load_xT(self, x_d, ntok, xT, xkey, stage, t0=0):
        P = self.P
        if isinstance(x_d, FM):
            keys = [(xkey, g) for g in range(t0 // 512, (t0 + ntok - 1) // 512 + 1)]
            for half in range(2):
                P.dma('sp', lambda e, half=half: e.dma_start(out=xT[:, half * 4:(half + 1) * 4, t0:t0 + ntok], in_=x_d.ap[:, half * 4:(half + 1) * 4, x_d.t0:x_d.t0 + ntok]), writes=keys)
            return
        for tt in range(ntok // 128):
            sb_ = self.nextrot('stage', 2)
            P.dma('sp', lambda e, tt=tt, sb_=sb_: e.dma_start(out=stage[sb_][:], in_=x_d[tt * 128:(tt + 1) * 128, :]), writes=[('stage', sb_)])
            for half in range(2):
                pb = 4 + self.nextrot('ldbank', 2)
                for j in range(4):
                    kc = half * 4 + j
                    P.op('pe', lambda e, sb_=sb_, kc=kc, pb=pb, j=j: e.transpose(self.bank(pb)[:, j * 128:(j + 1) * 128], stage[sb_][:, kc * 128:(kc + 1) * 128], self.ident[:]),
                         reads=[('stage', sb_), 'ident'], writes=[('ps', pb)])
                eng = 'act' if half == 0 else 'dve'
                dst = xT[:, half * 4:(half + 1) * 4, t0 + tt * 128:t0 + (tt + 1) * 128]
                src = self.bank(pb).rearrange("p (a b) -> p a b", a=4)
                if eng == 'act':
                    P.op('act', lambda e, dst=dst, src=src: e.activation(out=dst, in_=src, func=AF.Copy), writes=[('ps', pb), (xkey, (t0 + tt * 128) // 512)])
                else:
                    P.op('dve', lambda e, dst=dst, src=src: e.tensor_copy(out=dst, in_=src), writes=[('ps', pb), (xkey, (t0 + tt * 128) // 512)])

    def store_x(self, xT, xkey, ntok, out_d, ostage, scale_ap=None):
        P = self.P
        if isinstance(out_d, FM):
            keys = [(xkey, g) for g in range(0, (ntok - 1) // 512 + 1)]
            for half in range(2):
                P.dma('sp', lambda e, half=half: e.dma_start(out=out_d.ap[:, half * 4:(half + 1) * 4, out_d.t0:out_d.t0 + ntok], in_=xT[:, half * 4:(half + 1) * 4, 0:ntok]), reads=keys)
            return
        for tt in range(ntok // 128):
            ob = self.nextrot('ostage', 2)
            for half in range(2):
                pb = 4 + self.nextrot('ldbank', 2)
                for j in range(4):
                    kc = half * 4 + j
                    P.op('pe', lambda e, kc=kc, pb=pb, j=j, tt=tt: e.transpose(self.bank(pb)[:, j * 128:(j + 1) * 128], xT[:, kc, tt * 128:(tt + 1) * 128], self.ident[:]),
                         reads=[(xkey, tt // 4), 'ident'], writes=[('ps', pb)])
                dst = ostage[ob][:, half * 512:(half + 1) * 512]
                if half == 0:
                    P.op('act', lambda e, dst=dst, pb=pb: e.activation(out=dst, in_=self.bank(pb), func=AF.Copy), writes=[('ps', pb), ('ostage', ob, half)])
                else:
                    P.op('dve', lambda e, dst=dst, pb=pb: e.tensor_copy(out=dst, in_=self.bank(pb)), writes=[('ps', pb), ('ostage', ob, half)])
            P.dma('sp', lambda e, ob=ob, tt=tt: e.dma_start(out=out_d[tt * 128:(tt + 1) * 128, :], in_=ostage[ob][:]),
                  reads=[('ostage', ob, 0), ('ostage', ob, 1)])

    def norm_mod(self, xT, xkey, ntok, A, B, mkey, hT, hkey, scr, t0=0, ht0=0):
        P = self.P
        tgs = min(512, ntok)
        for g in range(ntok // tgs):
            c0 = t0 + g * tgs
            h0 = ht0 + g * tgs
            pb = 6 + self.nextrot('nbank', 2)
            for kc in range(KC):
                sq = self.nextrot('sq', 2)
                P.op('pool', lambda e, kc=kc, sq=sq, c0=c0: e.tensor_tensor(out=scr['sq'][sq][:, :tgs], in0=xT[:, kc, c0:c0 + tgs], in1=xT[:, kc, c0:c0 + tgs], op=ALU.mult),
                     reads=[(xkey, c0 // 512)], writes=[('sq', sq)])
                P.op('pe', lambda e, kc=kc, sq=sq, pb=pb: e.matmul(self.bank(pb, tgs), lhsT=self.ones_bf[:], rhs=scr['sq'][sq][:, :tgs], start=(kc == 0), stop=(kc == KC - 1)),
                     reads=[('sq', sq), 'ones_bf'], writes=[('ps', pb)])
            rs = scr['rs']
            P.op('act', lambda e, pb=pb: e.activation(out=rs[:, :tgs], in_=self.bank(pb, tgs), func=AF.Ln, scale=1.0 / D, bias=self.eps_t[:]),
                 reads=['eps_t'], writes=[('ps', pb), 'rs'])
            P.op('act', lambda e: e.activation(out=rs[:, :tgs], in_=rs[:, :tgs], func=AF.Exp, scale=-0.5), writes=['rs'])
            for kc in range(KC):
                tb = self.nextrot('tmp', 2)
                P.op('dve', lambda e, kc=kc, tb=tb, c0=c0: e.scalar_tensor_tensor(out=scr['tmp'][tb][:, :tgs], in0=xT[:, kc, c0:c0 + tgs], scalar=A[:, kc:kc + 1],
                                                                                in1=rs[:, :tgs], op0=ALU.mult, op1=ALU.mult),
                     reads=[(xkey, c0 // 512), 'rs', mkey], writes=[('tmp', tb)])
                P.op('act', lambda e, kc=kc, tb=tb, h0=h0: e.activation(out=hT[:, kc, h0:h0 + tgs], in_=scr['tmp'][tb][:, :tgs], func=AF.Identity, bias=B[:, kc:kc + 1]),
                     reads=[('tmp', tb), mkey], writes=[(hkey, h0 // 512)])

    def load_w(self, dst, wkey, w_d, r0, nkc, c0, ncol):
        P = self.P
        for k in range(nkc):
            P.dma('pool', lambda e, k=k: e.dma_start(out=dst[:, k, 0:ncol], in_=w_d[r0 + k * 128:r0 + (k + 1) * 128, c0:c0 + ncol]),
                  writes=[(wkey, k)])

    def mlp(self, xT, xkey, ntok, hT, hkey, G, mkey, w1_d, w2_d, wbuf, aT, rbuf):
        P = self.P
        FB = 512
        nfb = DFF // FB
        tgs = min(512, ntok)
        ntg = ntok // tgs

        def ff1(j):
            wb = j % 2
            self.load_w(wbuf['w1'][wb], ('w1', wb), w1_d, 0, KC, j * FB, FB)
            self.load_w(wbuf['w2'][wb], ('w2', wb), w2_d, j * FB, FB // 128, 0, D)
            for fc in range(FB // 128):
                for g in range(ntg):
                    pb = self.nextrot('ff1bank', 3)
                    for kc in range(KC):
                        P.op('pe', lambda e, wb=wb, fc=fc, g=g, kc=kc, pb=pb: e.matmul(self.bank(pb, tgs), lhsT=wbuf['w1'][wb][:, kc, fc * 128:(fc + 1) * 128],
                                                                                    rhs=hT[:, kc, g * tgs:(g + 1) * tgs], start=(kc == 0), stop=(kc == KC - 1)),
                             reads=[(('w1', wb), kc), (hkey, g)], writes=[('ps', pb)])
                    rb = self.nextrot('rbuf', 2)
                    P.op('act', lambda e, pb=pb, rb=rb: e.activation(out=rbuf[rb][:, :tgs], in_=self.bank(pb, tgs), func=AF.Relu), writes=[('ps', pb), ('rbuf', rb)])
                    P.op('pool', lambda e, rb=rb, wb=wb, fc=fc, g=g: e.tensor_tensor(out=aT[wb][:, fc, g * tgs:(g + 1) * tgs], in0=rbuf[rb][:, :tgs], in1=rbuf[rb][:, :tgs], op=ALU.mult),
                         reads=[('rbuf', rb)], writes=[('aT', wb, fc, g)])

        def ff2(j):
            wb = j % 2
            for dc in range(KC):
                for g in range(ntg):
                    pb = 3 + self.nextrot('ff2bank', 3)
                    nf = FB // 128
                    for fc in range(nf):
                        P.op('pe', lambda e, wb=wb, fc=fc, g=g, dc=dc, pb=pb: e.matmul(self.bank(pb, tgs), lhsT=wbuf['w2'][wb][:, fc, dc * 128:(dc + 1) * 128],
                                                                                    rhs=aT[wb][:, fc, g * tgs:(g + 1) * tgs], start=(fc == 0), stop=(fc == nf - 1)),
                             reads=[(('w2', wb), fc), ('aT', wb, fc, g)], writes=[('ps', pb)])
                    P.op('dve', lambda e, dc=dc, g=g, pb=pb: e.scalar_tensor_tensor(out=xT[:, dc, g * tgs:(g + 1) * tgs], in0=self.bank(pb, tgs), scalar=G[:, dc:dc + 1],
                                                                                  in1=xT[:, dc, g * tgs:(g + 1) * tgs], op0=ALU.mult, op1=ALU.add),
                         reads=[mkey], writes=[('ps', pb), (xkey, g)])

        ff1(0)
        for j in range(nfb):
            if j + 1 < nfb:
                ff1(j + 1)
            ff2(j)


def load_cast(kb, dst, key, src_d):
    kb.P.dma('pool', lambda e: e.dma_start(out=dst, in_=src_d), writes=[key])


def proj_fm(kb, hT, hkey, t0, n, W, wkey, col0, pb):
    for kc in range(KC):
        kb.P.op('pe', lambda e, kc=kc: e.matmul(kb.bank(pb, n), lhsT=W[:, kc, col0:col0 + 128], rhs=hT[:, kc, t0:t0 + n],
                                               start=(kc == 0), stop=(kc == KC - 1)),
                reads=[(wkey, kc), (hkey, t0 // 512)], writes=[('ps', pb)])


def qk_norm_rope(kb, pb, n, gain, rope, qscale, out_ap, outkeys, C):
    P = kb.P
    r = kb.nextrot('qkr', 2)
    kg, k2, rs, t1 = C['kg'][r], C['k2'][r], C['rs2'][r], C['t1'][r]
    P.op('act', lambda e: e.activation(out=kg[:, :n], in_=kb.bank(pb, n), func=AF.Copy, scale=gain[:, 0:1]), reads=['gains'], writes=[('ps', pb), ('kg', r)])
    P.op('act', lambda e: e.activation(out=k2[:, :n], in_=kb.bank(pb, n), func=AF.Square), writes=[('ps', pb), ('k2', r)])
    P.op('pe', lambda e: e.matmul(kb.bank(2, n), lhsT=C['bones'][:], rhs=k2[:, :n], start=True, stop=True), reads=[('k2', r), 'bones'], writes=[('ps', 2)])
    if rope is not None:
        P.op('pe', lambda e: e.matmul(kb.bank(3, n), lhsT=C['rmat'][:], rhs=kg[:, :n], start=True, stop=True), reads=[('kg', r), 'rmat'], writes=[('ps', 3)])
    P.op('act', lambda e: e.activation(out=rs[:, :n], in_=kb.bank(2, n), func=AF.Ln, scale=1.0 / 64, bias=kb.eps_t[:]), reads=['eps_t'], writes=[('ps', 2), ('rs2', r)])
    if qscale:
        P.op('act', lambda e: e.activation(out=rs[:, :n], in_=rs[:, :n], func=AF.Exp, scale=-0.5, bias=C['lnq'][:]), reads=['lnq'], writes=[('rs2', r)])
    else:
        P.op('act', lambda e: e.activation(out=rs[:, :n], in_=rs[:, :n], func=AF.Exp, scale=-0.5), writes=[('rs2', r)])
    if rope is not None:
        cos_ap, sin_ap, rkey = rope
        P.op('dve', lambda e: e.tensor_tensor(out=t1[:, :n], in0=kg[:, :n], in1=cos_ap, op=ALU.mult), reads=[('kg', r), rkey], writes=[('t1', r)])
        P.op('dve', lambda e: e.tensor_tensor(out=kg[:, :n], in0=kb.bank(3, n), in1=sin_ap, op=ALU.mult), reads=[rkey], writes=[('ps', 3), ('kg', r)])
        P.op('dve', lambda e: e.tensor_tensor(out=t1[:, :n], in0=t1[:, :n], in1=kg[:, :n], op=ALU.add), reads=[('kg', r)], writes=[('t1', r)])
        P.op('dve', lambda e: e.tensor_tensor(out=out_ap, in0=t1[:, :n], in1=rs[:, :n], op=ALU.mult), reads=[('t1', r), ('rs2', r)], writes=outkeys)
    else:
        P.op('dve', lambda e: e.tensor_tensor(out=out_ap, in0=kg[:, :n], in1=rs[:, :n], op=ALU.mult), reads=[('kg', r), ('rs2', r)], writes=outkeys)


def attention(kb, q_ap, qkeys, NQ, tiles, dst_a, dst_b, dstkeys, C):
    P = kb.P
    oset = kb.nextrot('oset', 2)
    o0 = 4 + 2 * oset
    nt = len(tiles)
    ssets = []

    def qk(i):
        t = tiles[i]
        ss = kb.nextrot('sset', 2)
        ssets.append(ss)
        s0 = 2 * ss
        P.op('pe', lambda e: e.matmul(kb.bank(s0, NQ), lhsT=t['ka'], rhs=q_ap[0:64, :], start=True, stop=True), reads=t['keys'] + qkeys, writes=[('ps', s0)])
        P.op('pe', lambda e: e.matmul(kb.bank(s0 + 1, NQ), lhsT=t['kb'], rhs=q_ap[64:128, :], start=True, stop=True), reads=t['keys'] + qkeys, writes=[('ps', s0 + 1)])

    qk(0)
    for i in range(nt):
        if i + 1 < nt:
            qk(i + 1)
        t = tiles[i]
        s0 = 2 * ssets[i]
        pbuf = kb.nextrot('pbuf', 3)
        pb_ = C['pbuf'][pbuf]
        src = kb.ps[:, s0 * 512:(s0 + 2) * 512].rearrange("p (h n) -> p h n", h=2)[:, :, 0:NQ]
        if t.get('bias_a') is not None:
            sb_ = C['sbias'][kb.nextrot('sbias', 2)]
            P.op('dve', lambda e, sb_=sb_, t=t, s0=s0: e.tensor_tensor(out=sb_[:, 0, 0:NQ], in0=kb.bank(s0, NQ), in1=t['bias_a'], op=ALU.add), reads=t['bkeys'], writes=[('ps', s0), ('sbias', id(sb_), 0)])
            P.op('dve', lambda e, sb_=sb_, t=t, s0=s0: e.tensor_tensor(out=sb_[:, 1, 0:NQ], in0=kb.bank(s0 + 1, NQ), in1=t['bias_b'], op=ALU.add), reads=t['bkeys'], writes=[('ps', s0 + 1), ('sbias', id(sb_), 1)])
            P.op('act', lambda e, sb_=sb_, pb_=pb_: e.activation(out=pb_[:, :, 0:NQ], in_=sb_[:, :, 0:NQ], func=AF.Exp), reads=[('sbias', id(sb_), 0), ('sbias', id(sb_), 1)], writes=[('pbuf', pbuf)])
        else:
            P.op('act', lambda e, src=src, pb_=pb_: e.activation(out=pb_[:, :, 0:NQ], in_=src, func=AF.Exp), writes=[('ps', s0), ('ps', s0 + 1), ('pbuf', pbuf)])
        P.op('pe', lambda e, t=t, pb_=pb_, i=i: e.matmul(kb.bank(o0, NQ), lhsT=t['va'], rhs=pb_[:, 0, 0:NQ], start=(i == 0), stop=(i == nt - 1)), reads=t['keys'] + [('pbuf', pbuf)], writes=[('ps', o0)])
        P.op('pe', lambda e, t=t, pb_=pb_, i=i: e.matmul(kb.bank(o0 + 1, NQ), lhsT=t['vb'], rhs=pb_[:, 1, 0:NQ], start=(i == 0), stop=(i == nt - 1)), reads=t['keys'] + [('pbuf', pbuf)], writes=[('ps', o0 + 1)])
    rcr = kb.nextrot('rc', 2)
    rc = C['rc'][rcr]
    P.op('dve', lambda e: e.reciprocal(out=rc[64:128, 0:NQ], in_=kb.bank(o0, NQ)[64:128, :]), writes=[('ps', o0), ('rc', rcr, 0)])
    P.op('dve', lambda e: e.tensor_tensor(out=dst_a, in0=kb.bank(o0, NQ)[0:64, :], in1=rc[64:128, 0:NQ], op=ALU.mult), reads=[('rc', rcr, 0)], writes=[('ps', o0)] + dstkeys)
    P.op('dve', lambda e: e.reciprocal(out=rc[0:64, 0:NQ], in_=kb.bank(o0 + 1, NQ)[0:64, :]), writes=[('ps', o0 + 1), ('rc', rcr, 1)])
    P.op('dve', lambda e: e.tensor_tensor(out=dst_b, in0=kb.bank(o0 + 1, NQ)[64:128, :], in1=rc[0:64, 0:NQ], op=ALU.mult), reads=[('rc', rcr, 1)], writes=[('ps', o0 + 1)] + dstkeys)


def attn_scratch(kb, with_bias=False):
    C = dict(pbuf=[kb.sb("pbuf%d" % i, [128, 2, 512], BF16) for i in range(3)],
             rc=[kb.sb("rc%d" % i, [128, 512], F32) for i in range(2)])
    if with_bias:
        C['sbias'] = [kb.sb("sbias%d" % i, [128, 2, 512], F32) for i in range(2)]
    return C


def qk_scratch(kb, C):
    C['kg'] = [kb.sb("kg%d" % i, [128, 512], BF16) for i in range(2)]
    C['k2'] = [kb.sb("k2%d" % i, [128, 512], BF16) for i in range(2)]
    C['rs2'] = [kb.sb("rs2%d" % i, [128, 512], F32) for i in range(2)]
    C['t1'] = [kb.sb("t1%d" % i, [128, 512], F32) for i in range(2)]


def norm_scratch(kb):
    return dict(sq=[kb.sb("sq%d" % i, [128, 512], BF16) for i in range(2)], rs=kb.sb("rs", [128, 512], F32),
                tmp=[kb.sb("tmp%d" % i, [128, 512], F32) for i in range(2)])


def mlp_bufs(kb, ntok):
    wbuf = dict(w1=[kb.sb("w1b%d" % i, [128, 8, 512], BF16) for i in range(3)], w2=[kb.sb("w2b%d" % i, [128, 4, 1024], BF16) for i in range(3)])
    aT = [kb.sb("aT%d" % i, [128, 4, ntok], BF16) for i in range(2)]
    rbuf = [kb.sb("rbuf%d" % i, [128, 512], F32) for i in range(2)]
    return wbuf, aT, rbuf


def resid_proj(kb, srcT, skey, t0, n, W, wkey, xT, xkey, xt0, G, mkey, bG=None):
    P = kb.P
    for dc in range(KC):
        pb = kb.nextrot('projbank', 2)
        proj_fm(kb, srcT, skey, t0, n, W, wkey, dc * 128, pb)
        P.op('dve', lambda e, dc=dc, pb=pb: e.scalar_tensor_tensor(out=xT[:, dc, xt0:xt0 + n], in0=kb.bank(pb, n), scalar=G[:, dc:dc + 1],
                                                                 in1=xT[:, dc, xt0:xt0 + n], op0=ALU.mult, op1=ALU.add),
             reads=[mkey], writes=[('ps', pb), (xkey, xt0 // 512)])
        if bG is not None:
            P.op('act', lambda e, dc=dc: e.activation(out=xT[:, dc, xt0:xt0 + n], in_=xT[:, dc, xt0:xt0 + n], func=AF.Identity, bias=bG[:, dc:dc + 1]),
                 reads=['bG'], writes=[(xkey, xt0 // 512)])


def build_l0(kb=None, io=None):
    own = kb is None
    if own:
        kb = KB()
    io = io or {}
    kb.pfx = '' if own else 'l0_'
    P, nc = kb.P, kb.nc
    NTOK = 2048
    NG0 = 4 if io.get("kv_gather") else 16
    xb_d = io.get("xb") or kb.din("xb", [NG0 * 512, D]); ctx_d = io.get("ctx") or kb.din("ctx", [256, D])
    cv_d = kb.din("cv", [128, 8, 2]); adaw_d = kb.din("adaw", [D, kb.ncol_ada()]); adab_d = kb.din("adab", [2, kb.ncol_ada()])
    g1_d = kb.din("g1", [128, 8]); g2_d = kb.din("g2", [128, 8])
    wqkv_d = kb.din("wqkv", [D, 1536]); gains_d = kb.din("gains", [128, 2])
    cos_d = kb.din("cos", [NG0, 128, 512]); sin_d = kb.din("sin", [NG0, 128, 512])
    wo_d = kb.din("wo", [D, D]); w1_d = kb.din("w1", [D, DFF]); w2_d = kb.din("w2", [DFF, D])
    rmat_d = kb.din("rmat", [128, 128]); bones_d = kb.din("bones", [128, 128])
    out_d = io.get("out") or kb.dout("out", [NTOK, D]); hctx_d = io.get("hctx") or kb.dout("hctx", [256, D])
    mid_d = io.get("mid") or out_d

    modsT = kb.sb("modsT", [128, 48, 2], F32)
    mvL = kb.sb("mvL", [128, 6, 8], F32); mvC = kb.sb("mvC", [128, 6, 8], F32)
    C = dict(rmat=kb.sb("rmat", [128, 128], BF16), bones=kb.sb("bones", [128, 128], BF16), lnq=kb.sb("lnq", [128, 1], F32))
    gains = kb.sb("gains", [128, 2], F32)
    with kb.phase():
        load_cast(kb, C['rmat'][:], 'rmat', rmat_d)
        load_cast(kb, C['bones'][:], 'bones', bones_d)
        P.op('pool', lambda e: e.memset(C['lnq'][:], float(np.log(0.125))), writes=['lnq'])
        P.dma('sp', lambda e: e.dma_start(out=gains[:], in_=gains_d), writes=['gains'])
        kb.mods(cv_d, adaw_d, adab_d, modsT)
        mL = kb.mod_vectors(modsT, g1_d, g2_d, 0, mvL, "L")
        mC = kb.mod_vectors(modsT, g1_d, g2_d, 1, mvC, "C")
    qgain, kgain = gains[:, 0:1], gains[:, 1:2]

    with kb.phase():
        KT = kb.sb("KT", [128, 2, 8448], BF16)
        Ve = kb.sb("Ve", [128, 66, 384], BF16)
        P.op('pool', lambda e: e.memset(Ve[:].rearrange("p t (a b c) -> p (t a) b c", a=2, b=3, c=64)[:, :, 1, :], 1.0), writes=['Ve_ones'])

        def kv_tiles(tile_ids, pr):
            out = []
            for kt in tile_ids:
                out.append(dict(ka=KT[0:64, pr, kt * 128:(kt + 1) * 128], kb=KT[64:128, pr, kt * 128:(kt + 1) * 128],
                                va=Ve[:, kt, pr * 192:pr * 192 + 128], vb=Ve[:, kt, pr * 192 + 64:pr * 192 + 192],
                                keys=[('KT', kt // 4), ('Ve', kt), 'Ve_ones']))
            return out

        def produce_kv(hT, hkey, n, g, wqkv, rope):
            for pr in range(2):
                pb = kb.nextrot('projbank', 2)
                proj_fm(kb, hT, hkey, 0, n, wqkv, 'wqkv', 1024 + pr * 128, pb)
                qk_norm_rope(kb, pb, n, kgain, rope, False, KT[:, pr, g * 512:g * 512 + n], [('KT', g)], C)
            for tt in range(n // 128):
                pb = kb.nextrot('projbank', 2)
                for kc in range(KC):
                    P.op('pe', lambda e, kc=kc, tt=tt, pb=pb: e.matmul(kb.bank(pb, 256), lhsT=hT[:, kc, tt * 128:(tt + 1) * 128], rhs=wqkv[:, kc, 1280:1536],
                                                                      start=(kc == 0), stop=(kc == KC - 1)),
                         reads=[('wqkv', kc), (hkey, 0)], writes=[('ps', pb)])
                kt = g * 4 + tt
                dst = Ve[:, kt, :].rearrange("p (a b c) -> p a b c", a=2, b=3, c=64)[:, :, ::2, :]
                src = kb.bank(pb, 256).rearrange("p (a b c) -> p a b c", a=2, b=2, c=64)
                P.op('dve', lambda e, dst=dst, src=src: e.tensor_copy(out=dst, in_=src), writes=[('ps', pb), ('Ve', kt)])

        with kb.phase():
            cT = kb.sb("cT", [128, 8, 256], F32)
            stage = [kb.sb("stage%d" % i, [128, 1024], F32) for i in range(2)]
            scr = norm_scratch(kb)
            with kb.phase():
                hcT = kb.sb("hcT", [128, 8, 256], BF16)
                QcT = kb.sb("QcT", [128, 8, 256], BF16)
                OcT = kb.sb("OcT", [128, 8, 256], BF16)
                qk_scratch(kb, C)
                C.update(attn_scratch(kb))
                wqkv = kb.sb("wqkv", [128, 8, 1536], BF16)
                wo = kb.sb("wo", [128, 8, 1024], BF16)
                kb.load_w(wqkv, 'wqkv', wqkv_d, 0, KC, 0, 1536)
                kb.load_w(wo, 'wo', wo_d, 0, KC, 0, 1024)
                kb.load_xT(ctx_d, 256, cT, 'cT', stage)
                kb.norm_mod(cT, 'cT', 256, mC['A1'], mC['B1'], mC['key'], hcT, 'hcT', scr)
                produce_kv(hcT, 'hcT', 256, 16, wqkv, None)
                for c in range(8):
                    pb = kb.nextrot('projbank', 2)
                    proj_fm(kb, hcT, 'hcT', 0, 256, wqkv, 'wqkv', c * 128, pb)
                    qk_norm_rope(kb, pb, 256, qgain, None, True, QcT[:, c, :], [('QcT', c)], C)
                for c in range(8):
                    attention(kb, QcT[:, c, :], [('QcT', c)], 256, kv_tiles([64, 65], c // 4), OcT[0:64, c, :], OcT[64:128, c, :], [('OcT', 0)], C)
                resid_proj(kb, OcT, 'OcT', 0, 256, wo, 'wo', cT, 'cT', 0, mC['G1'], mC['key'])
            with kb.phase():
                hc2 = kb.sb("hc2", [128, 8, 256], BF16)
                kb.norm_mod(cT, 'cT', 256, mC['A2'], mC['B2'], mC['key'], hc2, 'hc2', scr)
                wbuf, aT, rbuf = mlp_bufs(kb, 256)
                kb.mlp(cT, 'cT', 256, hc2, 'hc2', mC['G2'], mC['key'], w1_d, w2_d, wbuf, aT, rbuf)
                kb.store_x(cT, 'cT', 256, hctx_d, stage)

        with kb.phase():
            QT = kb.sb("QT", [128, 8, NTOK], BF16)
            with kb.phase():
                xtmp = kb.sb("xtmp", [128, 8, 512], F32)
                hTt = kb.sb("hTt", [128, 8, 512], BF16)
                stage = [kb.sb("stage%d" % i, [128, 1024], F32) for i in range(2)]
                scr = norm_scratch(kb)
                qk_scratch(kb, C)
                wqkv = kb.sb("wqkv", [128, 8, 1536], BF16)
                cs = [kb.sb("cs%d" % i, [128, 2, 512], F32) for i in range(2)]
                kb.load_w(wqkv, 'wqkv', wqkv_d, 0, KC, 0, 1536)
                for g in range(NG0):
                    kb.load_xT(tsl(xb_d, g * 512, (g + 1) * 512), 512, xtmp, 'xtmp', stage)
                    kb.norm_mod(xtmp, 'xtmp', 512, mL['A1'], mL['B1'], mL['key'], hTt, 'hTt', scr)
                    cb = g % 2
                    P.dma('sp', lambda e, g=g, cb=cb: e.dma_start(out=cs[cb][:, 0, :], in_=cos_d[g]), writes=[('cs', cb)])
                    P.dma('sp', lambda e, g=g, cb=cb: e.dma_start(out=cs[cb][:, 1, :], in_=sin_d[g]), writes=[('cs', cb)])
                    rope = (cs[cb][:, 0, :], cs[cb][:, 1, :], ('cs', cb))
                    produce_kv(hTt, 'hTt', 512, g, wqkv, rope)
                    if g < 4:
                        for c in range(8):
                            pb = kb.nextrot('projbank', 2)
                            proj_fm(kb, hTt, 'hTt', 0, 512, wqkv, 'wqkv', c * 128, pb)
                            qk_norm_rope(kb, pb, 512, qgain, rope, True, QT[:, c, g * 512:(g + 1) * 512], [('QT', c, g)], C)
            if io.get("kv_gather"):
                ktb = kb.dint("ktb", [256, 2048], BF16); ktg = kb.dint("ktg", [1024, 2048], BF16)
                veb = [kb.dint("veb%d" % h, [1024, 384], BF16) for h in range(2)]
                veg = [kb.dint("veg%d" % h, [4096, 384], BF16) for h in range(2)]
                with kb.phase():
                    P.dma('sp', lambda e: e.dma_start(out=ktb.rearrange("(p a) n -> p a n", a=2), in_=KT[:, :, 0:2048]), reads=[('KT', g_) for g_ in range(4)], writes=['ktb'])
                    for h in range(2):
                        P.dma('sp', lambda e, h=h: e.dma_start(out=veb[h].rearrange("(p t) c -> p t c", t=8), in_=Ve[:, h * 8:(h + 1) * 8, :]),
                              reads=[('Ve', kt_) for kt_ in range(16)] + ['Ve_ones'], writes=[('veb', h)])
                    P.coll([lambda e: e.collective_compute("AllGather", ALU.bypass, replica_groups=RG, ins=[ktb.opt()], outs=[ktg.opt()]),
                            lambda e: e.collective_compute("AllGather", ALU.bypass, replica_groups=RG, ins=[veb[0].opt()], outs=[veg[0].opt()]),
                            lambda e: e.collective_compute("AllGather", ALU.bypass, replica_groups=RG, ins=[veb[1].opt()], outs=[veg[1].opt()])],
                           reads=['ktb', ('veb', 0), ('veb', 1)], writes=['ktg', ('veg', 0), ('veg', 1)])
                    for r in range(4):
                        P.dma('sp', lambda e, r=r: e.dma_start(out=KT[:, :, r * 2048:(r + 1) * 2048], in_=ktg[r * 256:(r + 1) * 256, :].rearrange("(p a) n -> p a n", a=2)),
                              reads=['ktg'], writes=[('KT', g_) for g_ in range(r * 4, r * 4 + 4)])
                        for h in range(2):
                            P.dma('sp', lambda e, r=r, h=h: e.dma_start(out=Ve[:, r * 16 + h * 8:r * 16 + (h + 1) * 8, :], in_=veg[h][r * 1024:(r + 1) * 1024, :].rearrange("(p t) c -> p t c", t=8)),
                                  reads=[('veg', h)], writes=[('Ve', kt_) for kt_ in range(r * 16 + h * 8, r * 16 + (h + 1) * 8)])
            with kb.phase():
                OT = kb.sb("OT", [128, 8, NTOK], BF16)
                with kb.phase():
                    C.update(attn_scratch(kb))
                    for qg in range(4):
                        for c in range(8):
                            attention(kb, QT[:, c, qg * 512:(qg + 1) * 512], [('QT', c, qg)], 512, kv_tiles(list(range(66)), c // 4),
                                      OT[0:64, c, qg * 512:(qg + 1) * 512], OT[64:128, c, qg * 512:(qg + 1) * 512], [('OT', qg)], C)
                with kb.phase():
                    xtmp = kb.sb("xtmp", [128, 8, 512], F32)
                    stage = [kb.sb("stage%d" % i, [128, 1024], F32) for i in range(2)]
                    wo = kb.sb("wo", [128, 8, 1024], BF16)
                    ostage = [kb.sb("ostage%d" % i, [128, 1024], F32) for i in range(2)]
                    kb.load_w(wo, 'wo', wo_d, 0, KC, 0, 1024)
                    for g in range(4):
                        kb.load_xT(tsl(xb_d, g * 512, (g + 1) * 512), 512, xtmp, 'xtmp', stage)
                        resid_proj(kb, OT, 'OT', g * 512, 512, wo, 'wo', xtmp, 'xtmp', 0, mL['G1'], mL['key'])
                        kb.store_x(xtmp, 'xtmp', 512, tsl(mid_d, g * 512, (g + 1) * 512), ostage)
    mlp_tail(kb, mid_d, out_d, NTOK, mL, w1_d, w2_d)
    if own:
        P.close()
    return nc


def mlp_tail(kb, src_d, out_d, ntok, mL, w1_d, w2_d, final=None):
    with kb.phase():
        xT = kb.sb("xT", [128, 8, ntok], F32)
        with kb.phase():
            stage = [kb.sb("stage%d" % i, [128, 1024], F32) for i in range(2)]
            kb.load_xT(src_d, ntok, xT, 'xT', stage)
        with kb.phase():
            hT = kb.sb("hT", [128, 8, ntok], BF16)
            scr = norm_scratch(kb)
            kb.norm_mod(xT, 'xT', ntok, mL['A2'], mL['B2'], mL['key'], hT, 'hT', scr)
            wbuf, aT, rbuf = mlp_bufs(kb, ntok)
            kb.mlp(xT, 'xT', ntok, hT, 'hT', mL['G2'], mL['key'], w1_d, w2_d, wbuf, aT, rbuf)
        with kb.phase():
            ostage = [kb.sb("ostage%d" % i, [128, 1024], F32) for i in range(2)]
            if final is not None:
                yT = kb.sb("yT", [128, 8, ntok], F32)
                scr = norm_scratch(kb)
                kb.norm_mod(xT, 'xT', ntok, final[0], final[1], final[2], yT, 'yT', scr)
                kb.store_x(yT, 'yT', ntok, out_d, ostage)
            else:
                kb.store_x(xT, 'xT', ntok, out_d, ostage)


def fm(v):
    return np.ascontiguousarray(np.asarray(v, np.float32).reshape(8, 128).T)


def common_inputs(inp, layer, b):
    cv = np.stack([fm(inp['c'][b]), fm(inp['c_ctx'])], axis=-1)
    return dict(ident=np.eye(128, dtype=np.float32), cv=np.ascontiguousarray(cv), adaw=np.ascontiguousarray(inp['ada_w'][layer]),
                adab=np.ascontiguousarray(np.stack([inp['ada_b'][layer]] * 2)), g1=fm(inp['norm1_g'][layer]), g2=fm(inp['norm2_g'][layer]),
                w1=np.ascontiguousarray(inp['mlp_w1'][layer]), w2=np.ascontiguousarray(inp['mlp_w2'][layer]))


def rope_tables(order):
    t = np.asarray(order)
    row = (t // 64).astype(np.float32)
    col = (t % 64).astype(np.float32)
    inv = (10000.0 ** (-np.arange(16, dtype=np.float32) / 16)).astype(np.float32)
    ang = np.concatenate([row[:, None] * inv, col[:, None] * inv], axis=-1).astype(np.float32)
    idx = (np.arange(128) % 64) // 2
    a = ang[:, idx].T
    cos = np.cos(a).astype(np.float32).reshape(128, 16, 512).transpose(1, 0, 2)
    sin = np.sin(a).astype(np.float32).reshape(128, 16, 512).transpose(1, 0, 2)
    return np.ascontiguousarray(cos), np.ascontiguousarray(sin)


def gqa_chunk_heads():
    return [(c, 4 + c) if c < 4 else (8 + c - 4, 12 + c - 4) for c in range(8)]


def prep_l0(inp, b, q):
    d = common_inputs(inp, 0, b)
    order = np.concatenate([np.arange(q * 2048, 8192), np.arange(0, q * 2048)])
    d['xb'] = np.ascontiguousarray(inp['x'][b][order])
    d['ctx'] = np.ascontiguousarray(inp['ctx'][b])
    wqkv = inp['at_w_qkv'][0]
    qcols = np.concatenate([np.concatenate([np.arange(ha * 64, ha * 64 + 64), np.arange(hb * 64, hb * 64 + 64)]) for ha, hb in gqa_chunk_heads()])
    d['wqkv'] = np.ascontiguousarray(np.concatenate([wqkv[:, qcols], wqkv[:, 1024:]], axis=1))
    d['wo'] = np.ascontiguousarray(inp['at_w_o'][0][qcols, :])
    d['gains'] = np.ascontiguousarray(np.stack([np.tile(inp['at_q_g'][0], 2), np.tile(inp['at_k_g'][0], 2)], axis=1))
    d['cos'], d['sin'] = rope_tables(order)
    rmat = np.zeros((128, 128), np.float32)
    for i in range(64):
        rmat[2 * i + 1, 2 * i] = -1.0
        rmat[2 * i, 2 * i + 1] = 1.0
    d['rmat'] = rmat
    bones = np.zeros((128, 128), np.float32)
    bones[:64, :64] = 1.0
    bones[64:, 64:] = 1.0
    d['bones'] = bones
    return d


NA_CLASS = [0, 1] + [2] * 12 + [3, 4]
NA_ST = list(range(14)) + [12, 13]


def na_unit(kb, q_ap, qkeys, kt_ap, vt, tab_ap, tkey, st, dst, dstkeys, kvkeys, C):
    P = kb.P
    u = kb.nextrot('naunit', 2)
    b0 = 3 * u
    ob = 6 + u
    for j in range(7):
        kt = st + j
        P.op('pe', lambda e, j=j, kt=kt: e.matmul(kb.ps[:, b0 * 512 + j * 128:b0 * 512 + (j + 1) * 128], lhsT=kt_ap(kt), rhs=q_ap, start=True, stop=True),
             reads=kvkeys + qkeys, writes=[('ps', b0 + j // 4)])
    for j in range(2):
        kt = 20 + j
        P.op('pe', lambda e, j=j, kt=kt: e.matmul(kb.ps[:, (b0 + 2) * 512 + j * 128:(b0 + 2) * 512 + (j + 1) * 128], lhsT=kt_ap(kt), rhs=q_ap, start=True, stop=True),
             reads=kvkeys + qkeys, writes=[('ps', b0 + 2)])
    sbr = kb.nextrot('nasb', 2)
    sb_ = C['nasb'][sbr]
    P.op('dve', lambda e: e.tensor_tensor(out=sb_[:], in0=kb.ps[:, b0 * 512:b0 * 512 + 896], in1=tab_ap, op=ALU.add), reads=[tkey], writes=[('ps', b0), ('ps', b0 + 1), ('nasb', sbr)])
    pr = kb.nextrot('napw', 3)
    pw, pc = C['napw'][pr], C['napc'][pr]
    P.op('act', lambda e: e.activation(out=pw[:], in_=sb_[:], func=AF.Exp), reads=[('nasb', sbr)], writes=[('napw', pr)])
    P.op('act', lambda e: e.activation(out=pc[:], in_=kb.ps[:, (b0 + 2) * 512:(b0 + 2) * 512 + 256], func=AF.Exp), writes=[('ps', b0 + 2), ('napc', pr)])
    for j in range(9):
        kt = st + j if j < 7 else 20 + (j - 7)
        rhs = pw[:, j * 128:(j + 1) * 128] if j < 7 else pc[:, (j - 7) * 128:(j - 6) * 128]
        P.op('pe', lambda e, j=j, kt=kt, rhs=rhs: e.matmul(kb.bank(ob, 128), lhsT=vt(kt), rhs=rhs, start=(j == 0), stop=(j == 8)),
             reads=kvkeys + [('napw', pr), ('napc', pr)], writes=[('ps', ob)])
    return ob


def build_l1(kb=None, io=None):
    own = kb is None
    if own:
        kb = KB()
    io = io or {}
    kb.pfx = '' if own else 'l1_'
    P, nc = kb.P, kb.nc
    NTOK = 2048
    NH = 2560
    xh_d = io.get("xh") or kb.din("xh", [NH, D]); ctx_d = io.get("ctx") or kb.din("ctx", [256, D])
    cv_d = kb.din("cv", [128, 8, 2]); adaw_d = kb.din("adaw", [D, kb.ncol_ada()]); adab_d = kb.din("adab", [2, kb.ncol_ada()])
    g1_d = kb.din("g1", [128, 8]); g2_d = kb.din("g2", [128, 8])
    wqkv_d = kb.din("wqkv", [8, D, 384]); tab_d = kb.din("tab", [5, 8, 128, 2 * 7 * 128])
    wo_d = kb.din("wo", [D, D]); w1_d = kb.din("w1", [D, DFF]); w2_d = kb.din("w2", [DFF, D])
    out_d = io.get("out") or kb.dout("out", [NTOK, D])
    mid_d = io.get("mid") or out_d

    modsT = kb.sb("modsT", [128, 48, 2], F32)
    mvL = kb.sb("mvL", [128, 6, 8], F32); mvC = kb.sb("mvC", [128, 6, 8], F32)
    C = {}
    with kb.phase():
        kb.mods(cv_d, adaw_d, adab_d, modsT)
        mL = kb.mod_vectors(modsT, g1_d, g2_d, 0, mvL, "L")
        mC = kb.mod_vectors(modsT, g1_d, g2_d, 1, mvC, "C")
    with kb.phase():
        hT = kb.sb("hT", [128, 8, NH], BF16)
        hcT = kb.sb("hcT", [128, 8, 256], BF16)
        OT = kb.sb("OT", [128, 8, NTOK], BF16)
        with kb.phase():
            xtmp = kb.sb("xtmp", [128, 8, 512], F32)
            stage = [kb.sb("stage%d" % i, [128, 1024], F32) for i in range(2)]
            scr = norm_scratch(kb)
            for g in range(5):
                kb.load_xT(tsl(xh_d, g * 512, (g + 1) * 512), 512, xtmp, 'xtmp', stage)
                kb.norm_mod(xtmp, 'xtmp', 512, mL['A1'], mL['B1'], mL['key'], hT, 'hT', scr, t0=0, ht0=g * 512)
            kb.load_xT(ctx_d, 256, xtmp, 'xtmp', stage)
            kb.norm_mod(xtmp, 'xtmp', 256, mC['A1'], mC['B1'], mC['key'], hcT, 'hcT', scr)
        with kb.phase():
            C['rc'] = [kb.sb("rc%d" % i, [128, 512], F32) for i in range(2)]
            C['nasb'] = [kb.sb("nasb%d" % i, [128, 896], F32) for i in range(2)]
            C['napw'] = [kb.sb("napw%d" % i, [128, 896], BF16) for i in range(3)]
            C['napc'] = [kb.sb("napc%d" % i, [128, 256], BF16) for i in range(3)]
            wc = [kb.sb("wc%d" % i, [128, 8, 384], BF16) for i in range(2)]
            QTc = [kb.sb("QTc%d" % i, [128, NTOK], BF16) for i in range(2)]
            KTc = [kb.sb("KTc%d" % i, [128, NH + 256], BF16) for i in range(2)]
            Vec = [kb.sb("Vec%d" % i, [128, 22, 192], BF16) for i in range(2)]
            tabI = [kb.sb("tabI%d" % i, [128, 2, 7, 128], F32) for i in range(2)]
            tabS = [kb.sb("tabS%d" % i, [128, 2, 7, 128], F32) for i in range(2)]
            for i in range(2):
                P.op('pool', lambda e, i=i: e.memset(Vec[i][:, :, 64:128], 1.0), writes=[('Vones', i)])
            for c in range(8):
                b = c % 2
                kb.load_w(wc[b], ('wc', b), wqkv_d[c], 0, KC, 0, 384)
                P.dma('sp', lambda e, c=c, b=b: e.dma_start(out=tabI[b][:].rearrange("p a j q -> p (a j q)"), in_=tab_d[2, c]), writes=[('tabI', b)])
                for g in range(4):
                    pb = kb.nextrot('projbank', 2)
                    proj_fm(kb, hT, 'hT', 256 + g * 512, 512, wc[b], ('wc', b), 0, pb)
                    P.op('act', lambda e, g=g, b=b, pb=pb: e.activation(out=QTc[b][:, g * 512:(g + 1) * 512], in_=kb.bank(pb), func=AF.Copy, scale=0.125),
                         writes=[('ps', pb), ('QTc', b, g)])
                for g in range(5):
                    pb = kb.nextrot('projbank', 2)
                    proj_fm(kb, hT, 'hT', g * 512, 512, wc[b], ('wc', b), 128, pb)
                    P.op('act', lambda e, g=g, b=b, pb=pb: e.activation(out=KTc[b][:, g * 512:(g + 1) * 512], in_=kb.bank(pb), func=AF.Copy),
                         writes=[('ps', pb), ('KTc', b, g)])
                pb = kb.nextrot('projbank', 2)
                proj_fm(kb, hcT, 'hcT', 0, 256, wc[b], ('wc', b), 128, pb)
                P.op('act', lambda e, b=b, pb=pb: e.activation(out=KTc[b][:, NH:NH + 256], in_=kb.bank(pb, 256), func=AF.Copy), writes=[('ps', pb), ('KTc', b, 5)])
                for kt in range(22):
                    src_h, hk, t0 = (hT, 'hT', kt * 128) if kt < 20 else (hcT, 'hcT', (kt - 20) * 128)
                    pb = kb.nextrot('projbank', 2)
                    for kc in range(KC):
                        P.op('pe', lambda e, kc=kc, b=b, pb=pb, src_h=src_h, t0=t0: e.matmul(kb.bank(pb, 128), lhsT=src_h[:, kc, t0:t0 + 128], rhs=wc[b][:, kc, 256:384],
                                                                                          start=(kc == 0), stop=(kc == KC - 1)),
                             reads=[(('wc', b), kc), (hk, t0 // 512)], writes=[('ps', pb)])
                    dst = Vec[b][:, kt, :].rearrange("p (t s) -> p t s", s=64)[:, ::2, :]
                    src = kb.bank(pb, 128).rearrange("p (t s) -> p t s", s=64)
                    P.op('dve', lambda e, dst=dst, src=src: e.tensor_copy(out=dst, in_=src), writes=[('ps', pb), ('Vec', b, kt)])
                for rp in range(16):
                    cls = NA_CLASS[rp]
                    st = NA_ST[rp]
                    if cls == 2:
                        tab, tkey = tabI[b], ('tabI', b)
                    else:
                        sbuf_i = kb.nextrot('tabS', 2)
                        tab, tkey = tabS[sbuf_i], ('tabS', sbuf_i)
                        P.dma('sp', lambda e, c=c, cls=cls, tab=tab: e.dma_start(out=tab[:].rearrange("p a j q -> p (a j q)"), in_=tab_d[cls, c]), writes=[tkey])
                    kvkeys = [('KTc', b, g_) for g_ in range(6)] + [('Vec', b, kt_) for kt_ in range(22)] + [('Vones', b)]
                    for hh in range(2):
                        p0 = hh * 64
                        q_ap = QTc[b][p0:p0 + 64, rp * 128:(rp + 1) * 128]
                        kt_ap = (lambda kt, b=b, p0=p0: KTc[b][p0:p0 + 64, kt * 128:(kt + 1) * 128])
                        vt = (lambda kt, b=b, hh=hh: Vec[b][:, kt, hh * 64:hh * 64 + 128])
                        tab_ap = tab[:, hh, :, :].rearrange("p j q -> p (j q)")
                        ob = na_unit(kb, q_ap, [('QTc', b, rp // 4)], kt_ap, vt, tab_ap, tkey, st, None, None, kvkeys, C)
                        rcr = kb.nextrot('rc', 2)
                        rc = C['rc'][rcr]
                        if hh == 0:
                            P.op('dve', lambda e, rc=rc, ob=ob: e.reciprocal(out=rc[64:128, 0:128], in_=kb.bank(ob, 128)[64:128, :]), writes=[('ps', ob), ('rc', rcr)])
                            P.op('dve', lambda e, rc=rc, ob=ob, c=c, rp=rp: e.tensor_tensor(out=OT[0:64, c, rp * 128:(rp + 1) * 128], in0=kb.bank(ob, 128)[0:64, :], in1=rc[64:128, 0:128], op=ALU.mult),
                                 reads=[('rc', rcr)], writes=[('ps', ob), ('OT', rp // 4)])
                        else:
                            P.op('dve', lambda e, rc=rc, ob=ob: e.reciprocal(out=rc[0:64, 0:128], in_=kb.bank(ob, 128)[0:64, :]), writes=[('ps', ob), ('rc', rcr)])
                            P.op('dve', lambda e, rc=rc, ob=ob, c=c, rp=rp: e.tensor_tensor(out=OT[64:128, c, rp * 128:(rp + 1) * 128], in0=kb.bank(ob, 128)[64:128, :], in1=rc[0:64, 0:128], op=ALU.mult),
                                 reads=[('rc', rcr)], writes=[('ps', ob), ('OT', rp // 4)])
        with kb.phase():
            xtmp = kb.sb("xtmp", [128, 8, 512], F32)
            stage = [kb.sb("stage%d" % i, [128, 1024], F32) for i in range(2)]
            ostage = [kb.sb("ostage%d" % i, [128, 1024], F32) for i in range(2)]
            wo = kb.sb("wo", [128, 8, 1024], BF16)
            kb.load_w(wo, 'wo', wo_d, 0, KC, 0, 1024)
            for g in range(4):
                kb.load_xT(tsl(xh_d, 256 + g * 512, 256 + (g + 1) * 512), 512, xtmp, 'xtmp', stage)
                resid_proj(kb, OT, 'OT', g * 512, 512, wo, 'wo', xtmp, 'xtmp', 0, mL['G1'], mL['key'])
                kb.store_x(xtmp, 'xtmp', 512, tsl(mid_d, g * 512, (g + 1) * 512), ostage)
    mlp_tail(kb, mid_d, out_d, NTOK, mL, w1_d, w2_d)
    if own:
        P.close()
    return nc


def na_bias_tables(rpb, qq):
    NEG = np.float32(-30000.0)
    tab = np.full((5, 16, 2, 64, 7, 2, 64), NEG, np.float32)
    cq = np.arange(64)
    cs = np.clip(cq - 8, 0, 48)
    ck = np.arange(64)
    colvalid = (ck[:, None] >= cs[None, :]) & (ck[:, None] < cs[None, :] + 16)
    colidx = np.clip(ck[:, None] - cq[None, :] + 15, 0, 30)
    rep_rp = {0: 0, 1: 1, 2: 2, 3: 14, 4: 15}
    for cls in range(5):
        rp = rep_rp[cls]
        st = NA_ST[rp]
        for bq in range(2):
            r = 32 * qq + 2 * rp + bq
            rs = min(max(r - 4, 0), 120)
            for j in range(7):
                for a in range(2):
                    kr = 32 * qq - 4 + 2 * (st + j) + a
                    if kr < rs or kr >= rs + 8 or kr < 0 or kr > 127:
                        continue
                    vals = rpb[:, kr - r + 7, :][:, colidx]
                    tab[cls, :, a, :, j, bq, :] = np.where(colvalid[None], vals, NEG)
    tab = tab.reshape(5, 8, 2, 128, 7, 128)
    tab = tab.transpose(0, 1, 3, 2, 4, 5).reshape(5, 8, 128, 2 * 7 * 128)
    return np.ascontiguousarray(tab)


def prep_l1(inp, x1, hctx1, b, q):
    d = common_inputs(inp, 1, b)
    if x1 is not None:
        xh = np.zeros((2560, D), np.float32)
        lo = q * 2048 - 256
        hi = lo + 2560
        s0, s1 = max(lo, 0), min(hi, 8192)
        xh[s0 - lo:s1 - lo] = x1[b][s0:s1]
        d['xh'] = xh
        d['ctx'] = np.ascontiguousarray(hctx1[b])
    w = inp['na_w_qkv'][0]
    d['wqkv'] = np.ascontiguousarray(np.stack([np.concatenate([w[:, c * 128:(c + 1) * 128], w[:, 1024 + c * 128:1024 + (c + 1) * 128],
                                                              w[:, 2048 + c * 128:2048 + (c + 1) * 128]], axis=1) for c in range(8)]))
    d['tab'] = na_bias_tables(inp['na_rpb'][0], q)
    d['wo'] = np.ascontiguousarray(inp['na_w_o'][0])
    return d


def build_l2(kb=None, io=None):
    own = kb is None
    if own:
        kb = KB()
    io = io or {}
    kb.pfx = '' if own else 'l2_'
    P, nc = kb.P, kb.nc
    NTOK = 2048
    NH = 2304
    xh_d = io.get("xh") or kb.din("xh", [NH, D])
    cv_d = kb.din("cv", [128, 8, 2]); adaw_d = kb.din("adaw", [D, kb.ncol_ada()]); adab_d = kb.din("adab", [2, kb.ncol_ada()])
    g1_d = kb.din("g1", [128, 8]); g2_d = kb.din("g2", [128, 8])
    wpw1_d = kb.din("wpw1", [8, D, 256]); vecs_d = kb.din("vecs", [128, 6, 8]); wdw_d = kb.din("wdw", [128, 8, 31]); mask_d = kb.din("mask", [128, 2])
    wpw2_d = kb.din("wpw2", [D, D]); w1_d = kb.din("w1", [D, DFF]); w2_d = kb.din("w2", [DFF, D])
    out_d = io.get("out") or kb.dout("out", [NTOK, D])
    mid_d = io.get("mid") or out_d

    modsT = kb.sb("modsT", [128, 48, 2], F32)
    mvL = kb.sb("mvL", [128, 6, 8], F32)
    vecs = kb.sb("vecs", [128, 6, 8], F32)
    wdw = kb.sb("wdw", [128, 8, 31], F32)
    mask = kb.sb("mask", [128, 2], F32)
    bG = kb.sb("bG", [128, 8], F32)
    identb = kb.sb("identb", [128, 128], BF16)
    with kb.phase():
        kb.mods(cv_d, adaw_d, adab_d, modsT)
        mL = kb.mod_vectors(modsT, g1_d, g2_d, 0, mvL, "L")
        P.dma('sp', lambda e: e.dma_start(out=vecs[:], in_=vecs_d), writes=['vecs'])
        P.dma('sp', lambda e: e.dma_start(out=wdw[:], in_=wdw_d), writes=['wdw'])
        P.dma('sp', lambda e: e.dma_start(out=mask[:], in_=mask_d), writes=['mask'])
        P.op('dve', lambda e: e.tensor_tensor(out=bG[:], in0=vecs[:, 5, :], in1=mL['G1'], op=ALU.mult), reads=['vecs', mL['key']], writes=['bG'])
        P.op('dve', lambda e: e.tensor_copy(out=identb[:], in_=kb.ident[:]), reads=['ident'], writes=['identb'])
    with kb.phase():
        vT = kb.sb("vT", [128, 8, NTOK], BF16)
        with kb.phase():
            uT = kb.sb("uT", [128, 8, NH], BF16)
            with kb.phase():
                hT = kb.sb("hT", [128, 8, NH], BF16)
                with kb.phase():
                    xtmp = kb.sb("xtmp", [128, 8, 512], F32)
                    stage = [kb.sb("stage%d" % i, [128, 1024], F32) for i in range(2)]
                    scr = norm_scratch(kb)
                    for g in range(5):
                        n = 512 if g < 4 else 256
                        kb.load_xT(tsl(xh_d, g * 512, g * 512 + n), n, xtmp, 'xtmp', stage)
                        kb.norm_mod(xtmp, 'xtmp', n, mL['A1'], mL['B1'], mL['key'], hT, 'hT', scr, t0=0, ht0=g * 512)
                with kb.phase():
                    wp = [kb.sb("wp%d" % i, [128, 8, 256], BF16) for i in range(2)]
                    sig = [kb.sb("sig%d" % i, [128, 512], F32) for i in range(2)]
                    for fc in range(8):
                        b = fc % 2
                        kb.load_w(wp[b], ('wp', b), wpw1_d[fc], 0, KC, 0, 256)
                        for g in range(5):
                            n = 512 if g < 4 else 256
                            pa = kb.nextrot('projbank', 2)
                            proj_fm(kb, hT, 'hT', g * 512, n, wp[b], ('wp', b), 0, pa)
                            pg = 2 + kb.nextrot('projbank2', 2)
                            proj_fm(kb, hT, 'hT', g * 512, n, wp[b], ('wp', b), 128, pg)
                            sb_ = kb.nextrot('sig', 2)
                            P.op('act', lambda e, sb_=sb_, pg=pg, fc=fc, n=n: e.activation(out=sig[sb_][:, :n], in_=kb.bank(pg, n), func=AF.Sigmoid, bias=vecs[:, 1, fc:fc + 1]),
                                 reads=['vecs'], writes=[('ps', pg), ('sig', sb_)])
                            P.op('dve', lambda e, sb_=sb_, pa=pa, fc=fc, g=g, n=n: e.scalar_tensor_tensor(out=uT[:, fc, g * 512:g * 512 + n], in0=kb.bank(pa, n), scalar=vecs[:, 0, fc:fc + 1],
                                                                                                      in1=sig[sb_][:, :n], op0=ALU.add, op1=ALU.mult),
                                 reads=['vecs', ('sig', sb_)], writes=[('ps', pa), ('uT', fc, g)])
                        P.op('dve', lambda e, fc=fc: e.tensor_scalar(out=uT[:, fc, 0:128], in0=uT[:, fc, 0:128], scalar1=mask[:, 0:1], scalar2=None, op0=ALU.mult),
                             reads=['mask'], writes=[('uT', fc, 0)])
                        P.op('dve', lambda e, fc=fc: e.tensor_scalar(out=uT[:, fc, 2176:2304], in0=uT[:, fc, 2176:2304], scalar1=mask[:, 1:2], scalar2=None, op0=ALU.mult),
                             reads=['mask'], writes=[('uT', fc, 4)])
            with kb.phase():
                dg = kb.sb("dg", [128, 8, 31, 128], BF16)
                cT = kb.sb("cT", [128, 8, 512], F32)
                cbf = [kb.sb("cbf%d" % i, [128, 512], BF16) for i in range(2)]
                c2 = [kb.sb("c2%d" % i, [128, 512], BF16) for i in range(2)]
                mean = kb.sb("mean", [128, 512], F32); msq = kb.sb("msq", [128, 512], F32); rstd = kb.sb("rstd", [128, 512], F32)
                tt_ = [kb.sb("tt%d" % i, [128, 512], F32) for i in range(2)]
                for fc in range(8):
                    P.op('dve', lambda e, fc=fc: e.tensor_tensor(out=dg[:, fc, :, :], in0=identb[:].unsqueeze(1).broadcast_to([128, 31, 128]),
                                                                in1=wdw[:, fc, :].unsqueeze(2).broadcast_to([128, 31, 128]), op=ALU.mult),
                         reads=['identb', 'wdw'], writes=[('dg', fc)])
                for tg in range(4):
                    for fc in range(8):
                        pb = kb.nextrot('projbank', 2)
                        for j in range(31):
                            o = 128 + tg * 512 + j - 15
                            P.op('pe', lambda e, fc=fc, j=j, o=o, pb=pb: e.matmul(kb.bank(pb), lhsT=dg[:, fc, j, :], rhs=uT[:, fc, o:o + 512], start=(j == 0), stop=(j == 30)),
                                 reads=[('dg', fc)] + [('uT', fc, gg) for gg in range(5)], writes=[('ps', pb)])
                        P.op('act', lambda e, fc=fc, pb=pb: e.activation(out=cT[:, fc, :], in_=kb.bank(pb), func=AF.Identity, bias=vecs[:, 2, fc:fc + 1]),
                             reads=['vecs'], writes=[('ps', pb), ('cT', fc)])
                        r = kb.nextrot('cbf', 2)
                        P.op('dve', lambda e, fc=fc, r=r: e.tensor_copy(out=cbf[r][:], in_=cT[:, fc, :]), reads=[('cT', fc)], writes=[('cbf', r)])
                        P.op('act', lambda e, fc=fc, r=r: e.activation(out=c2[r][:], in_=cT[:, fc, :], func=AF.Square), reads=[('cT', fc)], writes=[('c2', r)])
                        P.op('pe', lambda e, fc=fc, r=r: e.matmul(kb.bank(6), lhsT=kb.ones_bf[:], rhs=cbf[r][:], start=(fc == 0), stop=(fc == 7)), reads=[('cbf', r), 'ones_bf'], writes=[('ps', 6)])
                        P.op('pe', lambda e, fc=fc, r=r: e.matmul(kb.bank(7), lhsT=kb.ones_bf[:], rhs=c2[r][:], start=(fc == 0), stop=(fc == 7)), reads=[('c2', r), 'ones_bf'], writes=[('ps', 7)])
                    P.op('act', lambda e: e.activation(out=mean[:], in_=kb.bank(6), func=AF.Copy, scale=1.0 / D), writes=[('ps', 6), 'mean'])
                    P.op('dve', lambda e: e.tensor_tensor(out=msq[:], in0=mean[:], in1=mean[:], op=ALU.mult), reads=['mean'], writes=['msq'])
                    P.op('dve', lambda e: e.scalar_tensor_tensor(out=msq[:], in0=kb.bank(7), scalar=1.0 / D, in1=msq[:], op0=ALU.mult, op1=ALU.subtract), writes=[('ps', 7), 'msq'])
                    P.op('act', lambda e: e.activation(out=rstd[:], in_=msq[:], func=AF.Ln, bias=kb.eps_t[:]), reads=['msq', 'eps_t'], writes=['rstd'])
                    P.op('act', lambda e: e.activation(out=rstd[:], in_=rstd[:], func=AF.Exp, scale=-0.5), writes=['rstd'])
                    for fc in range(8):
                        r = kb.nextrot('tt', 2)
                        P.op('dve', lambda e, fc=fc, r=r: e.tensor_tensor(out=tt_[r][:], in0=cT[:, fc, :], in1=mean[:], op=ALU.subtract), reads=[('cT', fc), 'mean'], writes=[('tt', r)])
                        P.op('dve', lambda e, r=r: e.tensor_tensor(out=tt_[r][:], in0=tt_[r][:], in1=rstd[:], op=ALU.mult), reads=['rstd'], writes=[('tt', r)])
                        P.op('act', lambda e, fc=fc, r=r, tg=tg: e.activation(out=vT[:, fc, tg * 512:(tg + 1) * 512], in_=tt_[r][:], func=AF.Silu, scale=vecs[:, 3, fc:fc + 1], bias=vecs[:, 4, fc:fc + 1]),
                             reads=[('tt', r), 'vecs'], writes=[('vT', tg)])
        with kb.phase():
            xtmp = kb.sb("xtmp", [128, 8, 512], F32)
            stage = [kb.sb("stage%d" % i, [128, 1024], F32) for i in range(2)]
            ostage = [kb.sb("ostage%d" % i, [128, 1024], F32) for i in range(2)]
            wo = kb.sb("wo", [128, 8, 1024], BF16)
            kb.load_w(wo, 'wo', wpw2_d, 0, KC, 0, 1024)
            for g in range(4):
                kb.load_xT(tsl(xh_d, 128 + g * 512, 128 + (g + 1) * 512), 512, xtmp, 'xtmp', stage)
                resid_proj(kb, vT, 'vT', g * 512, 512, wo, 'wo', xtmp, 'xtmp', 0, mL['G1'], mL['key'], bG=bG)
                kb.store_x(xtmp, 'xtmp', 512, tsl(mid_d, g * 512, (g + 1) * 512), ostage)
    mlp_tail(kb, mid_d, out_d, NTOK, mL, w1_d, w2_d)
    if own:
        P.close()
    return nc


def prep_l2(inp, x2, b, q):
    d = common_inputs(inp, 2, b)
    if x2 is not None:
        xh = np.zeros((2304, D), np.float32)
        lo = q * 2048 - 128
        hi = lo + 2304
        s0, s1 = max(lo, 0), min(hi, 8192)
        xh[s0 - lo:s1 - lo] = x2[b][s0:s1]
        d['xh'] = xh
    w = inp['cv_w_pw1'][0]
    d['wpw1'] = np.ascontiguousarray(np.stack([np.concatenate([w[:, c * 128:(c + 1) * 128], w[:, 1024 + c * 128:1024 + (c + 1) * 128]], axis=1) for c in range(8)]))
    bp = inp['cv_b_pw1'][0]
    d['vecs'] = np.ascontiguousarray(np.stack([fm(bp[:1024]), fm(bp[1024:]), fm(inp['cv_b_dw'][0]), fm(inp['cv_ln_g'][0]), fm(inp['cv_ln_b'][0]), fm(inp['cv_b_pw2'][0])], axis=1))
    d['wdw'] = np.ascontiguousarray(inp['cv_w_dw'][0].T.reshape(8, 128, 31).transpose(1, 0, 2))
    m = np.ones((128, 2), np.float32)
    if q == 0:
        m[:, 0] = 0.0
    if q == 3:
        m[:, 1] = 0.0
    d['mask'] = m
    d['wpw2'] = np.ascontiguousarray(inp['cv_w_pw2'][0])
    return d


def build_l3a(kb=None, io=None):
    own = kb is None
    if own:
        kb = KB()
    io = io or {}
    kb.pfx = '' if own else 'l3a_'
    P, nc = kb.P, kb.nc
    NTOK = 2048
    x_d = io.get("x") or kb.din("x", [NTOK, D])
    cv_d = kb.din("cv", [128, 8, 2]); adaw_d = kb.din("adaw", [D, kb.ncol_ada()]); adab_d = kb.din("adab", [2, kb.ncol_ada()])
    g1_d = kb.din("g1", [128, 8]); g2_d = kb.din("g2", [128, 8])
    csd_d = kb.din("csd", [256, 512])
    pq_d = io.get("pq") or kb.dout("pq", [NTOK, 2048], BF16)
    modsT = kb.sb("modsT", [128, 48, 2], F32)
    mvL = kb.sb("mvL", [128, 6, 8], F32)
    with kb.phase():
        kb.mods(cv_d, adaw_d, adab_d, modsT)
        mL = kb.mod_vectors(modsT, g1_d, g2_d, 0, mvL, "L")
    io['mL_out'] = mL
    with kb.phase():
        hT = kb.sb("hT", [128, 8, NTOK], BF16)
        csd = kb.sb("csd", [128, 2, 512], BF16)
        kb.load_w(csd, 'csd', csd_d, 0, 2, 0, 512)
        with kb.phase():
            xtmp = kb.sb("xtmp", [128, 8, 512], F32)
            stage = [kb.sb("stage%d" % i, [128, 1024], F32) for i in range(2)]
            scr = norm_scratch(kb)
            for g in range(4):
                kb.load_xT(tsl(x_d, g * 512, (g + 1) * 512), 512, xtmp, 'xtmp', stage)
                kb.norm_mod(xtmp, 'xtmp', 512, mL['A1'], mL['B1'], mL['key'], hT, 'hT', scr, t0=0, ht0=g * 512)
        with kb.phase():
            pqs = [kb.sb("pqs%d" % i, [128, 2048], BF16) for i in range(2)]
            for tt in range(16):
                ob = tt % 2
                for grp in range(4):
                    pb = kb.nextrot('projbank', 4)
                    for kl in range(2):
                        kc = grp * 2 + kl
                        P.op('pe', lambda e, kc=kc, kl=kl, tt=tt, pb=pb: e.matmul(kb.bank(pb), lhsT=hT[:, kc, tt * 128:(tt + 1) * 128], rhs=csd[:, kl, :], start=(kl == 0), stop=(kl == 1)),
                             reads=[(('csd'), kl), ('hT', tt // 4)], writes=[('ps', pb)])
                    if grp % 2 == 0:
                        P.op('act', lambda e, ob=ob, grp=grp, pb=pb: e.activation(out=pqs[ob][:, grp * 512:(grp + 1) * 512], in_=kb.bank(pb), func=AF.Copy), writes=[('ps', pb), ('pqs', ob, grp)])
                    else:
                        P.op('dve', lambda e, ob=ob, grp=grp, pb=pb: e.tensor_copy(out=pqs[ob][:, grp * 512:(grp + 1) * 512], in_=kb.bank(pb)), writes=[('ps', pb), ('pqs', ob, grp)])
                P.dma('sp', lambda e, ob=ob, tt=tt: e.dma_start(out=pq_d[tt * 128:(tt + 1) * 128, :], in_=pqs[ob][:]), reads=[('pqs', ob, g_) for g_ in range(4)], writes=[('pqd', tt)])
                if io.get('pqg') is not None and tt % 2 == 1:
                    c = tt // 2
                    pqg = io['pqg']
                    P.coll(lambda e, c=c, pqg=pqg: e.collective_compute("AllGather", ALU.bypass, replica_groups=RG, ins=[pq_d[c * 256:(c + 1) * 256, :].opt()], outs=[pqg[c * 1024:(c + 1) * 1024, :].opt()]),
                           reads=[('pqd', tt - 1), ('pqd', tt)], writes=[('pqg', c)])
    if own:
        P.close()
    return nc


def build_l3b(kb=None, io=None):
    own = kb is None
    if own:
        kb = KB()
    io = io or {}
    kb.pfx = '' if own else 'l3b_'
    P, nc = kb.P, kb.nc
    NTOK = 2048
    x_d = io.get("x") or kb.din("x", [NTOK, D])
    cv_d = kb.din("cv", [128, 8, 2]); adaw_d = kb.din("adaw", [D, kb.ncol_ada()]); adab_d = kb.din("adab", [2, kb.ncol_ada()])
    g1_d = kb.din("g1", [128, 8]); g2_d = kb.din("g2", [128, 8])
    pq_d = io.get("pq") or kb.din("pq", [8192, 2048], BF16)
    cn_d = kb.din("cn", [8192, NTOK], BF16); sn_d = kb.din("sn", [8192, NTOK], BF16)
    ftw_d = kb.din("ftw", [D, D]); vecs_d = kb.din("vecs", [128, 2, 8])
    w1_d = kb.din("w1", [D, DFF]); w2_d = kb.din("w2", [DFF, D])
    out_d = io.get("out") or kb.dout("out", [NTOK, D])
    mid_d = io.get("mid") or out_d
    modsT = kb.sb("modsT", [128, 48, 2], F32)
    mvL = kb.sb("mvL", [128, 6, 8], F32)
    vecs = kb.sb("vecs", [128, 2, 8], F32)
    bG = kb.sb("bG", [128, 8], F32)
    zeros = kb.sb("zeros", [128, 8], F32)
    with kb.phase():
        if io.get('mL') is not None:
            mL = io['mL']
        else:
            kb.mods(cv_d, adaw_d, adab_d, modsT)
            mL = kb.mod_vectors(modsT, g1_d, g2_d, 0, mvL, "L")
        P.dma('sp', lambda e: e.dma_start(out=vecs[:], in_=vecs_d), writes=['vecs'])
        P.op('dve', lambda e: e.tensor_tensor(out=bG[:], in0=vecs[:, 0, :], in1=mL['G1'], op=ALU.mult), reads=['vecs', mL['key']], writes=['bG'])
        P.op('pool', lambda e: e.memset(zeros[:], 0.0), writes=['zeros'])
    with kb.phase():
        zT = kb.sb("zT", [128, 8, NTOK], BF16)
        with kb.phase():
            pqb = [kb.sb("pqb%d" % i, [128, 2048], BF16) for i in range(3)]
            tb = [kb.sb("tb%d" % i, [128, 2, 512], BF16) for i in range(3)]
            tokmap = io.get('tokmap') or (lambda nt: nt * 128)
            for kg in range(4):
                for nt in range(64):
                    tk = tokmap(nt)
                    r = kb.nextrot('pqb', 3)
                    P.dma('sp', lambda e, r=r, nt=nt: e.dma_start(out=pqb[r][:], in_=pq_d[nt * 128:(nt + 1) * 128, :]), writes=[('pqb', r)])
                    P.dma('sp', lambda e, r=r, tk=tk, kg=kg: e.dma_start(out=tb[r][:, 0, :], in_=cn_d[tk:tk + 128, kg * 512:(kg + 1) * 512]), writes=[('tb', r, 0)])
                    P.dma('sp', lambda e, r=r, tk=tk, kg=kg: e.dma_start(out=tb[r][:, 1, :], in_=sn_d[tk:tk + 128, kg * 512:(kg + 1) * 512]), writes=[('tb', r, 1)])
                    for fz in range(8):
                        grp, jh = fz // 2, fz % 2
                        P.op('pe', lambda e, r=r, fz=fz, grp=grp, jh=jh, nt=nt: e.matmul(kb.bank(fz), lhsT=pqb[r][:, grp * 512 + jh * 128:grp * 512 + jh * 128 + 128], rhs=tb[r][:, 0, :],
                                                                                      start=(nt == 0), stop=False),
                             reads=[('pqb', r), ('tb', r, 0)], writes=[('ps', fz)])
                        P.op('pe', lambda e, r=r, fz=fz, grp=grp, jh=jh, nt=nt: e.matmul(kb.bank(fz), lhsT=pqb[r][:, grp * 512 + 256 + jh * 128:grp * 512 + 256 + jh * 128 + 128], rhs=tb[r][:, 1, :],
                                                                                      start=False, stop=(nt == 63)),
                             reads=[('pqb', r), ('tb', r, 1)], writes=[('ps', fz)])
                for fz in range(8):
                    if fz % 2 == 0:
                        P.op('act', lambda e, fz=fz, kg=kg: e.activation(out=zT[:, fz, kg * 512:(kg + 1) * 512], in_=kb.bank(fz), func=AF.Copy), writes=[('ps', fz), ('zT', kg)])
                    else:
                        P.op('dve', lambda e, fz=fz, kg=kg: e.tensor_copy(out=zT[:, fz, kg * 512:(kg + 1) * 512], in_=kb.bank(fz)), writes=[('ps', fz), ('zT', kg)])
        with kb.phase():
            xtmp = kb.sb("xtmp", [128, 8, 512], F32)
            stage = [kb.sb("stage%d" % i, [128, 1024], F32) for i in range(2)]
            ostage = [kb.sb("ostage%d" % i, [128, 1024], F32) for i in range(2)]
            wo = kb.sb("wo", [128, 8, 1024], BF16)
            kb.load_w(wo, 'wo', ftw_d, 0, KC, 0, 1024)
            for g in range(4):
                kb.load_xT(tsl(x_d, g * 512, (g + 1) * 512), 512, xtmp, 'xtmp', stage)
                resid_proj(kb, zT, 'zT', g * 512, 512, wo, 'wo', xtmp, 'xtmp', 0, mL['G1'], mL['key'], bG=bG)
                kb.store_x(xtmp, 'xtmp', 512, tsl(mid_d, g * 512, (g + 1) * 512), ostage)
    mlp_tail(kb, mid_d, out_d, NTOK, mL, w1_d, w2_d, final=(vecs[:, 1, :], zeros[:], 'vecs'))
    if own:
        P.close()
    return nc


def prep_l3a(inp, x3, b, q):
    d = common_inputs(inp, 3, b)
    for k in ('w1', 'w2'):
        d.pop(k)
    if x3 is not None:
        d['x'] = np.ascontiguousarray(x3[b][q * 2048:(q + 1) * 2048])
    dd = np.arange(256)[:, None].astype(np.int64)
    jj = np.arange(256)[None, :].astype(np.int64)
    ang = 2.0 * np.pi * ((dd * jj) % 256).astype(np.float64) / 256.0
    d['csd'] = np.ascontiguousarray(np.concatenate([np.cos(ang) / 16.0, np.sin(ang) / 16.0], axis=1).astype(np.float32))
    return d


_DFT_CACHE = {}


def seq_dft_tables(q):
    if q not in _DFT_CACHE:
        import ml_dtypes
        n = np.arange(8192, dtype=np.int64)[:, None]
        k = np.arange(q * 2048, (q + 1) * 2048, dtype=np.int64)[None, :]
        ang = 2.0 * np.pi * ((n * k) % 8192).astype(np.float64) / 8192.0
        s = 1.0 / np.sqrt(8192.0)
        _DFT_CACHE[q] = (np.ascontiguousarray((np.cos(ang) * s).astype(np.float32).astype(ml_dtypes.bfloat16)),
                         np.ascontiguousarray((-np.sin(ang) * s).astype(np.float32).astype(ml_dtypes.bfloat16)))
    return _DFT_CACHE[q]


def prep_l3b(inp, x3, pq_b, b, q):
    d = common_inputs(inp, 3, b)
    if x3 is not None:
        d['x'] = np.ascontiguousarray(x3[b][q * 2048:(q + 1) * 2048])
        d['pq'] = pq_b
    d['cn'], d['sn'] = seq_dft_tables(q)
    d['ftw'] = np.ascontiguousarray(inp['ft_w'][0])
    d['vecs'] = np.ascontiguousarray(np.stack([fm(inp['ft_b'][0]), fm(inp['final_g'])], axis=1))
    return d


CORES = [(b, q) for b in range(2) for q in range(4)]


def _run(nc, maps):
    res = run_bass_kernel_spmd(nc, maps, core_ids=list(range(8)))
    return res.results


def kernel_unfused(**inputs):
    inp = {k: np.asarray(v) for k, v in inputs.items()}
    r = _run(build_l0(), [prep_l0(inp, b, q) for b, q in CORES])
    x1 = np.stack([np.concatenate([r[b * 4 + q]["out"] for q in range(4)], axis=0) for b in range(2)])
    hctx1 = np.stack([r[b * 4]["hctx"] for b in range(2)])
    r = _run(build_l1(), [prep_l1(inp, x1, hctx1, b, q) for b, q in CORES])
    x2 = np.stack([np.concatenate([r[b * 4 + q]["out"] for q in range(4)], axis=0) for b in range(2)])
    r = _run(build_l2(), [prep_l2(inp, x2, b, q) for b, q in CORES])
    x3 = np.stack([np.concatenate([r[b * 4 + q]["out"] for q in range(4)], axis=0) for b in range(2)])
    r = _run(build_l3a(), [prep_l3a(inp, x3, b, q) for b, q in CORES])
    pq = [np.ascontiguousarray(np.concatenate([r[b * 4 + q]["pq"] for q in range(4)], axis=0)) for b in range(2)]
    r = _run(build_l3b(), [prep_l3b(inp, x3, pq[b], b, q) for b, q in CORES])
    out = np.stack([np.concatenate([r[b * 4 + q]["out"] for q in range(4)], axis=0) for b in range(2)])
    return out.astype(np.float32)


def halo_exchange(kb, src, dst, H, sel, tag, copy_mid=True):
    P, nc = kb.P, kb.nc
    bF = kb.dint("bounceF" + tag, [1024, H]); bL = kb.dint("bounceL" + tag, [1024, H])
    gF = kb.dint("gathF" + tag, [4096, H]); gL = kb.dint("gathL" + tag, [4096, H])
    with kb.phase():
        P.dma('pool', lambda e: e.dma_start(out=bF.rearrange("(p k) h -> p k h", k=8), in_=src.ap[:, :, src.t0:src.t0 + H]), writes=['bF'])
        P.dma('pool', lambda e: e.dma_start(out=bL.rearrange("(p k) h -> p k h", k=8), in_=src.ap[:, :, src.t0 + 2048 - H:src.t0 + 2048]), writes=['bL'])
        P.coll([lambda e: e.collective_compute("AllGather", ALU.bypass, replica_groups=RG, ins=[bF.opt()], outs=[gF.opt()]),
                lambda e: e.collective_compute("AllGather", ALU.bypass, replica_groups=RG, ins=[bL.opt()], outs=[gL.opt()])], reads=['bF', 'bL'], writes=['gF', 'gL'])
        if copy_mid:
            for hf in range(2):
                P.dma('pool', lambda e, hf=hf: e.dma_start(out=dst.ap[:, hf * 4:(hf + 1) * 4, H:H + 2048], in_=src.ap[:, hf * 4:(hf + 1) * 4, :]), writes=[('dst', hf)])
        cand = [kb.sb("cand%d" % i, [128, 4, 8, H], F32) for i in range(2)]
        acc = [kb.sb("hacc%d" % i, [128, 8, H], F32) for i in range(2)]
        for side in range(2):
            gsrc, gkey = (gL, 'gL') if side == 0 else (gF, 'gF')
            srcv = gsrc.rearrange("(r p k) h -> p r k h", r=4, k=8)
            for r in range(4):
                P.dma('sp', lambda e, side=side, srcv=srcv, r=r: e.dma_start(out=cand[side][:, r, :, :], in_=srcv[:, r, :, :]), reads=[gkey], writes=[('cand', side, r)])
            P.op('dve', lambda e, side=side: e.tensor_scalar(out=acc[side][:], in0=cand[side][:, 0, :, :], scalar1=sel[:, side * 4:side * 4 + 1], scalar2=None, op0=ALU.mult),
                 reads=[('cand', side, 0), 'sel'], writes=[('hacc', side)])
            for r in range(1, 4):
                P.op('dve', lambda e, side=side, r=r: e.scalar_tensor_tensor(out=acc[side][:], in0=cand[side][:, r, :, :], scalar=sel[:, side * 4 + r:side * 4 + r + 1], in1=acc[side][:],
                                                                          op0=ALU.mult, op1=ALU.add),
                     reads=[('cand', side, r), 'sel'], writes=[('hacc', side)])
            d0 = 0 if side == 0 else H + 2048
            P.dma('sp', lambda e, side=side, d0=d0: e.dma_start(out=dst.ap[:, :, d0:d0 + H], in_=acc[side][:]), reads=[('hacc', side)], writes=[('dsth', side)])


def pq_tokmap(nt):
    c, r, half = nt // 8, (nt % 8) // 2, nt % 2
    return r * 2048 + c * 256 + half * 128


def build_fused():
    kb = KB()
    P, nc = kb.P, kb.nc
    kb.split_mods = True
    xb_d = kb.din("xb", [2048, D]); ctx_d = kb.din("ctx", [256, D]); sel_d = kb.din("sel", [128, 8])
    out_d = kb.dout("out", [2048, D])
    sel = kb.sb("sel", [128, 8], F32)
    P.dma('sp', lambda e: e.dma_start(out=sel[:], in_=sel_d), writes=['sel'])

    def fmt(name, n):
        return FM(kb.dint(name, [128, 8, n]))
    mid = fmt("mid", 2048)
    hc1 = fmt("hc1", 256)
    xh1 = fmt("xh1", 2560)
    xa = FM(xh1.ap, 256)
    build_l0(kb, dict(xb=xb_d, ctx=ctx_d, out=xa, hctx=hc1, mid=mid, kv_gather=True))
    halo_exchange(kb, xa, xh1, 256, sel, "1", copy_mid=False)
    xh2 = fmt("xh2", 2304)
    xb2 = FM(xh2.ap, 128)
    build_l1(kb, dict(xh=xh1, ctx=hc1, out=xb2, mid=mid))
    halo_exchange(kb, xb2, xh2, 128, sel, "2", copy_mid=False)
    xc = fmt("xc", 2048)
    build_l2(kb, dict(xh=xh2, out=xc, mid=mid))
    pqo = kb.dint("pqo", [2048, 2048], BF16); pqg = kb.dint("pqg", [8192, 2048], BF16)
    io3 = dict(x=xc, pq=pqo, pqg=pqg)
    build_l3a(kb, io3)
    build_l3b(kb, dict(x=xc, pq=pqg, out=out_d, mid=mid, tokmap=pq_tokmap, mL=io3['mL_out']))
    P.close()
    return nc


def prep_fused(inp, b, q):
    d0 = prep_l0(inp, b, q)
    out = {k: d0[k] for k in ('ident', 'cv', 'xb', 'ctx')}
    sel = np.zeros((128, 8), np.float32)
    if q > 0:
        sel[:, q - 1] = 1.0
    if q < 3:
        sel[:, 4 + q + 1] = 1.0
    out['sel'] = sel
    out['xb'] = np.ascontiguousarray(out['xb'][:2048])
    for pfx, dd in (('l0_', d0), ('l1_', prep_l1(inp, None, None, b, q)), ('l2_', prep_l2(inp, None, b, q)),
                    ('l3a_', prep_l3a(inp, None, b, q)), ('l3b_', prep_l3b(inp, None, None, b, q))):
        for k, v in dd.items():
            if k in ('ident', 'cv', 'xb', 'ctx'):
                continue
            if k in ('adaw', 'adab'):
                v = np.ascontiguousarray(v[:, q * 1536:(q + 1) * 1536])
            if pfx == 'l0_' and k in ('cos', 'sin'):
                v = np.ascontiguousarray(v[:4])
            out[pfx + k] = v
    return out


def kernel(**inputs):
    inp = {k: np.asarray(v) for k, v in inputs.items()}
    r = _run(build_fused(), [prep_fused(inp, b, q) for b, q in CORES])
    out = np.stack([np.concatenate([r[b * 4 + q]["out"] for q in range(4)], axis=0) for b in range(2)])
    return out.astype(np.float32)
```

```python
import contextlib
import numpy as np
from concourse.bass_utils import run_bass_kernel_spmd
import concourse.bass as bass
import concourse.mybir as mybir

F32 = mybir.dt.float32
BF16 = mybir.dt.bfloat16
AF = mybir.ActivationFunctionType
ALU = mybir.AluOpType
AX = mybir.AxisListType

ENGS = ('pe', 'act', 'dve', 'pool', 'sp')
RG = [[0, 1, 2, 3], [4, 5, 6, 7]]
NDMASLOT = 8


class Op:
    __slots__ = ('eng', 'fn', 'deps', 'sig', 'sigval', 'dma', 'dslot', 'dval', 'prev_slot_op', 'seq', 'dinc')

    def __init__(self, eng, fn, dma):
        self.eng = eng
        self.fn = fn
        self.deps = []
        self.sig = False
        self.sigval = 0
        self.dma = dma
        self.dslot = None
        self.dval = 0
        self.prev_slot_op = None
        self.dinc = 16


class Prog:
    def __init__(self, nc, same_engine_sync=True):
        self.nc = nc
        self.same = same_engine_sync
        self.E = {'pe': nc.tensor, 'act': nc.scalar, 'dve': nc.vector, 'pool': nc.gpsimd, 'sp': nc.sync}
        self.sem = {}
        self._ctx = []
        for e in ('pe', 'act', 'dve', 'pool'):
            self.sem[e] = self._enter(nc.semaphore('prog_' + e))
        self.dsem = {}
        for q in ('sp', 'pool', 'act'):
            self.dsem[q] = [self._enter(nc.semaphore('dma_%s_%d' % (q, i))) for i in range(NDMASLOT)]
        self.ccsem = self._enter(nc.semaphore('ccsem'))
        self.cccnt = 0
        self.ccscratch = self._enter(nc.sbuf_tensor('ccscratch', [128, 8], F32))
        self.cnt = {e: 0 for e in ENGS}
        self.dcnt = {q: 0 for q in ('sp', 'pool', 'act')}
        self.last_dma = {q: [None] * NDMASLOT for q in ('sp', 'pool', 'act')}
        self.nops = 0
        self._reset_phase()

    def _enter(self, cm):
        v = cm.__enter__()
        self._ctx.append(cm)
        return v

    def alloc(self, cm):
        return self._enter(cm)

    def close(self):
        for cm in reversed(self._ctx):
            cm.__exit__(None, None, None)
        self._ctx = []

    def _reset_phase(self):
        self.ops = {e: [] for e in ENGS}
        self.order = []
        self.last_writer = {}
        self.readers = {}

    def _record(self, eng, fn, reads, writes, dma):
        op = Op(eng, fn, dma)
        deps = set()
        for k in reads:
            w = self.last_writer.get(k)
            if w is not None:
                deps.add(w)
        for k in writes:
            w = self.last_writer.get(k)
            if w is not None:
                deps.add(w)
            for r in self.readers.get(k, ()):
                deps.add(r)
        deps.discard(op)
        for k in reads:
            self.readers.setdefault(k, []).append(op)
        for k in writes:
            self.last_writer[k] = op
            self.readers[k] = []
        best = {}
        out = []
        for d in deps:
            if d.dma:
                out.append(d)
                continue
            if d.eng == eng and not dma:
                if eng == 'pe' or not self.same:
                    continue
            b = best.get(d.eng)
            if b is None or d.seq > b.seq:
                best[d.eng] = d
        out.extend(best.values())
        op.deps = out
        for d in out:
            if not d.dma:
                d.sig = True
        op.seq = len(self.order)
        self.ops[eng].append(op)
        self.order.append(op)
        self.nops += 1
        return op

    def op(self, eng, fn, reads=(), writes=()):
        return self._record(eng, fn, reads, writes, False)

    def coll(self, fn, reads=(), writes=()):
        fns = fn if isinstance(fn, (list, tuple)) else [fn]

        def wrapped(e):
            for f in fns:
                ins = f(e)
                self.cccnt += 1
                ins.then_inc(self.ccsem)
            e.wait_ge(self.ccsem, self.cccnt)
            return e.memset(self.ccscratch[:], 0.0)
        return self._record('pool', wrapped, reads, writes, False)

    def dma(self, q, fn, reads=(), writes=()):
        op = self._record(q, fn, reads, writes, True)
        op.dinc = 16
        j = self.dcnt[q]
        self.dcnt[q] += 1
        slot = j % NDMASLOT
        op.dslot = self.dsem[q][slot]
        op.dval = 16 * (j // NDMASLOT + 1)
        op.prev_slot_op = self.last_dma[q][slot]
        self.last_dma[q][slot] = op
        return op

    def flush(self, final_wait=()):
        for e in ENGS:
            c = self.cnt[e]
            for op in self.ops[e]:
                if op.sig and not op.dma:
                    c += 1
                    op.sigval = c
            self.cnt[e] = c
        ops = self.ops
        sem = self.sem
        lastd = {q: list(v) for q, v in self.last_dma.items()}
        anyd = any(d is not None for v in lastd.values() for d in v)

        def run(engname):
            def body(eng):
                known = {e: 0 for e in ENGS}
                kd = {}
                for op in ops[engname]:
                    if op.dma and op.prev_slot_op is not None:
                        p = op.prev_slot_op
                        key = id(p.dslot)
                        if kd.get(key, 0) < p.dval:
                            eng.wait_ge(p.dslot, p.dval)
                            kd[key] = p.dval
                    for d in op.deps:
                        if d.dma:
                            key = id(d.dslot)
                            if kd.get(key, 0) < d.dval:
                                eng.wait_ge(d.dslot, d.dval)
                                kd[key] = d.dval
                        else:
                            if known[d.eng] < d.sigval:
                                eng.wait_ge(sem[d.eng], d.sigval)
                                known[d.eng] = d.sigval
                    ins = op.fn(eng)
                    if op.dma:
                        if op.dinc == 1:
                            ins.then_inc(op.dslot)
                        else:
                            ins.then_inc(op.dslot, 16)
                    elif op.sig:
                        ins.then_inc(sem[op.eng], 1)
                if engname == 'sp':
                    for q in lastd:
                        for d in lastd[q]:
                            if d is not None:
                                eng.wait_ge(d.dslot, d.dval)
            return body

        with self.nc.Block(no_gpsimd_drain=True) as block:
            if ops['sp'] or anyd:
                block.sync(run('sp'))
            if ops['pe']:
                block.tensor(run('pe'))
            if ops['act']:
                block.scalar(run('act'))
            if ops['dve']:
                block.vector(run('dve'))
            if ops['pool']:
                block.gpsimd(run('pool'))
        for q in self.last_dma:
            self.last_dma[q] = [None] * NDMASLOT
        self._reset_phase()
D = 1024
KC = 8
DFF = 4096
EPS = 1e-6


class FM:
    def __init__(self, ap, t0=0):
        self.ap = ap
        self.t0 = t0


def tsl(x, a, b):
    if isinstance(x, FM):
        return FM(x.ap, x.t0 + a)
    return x[a:b, :]


class KB:
    def __init__(self, nt=2048):
        self.nc = nc = bass.Bass("TRN2", target_bir_lowering=False)
        self.P = P = Prog(nc)
        self.NT = nt
        self.stack = []
        self.uid = 0
        self.pfx = ''
        self.shared = {}
        self.split_mods = False
        self.ps = P.alloc(nc.psum_tensor("ps", [128, 4096], F32))
        self.ident = self.sb("ident_sb", [128, 128], F32)
        self.ones_bf = self.sb("ones_bf", [128, 128], BF16)
        self.eps_t = self.sb("eps_t", [128, 1], F32)
        self.rot = {}
        ident_d = self.nc.dram_tensor("ident", [128, 128], F32, kind="ExternalInput").ap()
        P.dma('sp', lambda e: e.dma_start(out=self.ident[:], in_=ident_d), writes=['ident'])
        P.op('pool', lambda e: e.memset(self.ones_bf[:], 1.0), writes=['ones_bf'])
        P.op('pool', lambda e: e.memset(self.eps_t[:], EPS), writes=['eps_t'])

    def sb(self, name, shape, dt):
        self.uid += 1
        name = "s%d_%s" % (self.uid, name)
        if self.stack:
            return self.stack[-1].enter_context(self.nc.sbuf_tensor(name, shape, dt))
        return self.P.alloc(self.nc.sbuf_tensor(name, shape, dt))

    @contextlib.contextmanager
    def phase(self):
        st = contextlib.ExitStack()
        self.stack.append(st)
        try:
            yield
            self.P.flush()
        finally:
            self.stack.pop()
            st.close()

    def din(self, name, shape, dt=F32):
        if name in ('cv',):
            if name not in self.shared:
                self.shared[name] = self.nc.dram_tensor(name, list(shape), dt, kind="ExternalInput").ap()
            return self.shared[name]
        return self.nc.dram_tensor(self.pfx + name, list(shape), dt, kind="ExternalInput").ap()

    def dint(self, name, shape, dt=F32):
        return self.nc.dram_tensor(name, list(shape), dt).ap()

    def dout(self, name, shape, dt=F32):
        return self.nc.dram_tensor(name, list(shape), dt, kind="ExternalOutput").ap()

    def ncol_ada(self):
        return 1536 if self.split_mods else 6144

    def bank(self, i, n=512, p0=0, p1=128):
        return self.ps[p0:p1, i * 512:i * 512 + n]

    def nextrot(self, name, n):
        v = self.rot.get(name, 0)
        self.rot[name] = v + 1
        return v % n

    def mods(self, cv_d, adaw_d, adab_d, modsT):
        P, nc = self.P, self.nc
        cv = self.sb("cv", [128, 8, 2], F32)
        sT = self.sb("sT", [128, 8, 2], F32)
        mrow = self.sb("mrow", [2, 6144], F32)
        adab = self.sb("adab", [2, 1536 if self.split_mods else 6144], F32)
        wst = [self.sb("wst%d" % i, [128, 8, 512], F32) for i in range(2)]
        P.dma('sp', lambda e: e.dma_start(out=cv[:], in_=cv_d), writes=['cv'])
        P.dma('sp', lambda e: e.dma_start(out=adab[:], in_=adab_d), writes=['adab'])
        P.op('act', lambda e: e.activation(out=sT[:], in_=cv[:], func=AF.Silu), reads=['cv'], writes=['sT'])
        ncg = 3 if self.split_mods else 12
        mpart = self.sb("mpart", [2, 1536], F32) if self.split_mods else None
        for cg in range(ncg):
            b = cg % 2
            src = adaw_d[:, cg * 512:(cg + 1) * 512].rearrange("(kc p) c -> p kc c", p=128)
            for h in range(2):
                P.dma('sp', lambda e, b=b, h=h, src=src: e.dma_start(out=wst[b][:, h * 4:(h + 1) * 4, :], in_=src[:, h * 4:(h + 1) * 4, :]),
                      writes=[('wst', b, h)])
            pb = 6 + (cg % 2)
            for kc in range(KC):
                P.op('pe', lambda e, b=b, kc=kc, pb=pb: e.matmul(self.bank(pb, 512, 0, 2), lhsT=sT[:, kc, :], rhs=wst[b][:, kc, :],
                                                                 start=(kc == 0), stop=(kc == KC - 1)),
                     reads=['sT', ('wst', b, kc // 4)], writes=[('ps', pb)])
            dstrow = mpart if self.split_mods else mrow
            P.op('dve', lambda e, cg=cg, pb=pb, dstrow=dstrow: e.tensor_tensor(out=dstrow[:, cg * 512:(cg + 1) * 512], in0=self.bank(pb, 512, 0, 2),
                                                                              in1=adab[:, cg * 512:(cg + 1) * 512], op=ALU.add),
                 reads=['adab'], writes=[('ps', pb), ('mrow', cg)])
        if self.split_mods:
            self.uid += 1
            mb = self.dint("modb%d" % self.uid, [2, 1536]); mg = self.dint("modg%d" % self.uid, [8, 1536])
            P.dma('sp', lambda e: e.dma_start(out=mb, in_=mpart[:]), reads=[('mrow', g_) for g_ in range(3)], writes=['modb'])
            P.coll(lambda e: e.collective_compute("AllGather", ALU.bypass, replica_groups=RG, ins=[mb.opt()], outs=[mg.opt()]), reads=['modb'], writes=['modg'])
            P.dma('sp', lambda e: e.dma_start(out=mrow[:].rearrange("t (r j) -> t r j", r=4), in_=mg.rearrange("(r t) j -> t r j", t=2)), reads=['modg'],
                  writes=[('mrow', g_) for g_ in range(12)])
        for ch in range(48):
            P.op('pe', lambda e, ch=ch: e.transpose(self.ps[:, 6 * 512 + ch * 2:6 * 512 + ch * 2 + 2], mrow[0:2, ch * 128:(ch + 1) * 128], self.ident[0:2, 0:2]),
                 reads=[('mrow', ch // 4), 'ident'], writes=[('ps', 6)])
        P.op('dve', lambda e: e.tensor_copy(out=modsT[:].rearrange("p a b -> p (a b)"), in_=self.ps[:, 6 * 512:6 * 512 + 96]),
             writes=[('ps', 6), 'modsT'])
        return modsT

    def mod_vectors(self, modsT, g1_d, g2_d, col, mv, tag):
        P = self.P
        gg = self.sb("gg" + tag, [128, 2, 8], F32)
        P.dma('sp', lambda e: e.dma_start(out=gg[:, 0, :], in_=g1_d), writes=['gg' + tag])
        P.dma('sp', lambda e: e.dma_start(out=gg[:, 1, :], in_=g2_d), writes=['gg' + tag])
        P.op('dve', lambda e: e.tensor_copy(out=mv[:].rearrange("p m k -> p (m k)"), in_=modsT[:, :, col]), reads=['modsT'], writes=['mv' + tag])
        for j, m in ((0, 1), (1, 4)):
            P.op('dve', lambda e, j=j, m=m: e.scalar_tensor_tensor(out=mv[:, m, :], in0=mv[:, m, :], scalar=1.0, in1=gg[:, j, :],
                                                                   op0=ALU.add, op1=ALU.mult),
                 reads=['gg' + tag], writes=['mv' + tag])
        key = 'mv' + tag
        return dict(A1=mv[:, 1, :], B1=mv[:, 0, :], G1=mv[:, 2, :], A2=mv[:, 4, :], B2=mv[:, 3, :], G2=mv[:, 5, :], key=key)

    def load_xT(self, x_d, ntok, xT, xkey, stage, t0=0):
        P = self.P
        if isinstance(x_d, FM):
            keys = [(xkey, g) for g in range(t0 // 512, (t0 + ntok - 1) // 512 + 1)]
            for half in range(2):
                P.dma('sp', lambda e, half=half: e.dma_start(out=xT[:, half * 4:(half + 1) * 4, t0:t0 + ntok], in_=x_d.ap[:, half * 4:(half + 1) * 4, x_d.t0:x_d.t0 + ntok]), writes=keys)
            return
        for tt in range(ntok // 128):
            sb_ = self.nextrot('stage', 2)
            P.dma('sp', lambda e, tt=tt, sb_=sb_: e.dma_start(out=stage[sb_][:], in_=x_d[tt * 128:(tt + 1) * 128, :]), writes=[('stage', sb_)])
            for half in range(2):
                pb = 4 + self.nextrot('ldbank', 2)
                for j in range(4):
                    kc = half * 4 + j
                    P.op('pe', lambda e, sb_=sb_, kc=kc, pb=pb, j=j: e.transpose(self.bank(pb)[:, j * 128:(j + 1) * 128], stage[sb_][:, kc * 128:(kc + 1) * 128], self.ident[:]),
                         reads=[('stage', sb_), 'ident'], writes=[('ps', pb)])
                eng = 'act' if half == 0 else 'dve'
                dst = xT[:, half * 4:(half + 1) * 4, t0 + tt * 128:t0 + (tt + 1) * 128]
                src = self.bank(pb).rearrange("p (a b) -> p a b", a=4)
                if eng == 'act':
                    P.op('act', lambda e, dst=dst, src=src: e.activation(out=dst, in_=src, func=AF.Copy), writes=[('ps', pb), (xkey, (t0 + tt * 128) // 512)])
                else:
                    P.op('dve', lambda e, dst=dst, src=src: e.tensor_copy(out=dst, in_=src), writes=[('ps', pb), (xkey, (t0 + tt * 128) // 512)])

    def store_x(self, xT, xkey, ntok, out_d, ostage, scale_ap=None):
        P = self.P
        if isinstance(out_d, FM):
            keys = [(xkey, g) for g in range(0, (ntok - 1) // 512 + 1)]
            for half in range(2):
                P.dma('sp', lambda e, half=half: e.dma_start(out=out_d.ap[:, half * 4:(half + 1) * 4, out_d.t0:out_d.t0 + ntok], in_=xT[:, half * 4:(half + 1) * 4, 0:ntok]), reads=keys)
            return
        for tt in range(ntok // 128):
            ob = self.nextrot('ostage', 2)
            for half in range(2):
                pb = 4 + self.nextrot('ldbank', 2)
                for j in range(4):
                    kc = half * 4 + j
                    P.op('pe', lambda e, kc=kc, pb=pb, j=j, tt=tt: e.transpose(self.bank(pb)[:, j * 128:(j + 1) * 128], xT[:, kc, tt * 128:(tt + 1) * 128], self.ident[:]),
                         reads=[(xkey, tt // 4), 'ident'], writes=[('ps', pb)])
                dst = ostage[ob][:, half * 512:(half + 1) * 512]
                if half == 0:
                    P.op('act', lambda e, dst=dst, pb=pb: e.activation(out=dst, in_=self.bank(pb), func=AF.Copy), writes=[('ps', pb), ('ostage', ob, half)])
                else:
                    P.op('dve', lambda e, dst=dst, pb=pb: e.tensor_copy(out=dst, in_=self.bank(pb)), writes=[('ps', pb), ('ostage', ob, half)])
            P.dma('sp', lambda e, ob=ob, tt=tt: e.dma_start(out=out_d[tt * 128:(tt + 1) * 128, :], in_=ostage[ob][:]),
                  reads=[('ostage', ob, 0), ('ostage', ob, 1)])

    def norm_mod(self, xT, xkey, ntok, A, B, mkey, hT, hkey, scr, t0=0, ht0=0):
        P = self.P
        tgs = min(512, ntok)
        for g in range(ntok // tgs):
            c0 = t0 + g * tgs
            h0 = ht0 + g * tgs
            pb = 6 + self.nextrot('nbank', 2)
            for kc in range(KC):
                sq = self.nextrot('sq', 2)
                P.op('pool', lambda e, kc=kc, sq=sq, c0=c0: e.tensor_tensor(out=scr['sq'][sq][:, :tgs], in0=xT[:, kc, c0:c0 + tgs], in1=xT[:, kc, c0:c0 + tgs], op=ALU.mult),
                     reads=[(xkey, c0 // 512)], writes=[('sq', sq)])
                P.op('pe', lambda e, kc=kc, sq=sq, pb=pb: e.matmul(self.bank(pb, tgs), lhsT=self.ones_bf[:], rhs=scr['sq'][sq][:, :tgs], start=(kc == 0), stop=(kc == KC - 1)),
                     reads=[('sq', sq), 'ones_bf'], writes=[('ps', pb)])
            rs = scr['rs']
            P.op('act', lambda e, pb=pb: e.activation(out=rs[:, :tgs], in_=self.bank(pb, tgs), func=AF.Ln, scale=1.0 / D, bias=self.eps_t[:]),
                 reads=['eps_t'], writes=[('ps', pb), 'rs'])
            P.op('act', lambda e: e.activation(out=rs[:, :tgs], in_=rs[:, :tgs], func=AF.Exp, scale=-0.5), writes=['rs'])
            for kc in range(KC):
                tb = self.nextrot('tmp', 2)
                P.op('dve', lambda e, kc=kc, tb=tb, c0=c0: e.scalar_tensor_tensor(out=scr['tmp'][tb][:, :tgs], in0=xT[:, kc, c0:c0 + tgs], scalar=A[:, kc:kc + 1],
                                                                                in1=rs[:, :tgs], op0=ALU.mult, op1=ALU.mult),
                     reads=[(xkey, c0 // 512), 'rs', mkey], writes=[('tmp', tb)])
                P.op('act', lambda e, kc=kc, tb=tb, h0=h0: e.activation(out=hT[:, kc, h0:h0 + tgs], in_=scr['tmp'][tb][:, :tgs], func=AF.Identity, bias=B[:, kc:kc + 1]),
                     reads=[('tmp', tb), mkey], writes=[(hkey, h0 // 512)])

    def load_w(self, dst, wkey, w_d, r0, nkc, c0, ncol):
        P = self.P
        for k in range(nkc):
            P.dma('pool', lambda e, k=k: e.dma_start(out=dst[:, k, 0:ncol], in_=w_d[r0 + k * 128:r0 + (k + 1) * 128, c0:c0 + ncol]),
                  writes=[(wkey, k)])

    def mlp(self, xT, xkey, ntok, hT, hkey, G, mkey, w1_d, w2_d, wbuf, aT, rbuf):
        P = self.P
        FB = 512
        nfb = DFF // FB
        tgs = min(512, ntok)
        ntg = ntok // tgs

        def ff1(j):
            wb = j % 2
            self.load_w(wbuf['w1'][wb], ('w1', wb), w1_d, 0, KC, j * FB, FB)
            self.load_w(wbuf['w2'][wb], ('w2', wb), w2_d, j * FB, FB // 128, 0, D)
            for fc in range(FB // 128):
                for g in range(ntg):
                    pb = self.nextrot('ff1bank', 3)
                    for kc in range(KC):
                        P.op('pe', lambda e, wb=wb, fc=fc, g=g, kc=kc, pb=pb: e.matmul(self.bank(pb, tgs), lhsT=wbuf['w1'][wb][:, kc, fc * 128:(fc + 1) * 128],
                                                                                    rhs=hT[:, kc, g * tgs:(g + 1) * tgs], start=(kc == 0), stop=(kc == KC - 1)),
                             reads=[(('w1', wb), kc), (hkey, g)], writes=[('ps', pb)])
                    rb = self.nextrot('rbuf', 2)
                    P.op('act', lambda e, pb=pb, rb=rb: e.activation(out=rbuf[rb][:, :tgs], in_=self.bank(pb, tgs), func=AF.Relu), writes=[('ps', pb), ('rbuf', rb)])
                    P.op('pool', lambda e, rb=rb, wb=wb, fc=fc, g=g: e.tensor_tensor(out=aT[wb][:, fc, g * tgs:(g + 1) * tgs], in0=rbuf[rb][:, :tgs], in1=rbuf[rb][:, :tgs], op=ALU.mult),
                         reads=[('rbuf', rb)], writes=[('aT', wb, fc, g)])

        def ff2(j):
            wb = j % 2
            for dc in range(KC):
                for g in range(ntg):
                    pb = 3 + self.nextrot('ff2bank', 3)
                    nf = FB // 128
                    for fc in range(nf):
                        P.op('pe', lambda e, wb=wb, fc=fc, g=g, dc=dc, pb=pb: e.matmul(self.bank(pb, tgs), lhsT=wbuf['w2'][wb][:, fc, dc * 128:(dc + 1) * 128],
                                                                                    rhs=aT[wb][:, fc, g * tgs:(g + 1) * tgs], start=(fc == 0), stop=(fc == nf - 1)),
                             reads=[(('w2', wb), fc), ('aT', wb, fc, g)], writes=[('ps', pb)])
                    P.op('dve', lambda e, dc=dc, g=g, pb=pb: e.scalar_tensor_tensor(out=xT[:, dc, g * tgs:(g + 1) * tgs], in0=self.bank(pb, tgs), scalar=G[:, dc:dc + 1],
                                                                                  in1=xT[:, dc, g * tgs:(g + 1) * tgs], op0=ALU.mult, op1=ALU.add),
                         reads=[mkey], writes=[('ps', pb), (xkey, g)])

        ff1(0)
        for j in range(nfb):
            if j + 1 < nfb:
                ff1(j + 1)
            ff2(j)


def load_cast(kb, dst, key, src_d):
    kb.P.dma('pool', lambda e: e.dma_start(out=dst, in_=src_d), writes=[key])


def proj_fm(kb, hT, hkey, t0, n, W, wkey, col0, pb):
    for kc in range(KC):
        kb.P.op('pe', lambda e, kc=kc: e.matmul(kb.bank(pb, n), lhsT=W[:, kc, col0:col0 + 128], rhs=hT[:, kc, t0:t0 + n],
                                               start=(kc == 0), stop=(kc == KC - 1)),
                reads=[(wkey, kc), (hkey, t0 // 512)], writes=[('ps', pb)])


def qk_norm_rope(kb, pb, n, gain, rope, qscale, out_ap, outkeys, C):
    P = kb.P
    r = kb.nextrot('qkr', 2)
    kg, k2, rs, t1 = C['kg'][r], C['k2'][r], C['rs2'][r], C['t1'][r]
    P.op('act', lambda e: e.activation(out=kg[:, :n], in_=kb.bank(pb, n), func=AF.Copy, scale=gain[:, 0:1]), reads=['gains'], writes=[('ps', pb), ('kg', r)])
    P.op('act', lambda e: e.activation(out=k2[:, :n], in_=kb.bank(pb, n), func=AF.Square), writes=[('ps', pb), ('k2', r)])
    P.op('pe', lambda e: e.matmul(kb.bank(2, n), lhsT=C['bones'][:], rhs=k2[:, :n], start=True, stop=True), reads=[('k2', r), 'bones'], writes=[('ps', 2)])
    if rope is not None:
        P.op('pe', lambda e: e.matmul(kb.bank(3, n), lhsT=C['rmat'][:], rhs=kg[:, :n], start=True, stop=True), reads=[('kg', r), 'rmat'], writes=[('ps', 3)])
    P.op('act', lambda e: e.activation(out=rs[:, :n], in_=kb.bank(2, n), func=AF.Ln, scale=1.0 / 64, bias=kb.eps_t[:]), reads=['eps_t'], writes=[('ps', 2), ('rs2', r)])
    if qscale:
        P.op('act', lambda e: e.activation(out=rs[:, :n], in_=rs[:, :n], func=AF.Exp, scale=-0.5, bias=C['lnq'][:]), reads=['lnq'], writes=[('rs2', r)])
    else:
        P.op('act', lambda e: e.activation(out=rs[:, :n], in_=rs[:, :n], func=AF.Exp, scale=-0.5), writes=[('rs2', r)])
    if rope is not None:
        cos_ap, sin_ap, rkey = rope
        P.op('dve', lambda e: e.tensor_tensor(out=t1[:, :n], in0=kg[:, :n], in1=cos_ap, op=ALU.mult), reads=[('kg', r), rkey], writes=[('t1', r)])
        P.op('dve', lambda e: e.tensor_tensor(out=kg[:, :n], in0=kb.bank(3, n), in1=sin_ap, op=ALU.mult), reads=[rkey], writes=[('ps', 3), ('kg', r)])
        P.op('dve', lambda e: e.tensor_tensor(out=t1[:, :n], in0=t1[:, :n], in1=kg[:, :n], op=ALU.add), reads=[('kg', r)], writes=[('t1', r)])
        P.op('dve', lambda e: e.tensor_tensor(out=out_ap, in0=t1[:, :n], in1=rs[:, :n], op=ALU.mult), reads=[('t1', r), ('rs2', r)], writes=outkeys)
    else:
        P.op('dve', lambda e: e.tensor_tensor(out=out_ap, in0=kg[:, :n], in1=rs[:, :n], op=ALU.mult), reads=[('kg', r), ('rs2', r)], writes=outkeys)


def attention(kb, q_ap, qkeys, NQ, tiles, dst_a, dst_b, dstkeys, C):
    P = kb.P
    oset = kb.nextrot('oset', 2)
    o0 = 4 + 2 * oset
    nt = len(tiles)
    ssets = []

    def qk(i):
        t = tiles[i]
        ss = kb.nextrot('sset', 2)
        ssets.append(ss)
        s0 = 2 * ss
        P.op('pe', lambda e: e.matmul(kb.bank(s0, NQ), lhsT=t['ka'], rhs=q_ap[0:64, :], start=True, stop=True), reads=t['keys'] + qkeys, writes=[('ps', s0)])
        P.op('pe', lambda e: e.matmul(kb.bank(s0 + 1, NQ), lhsT=t['kb'], rhs=q_ap[64:128, :], start=True, stop=True), reads=t['keys'] + qkeys, writes=[('ps', s0 + 1)])

    qk(0)
    for i in range(nt):
        if i + 1 < nt:
            qk(i + 1)
        t = tiles[i]
        s0 = 2 * ssets[i]
        pbuf = kb.nextrot('pbuf', 3)
        pb_ = C['pbuf'][pbuf]
        src = kb.ps[:, s0 * 512:(s0 + 2) * 512].rearrange("p (h n) -> p h n", h=2)[:, :, 0:NQ]
        if t.get('bias_a') is not None:
            sb_ = C['sbias'][kb.nextrot('sbias', 2)]
            P.op('dve', lambda e, sb_=sb_, t=t, s0=s0: e.tensor_tensor(out=sb_[:, 0, 0:NQ], in0=kb.bank(s0, NQ), in1=t['bias_a'], op=ALU.add), reads=t['bkeys'], writes=[('ps', s0), ('sbias', id(sb_), 0)])
            P.op('dve', lambda e, sb_=sb_, t=t, s0=s0: e.tensor_tensor(out=sb_[:, 1, 0:NQ], in0=kb.bank(s0 + 1, NQ), in1=t['bias_b'], op=ALU.add), reads=t['bkeys'], writes=[('ps', s0 + 1), ('sbias', id(sb_), 1)])
            P.op('act', lambda e, sb_=sb_, pb_=pb_: e.activation(out=pb_[:, :, 0:NQ], in_=sb_[:, :, 0:NQ], func=AF.Exp), reads=[('sbias', id(sb_), 0), ('sbias', id(sb_), 1)], writes=[('pbuf', pbuf)])
        else:
            P.op('act', lambda e, src=src, pb_=pb_: e.activation(out=pb_[:, :, 0:NQ], in_=src, func=AF.Exp), writes=[('ps', s0), ('ps', s0 + 1), ('pbuf', pbuf)])
        P.op('pe', lambda e, t=t, pb_=pb_, i=i: e.matmul(kb.bank(o0, NQ), lhsT=t['va'], rhs=pb_[:, 0, 0:NQ], start=(i == 0), stop=(i == nt - 1)), reads=t['keys'] + [('pbuf', pbuf)], writes=[('ps', o0)])
        P.op('pe', lambda e, t=t, pb_=pb_, i=i: e.matmul(kb.bank(o0 + 1, NQ), lhsT=t['vb'], rhs=pb_[:, 1, 0:NQ], start=(i == 0), stop=(i == nt - 1)), reads=t['keys'] + [('pbuf', pbuf)], writes=[('ps', o0 + 1)])
    rcr = kb.nextrot('rc', 2)
    rc = C['rc'][rcr]
    P.op('dve', lambda e: e.reciprocal(out=rc[64:128, 0:NQ], in_=kb.bank(o0, NQ)[64:128, :]), writes=[('ps', o0), ('rc', rcr, 0)])
    P.op('dve', lambda e: e.tensor_tensor(out=dst_a, in0=kb.bank(o0, NQ)[0:64, :], in1=rc[64:128, 0:NQ], op=ALU.mult), reads=[('rc', rcr, 0)], writes=[('ps', o0)] + dstkeys)
    P.op('dve', lambda e: e.reciprocal(out=rc[0:64, 0:NQ], in_=kb.bank(o0 + 1, NQ)[0:64, :]), writes=[('ps', o0 + 1), ('rc', rcr, 1)])
    P.op('dve', lambda e: e.tensor_tensor(out=dst_b, in0=kb.bank(o0 + 1, NQ)[64:128, :], in1=rc[0:64, 0:NQ], op=ALU.mult), reads=[('rc', rcr, 1)], writes=[('ps', o0 + 1)] + dstkeys)


def attn_scratch(kb, with_bias=False):
    C = dict(pbuf=[kb.sb("pbuf%d" % i, [128, 2, 512], BF16) for i in range(3)],
             rc=[kb.sb("rc%d" % i, [128, 512], F32) for i in range(2)])
    if with_bias:
        C['sbias'] = [kb.sb("sbias%d" % i, [128, 2, 512], F32) for i in range(2)]
    return C


def qk_scratch(kb, C):
    C['kg'] = [kb.sb("kg%d" % i, [128, 512], BF16) for i in range(2)]
    C['k2'] = [kb.sb("k2%d" % i, [128, 512], BF16) for i in range(2)]
    C['rs2'] = [kb.sb("rs2%d" % i, [128, 512], F32) for i in range(2)]
    C['t1'] = [kb.sb("t1%d" % i, [128, 512], F32) for i in range(2)]


def norm_scratch(kb):
    return dict(sq=[kb.sb("sq%d" % i, [128, 512], BF16) for i in range(2)], rs=kb.sb("rs", [128, 512], F32),
                tmp=[kb.sb("tmp%d" % i, [128, 512], F32) for i in range(2)])


def mlp_bufs(kb, ntok):
    wbuf = dict(w1=[kb.sb("w1b%d" % i, [128, 8, 512], BF16) for i in range(3)], w2=[kb.sb("w2b%d" % i, [128, 4, 1024], BF16) for i in range(3)])
    aT = [kb.sb("aT%d" % i, [128, 4, ntok], BF16) for i in range(2)]
    rbuf = [kb.sb("rbuf%d" % i, [128, 512], F32) for i in range(2)]
    return wbuf, aT, rbuf


def resid_proj(kb, srcT, skey, t0, n, W, wkey, xT, xkey, xt0, G, mkey, bG=None):
    P = kb.P
    for dc in range(KC):
        pb = kb.nextrot('projbank', 2)
        proj_fm(kb, srcT, skey, t0, n, W, wkey, dc * 128, pb)
        P.op('dve', lambda e, dc=dc, pb=pb: e.scalar_tensor_tensor(out=xT[:, dc, xt0:xt0 + n], in0=kb.bank(pb, n), scalar=G[:, dc:dc + 1],
                                                                 in1=xT[:, dc, xt0:xt0 + n], op0=ALU.mult, op1=ALU.add),
             reads=[mkey], writes=[('ps', pb), (xkey, xt0 // 512)])
        if bG is not None:
            P.op('act', lambda e, dc=dc: e.activation(out=xT[:, dc, xt0:xt0 + n], in_=xT[:, dc, xt0:xt0 + n], func=AF.Identity, bias=bG[:, dc:dc + 1]),
                 reads=['bG'], writes=[(xkey, xt0 // 512)])


def build_l0(kb=None, io=None):
    own = kb is None
    if own:
        kb = KB()
    io = io or {}
    kb.pfx = '' if own else 'l0_'
    P, nc = kb.P, kb.nc
    NTOK = 2048
    NG0 = 4 if io.get("kv_gather") else 16
    xb_d = io.get("xb") or kb.din("xb", [NG0 * 512, D]); ctx_d = io.get("ctx") or kb.din("ctx", [256, D])
    cv_d = kb.din("cv", [128, 8, 2]); adaw_d = kb.din("adaw", [D, kb.ncol_ada()]); adab_d = kb.din("adab", [2, kb.ncol_ada()])
    g1_d = kb.din("g1", [128, 8]); g2_d = kb.din("g2", [128, 8])
    wqkv_d = kb.din("wqkv", [D, 1536]); gains_d = kb.din("gains", [128, 2])
    cos_d = kb.din("cos", [NG0, 128, 512]); sin_d = kb.din("sin", [NG0, 128, 512])
    wo_d = kb.din("wo", [D, D]); w1_d = kb.din("w1", [D, DFF]); w2_d = kb.din("w2", [DFF, D])
    rmat_d = kb.din("rmat", [128, 128]); bones_d = kb.din("bones", [128, 128])
    out_d = io.get("out") or kb.dout("out", [NTOK, D]); hctx_d = io.get("hctx") or kb.dout("hctx", [256, D])
    mid_d = io.get("mid") or out_d

    modsT = kb.sb("modsT", [128, 48, 2], F32)
    mvL = kb.sb("mvL", [128, 6, 8], F32); mvC = kb.sb("mvC", [128, 6, 8], F32)
    C = dict(rmat=kb.sb("rmat", [128, 128], BF16), bones=kb.sb("bones", [128, 128], BF16), lnq=kb.sb("lnq", [128, 1], F32))
    gains = kb.sb("gains", [128, 2], F32)
    with kb.phase():
        load_cast(kb, C['rmat'][:], 'rmat', rmat_d)
        load_cast(kb, C['bones'][:], 'bones', bones_d)
        P.op('pool', lambda e: e.memset(C['lnq'][:], float(np.log(0.125))), writes=['lnq'])
        P.dma('sp', lambda e: e.dma_start(out=gains[:], in_=gains_d), writes=['gains'])
        kb.mods(cv_d, adaw_d, adab_d, modsT)
        mL = kb.mod_vectors(modsT, g1_d, g2_d, 0, mvL, "L")
        mC = kb.mod_vectors(modsT, g1_d, g2_d, 1, mvC, "C")
    qgain, kgain = gains[:, 0:1], gains[:, 1:2]

    with kb.phase():
        KT = kb.sb("KT", [128, 2, 8448], BF16)
        Ve = kb.sb("Ve", [128, 66, 384], BF16)
        P.op('pool', lambda e: e.memset(Ve[:].rearrange("p t (a b c) -> p (t a) b c", a=2, b=3, c=64)[:, :, 1, :], 1.0), writes=['Ve_ones'])

        def kv_tiles(tile_ids, pr):
            out = []
            for kt in tile_ids:
                out.append(dict(ka=KT[0:64, pr, kt * 128:(kt + 1) * 128], kb=KT[64:128, pr, kt * 128:(kt + 1) * 128],
                                va=Ve[:, kt, pr * 192:pr * 192 + 128], vb=Ve[:, kt, pr * 192 + 64:pr * 192 + 192],
                                keys=[('KT', kt // 4), ('Ve', kt), 'Ve_ones']))
            return out

        def produce_kv(hT, hkey, n, g, wqkv, rope):
            for pr in range(2):
                pb = kb.nextrot('projbank', 2)
                proj_fm(kb, hT, hkey, 0, n, wqkv, 'wqkv', 1024 + pr * 128, pb)
                qk_norm_rope(kb, pb, n, kgain, rope, False, KT[:, pr, g * 512:g * 512 + n], [('KT', g)], C)
            for tt in range(n // 128):
                pb = kb.nextrot('projbank', 2)
                for kc in range(KC):
                    P.op('pe', lambda e, kc=kc, tt=tt, pb=pb: e.matmul(kb.bank(pb, 256), lhsT=hT[:, kc, tt * 128:(tt + 1) * 128], rhs=wqkv[:, kc, 1280:1536],
                                                                      start=(kc == 0), stop=(kc == KC - 1)),
                         reads=[('wqkv', kc), (hkey, 0)], writes=[('ps', pb)])
                kt = g * 4 + tt
                dst = Ve[:, kt, :].rearrange("p (a b c) -> p a b c", a=2, b=3, c=64)[:, :, ::2, :]
                src = kb.bank(pb, 256).rearrange("p (a b c) -> p a b c", a=2, b=2, c=64)
                P.op('dve', lambda e, dst=dst, src=src: e.tensor_copy(out=dst, in_=src), writes=[('ps', pb), ('Ve', kt)])

        with kb.phase():
            cT = kb.sb("cT", [128, 8, 256], F32)
            stage = [kb.sb("stage%d" % i, [128, 1024], F32) for i in range(2)]
            scr = norm_scratch(kb)
            with kb.phase():
                hcT = kb.sb("hcT", [128, 8, 256], BF16)
                QcT = kb.sb("QcT", [128, 8, 256], BF16)
                OcT = kb.sb("OcT", [128, 8, 256], BF16)
                qk_scratch(kb, C)
                C.update(attn_scratch(kb))
                wqkv = kb.sb("wqkv", [128, 8, 1536], BF16)
                wo = kb.sb("wo", [128, 8, 1024], BF16)
                kb.load_w(wqkv, 'wqkv', wqkv_d, 0, KC, 0, 1536)
                kb.load_w(wo, 'wo', wo_d, 0, KC, 0, 1024)
                kb.load_xT(ctx_d, 256, cT, 'cT', stage)
                kb.norm_mod(cT, 'cT', 256, mC['A1'], mC['B1'], mC['key'], hcT, 'hcT', scr)
                produce_kv(hcT, 'hcT', 256, 16, wqkv, None)
                for c in range(8):
                    pb = kb.nextrot('projbank', 2)
                    proj_fm(kb, hcT, 'hcT', 0, 256, wqkv, 'wqkv', c * 128, pb)
                    qk_norm_rope(kb, pb, 256, qgain, None, True, QcT[:, c, :], [('QcT', c)], C)
                for c in range(8):
                    attention(kb, QcT[:, c, :], [('QcT', c)], 256, kv_tiles([64, 65], c // 4), OcT[0:64, c, :], OcT[64:128, c, :], [('OcT', 0)], C)
                resid_proj(kb, OcT, 'OcT', 0, 256, wo, 'wo', cT, 'cT', 0, mC['G1'], mC['key'])
            with kb.phase():
                hc2 = kb.sb("hc2", [128, 8, 256], BF16)
                kb.norm_mod(cT, 'cT', 256, mC['A2'], mC['B2'], mC['key'], hc2, 'hc2', scr)
                wbuf, aT, rbuf = mlp_bufs(kb, 256)
                kb.mlp(cT, 'cT', 256, hc2, 'hc2', mC['G2'], mC['key'], w1_d, w2_d, wbuf, aT, rbuf)
                kb.store_x(cT, 'cT', 256, hctx_d, stage)

        with kb.phase():
            QT = kb.sb("QT", [128, 8, NTOK], BF16)
            with kb.phase():
                xtmp = kb.sb("xtmp", [128, 8, 512], F32)
                hTt = kb.sb("hTt", [128, 8, 512], BF16)
                stage = [kb.sb("stage%d" % i, [128, 1024], F32) for i in range(2)]
                scr = norm_scratch(kb)
                qk_scratch(kb, C)
                wqkv = kb.sb("wqkv", [128, 8, 1536], BF16)
                cs = [kb.sb("cs%d" % i, [128, 2, 512], F32) for i in range(2)]
                kb.load_w(wqkv, 'wqkv', wqkv_d, 0, KC, 0, 1536)
                for g in range(NG0):
                    kb.load_xT(tsl(xb_d, g * 512, (g + 1) * 512), 512, xtmp, 'xtmp', stage)
                    kb.norm_mod(xtmp, 'xtmp', 512, mL['A1'], mL['B1'], mL['key'], hTt, 'hTt', scr)
                    cb = g % 2
                    P.dma('sp', lambda e, g=g, cb=cb: e.dma_start(out=cs[cb][:, 0, :], in_=cos_d[g]), writes=[('cs', cb)])
                    P.dma('sp', lambda e, g=g, cb=cb: e.dma_start(out=cs[cb][:, 1, :], in_=sin_d[g]), writes=[('cs', cb)])
                    rope = (cs[cb][:, 0, :], cs[cb][:, 1, :], ('cs', cb))
                    produce_kv(hTt, 'hTt', 512, g, wqkv, rope)
                    if g < 4:
                        for c in range(8):
                            pb = kb.nextrot('projbank', 2)
                            proj_fm(kb, hTt, 'hTt', 0, 512, wqkv, 'wqkv', c * 128, pb)
                            qk_norm_rope(kb, pb, 512, qgain, rope, True, QT[:, c, g * 512:(g + 1) * 512], [('QT', c, g)], C)
            if io.get("kv_gather"):
                ktb = kb.dint("ktb", [256, 2048], BF16); ktg = kb.dint("ktg", [1024, 2048], BF16)
                veb = [kb.dint("veb%d" % h, [1024, 384], BF16) for h in range(2)]
                veg = [kb.dint("veg%d" % h, [4096, 384], BF16) for h in range(2)]
                with kb.phase():
                    P.dma('sp', lambda e: e.dma_start(out=ktb.rearrange("(p a) n -> p a n", a=2), in_=KT[:, :, 0:2048]), reads=[('KT', g_) for g_ in range(4)], writes=['ktb'])
                    for h in range(2):
                        P.dma('sp', lambda e, h=h: e.dma_start(out=veb[h].rearrange("(p t) c -> p t c", t=8), in_=Ve[:, h * 8:(h + 1) * 8, :]),
                              reads=[('Ve', kt_) for kt_ in range(16)] + ['Ve_ones'], writes=[('veb', h)])
                    P.coll([lambda e: e.collective_compute("AllGather", ALU.bypass, replica_groups=RG, ins=[ktb.opt()], outs=[ktg.opt()]),
                            lambda e: e.collective_compute("AllGather", ALU.bypass, replica_groups=RG, ins=[veb[0].opt()], outs=[veg[0].opt()]),
                            lambda e: e.collective_compute("AllGather", ALU.bypass, replica_groups=RG, ins=[veb[1].opt()], outs=[veg[1].opt()])],
                           reads=['ktb', ('veb', 0), ('veb', 1)], writes=['ktg', ('veg', 0), ('veg', 1)])
                    for r in range(4):
                        P.dma('sp', lambda e, r=r: e.dma_start(out=KT[:, :, r * 2048:(r + 1) * 2048], in_=ktg[r * 256:(r + 1) * 256, :].rearrange("(p a) n -> p a n", a=2)),
                              reads=['ktg'], writes=[('KT', g_) for g_ in range(r * 4, r * 4 + 4)])
                        for h in range(2):
                            P.dma('sp', lambda e, r=r, h=h: e.dma_start(out=Ve[:, r * 16 + h * 8:r * 16 + (h + 1) * 8, :], in_=veg[h][r * 1024:(r + 1) * 1024, :].rearrange("(p t) c -> p t c", t=8)),
                                  reads=[('veg', h)], writes=[('Ve', kt_) for kt_ in range(r * 16 + h * 8, r * 16 + (h + 1) * 8)])
            with kb.phase():
                OT = kb.sb("OT", [128, 8, NTOK], BF16)
                with kb.phase():
                    C.update(attn_scratch(kb))
                    for qg in range(4):
                        for c in range(8):
                            attention(kb, QT[:, c, qg * 512:(qg + 1) * 512], [('QT', c, qg)], 512, kv_tiles(list(range(66)), c // 4),
                                      OT[0:64, c, qg * 512:(qg + 1) * 512], OT[64:128, c, qg * 512:(qg + 1) * 512], [('OT', qg)], C)
                with kb.phase():
                    xtmp = kb.sb("xtmp", [128, 8, 512], F32)
                    stage = [kb.sb("stage%d" % i, [128, 1024], F32) for i in range(2)]
                    wo = kb.sb("wo", [128, 8, 1024], BF16)
                    ostage = [kb.sb("ostage%d" % i, [128, 1024], F32) for i in range(2)]
                    kb.load_w(wo, 'wo', wo_d, 0, KC, 0, 1024)
                    for g in range(4):
                        kb.load_xT(tsl(xb_d, g * 512, (g + 1) * 512), 512, xtmp, 'xtmp', stage)
                        resid_proj(kb, OT, 'OT', g * 512, 512, wo, 'wo', xtmp, 'xtmp', 0, mL['G1'], mL['key'])
                        kb.store_x(xtmp, 'xtmp', 512, tsl(mid_d, g * 512, (g + 1) * 512), ostage)
    mlp_tail(kb, mid_d, out_d, NTOK, mL, w1_d, w2_d)
    if own:
        P.close()
    return nc


def mlp_tail(kb, src_d, out_d, ntok, mL, w1_d, w2_d, final=None):
    with kb.phase():
        xT = kb.sb("xT", [128, 8, ntok], F32)
        with kb.phase():
            stage = [kb.sb("stage%d" % i, [128, 1024], F32) for i in range(2)]
            kb.load_xT(src_d, ntok, xT, 'xT', stage)
        with kb.phase():
            hT = kb.sb("hT", [128, 8, ntok], BF16)
            scr = norm_scratch(kb)
            kb.norm_mod(xT, 'xT', ntok, mL['A2'], mL['B2'], mL['key'], hT, 'hT', scr)
            wbuf, aT, rbuf = mlp_bufs(kb, ntok)
            kb.mlp(xT, 'xT', ntok, hT, 'hT', mL['G2'], mL['key'], w1_d, w2_d, wbuf, aT, rbuf)
        with kb.phase():
            ostage = [kb.sb("ostage%d" % i, [128, 1024], F32) for i in range(2)]
            if final is not None:
                yT = kb.sb("yT", [128, 8, ntok], F32)
                scr = norm_scratch(kb)
                kb.norm_mod(xT, 'xT', ntok, final[0], final[1], final[2], yT, 'yT', scr)
                kb.store_x(yT, 'yT', ntok, out_d, ostage)
            else:
                kb.store_x(xT, 'xT', ntok, out_d, ostage)


def fm(v):
    return np.ascontiguousarray(np.asarray(v, np.float32).reshape(8, 128).T)


def common_inputs(inp, layer, b):
    cv = np.stack([fm(inp['c'][b]), fm(inp['c_ctx'])], axis=-1)
    return dict(ident=np.eye(128, dtype=np.float32), cv=np.ascontiguousarray(cv), adaw=np.ascontiguousarray(inp['ada_w'][layer]),
                adab=np.ascontiguousarray(np.stack([inp['ada_b'][layer]] * 2)), g1=fm(inp['norm1_g'][layer]), g2=fm(inp['norm2_g'][layer]),
                w1=np.ascontiguousarray(inp['mlp_w1'][layer]), w2=np.ascontiguousarray(inp['mlp_w2'][layer]))


def rope_tables(order):
    t = np.asarray(order)
    row = (t // 64).astype(np.float32)
    col = (t % 64).astype(np.float32)
    inv = (10000.0 ** (-np.arange(16, dtype=np.float32) / 16)).astype(np.float32)
    ang = np.concatenate([row[:, None] * inv, col[:, None] * inv], axis=-1).astype(np.float32)
    idx = (np.arange(128) % 64) // 2
    a = ang[:, idx].T
    cos = np.cos(a).astype(np.float32).reshape(128, 16, 512).transpose(1, 0, 2)
    sin = np.sin(a).astype(np.float32).reshape(128, 16, 512).transpose(1, 0, 2)
    return np.ascontiguousarray(cos), np.ascontiguousarray(sin)


def gqa_chunk_heads():
    return [(c, 4 + c) if c < 4 else (8 + c - 4, 12 + c - 4) for c in range(8)]


def prep_l0(inp, b, q):
    d = common_inputs(inp, 0, b)
    order = np.concatenate([np.arange(q * 2048, 8192), np.arange(0, q * 2048)])
    d['xb'] = np.ascontiguousarray(inp['x'][b][order])
    d['ctx'] = np.ascontiguousarray(inp['ctx'][b])
    wqkv = inp['at_w_qkv'][0]
    qcols = np.concatenate([np.concatenate([np.arange(ha * 64, ha * 64 + 64), np.arange(hb * 64, hb * 64 + 64)]) for ha, hb in gqa_chunk_heads()])
    d['wqkv'] = np.ascontiguousarray(np.concatenate([wqkv[:, qcols], wqkv[:, 1024:]], axis=1))
    d['wo'] = np.ascontiguousarray(inp['at_w_o'][0][qcols, :])
    d['gains'] = np.ascontiguousarray(np.stack([np.tile(inp['at_q_g'][0], 2), np.tile(inp['at_k_g'][0], 2)], axis=1))
    d['cos'], d['sin'] = rope_tables(order)
    rmat = np.zeros((128, 128), np.float32)
    for i in range(64):
        rmat[2 * i + 1, 2 * i] = -1.0
        rmat[2 * i, 2 * i + 1] = 1.0
    d['rmat'] = rmat
    bones = np.zeros((128, 128), np.float32)
    bones[:64, :64] = 1.0
    bones[64:, 64:] = 1.0
    d['bones'] = bones
    return d


NA_CLASS = [0, 1] + [2] * 12 + [3, 4]
NA_ST = list(range(14)) + [12, 13]


def na_unit(kb, q_ap, qkeys, kt_ap, vt, tab_ap, tkey, st, dst, dstkeys, kvkeys, C):
    P = kb.P
    u = kb.nextrot('naunit', 2)
    b0 = 3 * u
    ob = 6 + u
    for j in range(7):
        kt = st + j
        P.op('pe', lambda e, j=j, kt=kt: e.matmul(kb.ps[:, b0 * 512 + j * 128:b0 * 512 + (j + 1) * 128], lhsT=kt_ap(kt), rhs=q_ap, start=True, stop=True),
             reads=kvkeys + qkeys, writes=[('ps', b0 + j // 4)])
    for j in range(2):
        kt = 20 + j
        P.op('pe', lambda e, j=j, kt=kt: e.matmul(kb.ps[:, (b0 + 2) * 512 + j * 128:(b0 + 2) * 512 + (j + 1) * 128], lhsT=kt_ap(kt), rhs=q_ap, start=True, stop=True),
             reads=kvkeys + qkeys, writes=[('ps', b0 + 2)])
    sbr = kb.nextrot('nasb', 2)
    sb_ = C['nasb'][sbr]
    P.op('dve', lambda e: e.tensor_tensor(out=sb_[:], in0=kb.ps[:, b0 * 512:b0 * 512 + 896], in1=tab_ap, op=ALU.add), reads=[tkey], writes=[('ps', b0), ('ps', b0 + 1), ('nasb', sbr)])
    pr = kb.nextrot('napw', 3)
    pw, pc = C['napw'][pr], C['napc'][pr]
    P.op('act', lambda e: e.activation(out=pw[:], in_=sb_[:], func=AF.Exp), reads=[('nasb', sbr)], writes=[('napw', pr)])
    P.op('act', lambda e: e.activation(out=pc[:], in_=kb.ps[:, (b0 + 2) * 512:(b0 + 2) * 512 + 256], func=AF.Exp), writes=[('ps', b0 + 2), ('napc', pr)])
    for j in range(9):
        kt = st + j if j < 7 else 20 + (j - 7)
        rhs = pw[:, j * 128:(j + 1) * 128] if j < 7 else pc[:, (j - 7) * 128:(j - 6) * 128]
        P.op('pe', lambda e, j=j, kt=kt, rhs=rhs: e.matmul(kb.bank(ob, 128), lhsT=vt(kt), rhs=rhs, start=(j == 0), stop=(j == 8)),
             reads=kvkeys + [('napw', pr), ('napc', pr)], writes=[('ps', ob)])
    return ob


def build_l1(kb=None, io=None):
    own = kb is None
    if own:
        kb = KB()
    io = io or {}
    kb.pfx = '' if own else 'l1_'
    P, nc = kb.P, kb.nc
    NTOK = 2048
    NH = 2560
    xh_d = io.get("xh") or kb.din("xh", [NH, D]); ctx_d = io.get("ctx") or kb.din("ctx", [256, D])
    cv_d = kb.din("cv", [128, 8, 2]); adaw_d = kb.din("adaw", [D, kb.ncol_ada()]); adab_d = kb.din("adab", [2, kb.ncol_ada()])
    g1_d = kb.din("g1", [128, 8]); g2_d = kb.din("g2", [128, 8])
    wqkv_d = kb.din("wqkv", [8, D, 384]); tab_d = kb.din("tab", [5, 8, 128, 2 * 7 * 128])
    wo_d = kb.din("wo", [D, D]); w1_d = kb.din("w1", [D, DFF]); w2_d = kb.din("w2", [DFF, D])
    out_d = io.get("out") or kb.dout("out", [NTOK, D])
    mid_d = io.get("mid") or out_d

    modsT = kb.sb("modsT", [128, 48, 2], F32)
    mvL = kb.sb("mvL", [128, 6, 8], F32); mvC = kb.sb("mvC", [128, 6, 8], F32)
    C = {}
    with kb.phase():
        if io.get('pre1'):
            io['pre1']()
        kb.mods(cv_d, adaw_d, adab_d, modsT)
        mL = kb.mod_vectors(modsT, g1_d, g2_d, 0, mvL, "L")
        mC = kb.mod_vectors(modsT, g1_d, g2_d, 1, mvC, "C")
        if io.get('pre2'):
            io['pre2']()
    with kb.phase():
        hT = kb.sb("hT", [128, 8, NH], BF16)
        hcT = kb.sb("hcT", [128, 8, 256], BF16)
        OT = kb.sb("OT", [128, 8, NTOK], BF16)
        with kb.phase():
            xtmp = kb.sb("xtmp", [128, 8, 512], F32)
            stage = [kb.sb("stage%d" % i, [128, 1024], F32) for i in range(2)]
            scr = norm_scratch(kb)
            for g in range(5):
                kb.load_xT(tsl(xh_d, g * 512, (g + 1) * 512), 512, xtmp, 'xtmp', stage)
                kb.norm_mod(xtmp, 'xtmp', 512, mL['A1'], mL['B1'], mL['key'], hT, 'hT', scr, t0=0, ht0=g * 512)
            kb.load_xT(ctx_d, 256, xtmp, 'xtmp', stage)
            kb.norm_mod(xtmp, 'xtmp', 256, mC['A1'], mC['B1'], mC['key'], hcT, 'hcT', scr)
        with kb.phase():
            C['rc'] = [kb.sb("rc%d" % i, [128, 512], F32) for i in range(2)]
            C['nasb'] = [kb.sb("nasb%d" % i, [128, 896], F32) for i in range(2)]
            C['napw'] = [kb.sb("napw%d" % i, [128, 896], BF16) for i in range(3)]
            C['napc'] = [kb.sb("napc%d" % i, [128, 256], BF16) for i in range(3)]
            wc = [kb.sb("wc%d" % i, [128, 8, 384], BF16) for i in range(2)]
            QTc = [kb.sb("QTc%d" % i, [128, NTOK], BF16) for i in range(2)]
            KTc = [kb.sb("KTc%d" % i, [128, NH + 256], BF16) for i in range(2)]
            Vec = [kb.sb("Vec%d" % i, [128, 22, 192], BF16) for i in range(2)]
            tabI = [kb.sb("tabI%d" % i, [128, 2, 7, 128], F32) for i in range(2)]
            tabS = [kb.sb("tabS%d" % i, [128, 2, 7, 128], F32) for i in range(2)]
            for i in range(2):
                P.op('pool', lambda e, i=i: e.memset(Vec[i][:, :, 64:128], 1.0), writes=[('Vones', i)])
            for c in range(8):
                b = c % 2
                kb.load_w(wc[b], ('wc', b), wqkv_d[c], 0, KC, 0, 384)
                P.dma('sp', lambda e, c=c, b=b: e.dma_start(out=tabI[b][:].rearrange("p a j q -> p (a j q)"), in_=tab_d[2, c]), writes=[('tabI', b)])
                for g in range(4):
                    pb = kb.nextrot('projbank', 2)
                    proj_fm(kb, hT, 'hT', 256 + g * 512, 512, wc[b], ('wc', b), 0, pb)
                    P.op('act', lambda e, g=g, b=b, pb=pb: e.activation(out=QTc[b][:, g * 512:(g + 1) * 512], in_=kb.bank(pb), func=AF.Copy, scale=0.125),
                         writes=[('ps', pb), ('QTc', b, g)])
                for g in range(5):
                    pb = kb.nextrot('projbank', 2)
                    proj_fm(kb, hT, 'hT', g * 512, 512, wc[b], ('wc', b), 128, pb)
                    P.op('act', lambda e, g=g, b=b, pb=pb: e.activation(out=KTc[b][:, g * 512:(g + 1) * 512], in_=kb.bank(pb), func=AF.Copy),
                         writes=[('ps', pb), ('KTc', b, g)])
                pb = kb.nextrot('projbank', 2)
                proj_fm(kb, hcT, 'hcT', 0, 256, wc[b], ('wc', b), 128, pb)
                P.op('act', lambda e, b=b, pb=pb: e.activation(out=KTc[b][:, NH:NH + 256], in_=kb.bank(pb, 256), func=AF.Copy), writes=[('ps', pb), ('KTc', b, 5)])
                for kt in range(22):
                    src_h, hk, t0 = (hT, 'hT', kt * 128) if kt < 20 else (hcT, 'hcT', (kt - 20) * 128)
                    pb = kb.nextrot('projbank', 2)
                    for kc in range(KC):
                        P.op('pe', lambda e, kc=kc, b=b, pb=pb, src_h=src_h, t0=t0: e.matmul(kb.bank(pb, 128), lhsT=src_h[:, kc, t0:t0 + 128], rhs=wc[b][:, kc, 256:384],
                                                                                          start=(kc == 0), stop=(kc == KC - 1)),
                             reads=[(('wc', b), kc), (hk, t0 // 512)], writes=[('ps', pb)])
                    dst = Vec[b][:, kt, :].rearrange("p (t s) -> p t s", s=64)[:, ::2, :]
                    src = kb.bank(pb, 128).rearrange("p (t s) -> p t s", s=64)
                    P.op('dve', lambda e, dst=dst, src=src: e.tensor_copy(out=dst, in_=src), writes=[('ps', pb), ('Vec', b, kt)])
                for rp in range(16):
                    cls = NA_CLASS[rp]
                    st = NA_ST[rp]
                    if cls == 2:
                        tab, tkey = tabI[b], ('tabI', b)
                    else:
                        sbuf_i = kb.nextrot('tabS', 2)
                        tab, tkey = tabS[sbuf_i], ('tabS', sbuf_i)
                        P.dma('sp', lambda e, c=c, cls=cls, tab=tab: e.dma_start(out=tab[:].rearrange("p a j q -> p (a j q)"), in_=tab_d[cls, c]), writes=[tkey])
                    kvkeys = [('KTc', b, g_) for g_ in range(6)] + [('Vec', b, kt_) for kt_ in range(22)] + [('Vones', b)]
                    for hh in range(2):
                        p0 = hh * 64
                        q_ap = QTc[b][p0:p0 + 64, rp * 128:(rp + 1) * 128]
                        kt_ap = (lambda kt, b=b, p0=p0: KTc[b][p0:p0 + 64, kt * 128:(kt + 1) * 128])
                        vt = (lambda kt, b=b, hh=hh: Vec[b][:, kt, hh * 64:hh * 64 + 128])
                        tab_ap = tab[:, hh, :, :].rearrange("p j q -> p (j q)")
                        ob = na_unit(kb, q_ap, [('QTc', b, rp // 4)], kt_ap, vt, tab_ap, tkey, st, None, None, kvkeys, C)
                        rcr = kb.nextrot('rc', 2)
                        rc = C['rc'][rcr]
                        if hh == 0:
                            P.op('dve', lambda e, rc=rc, ob=ob: e.reciprocal(out=rc[64:128, 0:128], in_=kb.bank(ob, 128)[64:128, :]), writes=[('ps', ob), ('rc', rcr)])
                            P.op('dve', lambda e, rc=rc, ob=ob, c=c, rp=rp: e.tensor_tensor(out=OT[0:64, c, rp * 128:(rp + 1) * 128], in0=kb.bank(ob, 128)[0:64, :], in1=rc[64:128, 0:128], op=ALU.mult),
                                 reads=[('rc', rcr)], writes=[('ps', ob), ('OT', rp // 4)])
                        else:
                            P.op('dve', lambda e, rc=rc, ob=ob: e.reciprocal(out=rc[0:64, 0:128], in_=kb.bank(ob, 128)[0:64, :]), writes=[('ps', ob), ('rc', rcr)])
                            P.op('dve', lambda e, rc=rc, ob=ob, c=c, rp=rp: e.tensor_tensor(out=OT[64:128, c, rp * 128:(rp + 1) * 128], in0=kb.bank(ob, 128)[64:128, :], in1=rc[0:64, 0:128], op=ALU.mult),
                                 reads=[('rc', rcr)], writes=[('ps', ob), ('OT', rp // 4)])
        with kb.phase():
            xtmp = kb.sb("xtmp", [128, 8, 512], F32)
            stage = [kb.sb("stage%d" % i, [128, 1024], F32) for i in range(2)]
            ostage = [kb.sb("ostage%d" % i, [128, 1024], F32) for i in range(2)]
            wo = kb.sb("wo", [128, 8, 1024], BF16)
            kb.load_w(wo, 'wo', wo_d, 0, KC, 0, 1024)
            for g in range(4):
                kb.load_xT(tsl(xh_d, 256 + g * 512, 256 + (g + 1) * 512), 512, xtmp, 'xtmp', stage)
                resid_proj(kb, OT, 'OT', g * 512, 512, wo, 'wo', xtmp, 'xtmp', 0, mL['G1'], mL['key'])
                kb.store_x(xtmp, 'xtmp', 512, tsl(mid_d, g * 512, (g + 1) * 512), ostage)
    mlp_tail(kb, mid_d, out_d, NTOK, mL, w1_d, w2_d)
    if own:
        P.close()
    return nc


def na_bias_tables(rpb, qq):
    NEG = np.float32(-30000.0)
    tab = np.full((5, 16, 2, 64, 7, 2, 64), NEG, np.float32)
    cq = np.arange(64)
    cs = np.clip(cq - 8, 0, 48)
    ck = np.arange(64)
    colvalid = (ck[:, None] >= cs[None, :]) & (ck[:, None] < cs[None, :] + 16)
    colidx = np.clip(ck[:, None] - cq[None, :] + 15, 0, 30)
    rep_rp = {0: 0, 1: 1, 2: 2, 3: 14, 4: 15}
    for cls in range(5):
        rp = rep_rp[cls]
        st = NA_ST[rp]
        for bq in range(2):
            r = 32 * qq + 2 * rp + bq
            rs = min(max(r - 4, 0), 120)
            for j in range(7):
                for a in range(2):
                    kr = 32 * qq - 4 + 2 * (st + j) + a
                    if kr < rs or kr >= rs + 8 or kr < 0 or kr > 127:
                        continue
                    vals = rpb[:, kr - r + 7, :][:, colidx]
                    tab[cls, :, a, :, j, bq, :] = np.where(colvalid[None], vals, NEG)
    tab = tab.reshape(5, 8, 2, 128, 7, 128)
    tab = tab.transpose(0, 1, 3, 2, 4, 5).reshape(5, 8, 128, 2 * 7 * 128)
    return np.ascontiguousarray(tab)


def prep_l1(inp, x1, hctx1, b, q):
    d = common_inputs(inp, 1, b)
    if x1 is not None:
        xh = np.zeros((2560, D), np.float32)
        lo = q * 2048 - 256
        hi = lo + 2560
        s0, s1 = max(lo, 0), min(hi, 8192)
        xh[s0 - lo:s1 - lo] = x1[b][s0:s1]
        d['xh'] = xh
        d['ctx'] = np.ascontiguousarray(hctx1[b])
    w = inp['na_w_qkv'][0]
    d['wqkv'] = np.ascontiguousarray(np.stack([np.concatenate([w[:, c * 128:(c + 1) * 128], w[:, 1024 + c * 128:1024 + (c + 1) * 128],
                                                              w[:, 2048 + c * 128:2048 + (c + 1) * 128]], axis=1) for c in range(8)]))
    d['tab'] = na_bias_tables(inp['na_rpb'][0], q)
    d['wo'] = np.ascontiguousarray(inp['na_w_o'][0])
    return d


def build_l2(kb=None, io=None):
    own = kb is None
    if own:
        kb = KB()
    io = io or {}
    kb.pfx = '' if own else 'l2_'
    P, nc = kb.P, kb.nc
    NTOK = 2048
    NH = 2304
    xh_d = io.get("xh") or kb.din("xh", [NH, D])
    cv_d = kb.din("cv", [128, 8, 2]); adaw_d = kb.din("adaw", [D, kb.ncol_ada()]); adab_d = kb.din("adab", [2, kb.ncol_ada()])
    g1_d = kb.din("g1", [128, 8]); g2_d = kb.din("g2", [128, 8])
    wpw1_d = kb.din("wpw1", [8, D, 256]); vecs_d = kb.din("vecs", [128, 6, 8]); wdw_d = kb.din("wdw", [128, 8, 31]); mask_d = kb.din("mask", [128, 2])
    wpw2_d = kb.din("wpw2", [D, D]); w1_d = kb.din("w1", [D, DFF]); w2_d = kb.din("w2", [DFF, D])
    out_d = io.get("out") or kb.dout("out", [NTOK, D])
    mid_d = io.get("mid") or out_d

    modsT = kb.sb("modsT", [128, 48, 2], F32)
    mvL = kb.sb("mvL", [128, 6, 8], F32)
    vecs = kb.sb("vecs", [128, 6, 8], F32)
    wdw = kb.sb("wdw", [128, 8, 31], F32)
    mask = kb.sb("mask", [128, 2], F32)
    bG = kb.sb("bG", [128, 8], F32)
    identb = kb.sb("identb", [128, 128], BF16)
    with kb.phase():
        if io.get('pre1'):
            io['pre1']()
        kb.mods(cv_d, adaw_d, adab_d, modsT)
        mL = kb.mod_vectors(modsT, g1_d, g2_d, 0, mvL, "L")
        if io.get('pre2'):
            io['pre2']()
        P.dma('sp', lambda e: e.dma_start(out=vecs[:], in_=vecs_d), writes=['vecs'])
        P.dma('sp', lambda e: e.dma_start(out=wdw[:], in_=wdw_d), writes=['wdw'])
        P.dma('sp', lambda e: e.dma_start(out=mask[:], in_=mask_d), writes=['mask'])
        P.op('dve', lambda e: e.tensor_tensor(out=bG[:], in0=vecs[:, 5, :], in1=mL['G1'], op=ALU.mult), reads=['vecs', mL['key']], writes=['bG'])
        P.op('dve', lambda e: e.tensor_copy(out=identb[:], in_=kb.ident[:]), reads=['ident'], writes=['identb'])
    with kb.phase():
        vT = kb.sb("vT", [128, 8, NTOK], BF16)
        with kb.phase():
            uT = kb.sb("uT", [128, 8, NH], BF16)
            with kb.phase():
                hT = kb.sb("hT", [128, 8, NH], BF16)
                with kb.phase():
                    xtmp = kb.sb("xtmp", [128, 8, 512], F32)
                    stage = [kb.sb("stage%d" % i, [128, 1024], F32) for i in range(2)]
                    scr = norm_scratch(kb)
                    for g in range(5):
                        n = 512 if g < 4 else 256
                        kb.load_xT(tsl(xh_d, g * 512, g * 512 + n), n, xtmp, 'xtmp', stage)
                        kb.norm_mod(xtmp, 'xtmp', n, mL['A1'], mL['B1'], mL['key'], hT, 'hT', scr, t0=0, ht0=g * 512)
                with kb.phase():
                    wp = [kb.sb("wp%d" % i, [128, 8, 256], BF16) for i in range(2)]
                    sig = [kb.sb("sig%d" % i, [128, 512], F32) for i in range(2)]
                    for fc in range(8):
                        b = fc % 2
                        kb.load_w(wp[b], ('wp', b), wpw1_d[fc], 0, KC, 0, 256)
                        for g in range(5):
                            n = 512 if g < 4 else 256
                            pa = kb.nextrot('projbank', 2)
                            proj_fm(kb, hT, 'hT', g * 512, n, wp[b], ('wp', b), 0, pa)
                            pg = 2 + kb.nextrot('projbank2', 2)
                            proj_fm(kb, hT, 'hT', g * 512, n, wp[b], ('wp', b), 128, pg)
                            sb_ = kb.nextrot('sig', 2)
                            P.op('act', lambda e, sb_=sb_, pg=pg, fc=fc, n=n: e.activation(out=sig[sb_][:, :n], in_=kb.bank(pg, n), func=AF.Sigmoid, bias=vecs[:, 1, fc:fc + 1]),
                                 reads=['vecs'], writes=[('ps', pg), ('sig', sb_)])
                            P.op('dve', lambda e, sb_=sb_, pa=pa, fc=fc, g=g, n=n: e.scalar_tensor_tensor(out=uT[:, fc, g * 512:g * 512 + n], in0=kb.bank(pa, n), scalar=vecs[:, 0, fc:fc + 1],
                                                                                                      in1=sig[sb_][:, :n], op0=ALU.add, op1=ALU.mult),
                                 reads=['vecs', ('sig', sb_)], writes=[('ps', pa), ('uT', fc, g)])
                        P.op('dve', lambda e, fc=fc: e.tensor_scalar(out=uT[:, fc, 0:128], in0=uT[:, fc, 0:128], scalar1=mask[:, 0:1], scalar2=None, op0=ALU.mult),
                             reads=['mask'], writes=[('uT', fc, 0)])
                        P.op('dve', lambda e, fc=fc: e.tensor_scalar(out=uT[:, fc, 2176:2304], in0=uT[:, fc, 2176:2304], scalar1=mask[:, 1:2], scalar2=None, op0=ALU.mult),
                             reads=['mask'], writes=[('uT', fc, 4)])
            with kb.phase():
                dg = kb.sb("dg", [128, 8, 31, 128], BF16)
                cT = kb.sb("cT", [128, 8, 512], F32)
                cbf = [kb.sb("cbf%d" % i, [128, 512], BF16) for i in range(2)]
                c2 = [kb.sb("c2%d" % i, [128, 512], BF16) for i in range(2)]
                mean = kb.sb("mean", [128, 512], F32); msq = kb.sb("msq", [128, 512], F32); rstd = kb.sb("rstd", [128, 512], F32)
                tt_ = [kb.sb("tt%d" % i, [128, 512], F32) for i in range(2)]
                for fc in range(8):
                    P.op('dve', lambda e, fc=fc: e.tensor_tensor(out=dg[:, fc, :, :], in0=identb[:].unsqueeze(1).broadcast_to([128, 31, 128]),
                                                                in1=wdw[:, fc, :].unsqueeze(2).broadcast_to([128, 31, 128]), op=ALU.mult),
                         reads=['identb', 'wdw'], writes=[('dg', fc)])
                for tg in range(4):
                    for fc in range(8):
                        pb = kb.nextrot('projbank', 2)
                        for j in range(31):
                            o = 128 + tg * 512 + j - 15
                            P.op('pe', lambda e, fc=fc, j=j, o=o, pb=pb: e.matmul(kb.bank(pb), lhsT=dg[:, fc, j, :], rhs=uT[:, fc, o:o + 512], start=(j == 0), stop=(j == 30)),
                                 reads=[('dg', fc)] + [('uT', fc, gg) for gg in range(5)], writes=[('ps', pb)])
                        P.op('act', lambda e, fc=fc, pb=pb: e.activation(out=cT[:, fc, :], in_=kb.bank(pb), func=AF.Identity, bias=vecs[:, 2, fc:fc + 1]),
                             reads=['vecs'], writes=[('ps', pb), ('cT', fc)])
                        r = kb.nextrot('cbf', 2)
                        P.op('dve', lambda e, fc=fc, r=r: e.tensor_copy(out=cbf[r][:], in_=cT[:, fc, :]), reads=[('cT', fc)], writes=[('cbf', r)])
                        P.op('act', lambda e, fc=fc, r=r: e.activation(out=c2[r][:], in_=cT[:, fc, :], func=AF.Square), reads=[('cT', fc)], writes=[('c2', r)])
                        P.op('pe', lambda e, fc=fc, r=r: e.matmul(kb.bank(6), lhsT=kb.ones_bf[:], rhs=cbf[r][:], start=(fc == 0), stop=(fc == 7)), reads=[('cbf', r), 'ones_bf'], writes=[('ps', 6)])
                        P.op('pe', lambda e, fc=fc, r=r: e.matmul(kb.bank(7), lhsT=kb.ones_bf[:], rhs=c2[r][:], start=(fc == 0), stop=(fc == 7)), reads=[('c2', r), 'ones_bf'], writes=[('ps', 7)])
                    P.op('act', lambda e: e.activation(out=mean[:], in_=kb.bank(6), func=AF.Copy, scale=1.0 / D), writes=[('ps', 6), 'mean'])
                    P.op('dve', lambda e: e.tensor_tensor(out=msq[:], in0=mean[:], in1=mean[:], op=ALU.mult), reads=['mean'], writes=['msq'])
                    P.op('dve', lambda e: e.scalar_tensor_tensor(out=msq[:], in0=kb.bank(7), scalar=1.0 / D, in1=msq[:], op0=ALU.mult, op1=ALU.subtract), writes=[('ps', 7), 'msq'])
                    P.op('act', lambda e: e.activation(out=rstd[:], in_=msq[:], func=AF.Ln, bias=kb.eps_t[:]), reads=['msq', 'eps_t'], writes=['rstd'])
                    P.op('act', lambda e: e.activation(out=rstd[:], in_=rstd[:], func=AF.Exp, scale=-0.5), writes=['rstd'])
                    for fc in range(8):
                        r = kb.nextrot('tt', 2)
                        P.op('dve', lambda e, fc=fc, r=r: e.tensor_tensor(out=tt_[r][:], in0=cT[:, fc, :], in1=mean[:], op=ALU.subtract), reads=[('cT', fc), 'mean'], writes=[('tt', r)])
                        P.op('dve', lambda e, r=r: e.tensor_tensor(out=tt_[r][:], in0=tt_[r][:], in1=rstd[:], op=ALU.mult), reads=['rstd'], writes=[('tt', r)])
                        P.op('act', lambda e, fc=fc, r=r, tg=tg: e.activation(out=vT[:, fc, tg * 512:(tg + 1) * 512], in_=tt_[r][:], func=AF.Silu, scale=vecs[:, 3, fc:fc + 1], bias=vecs[:, 4, fc:fc + 1]),
                             reads=[('tt', r), 'vecs'], writes=[('vT', tg)])
        with kb.phase():
            xtmp = kb.sb("xtmp", [128, 8, 512], F32)
            stage = [kb.sb("stage%d" % i, [128, 1024], F32) for i in range(2)]
            ostage = [kb.sb("ostage%d" % i, [128, 1024], F32) for i in range(2)]
            wo = kb.sb("wo", [128, 8, 1024], BF16)
            kb.load_w(wo, 'wo', wpw2_d, 0, KC, 0, 1024)
            for g in range(4):
                kb.load_xT(tsl(xh_d, 128 + g * 512, 128 + (g + 1) * 512), 512, xtmp, 'xtmp', stage)
                resid_proj(kb, vT, 'vT', g * 512, 512, wo, 'wo', xtmp, 'xtmp', 0, mL['G1'], mL['key'], bG=bG)
                kb.store_x(xtmp, 'xtmp', 512, tsl(mid_d, g * 512, (g + 1) * 512), ostage)
    mlp_tail(kb, mid_d, out_d, NTOK, mL, w1_d, w2_d)
    if own:
        P.close()
    return nc


def prep_l2(inp, x2, b, q):
    d = common_inputs(inp, 2, b)
    if x2 is not None:
        xh = np.zeros((2304, D), np.float32)
        lo = q * 2048 - 128
        hi = lo + 2304
        s0, s1 = max(lo, 0), min(hi, 8192)
        xh[s0 - lo:s1 - lo] = x2[b][s0:s1]
        d['xh'] = xh
    w = inp['cv_w_pw1'][0]
    d['wpw1'] = np.ascontiguousarray(np.stack([np.concatenate([w[:, c * 128:(c + 1) * 128], w[:, 1024 + c * 128:1024 + (c + 1) * 128]], axis=1) for c in range(8)]))
    bp = inp['cv_b_pw1'][0]
    d['vecs'] = np.ascontiguousarray(np.stack([fm(bp[:1024]), fm(bp[1024:]), fm(inp['cv_b_dw'][0]), fm(inp['cv_ln_g'][0]), fm(inp['cv_ln_b'][0]), fm(inp['cv_b_pw2'][0])], axis=1))
    d['wdw'] = np.ascontiguousarray(inp['cv_w_dw'][0].T.reshape(8, 128, 31).transpose(1, 0, 2))
    m = np.ones((128, 2), np.float32)
    if q == 0:
        m[:, 0] = 0.0
    if q == 3:
        m[:, 1] = 0.0
    d['mask'] = m
    d['wpw2'] = np.ascontiguousarray(inp['cv_w_pw2'][0])
    return d


def l3_pq_loop(kb, io, hT, csd, pq_d):
    P = kb.P
    pqs = [kb.sb("pqs%d" % i, [128, 2048], BF16) for i in range(2)]
    for tt in range(16):
        ob = tt % 2
        for grp in range(4):
            pb = kb.nextrot('projbank', 4)
            for kl in range(2):
                kc = grp * 2 + kl
                P.op('pe', lambda e, kc=kc, kl=kl, tt=tt, pb=pb: e.matmul(kb.bank(pb), lhsT=hT[:, kc, tt * 128:(tt + 1) * 128], rhs=csd[:, kl, :], start=(kl == 0), stop=(kl == 1)),
                     reads=[(('csd'), kl), ('hT', tt // 4)], writes=[('ps', pb)])
            if grp % 2 == 0:
                P.op('act', lambda e, ob=ob, grp=grp, pb=pb: e.activation(out=pqs[ob][:, grp * 512:(grp + 1) * 512], in_=kb.bank(pb), func=AF.Copy), writes=[('ps', pb), ('pqs', ob, grp)])
            else:
                P.op('dve', lambda e, ob=ob, grp=grp, pb=pb: e.tensor_copy(out=pqs[ob][:, grp * 512:(grp + 1) * 512], in_=kb.bank(pb)), writes=[('ps', pb), ('pqs', ob, grp)])
        P.dma('sp', lambda e, ob=ob, tt=tt: e.dma_start(out=pq_d[tt * 128:(tt + 1) * 128, :], in_=pqs[ob][:]), reads=[('pqs', ob, g_) for g_ in range(4)], writes=[('pqd', tt)])
        if io.get('pqg') is not None and tt % 2 == 1:
            c = tt // 2
            pqg = io['pqg']
            P.coll(lambda e, c=c, pqg=pqg: e.collective_compute("AllGather", ALU.bypass, replica_groups=RG, ins=[pq_d[c * 256:(c + 1) * 256, :].opt()], outs=[pqg[c * 1024:(c + 1) * 1024, :].opt()]),
                   reads=[('pqd', tt - 1), ('pqd', tt)], writes=[('pqg', c)])


def l3_dft_loops(kb, io, pq_d, cn_d, sn_d, zT, chunk_keys=False):
    P = kb.P
    pqb = [kb.sb("pqb%d" % i, [128, 2048], BF16) for i in range(3)]
    tb = [kb.sb("tb%d" % i, [128, 2, 512], BF16) for i in range(3)]
    tokmap = io.get('tokmap') or (lambda nt: nt * 128)
    for kg in range(4):
        for nt in range(64):
            tk = tokmap(nt)
            r = kb.nextrot('pqb', 3)
            P.dma('sp', lambda e, r=r, nt=nt: e.dma_start(out=pqb[r][:], in_=pq_d[nt * 128:(nt + 1) * 128, :]), reads=([('pqg', nt // 8)] if chunk_keys else []), writes=[('pqb', r)])
            P.dma('sp', lambda e, r=r, tk=tk, kg=kg: e.dma_start(out=tb[r][:, 0, :], in_=cn_d[tk:tk + 128, kg * 512:(kg + 1) * 512]), writes=[('tb', r, 0)])
            P.dma('sp', lambda e, r=r, tk=tk, kg=kg: e.dma_start(out=tb[r][:, 1, :], in_=sn_d[tk:tk + 128, kg * 512:(kg + 1) * 512]), writes=[('tb', r, 1)])
            for fz in range(8):
                grp, jh = fz // 2, fz % 2
                P.op('pe', lambda e, r=r, fz=fz, grp=grp, jh=jh, nt=nt: e.matmul(kb.bank(fz), lhsT=pqb[r][:, grp * 512 + jh * 128:grp * 512 + jh * 128 + 128], rhs=tb[r][:, 0, :],
                                                                              start=(nt == 0), stop=False),
                     reads=[('pqb', r), ('tb', r, 0)], writes=[('ps', fz)])
                P.op('pe', lambda e, r=r, fz=fz, grp=grp, jh=jh, nt=nt: e.matmul(kb.bank(fz), lhsT=pqb[r][:, grp * 512 + 256 + jh * 128:grp * 512 + 256 + jh * 128 + 128], rhs=tb[r][:, 1, :],
                                                                              start=False, stop=(nt == 63)),
                     reads=[('pqb', r), ('tb', r, 1)], writes=[('ps', fz)])
        for fz in range(8):
            if fz % 2 == 0:
                P.op('act', lambda e, fz=fz, kg=kg: e.activation(out=zT[:, fz, kg * 512:(kg + 1) * 512], in_=kb.bank(fz), func=AF.Copy), writes=[('ps', fz), ('zT', kg)])
            else:
                P.op('dve', lambda e, fz=fz, kg=kg: e.tensor_copy(out=zT[:, fz, kg * 512:(kg + 1) * 512], in_=kb.bank(fz)), writes=[('ps', fz), ('zT', kg)])


def build_l3_fused(kb, io):
    P, nc = kb.P, kb.nc
    NTOK = 2048
    x_d = io["x"]; out_d = io["out"]; mid_d = io["mid"]; pqo = io["pq"]; pqg = io["pqg"]
    kb.pfx = 'l3a_'
    cv_d = kb.din("cv", [128, 8, 2]); adaw_d = kb.din("adaw", [D, kb.ncol_ada()]); adab_d = kb.din("adab", [2, kb.ncol_ada()])
    g1_d = kb.din("g1", [128, 8]); g2_d = kb.din("g2", [128, 8]); csd_d = kb.din("csd", [256, 512])
    kb.pfx = 'l3b_'
    cn_d = kb.din("cn", [8192, NTOK], BF16); sn_d = kb.din("sn", [8192, NTOK], BF16)
    ftw_d = kb.din("ftw", [D, D]); vecs_d = kb.din("vecs", [128, 2, 8])
    w1_d = kb.din("w1", [D, DFF]); w2_d = kb.din("w2", [DFF, D])
    modsT = kb.sb("modsT", [128, 48, 2], F32)
    mvL = kb.sb("mvL", [128, 6, 8], F32)
    vecs = kb.sb("vecs", [128, 2, 8], F32)
    bG = kb.sb("bG", [128, 8], F32)
    zeros = kb.sb("zeros", [128, 8], F32)
    with kb.phase():
        kb.mods(cv_d, adaw_d, adab_d, modsT)
        mL = kb.mod_vectors(modsT, g1_d, g2_d, 0, mvL, "L")
        P.dma('sp', lambda e: e.dma_start(out=vecs[:], in_=vecs_d), writes=['vecs'])
        P.op('dve', lambda e: e.tensor_tensor(out=bG[:], in0=vecs[:, 0, :], in1=mL['G1'], op=ALU.mult), reads=['vecs', mL['key']], writes=['bG'])
        P.op('pool', lambda e: e.memset(zeros[:], 0.0), writes=['zeros'])
    with kb.phase():
        zT = kb.sb("zT", [128, 8, NTOK], BF16)
        with kb.phase():
            hT = kb.sb("hT", [128, 8, NTOK], BF16)
            csd = kb.sb("csd", [128, 2, 512], BF16)
            kb.load_w(csd, 'csd', csd_d, 0, 2, 0, 512)
            with kb.phase():
                xtmp = kb.sb("xtmp", [128, 8, 512], F32)
                stage = [kb.sb("stage%d" % i, [128, 1024], F32) for i in range(2)]
                scr = norm_scratch(kb)
                for g in range(4):
                    kb.load_xT(tsl(x_d, g * 512, (g + 1) * 512), 512, xtmp, 'xtmp', stage)
                    kb.norm_mod(xtmp, 'xtmp', 512, mL['A1'], mL['B1'], mL['key'], hT, 'hT', scr, t0=0, ht0=g * 512)
            with kb.phase():
                l3_pq_loop(kb, dict(pqg=pqg), hT, csd, pqo)
                l3_dft_loops(kb, dict(tokmap=pq_tokmap), pqg, cn_d, sn_d, zT, chunk_keys=True)
        with kb.phase():
            xtmp = kb.sb("xtmp", [128, 8, 512], F32)
            stage = [kb.sb("stage%d" % i, [128, 1024], F32) for i in range(2)]
            ostage = [kb.sb("ostage%d" % i, [128, 1024], F32) for i in range(2)]
            wo = kb.sb("wo", [128, 8, 1024], BF16)
            kb.load_w(wo, 'wo', ftw_d, 0, KC, 0, 1024)
            for g in range(4):
                kb.load_xT(tsl(x_d, g * 512, (g + 1) * 512), 512, xtmp, 'xtmp', stage)
                resid_proj(kb, zT, 'zT', g * 512, 512, wo, 'wo', xtmp, 'xtmp', 0, mL['G1'], mL['key'], bG=bG)
                kb.store_x(xtmp, 'xtmp', 512, tsl(mid_d, g * 512, (g + 1) * 512), ostage)
    mlp_tail(kb, mid_d, out_d, NTOK, mL, w1_d, w2_d, final=(vecs[:, 1, :], zeros[:], 'vecs'))


def build_l3a(kb=None, io=None):
    own = kb is None
    if own:
        kb = KB()
    io = io or {}
    kb.pfx = '' if own else 'l3a_'
    P, nc = kb.P, kb.nc
    NTOK = 2048
    x_d = io.get("x") or kb.din("x", [NTOK, D])
    cv_d = kb.din("cv", [128, 8, 2]); adaw_d = kb.din("adaw", [D, kb.ncol_ada()]); adab_d = kb.din("adab", [2, kb.ncol_ada()])
    g1_d = kb.din("g1", [128, 8]); g2_d = kb.din("g2", [128, 8])
    csd_d = kb.din("csd", [256, 512])
    pq_d = io.get("pq") or kb.dout("pq", [NTOK, 2048], BF16)
    modsT = kb.sb("modsT", [128, 48, 2], F32)
    mvL = kb.sb("mvL", [128, 6, 8], F32)
    with kb.phase():
        kb.mods(cv_d, adaw_d, adab_d, modsT)
        mL = kb.mod_vectors(modsT, g1_d, g2_d, 0, mvL, "L")
    io['mL_out'] = mL
    with kb.phase():
        hT = kb.sb("hT", [128, 8, NTOK], BF16)
        csd = kb.sb("csd", [128, 2, 512], BF16)
        kb.load_w(csd, 'csd', csd_d, 0, 2, 0, 512)
        with kb.phase():
            xtmp = kb.sb("xtmp", [128, 8, 512], F32)
            stage = [kb.sb("stage%d" % i, [128, 1024], F32) for i in range(2)]
            scr = norm_scratch(kb)
            for g in range(4):
                kb.load_xT(tsl(x_d, g * 512, (g + 1) * 512), 512, xtmp, 'xtmp', stage)
                kb.norm_mod(xtmp, 'xtmp', 512, mL['A1'], mL['B1'], mL['key'], hT, 'hT', scr, t0=0, ht0=g * 512)
        with kb.phase():
            l3_pq_loop(kb, io, hT, csd, pq_d)
    if own:
        P.close()
    return nc


def build_l3b(kb=None, io=None):
    own = kb is None
    if own:
        kb = KB()
    io = io or {}
    kb.pfx = '' if own else 'l3b_'
    P, nc = kb.P, kb.nc
    NTOK = 2048
    x_d = io.get("x") or kb.din("x", [NTOK, D])
    cv_d = kb.din("cv", [128, 8, 2]); adaw_d = kb.din("adaw", [D, kb.ncol_ada()]); adab_d = kb.din("adab", [2, kb.ncol_ada()])
    g1_d = kb.din("g1", [128, 8]); g2_d = kb.din("g2", [128, 8])
    pq_d = io.get("pq") or kb.din("pq", [8192, 2048], BF16)
    cn_d = kb.din("cn", [8192, NTOK], BF16); sn_d = kb.din("sn", [8192, NTOK], BF16)
    ftw_d = kb.din("ftw", [D, D]); vecs_d = kb.din("vecs", [128, 2, 8])
    w1_d = kb.din("w1", [D, DFF]); w2_d = kb.din("w2", [DFF, D])
    out_d = io.get("out") or kb.dout("out", [NTOK, D])
    mid_d = io.get("mid") or out_d
    modsT = kb.sb("modsT", [128, 48, 2], F32)
    mvL = kb.sb("mvL", [128, 6, 8], F32)
    vecs = kb.sb("vecs", [128, 2, 8], F32)
    bG = kb.sb("bG", [128, 8], F32)
    zeros = kb.sb("zeros", [128, 8], F32)
    with kb.phase():
        if io.get('mL') is not None:
            mL = io['mL']
        else:
            kb.mods(cv_d, adaw_d, adab_d, modsT)
            mL = kb.mod_vectors(modsT, g1_d, g2_d, 0, mvL, "L")
        P.dma('sp', lambda e: e.dma_start(out=vecs[:], in_=vecs_d), writes=['vecs'])
        P.op('dve', lambda e: e.tensor_tensor(out=bG[:], in0=vecs[:, 0, :], in1=mL['G1'], op=ALU.mult), reads=['vecs', mL['key']], writes=['bG'])
        P.op('pool', lambda e: e.memset(zeros[:], 0.0), writes=['zeros'])
    with kb.phase():
        zT = kb.sb("zT", [128, 8, NTOK], BF16)
        with kb.phase():
            l3_dft_loops(kb, io, pq_d, cn_d, sn_d, zT)
        with kb.phase():
            xtmp = kb.sb("xtmp", [128, 8, 512], F32)
            stage = [kb.sb("stage%d" % i, [128, 1024], F32) for i in range(2)]
            ostage = [kb.sb("ostage%d" % i, [128, 1024], F32) for i in range(2)]
            wo = kb.sb("wo", [128, 8, 1024], BF16)
            kb.load_w(wo, 'wo', ftw_d, 0, KC, 0, 1024)
            for g in range(4):
                kb.load_xT(tsl(x_d, g * 512, (g + 1) * 512), 512, xtmp, 'xtmp', stage)
                resid_proj(kb, zT, 'zT', g * 512, 512, wo, 'wo', xtmp, 'xtmp', 0, mL['G1'], mL['key'], bG=bG)
                kb.store_x(xtmp, 'xtmp', 512, tsl(mid_d, g * 512, (g + 1) * 512), ostage)
    mlp_tail(kb, mid_d, out_d, NTOK, mL, w1_d, w2_d, final=(vecs[:, 1, :], zeros[:], 'vecs'))
    if own:
        P.close()
    return nc


def prep_l3a(inp, x3, b, q):
    d = common_inputs(inp, 3, b)
    for k in ('w1', 'w2'):
        d.pop(k)
    if x3 is not None:
        d['x'] = np.ascontiguousarray(x3[b][q * 2048:(q + 1) * 2048])
    dd = np.arange(256)[:, None].astype(np.int64)
    jj = np.arange(256)[None, :].astype(np.int64)
    ang = 2.0 * np.pi * ((dd * jj) % 256).astype(np.float64) / 256.0
    d['csd'] = np.ascontiguousarray(np.concatenate([np.cos(ang) / 16.0, np.sin(ang) / 16.0], axis=1).astype(np.float32))
    return d


_DFT_CACHE = {}


def seq_dft_tables(q):
    if q not in _DFT_CACHE:
        import ml_dtypes
        n = np.arange(8192, dtype=np.int64)[:, None]
        k = np.arange(q * 2048, (q + 1) * 2048, dtype=np.int64)[None, :]
        ang = 2.0 * np.pi * ((n * k) % 8192).astype(np.float64) / 8192.0
        s = 1.0 / np.sqrt(8192.0)
        _DFT_CACHE[q] = (np.ascontiguousarray((np.cos(ang) * s).astype(np.float32).astype(ml_dtypes.bfloat16)),
                         np.ascontiguousarray((-np.sin(ang) * s).astype(np.float32).astype(ml_dtypes.bfloat16)))
    return _DFT_CACHE[q]


def prep_l3b(inp, x3, pq_b, b, q):
    d = common_inputs(inp, 3, b)
    if x3 is not None:
        d['x'] = np.ascontiguousarray(x3[b][q * 2048:(q + 1) * 2048])
        d['pq'] = pq_b
    d['cn'], d['sn'] = seq_dft_tables(q)
    d['ftw'] = np.ascontiguousarray(inp['ft_w'][0])
    d['vecs'] = np.ascontiguousarray(np.stack([fm(inp['ft_b'][0]), fm(inp['final_g'])], axis=1))
    return d


CORES = [(b, q) for b in range(2) for q in range(4)]


def _run(nc, maps):
    res = run_bass_kernel_spmd(nc, maps, core_ids=list(range(8)))
    return res.results


def kernel_unfused(**inputs):
    inp = {k: np.asarray(v) for k, v in inputs.items()}
    r = _run(build_l0(), [prep_l0(inp, b, q) for b, q in CORES])
    x1 = np.stack([np.concatenate([r[b * 4 + q]["out"] for q in range(4)], axis=0) for b in range(2)])
    hctx1 = np.stack([r[b * 4]["hctx"] for b in range(2)])
    r = _run(build_l1(), [prep_l1(inp, x1, hctx1, b, q) for b, q in CORES])
    x2 = np.stack([np.concatenate([r[b * 4 + q]["out"] for q in range(4)], axis=0) for b in range(2)])
    r = _run(build_l2(), [prep_l2(inp, x2, b, q) for b, q in CORES])
    x3 = np.stack([np.concatenate([r[b * 4 + q]["out"] for q in range(4)], axis=0) for b in range(2)])
    r = _run(build_l3a(), [prep_l3a(inp, x3, b, q) for b, q in CORES])
    pq = [np.ascontiguousarray(np.concatenate([r[b * 4 + q]["pq"] for q in range(4)], axis=0)) for b in range(2)]
    r = _run(build_l3b(), [prep_l3b(inp, x3, pq[b], b, q) for b, q in CORES])
    out = np.stack([np.concatenate([r[b * 4 + q]["out"] for q in range(4)], axis=0) for b in range(2)])
    return out.astype(np.float32)


def halo_parts(kb, src, dst, H, sel, tag):
    P, nc = kb.P, kb.nc
    bF = kb.dint("bounceF" + tag, [1024, H]); bL = kb.dint("bounceL" + tag, [1024, H])
    gF = kb.dint("gathF" + tag, [4096, H]); gL = kb.dint("gathL" + tag, [4096, H])

    def part1():
        P.dma('pool', lambda e: e.dma_start(out=bF.rearrange("(p k) h -> p k h", k=8), in_=src.ap[:, :, src.t0:src.t0 + H]), writes=['bF'])
        P.dma('pool', lambda e: e.dma_start(out=bL.rearrange("(p k) h -> p k h", k=8), in_=src.ap[:, :, src.t0 + 2048 - H:src.t0 + 2048]), writes=['bL'])
        P.coll([lambda e: e.collective_compute("AllGather", ALU.bypass, replica_groups=RG, ins=[bF.opt()], outs=[gF.opt()]),
                lambda e: e.collective_compute("AllGather", ALU.bypass, replica_groups=RG, ins=[bL.opt()], outs=[gL.opt()])], reads=['bF', 'bL'], writes=['gF', 'gL'])

    def part2():
        cand = [kb.sb("cand%d" % i, [128, 4, 8, H], F32) for i in range(2)]
        acc = [kb.sb("hacc%d" % i, [128, 8, H], F32) for i in range(2)]
        for side in range(2):
            gsrc, gkey = (gL, 'gL') if side == 0 else (gF, 'gF')
            srcv = gsrc.rearrange("(r p k) h -> p r k h", r=4, k=8)
            for r in range(4):
                P.dma('sp', lambda e, side=side, srcv=srcv, r=r: e.dma_start(out=cand[side][:, r, :, :], in_=srcv[:, r, :, :]), reads=[gkey], writes=[('cand', side, r)])
            P.op('dve', lambda e, side=side: e.tensor_scalar(out=acc[side][:], in0=cand[side][:, 0, :, :], scalar1=sel[:, side * 4:side * 4 + 1], scalar2=None, op0=ALU.mult),
                 reads=[('cand', side, 0), 'sel'], writes=[('hacc', side)])
            for r in range(1, 4):
                P.op('dve', lambda e, side=side, r=r: e.scalar_tensor_tensor(out=acc[side][:], in0=cand[side][:, r, :, :], scalar=sel[:, side * 4 + r:side * 4 + r + 1], in1=acc[side][:],
                                                                          op0=ALU.mult, op1=ALU.add),
                     reads=[('cand', side, r), 'sel'], writes=[('hacc', side)])
            d0 = 0 if side == 0 else H + 2048
            P.dma('sp', lambda e, side=side, d0=d0: e.dma_start(out=dst.ap[:, :, d0:d0 + H], in_=acc[side][:]), reads=[('hacc', side)], writes=[('dsth', side)])
    return part1, part2


def pq_tokmap(nt):
    c, r, half = nt // 8, (nt % 8) // 2, nt % 2
    return r * 2048 + c * 256 + half * 128


def build_fused():
    kb = KB()
    P, nc = kb.P, kb.nc
    kb.split_mods = True
    xb_d = kb.din("xb", [2048, D]); ctx_d = kb.din("ctx", [256, D]); sel_d = kb.din("sel", [128, 8])
    out_d = kb.dout("out", [2048, D])
    sel = kb.sb("sel", [128, 8], F32)
    P.dma('sp', lambda e: e.dma_start(out=sel[:], in_=sel_d), writes=['sel'])

    def fmt(name, n):
        return FM(kb.dint(name, [128, 8, n]))
    mid = fmt("mid", 2048)
    hc1 = fmt("hc1", 256)
    xh1 = fmt("xh1", 2560)
    xa = FM(xh1.ap, 256)
    build_l0(kb, dict(xb=xb_d, ctx=ctx_d, out=xa, hctx=hc1, mid=mid, kv_gather=True))
    p1, p2 = halo_parts(kb, xa, xh1, 256, sel, "1")
    xh2 = fmt("xh2", 2304)
    xb2 = FM(xh2.ap, 128)
    build_l1(kb, dict(xh=xh1, ctx=hc1, out=xb2, mid=mid, pre1=p1, pre2=p2))
    p1, p2 = halo_parts(kb, xb2, xh2, 128, sel, "2")
    xc = fmt("xc", 2048)
    build_l2(kb, dict(xh=xh2, out=xc, mid=mid, pre1=p1, pre2=p2))
    pqo = kb.dint("pqo", [2048, 2048], BF16); pqg = kb.dint("pqg", [8192, 2048], BF16)
    build_l3_fused(kb, dict(x=xc, pq=pqo, pqg=pqg, out=out_d, mid=mid))
    P.close()
    return nc


def prep_fused(inp, b, q):
    d0 = prep_l0(inp, b, q)
    out = {k: d0[k] for k in ('ident', 'cv', 'xb', 'ctx')}
    sel = np.zeros((128, 8), np.float32)
    if q > 0:
        sel[:, q - 1] = 1.0
    if q < 3:
        sel[:, 4 + q + 1] = 1.0
    out['sel'] = sel
    out['xb'] = np.ascontiguousarray(out['xb'][:2048])
    for pfx, dd in (('l0_', d0), ('l1_', prep_l1(inp, None, None, b, q)), ('l2_', prep_l2(inp, None, b, q)),
                    ('l3a_', prep_l3a(inp, None, b, q)), ('l3b_', prep_l3b(inp, None, None, b, q))):
        for k, v in dd.items():
            if k in ('ident', 'cv', 'xb', 'ctx'):
                continue
            if pfx == 'l3b_' and k in ('adaw', 'adab', 'g1', 'g2'):
                continue
            if k in ('adaw', 'adab'):
                v = np.ascontiguousarray(v[:, q * 1536:(q + 1) * 1536])
            if pfx == 'l0_' and k in ('cos', 'sin'):
                v = np.ascontiguousarray(v[:4])
            out[pfx + k] = v
    return out


def kernel(**inputs):
    inp = {k: np.asarray(v) for k, v in inputs.items()}
    r = _run(build_fused(), [prep_fused(inp, b, q) for b, q in CORES])
    out = np.stack([np.concatenate([r[b * 4 + q]["out"] for q in range(4)], axis=0) for b in range(2)])
    return out.astype(np.float32)
```

```python
import contextlib
import numpy as np
from concourse.bass_utils import run_bass_kernel_spmd
import concourse.bass as bass
import concourse.mybir as mybir

F32 = mybir.dt.float32
BF16 = mybir.dt.bfloat16
AF = mybir.ActivationFunctionType
ALU = mybir.AluOpType
AX = mybir.AxisListType

ENGS = ('pe', 'act', 'dve', 'pool', 'sp')
RG = [[0, 1, 2, 3], [4, 5, 6, 7]]
NDMASLOT = 8


class Op:
    __slots__ = ('eng', 'fn', 'deps', 'sig', 'sigval', 'dma', 'dslot', 'dval', 'prev_slot_op', 'seq', 'dinc')

    def __init__(self, eng, fn, dma):
        self.eng = eng
        self.fn = fn
        self.deps = []
        self.sig = False
        self.sigval = 0
        self.dma = dma
        self.dslot = None
        self.dval = 0
        self.prev_slot_op = None
        self.dinc = 16


class Prog:
    def __init__(self, nc, same_engine_sync=True):
        self.nc = nc
        self.same = same_engine_sync
        self.E = {'pe': nc.tensor, 'act': nc.scalar, 'dve': nc.vector, 'pool': nc.gpsimd, 'sp': nc.sync}
        self.sem = {}
        self._ctx = []
        for e in ('pe', 'act', 'dve', 'pool'):
            self.sem[e] = self._enter(nc.semaphore('prog_' + e))
        self.dsem = {}
        for q in ('sp', 'pool', 'act'):
            self.dsem[q] = [self._enter(nc.semaphore('dma_%s_%d' % (q, i))) for i in range(NDMASLOT)]
        self.ccsem = self._enter(nc.semaphore('ccsem'))
        self.cccnt = 0
        self.ccscratch = self._enter(nc.sbuf_tensor('ccscratch', [128, 8], F32))
        self.cnt = {e: 0 for e in ENGS}
        self.dcnt = {q: 0 for q in ('sp', 'pool', 'act')}
        self.last_dma = {q: [None] * NDMASLOT for q in ('sp', 'pool', 'act')}
        self.nops = 0
        self._reset_phase()

    def _enter(self, cm):
        v = cm.__enter__()
        self._ctx.append(cm)
        return v

    def alloc(self, cm):
        return self._enter(cm)

    def close(self):
        for cm in reversed(self._ctx):
            cm.__exit__(None, None, None)
        self._ctx = []

    def _reset_phase(self):
        self.ops = {e: [] for e in ENGS}
        self.order = []
        self.last_writer = {}
        self.readers = {}

    def _record(self, eng, fn, reads, writes, dma):
        op = Op(eng, fn, dma)
        deps = set()
        for k in reads:
            w = self.last_writer.get(k)
            if w is not None:
                deps.add(w)
        for k in writes:
            w = self.last_writer.get(k)
            if w is not None:
                deps.add(w)
            for r in self.readers.get(k, ()):
                deps.add(r)
        deps.discard(op)
        for k in reads:
            self.readers.setdefault(k, []).append(op)
        for k in writes:
            self.last_writer[k] = op
            self.readers[k] = []
        best = {}
        out = []
        for d in deps:
            if d.dma:
                out.append(d)
                continue
            if d.eng == eng and not dma:
                if eng == 'pe' or not self.same:
                    continue
            b = best.get(d.eng)
            if b is None or d.seq > b.seq:
                best[d.eng] = d
        out.extend(best.values())
        op.deps = out
        for d in out:
            if not d.dma:
                d.sig = True
        op.seq = len(self.order)
        self.ops[eng].append(op)
        self.order.append(op)
        self.nops += 1
        return op

    def op(self, eng, fn, reads=(), writes=()):
        return self._record(eng, fn, reads, writes, False)

    def coll(self, fn, reads=(), writes=()):
        fns = fn if isinstance(fn, (list, tuple)) else [fn]

        def wrapped(e):
            for f in fns:
                ins = f(e)
                self.cccnt += 1
                ins.then_inc(self.ccsem)
            e.wait_ge(self.ccsem, self.cccnt)
            return e.memset(self.ccscratch[:], 0.0)
        return self._record('pool', wrapped, reads, writes, False)

    def dma(self, q, fn, reads=(), writes=()):
        op = self._record(q, fn, reads, writes, True)
        op.dinc = 16
        j = self.dcnt[q]
        self.dcnt[q] += 1
        slot = j % NDMASLOT
        op.dslot = self.dsem[q][slot]
        op.dval = 16 * (j // NDMASLOT + 1)
        op.prev_slot_op = self.last_dma[q][slot]
        self.last_dma[q][slot] = op
        return op

    def flush(self, final_wait=()):
        for e in ENGS:
            c = self.cnt[e]
            for op in self.ops[e]:
                if op.sig and not op.dma:
                    c += 1
                    op.sigval = c
            self.cnt[e] = c
        ops = self.ops
        sem = self.sem
        lastd = {q: list(v) for q, v in self.last_dma.items()}
        anyd = any(d is not None for v in lastd.values() for d in v)

        def run(engname):
            def body(eng):
                known = {e: 0 for e in ENGS}
                kd = {}
                for op in ops[engname]:
                    if op.dma and op.prev_slot_op is not None:
                        p = op.prev_slot_op
                        key = id(p.dslot)
                        if kd.get(key, 0) < p.dval:
                            eng.wait_ge(p.dslot, p.dval)
                            kd[key] = p.dval
                    for d in op.deps:
                        if d.dma:
                            key = id(d.dslot)
                            if kd.get(key, 0) < d.dval:
                                eng.wait_ge(d.dslot, d.dval)
                                kd[key] = d.dval
                        else:
                            if known[d.eng] < d.sigval:
                                eng.wait_ge(sem[d.eng], d.sigval)
                                known[d.eng] = d.sigval
                    ins = op.fn(eng)
                    if op.dma:
                        if op.dinc == 1:
                            ins.then_inc(op.dslot)
                        else:
                            ins.then_inc(op.dslot, 16)
                    elif op.sig:
                        ins.then_inc(sem[op.eng], 1)
                if engname == 'sp':
                    for q in lastd:
                        for d in lastd[q]:
                            if d is not None:
                                eng.wait_ge(d.dslot, d.dval)
            return body

        with self.nc.Block(no_gpsimd_drain=True) as block:
            if ops['sp'] or anyd:
                block.sync(run('sp'))
            if ops['pe']:
                block.tensor(run('pe'))
            if ops['act']:
                block.scalar(run('act'))
            if ops['dve']:
                block.vector(run('dve'))
            if ops['pool']:
                block.gpsimd(run('pool'))
        for q in self.last_dma:
            self.last_dma[q] = [None] * NDMASLOT
        self._reset_phase()
D = 1024
KC = 8
DFF = 4096
EPS = 1e-6


class FM:
    def __init__(self, ap, t0=0):
        self.ap = ap
        self.t0 = t0


def tsl(x, a, b):
    if isinstance(x, FM):
        return FM(x.ap, x.t0 + a)
    return x[a:b, :]


class KB:
    def __init__(self, nt=2048):
        self.nc = nc = bass.Bass("TRN2", target_bir_lowering=False)
        self.P = P = Prog(nc)
        self.NT = nt
        self.stack = []
        self.uid = 0
        self.pfx = ''
        self.shared = {}
        self.split_mods = False
        self.ps = P.alloc(nc.psum_tensor("ps", [128, 4096], F32))
        self.ident = self.sb("ident_sb", [128, 128], F32)
        self.ones_bf = self.sb("ones_bf", [128, 128], BF16)
        self.eps_t = self.sb("eps_t", [128, 1], F32)
        self.rot = {}
        ident_d = self.nc.dram_tensor("ident", [128, 128], F32, kind="ExternalInput").ap()
        P.dma('sp', lambda e: e.dma_start(out=self.ident[:], in_=ident_d), writes=['ident'])
        P.op('pool', lambda e: e.memset(self.ones_bf[:], 1.0), writes=['ones_bf'])
        P.op('pool', lambda e: e.memset(self.eps_t[:], EPS), writes=['eps_t'])

    def sb(self, name, shape, dt):
        self.uid += 1
        name = "s%d_%s" % (self.uid, name)
        if self.stack:
            return self.stack[-1].enter_context(self.nc.sbuf_tensor(name, shape, dt))
        return self.P.alloc(self.nc.sbuf_tensor(name, shape, dt))

    @contextlib.contextmanager
    def phase(self):
        st = contextlib.ExitStack()
        self.stack.append(st)
        try:
            yield
            self.P.flush()
        finally:
            self.stack.pop()
            st.close()

    def din(self, name, shape, dt=F32):
        if name in ('cv',):
            if name not in self.shared:
                self.shared[name] = self.nc.dram_tensor(name, list(shape), dt, kind="ExternalInput").ap()
            return self.shared[name]
        return self.nc.dram_tensor(self.pfx + name, list(shape), dt, kind="ExternalInput").ap()

    def dint(self, name, shape, dt=F32):
        return self.nc.dram_tensor(name, list(shape), dt).ap()

    def dout(self, name, shape, dt=F32):
        return self.nc.dram_tensor(name, list(shape), dt, kind="ExternalOutput").ap()

    def ncol_ada(self):
        return 1536 if self.split_mods else 6144

    def bank(self, i, n=512, p0=0, p1=128):
        return self.ps[p0:p1, i * 512:i * 512 + n]

    def nextrot(self, name, n):
        v = self.rot.get(name, 0)
        self.rot[name] = v + 1
        return v % n

    def mods(self, cv_d, adaw_d, adab_d, modsT):
        P, nc = self.P, self.nc
        cv = self.sb("cv", [128, 8, 2], F32)
        sT = self.sb("sT", [128, 8, 2], F32)
        mrow = self.sb("mrow", [2, 6144], F32)
        adab = self.sb("adab", [2, 1536 if self.split_mods else 6144], F32)
        wst = [self.sb("wst%d" % i, [128, 8, 512], F32) for i in range(2)]
        P.dma('sp', lambda e: e.dma_start(out=cv[:], in_=cv_d), writes=['cv'])
        P.dma('sp', lambda e: e.dma_start(out=adab[:], in_=adab_d), writes=['adab'])
        P.op('act', lambda e: e.activation(out=sT[:], in_=cv[:], func=AF.Silu), reads=['cv'], writes=['sT'])
        ncg = 3 if self.split_mods else 12
        mpart = self.sb("mpart", [2, 1536], F32) if self.split_mods else None
        for cg in range(ncg):
            b = cg % 2
            src = adaw_d[:, cg * 512:(cg + 1) * 512].rearrange("(kc p) c -> p kc c", p=128)
            for h in range(2):
                P.dma('sp', lambda e, b=b, h=h, src=src: e.dma_start(out=wst[b][:, h * 4:(h + 1) * 4, :], in_=src[:, h * 4:(h + 1) * 4, :]),
                      writes=[('wst', b, h)])
            pb = 6 + (cg % 2)
            for kc in range(KC):
                P.op('pe', lambda e, b=b, kc=kc, pb=pb: e.matmul(self.bank(pb, 512, 0, 2), lhsT=sT[:, kc, :], rhs=wst[b][:, kc, :],
                                                                 start=(kc == 0), stop=(kc == KC - 1)),
                     reads=['sT', ('wst', b, kc // 4)], writes=[('ps', pb)])
            dstrow = mpart if self.split_mods else mrow
            P.op('dve', lambda e, cg=cg, pb=pb, dstrow=dstrow: e.tensor_tensor(out=dstrow[:, cg * 512:(cg + 1) * 512], in0=self.bank(pb, 512, 0, 2),
                                                                              in1=adab[:, cg * 512:(cg + 1) * 512], op=ALU.add),
                 reads=['adab'], writes=[('ps', pb), ('mrow', cg)])
        if self.split_mods:
            self.uid += 1
            mb = self.dint("modb%d" % self.uid, [2, 1536]); mg = self.dint("modg%d" % self.uid, [8, 1536])
            P.dma('sp', lambda e: e.dma_start(out=mb, in_=mpart[:]), reads=[('mrow', g_) for g_ in range(3)], writes=['modb'])
            P.coll(lambda e: e.collective_compute("AllGather", ALU.bypass, replica_groups=RG, ins=[mb.opt()], outs=[mg.opt()]), reads=['modb'], writes=['modg'])
            P.dma('sp', lambda e: e.dma_start(out=mrow[:].rearrange("t (r j) -> t r j", r=4), in_=mg.rearrange("(r t) j -> t r j", t=2)), reads=['modg'],
                  writes=[('mrow', g_) for g_ in range(12)])
        for ch in range(48):
            P.op('pe', lambda e, ch=ch: e.transpose(self.ps[:, 6 * 512 + ch * 2:6 * 512 + ch * 2 + 2], mrow[0:2, ch * 128:(ch + 1) * 128], self.ident[0:2, 0:2]),
                 reads=[('mrow', ch // 4), 'ident'], writes=[('ps', 6)])
        P.op('dve', lambda e: e.tensor_copy(out=modsT[:].rearrange("p a b -> p (a b)"), in_=self.ps[:, 6 * 512:6 * 512 + 96]),
             writes=[('ps', 6), 'modsT'])
        return modsT

    def mod_vectors(self, modsT, g1_d, g2_d, col, mv, tag):
        P = self.P
        gg = self.sb("gg" + tag, [128, 2, 8], F32)
        P.dma('sp', lambda e: e.dma_start(out=gg[:, 0, :], in_=g1_d), writes=['gg' + tag])
        P.dma('sp', lambda e: e.dma_start(out=gg[:, 1, :], in_=g2_d), writes=['gg' + tag])
        P.op('dve', lambda e: e.tensor_copy(out=mv[:].rearrange("p m k -> p (m k)"), in_=modsT[:, :, col]), reads=['modsT'], writes=['mv' + tag])
        for j, m in ((0, 1), (1, 4)):
            P.op('dve', lambda e, j=j, m=m: e.scalar_tensor_tensor(out=mv[:, m, :], in0=mv[:, m, :], scalar=1.0, in1=gg[:, j, :],
                                                                   op0=ALU.add, op1=ALU.mult),
                 reads=['gg' + tag], writes=['mv' + tag])
        key = 'mv' + tag
        return dict(A1=mv[:, 1, :], B1=mv[:, 0, :], G1=mv[:, 2, :], A2=mv[:, 4, :], B2=mv[:, 3, :], G2=mv[:, 5, :], key=key)

    def load_xT(self, x_d, ntok, xT, xkey, stage, t0=0):
        P = self.P
        if isinstance(x_d, FM):
            keys = [(xkey, g) for g in range(t0 // 512, (t0 + ntok - 1) // 512 + 1)]
            for half in range(2):
                P.dma('sp', lambda e, half=half: e.dma_start(out=xT[:, half * 4:(half + 1) * 4, t0:t0 + ntok], in_=x_d.ap[:, half * 4:(half + 1) * 4, x_d.t0:x_d.t0 + ntok]), writes=keys)
            return
        for tt in range(ntok // 128):
            sb_ = self.nextrot('stage', 2)
            P.dma('sp', lambda e, tt=tt, sb_=sb_: e.dma_start(out=stage[sb_][:], in_=x_d[tt * 128:(tt + 1) * 128, :]), writes=[('stage', sb_)])
            for half in range(2):
                pb = 4 + self.nextrot('ldbank', 2)
                for j in range(4):
                    kc = half * 4 + j
                    P.op('pe', lambda e, sb_=sb_, kc=kc, pb=pb, j=j: e.transpose(self.bank(pb)[:, j * 128:(j + 1) * 128], stage[sb_][:, kc * 128:(kc + 1) * 128], self.ident[:]),
                         reads=[('stage', sb_), 'ident'], writes=[('ps', pb)])
                eng = 'act' if half == 0 else 'dve'
                dst = xT[:, half * 4:(half + 1) * 4, t0 + tt * 128:t0 + (tt + 1) * 128]
                src = self.bank(pb).rearrange("p (a b) -> p a b", a=4)
                if eng == 'act':
                    P.op('act', lambda e, dst=dst, src=src: e.activation(out=dst, in_=src, func=AF.Copy), writes=[('ps', pb), (xkey, (t0 + tt * 128) // 512)])
                else:
                    P.op('dve', lambda e, dst=dst, src=src: e.tensor_copy(out=dst, in_=src), writes=[('ps', pb), (xkey, (t0 + tt * 128) // 512)])

    def store_x(self, xT, xkey, ntok, out_d, ostage, scale_ap=None, src0=0):
        P = self.P
        if isinstance(out_d, FM):
            keys = [(xkey, g) for g in range(src0 // 512, (src0 + ntok - 1) // 512 + 1)]
            for half in range(2):
                P.dma('sp', lambda e, half=half: e.dma_start(out=out_d.ap[:, half * 4:(half + 1) * 4, out_d.t0:out_d.t0 + ntok], in_=xT[:, half * 4:(half + 1) * 4, src0:src0 + ntok]), reads=keys)
            return
        for tt in range(ntok // 128):
            ob = self.nextrot('ostage', 2)
            for half in range(2):
                pb = 4 + self.nextrot('ldbank', 2)
                for j in range(4):
                    kc = half * 4 + j
                    P.op('pe', lambda e, kc=kc, pb=pb, j=j, tt=tt: e.transpose(self.bank(pb)[:, j * 128:(j + 1) * 128], xT[:, kc, tt * 128:(tt + 1) * 128], self.ident[:]),
                         reads=[(xkey, tt // 4), 'ident'], writes=[('ps', pb)])
                dst = ostage[ob][:, half * 512:(half + 1) * 512]
                if half == 0:
                    P.op('act', lambda e, dst=dst, pb=pb: e.activation(out=dst, in_=self.bank(pb), func=AF.Copy), writes=[('ps', pb), ('ostage', ob, half)])
                else:
                    P.op('dve', lambda e, dst=dst, pb=pb: e.tensor_copy(out=dst, in_=self.bank(pb)), writes=[('ps', pb), ('ostage', ob, half)])
            P.dma('sp', lambda e, ob=ob, tt=tt: e.dma_start(out=out_d[tt * 128:(tt + 1) * 128, :], in_=ostage[ob][:]),
                  reads=[('ostage', ob, 0), ('ostage', ob, 1)])

    def norm_mod(self, xT, xkey, ntok, A, B, mkey, hT, hkey, scr, t0=0, ht0=0, sq_on_act=False):
        P = self.P
        tgs = min(512, ntok)
        for g in range(ntok // tgs):
            c0 = t0 + g * tgs
            h0 = ht0 + g * tgs
            pb = 6 + self.nextrot('nbank', 2)
            for kc in range(KC):
                sq = self.nextrot('sq', 2)
                if sq_on_act:
                    P.op('act', lambda e, kc=kc, sq=sq, c0=c0: e.activation(out=scr['sq'][sq][:, :tgs], in_=xT[:, kc, c0:c0 + tgs], func=AF.Square),
                         reads=[(xkey, c0 // 512)], writes=[('sq', sq)])
                else:
                    P.op('pool', lambda e, kc=kc, sq=sq, c0=c0: e.tensor_tensor(out=scr['sq'][sq][:, :tgs], in0=xT[:, kc, c0:c0 + tgs], in1=xT[:, kc, c0:c0 + tgs], op=ALU.mult),
                         reads=[(xkey, c0 // 512)], writes=[('sq', sq)])
                P.op('pe', lambda e, kc=kc, sq=sq, pb=pb: e.matmul(self.bank(pb, tgs), lhsT=self.ones_bf[:], rhs=scr['sq'][sq][:, :tgs], start=(kc == 0), stop=(kc == KC - 1)),
                     reads=[('sq', sq), 'ones_bf'], writes=[('ps', pb)])
            rs = scr['rs']
            P.op('act', lambda e, pb=pb: e.activation(out=rs[:, :tgs], in_=self.bank(pb, tgs), func=AF.Ln, scale=1.0 / D, bias=self.eps_t[:]),
                 reads=['eps_t'], writes=[('ps', pb), 'rs'])
            P.op('act', lambda e: e.activation(out=rs[:, :tgs], in_=rs[:, :tgs], func=AF.Exp, scale=-0.5), writes=['rs'])
            for kc in range(KC):
                tb = self.nextrot('tmp', 2)
                P.op('dve', lambda e, kc=kc, tb=tb, c0=c0: e.scalar_tensor_tensor(out=scr['tmp'][tb][:, :tgs], in0=xT[:, kc, c0:c0 + tgs], scalar=A[:, kc:kc + 1],
                                                                                in1=rs[:, :tgs], op0=ALU.mult, op1=ALU.mult),
                     reads=[(xkey, c0 // 512), 'rs', mkey], writes=[('tmp', tb)])
                P.op('act', lambda e, kc=kc, tb=tb, h0=h0: e.activation(out=hT[:, kc, h0:h0 + tgs], in_=scr['tmp'][tb][:, :tgs], func=AF.Identity, bias=B[:, kc:kc + 1]),
                     reads=[('tmp', tb), mkey], writes=[(hkey, h0 // 512)])

    def load_w(self, dst, wkey, w_d, r0, nkc, c0, ncol):
        P = self.P
        for k in range(nkc):
            P.dma('pool', lambda e, k=k: e.dma_start(out=dst[:, k, 0:ncol], in_=w_d[r0 + k * 128:r0 + (k + 1) * 128, c0:c0 + ncol]),
                  writes=[(wkey, k)])

    def mlp_preload(self, w1_d, w2_d, wbuf, nblk=3):
        for j in range(nblk):
            wb = j % 3
            self.load_w(wbuf['w1'][wb], ('w1', wb), w1_d, 0, KC, j * 512, 512)
            self.load_w(wbuf['w2'][wb], ('w2', wb), w2_d, j * 512, 4, 0, D)

    def mlp(self, xT, xkey, ntok, hT, hkey, G, mkey, w1_d, w2_d, wbuf, aT, rbuf, groups=None, preloaded=0):
        P = self.P
        FB = 512
        nfb = DFF // FB
        if groups is None:
            tgs = min(512, ntok)
            groups = [(g * tgs, tgs, G, mkey) for g in range(ntok // tgs)]

        def ff1(j):
            wb = j % 3
            if j >= preloaded:
                self.load_w(wbuf['w1'][wb], ('w1', wb), w1_d, 0, KC, j * FB, FB)
                self.load_w(wbuf['w2'][wb], ('w2', wb), w2_d, j * FB, FB // 128, 0, D)
            ab = j % 2
            for fc in range(FB // 128):
                for (t0g, n, _, _) in groups:
                    pb = self.nextrot('ff1bank', 3)
                    for kc in range(KC):
                        P.op('pe', lambda e, wb=wb, fc=fc, t0g=t0g, n=n, kc=kc, pb=pb: e.matmul(self.bank(pb, n), lhsT=wbuf['w1'][wb][:, kc, fc * 128:(fc + 1) * 128],
                                                                                            rhs=hT[:, kc, t0g:t0g + n], start=(kc == 0), stop=(kc == KC - 1)),
                             reads=[(('w1', wb), kc), (hkey, t0g // 512)], writes=[('ps', pb)])
                    rb = self.nextrot('rbuf', 2)
                    P.op('act', lambda e, pb=pb, rb=rb, n=n: e.activation(out=rbuf[rb][:, :n], in_=self.bank(pb, n), func=AF.Relu), writes=[('ps', pb), ('rbuf', rb)])
                    P.op('pool', lambda e, rb=rb, ab=ab, fc=fc, t0g=t0g, n=n: e.tensor_tensor(out=aT[ab][:, fc, t0g:t0g + n], in0=rbuf[rb][:, :n], in1=rbuf[rb][:, :n], op=ALU.mult),
                         reads=[('rbuf', rb)], writes=[('aT', ab, fc, t0g)])

        def ff2(j):
            wb = j % 3
            ab = j % 2
            for dc in range(KC):
                for (t0g, n, Gg, mk) in groups:
                    pb = 3 + self.nextrot('ff2bank', 3)
                    nf = FB // 128
                    for fc in range(nf):
                        P.op('pe', lambda e, wb=wb, ab=ab, fc=fc, t0g=t0g, n=n, dc=dc, pb=pb: e.matmul(self.bank(pb, n), lhsT=wbuf['w2'][wb][:, fc, dc * 128:(dc + 1) * 128],
                                                                                                   rhs=aT[ab][:, fc, t0g:t0g + n], start=(fc == 0), stop=(fc == nf - 1)),
                             reads=[(('w2', wb), fc), ('aT', ab, fc, t0g)], writes=[('ps', pb)])
                    P.op('dve', lambda e, dc=dc, t0g=t0g, n=n, pb=pb, Gg=Gg: e.scalar_tensor_tensor(out=xT[:, dc, t0g:t0g + n], in0=self.bank(pb, n), scalar=Gg[:, dc:dc + 1],
                                                                                                in1=xT[:, dc, t0g:t0g + n], op0=ALU.mult, op1=ALU.add),
                         reads=[mk], writes=[('ps', pb), (xkey, t0g // 512)])

        ff1(0)
        for j in range(nfb):
            if j + 1 < nfb:
                ff1(j + 1)
            ff2(j)


def load_cast(kb, dst, key, src_d):
    kb.P.dma('pool', lambda e: e.dma_start(out=dst, in_=src_d), writes=[key])


def proj_fm(kb, hT, hkey, t0, n, W, wkey, col0, pb):
    for kc in range(KC):
        kb.P.op('pe', lambda e, kc=kc: e.matmul(kb.bank(pb, n), lhsT=W[:, kc, col0:col0 + 128], rhs=hT[:, kc, t0:t0 + n],
                                               start=(kc == 0), stop=(kc == KC - 1)),
                reads=[(wkey, kc), (hkey, t0 // 512)], writes=[('ps', pb)])


def qk_norm_rope(kb, pb, n, gain, rope, qscale, out_ap, outkeys, C):
    P = kb.P
    r = kb.nextrot('qkr', 2)
    kg, k2, rs, t1 = C['kg'][r], C['k2'][r], C['rs2'][r], C['t1'][r]
    P.op('act', lambda e: e.activation(out=kg[:, :n], in_=kb.bank(pb, n), func=AF.Copy, scale=gain[:, 0:1]), reads=['gains'], writes=[('ps', pb), ('kg', r)])
    P.op('act', lambda e: e.activation(out=k2[:, :n], in_=kb.bank(pb, n), func=AF.Square), writes=[('ps', pb), ('k2', r)])
    P.op('pe', lambda e: e.matmul(kb.bank(2, n), lhsT=C['bones'][:], rhs=k2[:, :n], start=True, stop=True), reads=[('k2', r), 'bones'], writes=[('ps', 2)])
    if rope is not None:
        P.op('pe', lambda e: e.matmul(kb.bank(3, n), lhsT=C['rmat'][:], rhs=kg[:, :n], start=True, stop=True), reads=[('kg', r), 'rmat'], writes=[('ps', 3)])
    P.op('act', lambda e: e.activation(out=rs[:, :n], in_=kb.bank(2, n), func=AF.Ln, scale=1.0 / 64, bias=kb.eps_t[:]), reads=['eps_t'], writes=[('ps', 2), ('rs2', r)])
    if qscale:
        P.op('act', lambda e: e.activation(out=rs[:, :n], in_=rs[:, :n], func=AF.Exp, scale=-0.5, bias=C['lnq'][:]), reads=['lnq'], writes=[('rs2', r)])
    else:
        P.op('act', lambda e: e.activation(out=rs[:, :n], in_=rs[:, :n], func=AF.Exp, scale=-0.5), writes=[('rs2', r)])
    if rope is not None:
        cos_ap, sin_ap, rkey = rope
        P.op('dve', lambda e: e.tensor_tensor(out=t1[:, :n], in0=kg[:, :n], in1=cos_ap, op=ALU.mult), reads=[('kg', r), rkey], writes=[('t1', r)])
        P.op('dve', lambda e: e.tensor_tensor(out=kg[:, :n], in0=kb.bank(3, n), in1=sin_ap, op=ALU.mult), reads=[rkey], writes=[('ps', 3), ('kg', r)])
        P.op('dve', lambda e: e.tensor_tensor(out=t1[:, :n], in0=t1[:, :n], in1=kg[:, :n], op=ALU.add), reads=[('kg', r)], writes=[('t1', r)])
        P.op('dve', lambda e: e.tensor_tensor(out=out_ap, in0=t1[:, :n], in1=rs[:, :n], op=ALU.mult), reads=[('t1', r), ('rs2', r)], writes=outkeys)
    else:
        P.op('dve', lambda e: e.tensor_tensor(out=out_ap, in0=kg[:, :n], in1=rs[:, :n], op=ALU.mult), reads=[('kg', r), ('rs2', r)], writes=outkeys)


def attention(kb, q_ap, qkeys, NQ, tiles, dst_a, dst_b, dstkeys, C):
    P = kb.P
    oset = kb.nextrot('oset', 2)
    o0 = 4 + 2 * oset
    nt = len(tiles)
    ssets = []

    def qk(i):
        t = tiles[i]
        ss = kb.nextrot('sset', 2)
        ssets.append(ss)
        s0 = 2 * ss
        P.op('pe', lambda e: e.matmul(kb.bank(s0, NQ), lhsT=t['ka'], rhs=q_ap[0:64, :], start=True, stop=True), reads=t['keys'] + qkeys, writes=[('ps', s0)])
        P.op('pe', lambda e: e.matmul(kb.bank(s0 + 1, NQ), lhsT=t['kb'], rhs=q_ap[64:128, :], start=True, stop=True), reads=t['keys'] + qkeys, writes=[('ps', s0 + 1)])

    qk(0)
    for i in range(nt):
        if i + 1 < nt:
            qk(i + 1)
        t = tiles[i]
        s0 = 2 * ssets[i]
        pbuf = kb.nextrot('pbuf', 3)
        pb_ = C['pbuf'][pbuf]
        src = kb.ps[:, s0 * 512:(s0 + 2) * 512].rearrange("p (h n) -> p h n", h=2)[:, :, 0:NQ]
        if t.get('bias_a') is not None:
            sb_ = C['sbias'][kb.nextrot('sbias', 2)]
            P.op('dve', lambda e, sb_=sb_, t=t, s0=s0: e.tensor_tensor(out=sb_[:, 0, 0:NQ], in0=kb.bank(s0, NQ), in1=t['bias_a'], op=ALU.add), reads=t['bkeys'], writes=[('ps', s0), ('sbias', id(sb_), 0)])
            P.op('dve', lambda e, sb_=sb_, t=t, s0=s0: e.tensor_tensor(out=sb_[:, 1, 0:NQ], in0=kb.bank(s0 + 1, NQ), in1=t['bias_b'], op=ALU.add), reads=t['bkeys'], writes=[('ps', s0 + 1), ('sbias', id(sb_), 1)])
            P.op('act', lambda e, sb_=sb_, pb_=pb_: e.activation(out=pb_[:, :, 0:NQ], in_=sb_[:, :, 0:NQ], func=AF.Exp), reads=[('sbias', id(sb_), 0), ('sbias', id(sb_), 1)], writes=[('pbuf', pbuf)])
        else:
            P.op('act', lambda e, src=src, pb_=pb_: e.activation(out=pb_[:, :, 0:NQ], in_=src, func=AF.Exp), writes=[('ps', s0), ('ps', s0 + 1), ('pbuf', pbuf)])
        P.op('pe', lambda e, t=t, pb_=pb_, i=i: e.matmul(kb.bank(o0, NQ), lhsT=t['va'], rhs=pb_[:, 0, 0:NQ], start=(i == 0), stop=(i == nt - 1)), reads=t['keys'] + [('pbuf', pbuf)], writes=[('ps', o0)])
        P.op('pe', lambda e, t=t, pb_=pb_, i=i: e.matmul(kb.bank(o0 + 1, NQ), lhsT=t['vb'], rhs=pb_[:, 1, 0:NQ], start=(i == 0), stop=(i == nt - 1)), reads=t['keys'] + [('pbuf', pbuf)], writes=[('ps', o0 + 1)])
    rcr = kb.nextrot('rc', 2)
    rc = C['rc'][rcr]
    P.op('dve', lambda e: e.reciprocal(out=rc[64:128, 0:NQ], in_=kb.bank(o0, NQ)[64:128, :]), writes=[('ps', o0), ('rc', rcr, 0)])
    P.op('dve', lambda e: e.tensor_tensor(out=dst_a, in0=kb.bank(o0, NQ)[0:64, :], in1=rc[64:128, 0:NQ], op=ALU.mult), reads=[('rc', rcr, 0)], writes=[('ps', o0)] + dstkeys)
    P.op('dve', lambda e: e.reciprocal(out=rc[0:64, 0:NQ], in_=kb.bank(o0 + 1, NQ)[0:64, :]), writes=[('ps', o0 + 1), ('rc', rcr, 1)])
    P.op('dve', lambda e: e.tensor_tensor(out=dst_b, in0=kb.bank(o0 + 1, NQ)[64:128, :], in1=rc[0:64, 0:NQ], op=ALU.mult), reads=[('rc', rcr, 1)], writes=[('ps', o0 + 1)] + dstkeys)


def attn_scratch(kb, with_bias=False):
    C = dict(pbuf=[kb.sb("pbuf%d" % i, [128, 2, 512], BF16) for i in range(3)],
             rc=[kb.sb("rc%d" % i, [128, 512], F32) for i in range(2)])
    if with_bias:
        C['sbias'] = [kb.sb("sbias%d" % i, [128, 2, 512], F32) for i in range(2)]
    return C


def qk_scratch(kb, C):
    C['kg'] = [kb.sb("kg%d" % i, [128, 512], BF16) for i in range(2)]
    C['k2'] = [kb.sb("k2%d" % i, [128, 512], BF16) for i in range(2)]
    C['rs2'] = [kb.sb("rs2%d" % i, [128, 512], F32) for i in range(2)]
    C['t1'] = [kb.sb("t1%d" % i, [128, 512], F32) for i in range(2)]


def norm_scratch(kb):
    return dict(sq=[kb.sb("sq%d" % i, [128, 512], BF16) for i in range(2)], rs=kb.sb("rs", [128, 512], F32),
                tmp=[kb.sb("tmp%d" % i, [128, 512], F32) for i in range(2)])


def mlp_bufs(kb, ntok):
    wbuf = dict(w1=[kb.sb("w1b%d" % i, [128, 8, 512], BF16) for i in range(3)], w2=[kb.sb("w2b%d" % i, [128, 4, 1024], BF16) for i in range(3)])
    aT = [kb.sb("aT%d" % i, [128, 4, ntok], BF16) for i in range(2)]
    rbuf = [kb.sb("rbuf%d" % i, [128, 512], F32) for i in range(2)]
    return wbuf, aT, rbuf


def resid_proj(kb, srcT, skey, t0, n, W, wkey, xT, xkey, xt0, G, mkey, bG=None):
    P = kb.P
    for dc in range(KC):
        pb = kb.nextrot('projbank', 2)
        proj_fm(kb, srcT, skey, t0, n, W, wkey, dc * 128, pb)
        P.op('dve', lambda e, dc=dc, pb=pb: e.scalar_tensor_tensor(out=xT[:, dc, xt0:xt0 + n], in0=kb.bank(pb, n), scalar=G[:, dc:dc + 1],
                                                                 in1=xT[:, dc, xt0:xt0 + n], op0=ALU.mult, op1=ALU.add),
             reads=[mkey], writes=[('ps', pb), (xkey, xt0 // 512)])
        if bG is not None:
            P.op('act', lambda e, dc=dc: e.activation(out=xT[:, dc, xt0:xt0 + n], in_=xT[:, dc, xt0:xt0 + n], func=AF.Identity, bias=bG[:, dc:dc + 1]),
                 reads=['bG'], writes=[(xkey, xt0 // 512)])


def build_l0(kb=None, io=None):
    own = kb is None
    if own:
        kb = KB()
    io = io or {}
    kb.pfx = '' if own else 'l0_'
    P, nc = kb.P, kb.nc
    NTOK = 2048
    NG0 = 4 if io.get("kv_gather") else 16
    xb_d = io.get("xb") or kb.din("xb", [NG0 * 512, D]); ctx_d = io.get("ctx") or kb.din("ctx", [256, D])
    cv_d = kb.din("cv", [128, 8, 2]); adaw_d = kb.din("adaw", [D, kb.ncol_ada()]); adab_d = kb.din("adab", [2, kb.ncol_ada()])
    g1_d = kb.din("g1", [128, 8]); g2_d = kb.din("g2", [128, 8])
    wqkv_d = kb.din("wqkv", [D, 1536]); gains_d = kb.din("gains", [128, 2])
    cos_d = kb.din("cos", [NG0, 128, 512]); sin_d = kb.din("sin", [NG0, 128, 512])
    wo_d = kb.din("wo", [D, D]); w1_d = kb.din("w1", [D, DFF]); w2_d = kb.din("w2", [DFF, D])
    rmat_d = kb.din("rmat", [128, 128]); bones_d = kb.din("bones", [128, 128])
    out_d = io.get("out") or kb.dout("out", [NTOK, D]); hctx_d = io.get("hctx") or kb.dout("hctx", [256, D])
    mid_d = io.get("mid") or out_d

    modsT = kb.sb("modsT", [128, 48, 2], F32)
    mvL = kb.sb("mvL", [128, 6, 8], F32); mvC = kb.sb("mvC", [128, 6, 8], F32)
    C = dict(rmat=kb.sb("rmat", [128, 128], BF16), bones=kb.sb("bones", [128, 128], BF16), lnq=kb.sb("lnq", [128, 1], F32))
    gains = kb.sb("gains", [128, 2], F32)
    with kb.phase():
        load_cast(kb, C['rmat'][:], 'rmat', rmat_d)
        load_cast(kb, C['bones'][:], 'bones', bones_d)
        P.op('pool', lambda e: e.memset(C['lnq'][:], float(np.log(0.125))), writes=['lnq'])
        P.dma('sp', lambda e: e.dma_start(out=gains[:], in_=gains_d), writes=['gains'])
        kb.mods(cv_d, adaw_d, adab_d, modsT)
        mL = kb.mod_vectors(modsT, g1_d, g2_d, 0, mvL, "L")
        mC = kb.mod_vectors(modsT, g1_d, g2_d, 1, mvC, "C")
    qgain, kgain = gains[:, 0:1], gains[:, 1:2]

    with kb.phase():
        KT = kb.sb("KT", [128, 2, 8448], BF16)
        Ve = kb.sb("Ve", [128, 66, 384], BF16)
        P.op('pool', lambda e: e.memset(Ve[:].rearrange("p t (a b c) -> p (t a) b c", a=2, b=3, c=64)[:, :, 1, :], 1.0), writes=['Ve_ones'])

        def kv_tiles(tile_ids, pr):
            out = []
            for kt in tile_ids:
                out.append(dict(ka=KT[0:64, pr, kt * 128:(kt + 1) * 128], kb=KT[64:128, pr, kt * 128:(kt + 1) * 128],
                                va=Ve[:, kt, pr * 192:pr * 192 + 128], vb=Ve[:, kt, pr * 192 + 64:pr * 192 + 192],
                                keys=[('KT', kt // 4), ('Ve', kt), 'Ve_ones']))
            return out

        def produce_kv(hT, hkey, n, g, wqkv, rope):
            for pr in range(2):
                pb = kb.nextrot('projbank', 2)
                proj_fm(kb, hT, hkey, 0, n, wqkv, 'wqkv', 1024 + pr * 128, pb)
                qk_norm_rope(kb, pb, n, kgain, rope, False, KT[:, pr, g * 512:g * 512 + n], [('KT', g)], C)
            for tt in range(n // 128):
                pb = kb.nextrot('projbank', 2)
                for kc in range(KC):
                    P.op('pe', lambda e, kc=kc, tt=tt, pb=pb: e.matmul(kb.bank(pb, 256), lhsT=hT[:, kc, tt * 128:(tt + 1) * 128], rhs=wqkv[:, kc, 1280:1536],
                                                                      start=(kc == 0), stop=(kc == KC - 1)),
                         reads=[('wqkv', kc), (hkey, 0)], writes=[('ps', pb)])
                kt = g * 4 + tt
                dst = Ve[:, kt, :].rearrange("p (a b c) -> p a b c", a=2, b=3, c=64)[:, :, ::2, :]
                src = kb.bank(pb, 256).rearrange("p (a b c) -> p a b c", a=2, b=2, c=64)
                P.op('dve', lambda e, dst=dst, src=src: e.tensor_copy(out=dst, in_=src), writes=[('ps', pb), ('Ve', kt)])

        with kb.phase():
            cT = kb.sb("cT", [128, 8, 256], F32)
            stage = [kb.sb("stage%d" % i, [128, 1024], F32) for i in range(2)]
            scr = norm_scratch(kb)
            with kb.phase():
                hcT = kb.sb("hcT", [128, 8, 256], BF16)
                QcT = kb.sb("QcT", [128, 8, 256], BF16)
                OcT = kb.sb("OcT", [128, 8, 256], BF16)
                qk_scratch(kb, C)
                C.update(attn_scratch(kb))
                wqkv = kb.sb("wqkv", [128, 8, 1536], BF16)
                wo = kb.sb("wo", [128, 8, 1024], BF16)
                kb.load_w(wqkv, 'wqkv', wqkv_d, 0, KC, 0, 1536)
                kb.load_w(wo, 'wo', wo_d, 0, KC, 0, 1024)
                kb.load_xT(ctx_d, 256, cT, 'cT', stage)
                kb.norm_mod(cT, 'cT', 256, mC['A1'], mC['B1'], mC['key'], hcT, 'hcT', scr)
                produce_kv(hcT, 'hcT', 256, 16, wqkv, None)
                for c in range(8):
                    pb = kb.nextrot('projbank', 2)
                    proj_fm(kb, hcT, 'hcT', 0, 256, wqkv, 'wqkv', c * 128, pb)
                    qk_norm_rope(kb, pb, 256, qgain, None, True, QcT[:, c, :], [('QcT', c)], C)
                for c in range(8):
                    attention(kb, QcT[:, c, :], [('QcT', c)], 256, kv_tiles([64, 65], c // 4), OcT[0:64, c, :], OcT[64:128, c, :], [('OcT', 0)], C)
                resid_proj(kb, OcT, 'OcT', 0, 256, wo, 'wo', cT, 'cT', 0, mC['G1'], mC['key'])
            if io.get('cmid') is not None:
                with kb.phase():
                    kb.store_x(cT, 'cT', 256, io['cmid'], stage)
            else:
                with kb.phase():
                    hc2 = kb.sb("hc2", [128, 8, 256], BF16)
                    kb.norm_mod(cT, 'cT', 256, mC['A2'], mC['B2'], mC['key'], hc2, 'hc2', scr)
                    wbuf, aT, rbuf = mlp_bufs(kb, 256)
                    kb.mlp(cT, 'cT', 256, hc2, 'hc2', mC['G2'], mC['key'], w1_d, w2_d, wbuf, aT, rbuf)
                    kb.store_x(cT, 'cT', 256, hctx_d, stage)

        with kb.phase():
            QT = kb.sb("QT", [128, 8, NTOK], BF16)
            with kb.phase():
                xtmp = kb.sb("xtmp", [128, 8, 512], F32)
                hTt = kb.sb("hTt", [128, 8, 512], BF16)
                stage = [kb.sb("stage%d" % i, [128, 1024], F32) for i in range(2)]
                scr = norm_scratch(kb)
                qk_scratch(kb, C)
                wqkv = kb.sb("wqkv", [128, 8, 1536], BF16)
                cs = [kb.sb("cs%d" % i, [128, 2, 512], F32) for i in range(2)]
                kb.load_w(wqkv, 'wqkv', wqkv_d, 0, KC, 0, 1536)
                for g in range(NG0):
                    kb.load_xT(tsl(xb_d, g * 512, (g + 1) * 512), 512, xtmp, 'xtmp', stage)
                    kb.norm_mod(xtmp, 'xtmp', 512, mL['A1'], mL['B1'], mL['key'], hTt, 'hTt', scr)
                    cb = g % 2
                    P.dma('sp', lambda e, g=g, cb=cb: e.dma_start(out=cs[cb][:, 0, :], in_=cos_d[g]), writes=[('cs', cb)])
                    P.dma('sp', lambda e, g=g, cb=cb: e.dma_start(out=cs[cb][:, 1, :], in_=sin_d[g]), writes=[('cs', cb)])
                    rope = (cs[cb][:, 0, :], cs[cb][:, 1, :], ('cs', cb))
                    produce_kv(hTt, 'hTt', 512, g, wqkv, rope)
                    if g < 4:
                        for c in range(8):
                            pb = kb.nextrot('projbank', 2)
                            proj_fm(kb, hTt, 'hTt', 0, 512, wqkv, 'wqkv', c * 128, pb)
                            qk_norm_rope(kb, pb, 512, qgain, rope, True, QT[:, c, g * 512:(g + 1) * 512], [('QT', c, g)], C)
            if io.get("kv_gather"):
                ktb = kb.dint("ktb", [256, 2048], BF16); ktg = kb.dint("ktg", [1024, 2048], BF16)
                veb = [kb.dint("veb%d" % h, [1024, 384], BF16) for h in range(2)]
                veg = [kb.dint("veg%d" % h, [4096, 384], BF16) for h in range(2)]
                with kb.phase():
                    P.dma('sp', lambda e: e.dma_start(out=ktb.rearrange("(p a) n -> p a n", a=2), in_=KT[:, :, 0:2048]), reads=[('KT', g_) for g_ in range(4)], writes=['ktb'])
                    for h in range(2):
                        P.dma('sp', lambda e, h=h: e.dma_start(out=veb[h].rearrange("(p t) c -> p t c", t=8), in_=Ve[:, h * 8:(h + 1) * 8, :]),
                              reads=[('Ve', kt_) for kt_ in range(16)] + ['Ve_ones'], writes=[('veb', h)])
                    P.coll([lambda e: e.collective_compute("AllGather", ALU.bypass, replica_groups=RG, ins=[ktb.opt()], outs=[ktg.opt()]),
                            lambda e: e.collective_compute("AllGather", ALU.bypass, replica_groups=RG, ins=[veb[0].opt()], outs=[veg[0].opt()]),
                            lambda e: e.collective_compute("AllGather", ALU.bypass, replica_groups=RG, ins=[veb[1].opt()], outs=[veg[1].opt()])],
                           reads=['ktb', ('veb', 0), ('veb', 1)], writes=['ktg', ('veg', 0), ('veg', 1)])
                    for r in range(4):
                        P.dma('sp', lambda e, r=r: e.dma_start(out=KT[:, :, r * 2048:(r + 1) * 2048], in_=ktg[r * 256:(r + 1) * 256, :].rearrange("(p a) n -> p a n", a=2)),
                              reads=['ktg'], writes=[('KT', g_) for g_ in range(r * 4, r * 4 + 4)])
                        for h in range(2):
                            P.dma('sp', lambda e, r=r, h=h: e.dma_start(out=Ve[:, r * 16 + h * 8:r * 16 + (h + 1) * 8, :], in_=veg[h][r * 1024:(r + 1) * 1024, :].rearrange("(p t) c -> p t c", t=8)),
                                  reads=[('veg', h)], writes=[('Ve', kt_) for kt_ in range(r * 16 + h * 8, r * 16 + (h + 1) * 8)])
            with kb.phase():
                OT = kb.sb("OT", [128, 8, NTOK], BF16)
                with kb.phase():
                    C.update(attn_scratch(kb))
                    for qg in range(4):
                        for c in range(8):
                            attention(kb, QT[:, c, qg * 512:(qg + 1) * 512], [('QT', c, qg)], 512, kv_tiles(list(range(66)), c // 4),
                                      OT[0:64, c, qg * 512:(qg + 1) * 512], OT[64:128, c, qg * 512:(qg + 1) * 512], [('OT', qg)], C)
                with kb.phase():
                    xtmp = kb.sb("xtmp", [128, 8, 512], F32)
                    stage = [kb.sb("stage%d" % i, [128, 1024], F32) for i in range(2)]
                    wo = kb.sb("wo", [128, 8, 1024], BF16)
                    ostage = [kb.sb("ostage%d" % i, [128, 1024], F32) for i in range(2)]
                    kb.load_w(wo, 'wo', wo_d, 0, KC, 0, 1024)
                    for g in range(4):
                        kb.load_xT(tsl(xb_d, g * 512, (g + 1) * 512), 512, xtmp, 'xtmp', stage)
                        resid_proj(kb, OT, 'OT', g * 512, 512, wo, 'wo', xtmp, 'xtmp', 0, mL['G1'], mL['key'])
                        kb.store_x(xtmp, 'xtmp', 512, tsl(mid_d, g * 512, (g + 1) * 512), ostage)
    mlp_tail(kb, mid_d, out_d, NTOK, mL, w1_d, w2_d, extra=((io['cmid'], hctx_d, 256, mC) if io.get('cmid') is not None else None))
    if own:
        P.close()
    return nc


def mlp_tail(kb, src_d, out_d, ntok, mL, w1_d, w2_d, final=None, extra=None):
    ntot = ntok + (extra[2] if extra else 0)
    with kb.phase():
        xT = kb.sb("xT", [128, 8, ntot], F32)
        with kb.phase():
            stage = [kb.sb("stage%d" % i, [128, 1024], F32) for i in range(2)]
            kb.load_xT(src_d, ntok, xT, 'xT', stage)
            if extra:
                kb.load_xT(extra[0], extra[2], xT, 'xT', stage, t0=ntok)
        with kb.phase():
            hT = kb.sb("hT", [128, 8, ntot], BF16)
            scr = norm_scratch(kb)
            wbuf, aT, rbuf = mlp_bufs(kb, ntot)
            kb.mlp_preload(w1_d, w2_d, wbuf)
            kb.norm_mod(xT, 'xT', ntok, mL['A2'], mL['B2'], mL['key'], hT, 'hT', scr, sq_on_act=True)
            groups = [(g * 512, 512, mL['G2'], mL['key']) for g in range(ntok // 512)]
            if extra:
                mC = extra[3]
                kb.norm_mod(xT, 'xT', extra[2], mC['A2'], mC['B2'], mC['key'], hT, 'hT', scr, t0=ntok, ht0=ntok, sq_on_act=True)
                groups.append((ntok, extra[2], mC['G2'], mC['key']))
            kb.mlp(xT, 'xT', ntot, hT, 'hT', mL['G2'], mL['key'], w1_d, w2_d, wbuf, aT, rbuf, groups=groups, preloaded=3)
        with kb.phase():
            ostage = [kb.sb("ostage%d" % i, [128, 1024], F32) for i in range(2)]
            if final is not None:
                yT = kb.sb("yT", [128, 8, ntok], F32)
                scr = norm_scratch(kb)
                kb.norm_mod(xT, 'xT', ntok, final[0], final[1], final[2], yT, 'yT', scr)
                kb.store_x(yT, 'yT', ntok, out_d, ostage)
            else:
                kb.store_x(xT, 'xT', ntok, out_d, ostage)
                if extra:
                    kb.store_x(xT, 'xT', extra[2], extra[1], ostage, src0=ntok)


def fm(v):
    return np.ascontiguousarray(np.asarray(v, np.float32).reshape(8, 128).T)


def common_inputs(inp, layer, b):
    cv = np.stack([fm(inp['c'][b]), fm(inp['c_ctx'])], axis=-1)
    return dict(ident=np.eye(128, dtype=np.float32), cv=np.ascontiguousarray(cv), adaw=np.ascontiguousarray(inp['ada_w'][layer]),
                adab=np.ascontiguousarray(np.stack([inp['ada_b'][layer]] * 2)), g1=fm(inp['norm1_g'][layer]), g2=fm(inp['norm2_g'][layer]),
                w1=np.ascontiguousarray(inp['mlp_w1'][layer]), w2=np.ascontiguousarray(inp['mlp_w2'][layer]))


def rope_tables(order):
    t = np.asarray(order)
    row = (t // 64).astype(np.float32)
    col = (t % 64).astype(np.float32)
    inv = (10000.0 ** (-np.arange(16, dtype=np.float32) / 16)).astype(np.float32)
    ang = np.concatenate([row[:, None] * inv, col[:, None] * inv], axis=-1).astype(np.float32)
    idx = (np.arange(128) % 64) // 2
    a = ang[:, idx].T
    cos = np.cos(a).astype(np.float32).reshape(128, 16, 512).transpose(1, 0, 2)
    sin = np.sin(a).astype(np.float32).reshape(128, 16, 512).transpose(1, 0, 2)
    return np.ascontiguousarray(cos), np.ascontiguousarray(sin)


def gqa_chunk_heads():
    return [(c, 4 + c) if c < 4 else (8 + c - 4, 12 + c - 4) for c in range(8)]


def prep_l0(inp, b, q):
    d = common_inputs(inp, 0, b)
    order = np.concatenate([np.arange(q * 2048, 8192), np.arange(0, q * 2048)])
    d['xb'] = np.ascontiguousarray(inp['x'][b][order])
    d['ctx'] = np.ascontiguousarray(inp['ctx'][b])
    wqkv = inp['at_w_qkv'][0]
    qcols = np.concatenate([np.concatenate([np.arange(ha * 64, ha * 64 + 64), np.arange(hb * 64, hb * 64 + 64)]) for ha, hb in gqa_chunk_heads()])
    d['wqkv'] = np.ascontiguousarray(np.concatenate([wqkv[:, qcols], wqkv[:, 1024:]], axis=1))
    d['wo'] = np.ascontiguousarray(inp['at_w_o'][0][qcols, :])
    d['gains'] = np.ascontiguousarray(np.stack([np.tile(inp['at_q_g'][0], 2), np.tile(inp['at_k_g'][0], 2)], axis=1))
    d['cos'], d['sin'] = rope_tables(order)
    rmat = np.zeros((128, 128), np.float32)
    for i in range(64):
        rmat[2 * i + 1, 2 * i] = -1.0
        rmat[2 * i, 2 * i + 1] = 1.0
    d['rmat'] = rmat
    bones = np.zeros((128, 128), np.float32)
    bones[:64, :64] = 1.0
    bones[64:, 64:] = 1.0
    d['bones'] = bones
    return d


NA_CLASS = [0, 1] + [2] * 12 + [3, 4]
NA_ST = list(range(14)) + [12, 13]


def na_unit(kb, q_ap, qkeys, kt_ap, vt, tab_ap, tkey, st, dst, dstkeys, kvkeys, C):
    P = kb.P
    u = kb.nextrot('naunit', 2)
    b0 = 3 * u
    ob = 6 + u
    for j in range(7):
        kt = st + j
        P.op('pe', lambda e, j=j, kt=kt: e.matmul(kb.ps[:, b0 * 512 + j * 128:b0 * 512 + (j + 1) * 128], lhsT=kt_ap(kt), rhs=q_ap, start=True, stop=True),
             reads=kvkeys + qkeys, writes=[('ps', b0 + j // 4)])
    for j in range(2):
        kt = 20 + j
        P.op('pe', lambda e, j=j, kt=kt: e.matmul(kb.ps[:, (b0 + 2) * 512 + j * 128:(b0 + 2) * 512 + (j + 1) * 128], lhsT=kt_ap(kt), rhs=q_ap, start=True, stop=True),
             reads=kvkeys + qkeys, writes=[('ps', b0 + 2)])
    sbr = kb.nextrot('nasb', 2)
    sb_ = C['nasb'][sbr]
    P.op('dve', lambda e: e.tensor_tensor(out=sb_[:], in0=kb.ps[:, b0 * 512:b0 * 512 + 896], in1=tab_ap, op=ALU.add), reads=[tkey], writes=[('ps', b0), ('ps', b0 + 1), ('nasb', sbr)])
    pr = kb.nextrot('napw', 3)
    pw, pc = C['napw'][pr], C['napc'][pr]
    P.op('act', lambda e: e.activation(out=pw[:], in_=sb_[:], func=AF.Exp), reads=[('nasb', sbr)], writes=[('napw', pr)])
    P.op('act', lambda e: e.activation(out=pc[:], in_=kb.ps[:, (b0 + 2) * 512:(b0 + 2) * 512 + 256], func=AF.Exp), writes=[('ps', b0 + 2), ('napc', pr)])
    for j in range(9):
        kt = st + j if j < 7 else 20 + (j - 7)
        rhs = pw[:, j * 128:(j + 1) * 128] if j < 7 else pc[:, (j - 7) * 128:(j - 6) * 128]
        P.op('pe', lambda e, j=j, kt=kt, rhs=rhs: e.matmul(kb.bank(ob, 128), lhsT=vt(kt), rhs=rhs, start=(j == 0), stop=(j == 8)),
             reads=kvkeys + [('napw', pr), ('napc', pr)], writes=[('ps', ob)])
    return ob


def build_l1(kb=None, io=None):
    own = kb is None
    if own:
        kb = KB()
    io = io or {}
    kb.pfx = '' if own else 'l1_'
    P, nc = kb.P, kb.nc
    NTOK = 2048
    NH = 2560
    xh_d = io.get("xh") or kb.din("xh", [NH, D]); ctx_d = io.get("ctx") or kb.din("ctx", [256, D])
    cv_d = kb.din("cv", [128, 8, 2]); adaw_d = kb.din("adaw", [D, kb.ncol_ada()]); adab_d = kb.din("adab", [2, kb.ncol_ada()])
    g1_d = kb.din("g1", [128, 8]); g2_d = kb.din("g2", [128, 8])
    wqkv_d = kb.din("wqkv", [8, D, 384]); tab_d = kb.din("tab", [5, 8, 128, 2 * 7 * 128])
    wo_d = kb.din("wo", [D, D]); w1_d = kb.din("w1", [D, DFF]); w2_d = kb.din("w2", [DFF, D])
    out_d = io.get("out") or kb.dout("out", [NTOK, D])
    mid_d = io.get("mid") or out_d

    modsT = kb.sb("modsT", [128, 48, 2], F32)
    mvL = kb.sb("mvL", [128, 6, 8], F32); mvC = kb.sb("mvC", [128, 6, 8], F32)
    C = {}
    with kb.phase():
        if io.get('pre1'):
            io['pre1']()
        kb.mods(cv_d, adaw_d, adab_d, modsT)
        mL = kb.mod_vectors(modsT, g1_d, g2_d, 0, mvL, "L")
        mC = kb.mod_vectors(modsT, g1_d, g2_d, 1, mvC, "C")
        if io.get('pre2'):
            io['pre2']()
    with kb.phase():
        hT = kb.sb("hT", [128, 8, NH], BF16)
        hcT = kb.sb("hcT", [128, 8, 256], BF16)
        OT = kb.sb("OT", [128, 8, NTOK], BF16)
        with kb.phase():
            xtmp = kb.sb("xtmp", [128, 8, 512], F32)
            stage = [kb.sb("stage%d" % i, [128, 1024], F32) for i in range(2)]
            scr = norm_scratch(kb)
            for g in range(5):
                kb.load_xT(tsl(xh_d, g * 512, (g + 1) * 512), 512, xtmp, 'xtmp', stage)
                kb.norm_mod(xtmp, 'xtmp', 512, mL['A1'], mL['B1'], mL['key'], hT, 'hT', scr, t0=0, ht0=g * 512)
            kb.load_xT(ctx_d, 256, xtmp, 'xtmp', stage)
            kb.norm_mod(xtmp, 'xtmp', 256, mC['A1'], mC['B1'], mC['key'], hcT, 'hcT', scr)
        with kb.phase():
            C['rc'] = [kb.sb("rc%d" % i, [128, 512], F32) for i in range(2)]
            C['nasb'] = [kb.sb("nasb%d" % i, [128, 896], F32) for i in range(2)]
            C['napw'] = [kb.sb("napw%d" % i, [128, 896], BF16) for i in range(3)]
            C['napc'] = [kb.sb("napc%d" % i, [128, 256], BF16) for i in range(3)]
            wc = [kb.sb("wc%d" % i, [128, 8, 384], BF16) for i in range(2)]
            QTc = [kb.sb("QTc%d" % i, [128, NTOK], BF16) for i in range(2)]
            KTc = [kb.sb("KTc%d" % i, [128, NH + 256], BF16) for i in range(2)]
            Vec = [kb.sb("Vec%d" % i, [128, 22, 192], BF16) for i in range(2)]
            tabI = [kb.sb("tabI%d" % i, [128, 2, 7, 128], F32) for i in range(2)]
            tabS = [kb.sb("tabS%d" % i, [128, 2, 7, 128], F32) for i in range(2)]
            for i in range(2):
                P.op('pool', lambda e, i=i: e.memset(Vec[i][:, :, 64:128], 1.0), writes=[('Vones', i)])
            for c in range(8):
                b = c % 2
                kb.load_w(wc[b], ('wc', b), wqkv_d[c], 0, KC, 0, 384)
                P.dma('sp', lambda e, c=c, b=b: e.dma_start(out=tabI[b][:].rearrange("p a j q -> p (a j q)"), in_=tab_d[2, c]), writes=[('tabI', b)])
                for g in range(4):
                    pb = kb.nextrot('projbank', 2)
                    proj_fm(kb, hT, 'hT', 256 + g * 512, 512, wc[b], ('wc', b), 0, pb)
                    P.op('act', lambda e, g=g, b=b, pb=pb: e.activation(out=QTc[b][:, g * 512:(g + 1) * 512], in_=kb.bank(pb), func=AF.Copy, scale=0.125),
                         writes=[('ps', pb), ('QTc', b, g)])
                for g in range(5):
                    pb = kb.nextrot('projbank', 2)
                    proj_fm(kb, hT, 'hT', g * 512, 512, wc[b], ('wc', b), 128, pb)
                    P.op('act', lambda e, g=g, b=b, pb=pb: e.activation(out=KTc[b][:, g * 512:(g + 1) * 512], in_=kb.bank(pb), func=AF.Copy),
                         writes=[('ps', pb), ('KTc', b, g)])
                pb = kb.nextrot('projbank', 2)
                proj_fm(kb, hcT, 'hcT', 0, 256, wc[b], ('wc', b), 128, pb)
                P.op('act', lambda e, b=b, pb=pb: e.activation(out=KTc[b][:, NH:NH + 256], in_=kb.bank(pb, 256), func=AF.Copy), writes=[('ps', pb), ('KTc', b, 5)])
                for kt in range(22):
                    src_h, hk, t0 = (hT, 'hT', kt * 128) if kt < 20 else (hcT, 'hcT', (kt - 20) * 128)
                    pb = kb.nextrot('projbank', 2)
                    for kc in range(KC):
                        P.op('pe', lambda e, kc=kc, b=b, pb=pb, src_h=src_h, t0=t0: e.matmul(kb.bank(pb, 128), lhsT=src_h[:, kc, t0:t0 + 128], rhs=wc[b][:, kc, 256:384],
                                                                                          start=(kc == 0), stop=(kc == KC - 1)),
                             reads=[(('wc', b), kc), (hk, t0 // 512)], writes=[('ps', pb)])
                    dst = Vec[b][:, kt, :].rearrange("p (t s) -> p t s", s=64)[:, ::2, :]
                    src = kb.bank(pb, 128).rearrange("p (t s) -> p t s", s=64)
                    P.op('dve', lambda e, dst=dst, src=src: e.tensor_copy(out=dst, in_=src), writes=[('ps', pb), ('Vec', b, kt)])
                for rp in range(16):
                    cls = NA_CLASS[rp]
                    st = NA_ST[rp]
                    if cls == 2:
                        tab, tkey = tabI[b], ('tabI', b)
                    else:
                        sbuf_i = kb.nextrot('tabS', 2)
                        tab, tkey = tabS[sbuf_i], ('tabS', sbuf_i)
                        P.dma('sp', lambda e, c=c, cls=cls, tab=tab: e.dma_start(out=tab[:].rearrange("p a j q -> p (a j q)"), in_=tab_d[cls, c]), writes=[tkey])
                    kvkeys = [('KTc', b, g_) for g_ in range(6)] + [('Vec', b, kt_) for kt_ in range(22)] + [('Vones', b)]
                    for hh in range(2):
                        p0 = hh * 64
                        q_ap = QTc[b][p0:p0 + 64, rp * 128:(rp + 1) * 128]
                        kt_ap = (lambda kt, b=b, p0=p0: KTc[b][p0:p0 + 64, kt * 128:(kt + 1) * 128])
                        vt = (lambda kt, b=b, hh=hh: Vec[b][:, kt, hh * 64:hh * 64 + 128])
                        tab_ap = tab[:, hh, :, :].rearrange("p j q -> p (j q)")
                        ob = na_unit(kb, q_ap, [('QTc', b, rp // 4)], kt_ap, vt, tab_ap, tkey, st, None, None, kvkeys, C)
                        rcr = kb.nextrot('rc', 2)
                        rc = C['rc'][rcr]
                        if hh == 0:
                            P.op('dve', lambda e, rc=rc, ob=ob: e.reciprocal(out=rc[64:128, 0:128], in_=kb.bank(ob, 128)[64:128, :]), writes=[('ps', ob), ('rc', rcr)])
                            P.op('dve', lambda e, rc=rc, ob=ob, c=c, rp=rp: e.tensor_tensor(out=OT[0:64, c, rp * 128:(rp + 1) * 128], in0=kb.bank(ob, 128)[0:64, :], in1=rc[64:128, 0:128], op=ALU.mult),
                                 reads=[('rc', rcr)], writes=[('ps', ob), ('OT', rp // 4)])
                        else:
                            P.op('dve', lambda e, rc=rc, ob=ob: e.reciprocal(out=rc[0:64, 0:128], in_=kb.bank(ob, 128)[0:64, :]), writes=[('ps', ob), ('rc', rcr)])
                            P.op('dve', lambda e, rc=rc, ob=ob, c=c, rp=rp: e.tensor_tensor(out=OT[64:128, c, rp * 128:(rp + 1) * 128], in0=kb.bank(ob, 128)[64:128, :], in1=rc[0:64, 0:128], op=ALU.mult),
                                 reads=[('rc', rcr)], writes=[('ps', ob), ('OT', rp // 4)])
        with kb.phase():
            xtmp = kb.sb("xtmp", [128, 8, 512], F32)
            stage = [kb.sb("stage%d" % i, [128, 1024], F32) for i in range(2)]
            ostage = [kb.sb("ostage%d" % i, [128, 1024], F32) for i in range(2)]
            wo = kb.sb("wo", [128, 8, 1024], BF16)
            kb.load_w(wo, 'wo', wo_d, 0, KC, 0, 1024)
            for g in range(4):
                kb.load_xT(tsl(xh_d, 256 + g * 512, 256 + (g + 1) * 512), 512, xtmp, 'xtmp', stage)
                resid_proj(kb, OT, 'OT', g * 512, 512, wo, 'wo', xtmp, 'xtmp', 0, mL['G1'], mL['key'])
                kb.store_x(xtmp, 'xtmp', 512, tsl(mid_d, g * 512, (g + 1) * 512), ostage)
    mlp_tail(kb, mid_d, out_d, NTOK, mL, w1_d, w2_d)
    if own:
        P.close()
    return nc


def na_bias_tables(rpb, qq):
    NEG = np.float32(-30000.0)
    tab = np.full((5, 16, 2, 64, 7, 2, 64), NEG, np.float32)
    cq = np.arange(64)
    cs = np.clip(cq - 8, 0, 48)
    ck = np.arange(64)
    colvalid = (ck[:, None] >= cs[None, :]) & (ck[:, None] < cs[None, :] + 16)
    colidx = np.clip(ck[:, None] - cq[None, :] + 15, 0, 30)
    rep_rp = {0: 0, 1: 1, 2: 2, 3: 14, 4: 15}
    for cls in range(5):
        rp = rep_rp[cls]
        st = NA_ST[rp]
        for bq in range(2):
            r = 32 * qq + 2 * rp + bq
            rs = min(max(r - 4, 0), 120)
            for j in range(7):
                for a in range(2):
                    kr = 32 * qq - 4 + 2 * (st + j) + a
                    if kr < rs or kr >= rs + 8 or kr < 0 or kr > 127:
                        continue
                    vals = rpb[:, kr - r + 7, :][:, colidx]
                    tab[cls, :, a, :, j, bq, :] = np.where(colvalid[None], vals, NEG)
    tab = tab.reshape(5, 8, 2, 128, 7, 128)
    tab = tab.transpose(0, 1, 3, 2, 4, 5).reshape(5, 8, 128, 2 * 7 * 128)
    return np.ascontiguousarray(tab)


def prep_l1(inp, x1, hctx1, b, q):
    d = common_inputs(inp, 1, b)
    if x1 is not None:
        xh = np.zeros((2560, D), np.float32)
        lo = q * 2048 - 256
        hi = lo + 2560
        s0, s1 = max(lo, 0), min(hi, 8192)
        xh[s0 - lo:s1 - lo] = x1[b][s0:s1]
        d['xh'] = xh
        d['ctx'] = np.ascontiguousarray(hctx1[b])
    w = inp['na_w_qkv'][0]
    d['wqkv'] = np.ascontiguousarray(np.stack([np.concatenate([w[:, c * 128:(c + 1) * 128], w[:, 1024 + c * 128:1024 + (c + 1) * 128],
                                                              w[:, 2048 + c * 128:2048 + (c + 1) * 128]], axis=1) for c in range(8)]))
    d['tab'] = na_bias_tables(inp['na_rpb'][0], q)
    d['wo'] = np.ascontiguousarray(inp['na_w_o'][0])
    return d


def build_l2(kb=None, io=None):
    own = kb is None
    if own:
        kb = KB()
    io = io or {}
    kb.pfx = '' if own else 'l2_'
    P, nc = kb.P, kb.nc
    NTOK = 2048
    NH = 2304
    xh_d = io.get("xh") or kb.din("xh", [NH, D])
    cv_d = kb.din("cv", [128, 8, 2]); adaw_d = kb.din("adaw", [D, kb.ncol_ada()]); adab_d = kb.din("adab", [2, kb.ncol_ada()])
    g1_d = kb.din("g1", [128, 8]); g2_d = kb.din("g2", [128, 8])
    wpw1_d = kb.din("wpw1", [8, D, 256]); vecs_d = kb.din("vecs", [128, 6, 8]); wdw_d = kb.din("wdw", [128, 8, 31]); mask_d = kb.din("mask", [128, 2])
    wpw2_d = kb.din("wpw2", [D, D]); w1_d = kb.din("w1", [D, DFF]); w2_d = kb.din("w2", [DFF, D])
    out_d = io.get("out") or kb.dout("out", [NTOK, D])
    mid_d = io.get("mid") or out_d

    modsT = kb.sb("modsT", [128, 48, 2], F32)
    mvL = kb.sb("mvL", [128, 6, 8], F32)
    vecs = kb.sb("vecs", [128, 6, 8], F32)
    wdw = kb.sb("wdw", [128, 8, 31], F32)
    mask = kb.sb("mask", [128, 2], F32)
    bG = kb.sb("bG", [128, 8], F32)
    identb = kb.sb("identb", [128, 128], BF16)
    with kb.phase():
        if io.get('pre1'):
            io['pre1']()
        kb.mods(cv_d, adaw_d, adab_d, modsT)
        mL = kb.mod_vectors(modsT, g1_d, g2_d, 0, mvL, "L")
        if io.get('pre2'):
            io['pre2']()
        P.dma('sp', lambda e: e.dma_start(out=vecs[:], in_=vecs_d), writes=['vecs'])
        P.dma('sp', lambda e: e.dma_start(out=wdw[:], in_=wdw_d), writes=['wdw'])
        P.dma('sp', lambda e: e.dma_start(out=mask[:], in_=mask_d), writes=['mask'])
        P.op('dve', lambda e: e.tensor_tensor(out=bG[:], in0=vecs[:, 5, :], in1=mL['G1'], op=ALU.mult), reads=['vecs', mL['key']], writes=['bG'])
        P.op('dve', lambda e: e.tensor_copy(out=identb[:], in_=kb.ident[:]), reads=['ident'], writes=['identb'])
    with kb.phase():
        vT = kb.sb("vT", [128, 8, NTOK], BF16)
        with kb.phase():
            uT = kb.sb("uT", [128, 8, NH], BF16)
            with kb.phase():
                hT = kb.sb("hT", [128, 8, NH], BF16)
                with kb.phase():
                    xtmp = kb.sb("xtmp", [128, 8, 512], F32)
                    stage = [kb.sb("stage%d" % i, [128, 1024], F32) for i in range(2)]
                    scr = norm_scratch(kb)
                    for g in range(5):
                        n = 512 if g < 4 else 256
                        kb.load_xT(tsl(xh_d, g * 512, g * 512 + n), n, xtmp, 'xtmp', stage)
                        kb.norm_mod(xtmp, 'xtmp', n, mL['A1'], mL['B1'], mL['key'], hT, 'hT', scr, t0=0, ht0=g * 512)
                with kb.phase():
                    wp = [kb.sb("wp%d" % i, [128, 8, 256], BF16) for i in range(2)]
                    sig = [kb.sb("sig%d" % i, [128, 512], F32) for i in range(2)]
                    for fc in range(8):
                        b = fc % 2
                        kb.load_w(wp[b], ('wp', b), wpw1_d[fc], 0, KC, 0, 256)
                        for g in range(5):
                            n = 512 if g < 4 else 256
                            pa = kb.nextrot('projbank', 2)
                            proj_fm(kb, hT, 'hT', g * 512, n, wp[b], ('wp', b), 0, pa)
                            pg = 2 + kb.nextrot('projbank2', 2)
                            proj_fm(kb, hT, 'hT', g * 512, n, wp[b], ('wp', b), 128, pg)
                            sb_ = kb.nextrot('sig', 2)
                            P.op('act', lambda e, sb_=sb_, pg=pg, fc=fc, n=n: e.activation(out=sig[sb_][:, :n], in_=kb.bank(pg, n), func=AF.Sigmoid, bias=vecs[:, 1, fc:fc + 1]),
                                 reads=['vecs'], writes=[('ps', pg), ('sig', sb_)])
                            P.op('dve', lambda e, sb_=sb_, pa=pa, fc=fc, g=g, n=n: e.scalar_tensor_tensor(out=uT[:, fc, g * 512:g * 512 + n], in0=kb.bank(pa, n), scalar=vecs[:, 0, fc:fc + 1],
                                                                                                      in1=sig[sb_][:, :n], op0=ALU.add, op1=ALU.mult),
                                 reads=['vecs', ('sig', sb_)], writes=[('ps', pa), ('uT', fc, g)])
                        P.op('dve', lambda e, fc=fc: e.tensor_scalar(out=uT[:, fc, 0:128], in0=uT[:, fc, 0:128], scalar1=mask[:, 0:1], scalar2=None, op0=ALU.mult),
                             reads=['mask'], writes=[('uT', fc, 0)])
                        P.op('dve', lambda e, fc=fc: e.tensor_scalar(out=uT[:, fc, 2176:2304], in0=uT[:, fc, 2176:2304], scalar1=mask[:, 1:2], scalar2=None, op0=ALU.mult),
                             reads=['mask'], writes=[('uT', fc, 4)])
            with kb.phase():
                dg = kb.sb("dg", [128, 8, 31, 128], BF16)
                cT = kb.sb("cT", [128, 8, 512], F32)
                cbf = [kb.sb("cbf%d" % i, [128, 512], BF16) for i in range(2)]
                c2 = [kb.sb("c2%d" % i, [128, 512], BF16) for i in range(2)]
                mean = kb.sb("mean", [128, 512], F32); msq = kb.sb("msq", [128, 512], F32); rstd = kb.sb("rstd", [128, 512], F32)
                tt_ = [kb.sb("tt%d" % i, [128, 512], F32) for i in range(2)]
                for fc in range(8):
                    P.op('dve', lambda e, fc=fc: e.tensor_tensor(out=dg[:, fc, :, :], in0=identb[:].unsqueeze(1).broadcast_to([128, 31, 128]),
                                                                in1=wdw[:, fc, :].unsqueeze(2).broadcast_to([128, 31, 128]), op=ALU.mult),
                         reads=['identb', 'wdw'], writes=[('dg', fc)])
                for tg in range(4):
                    for fc in range(8):
                        pb = kb.nextrot('projbank', 2)
                        for j in range(31):
                            o = 128 + tg * 512 + j - 15
                            P.op('pe', lambda e, fc=fc, j=j, o=o, pb=pb: e.matmul(kb.bank(pb), lhsT=dg[:, fc, j, :], rhs=uT[:, fc, o:o + 512], start=(j == 0), stop=(j == 30)),
                                 reads=[('dg', fc)] + [('uT', fc, gg) for gg in range(5)], writes=[('ps', pb)])
                        P.op('act', lambda e, fc=fc, pb=pb: e.activation(out=cT[:, fc, :], in_=kb.bank(pb), func=AF.Identity, bias=vecs[:, 2, fc:fc + 1]),
                             reads=['vecs'], writes=[('ps', pb), ('cT', fc)])
                        r = kb.nextrot('cbf', 2)
                        P.op('dve', lambda e, fc=fc, r=r: e.tensor_copy(out=cbf[r][:], in_=cT[:, fc, :]), reads=[('cT', fc)], writes=[('cbf', r)])
                        P.op('act', lambda e, fc=fc, r=r: e.activation(out=c2[r][:], in_=cT[:, fc, :], func=AF.Square), reads=[('cT', fc)], writes=[('c2', r)])
                        P.op('pe', lambda e, fc=fc, r=r: e.matmul(kb.bank(6), lhsT=kb.ones_bf[:], rhs=cbf[r][:], start=(fc == 0), stop=(fc == 7)), reads=[('cbf', r), 'ones_bf'], writes=[('ps', 6)])
                        P.op('pe', lambda e, fc=fc, r=r: e.matmul(kb.bank(7), lhsT=kb.ones_bf[:], rhs=c2[r][:], start=(fc == 0), stop=(fc == 7)), reads=[('c2', r), 'ones_bf'], writes=[('ps', 7)])
                    P.op('act', lambda e: e.activation(out=mean[:], in_=kb.bank(6), func=AF.Copy, scale=1.0 / D), writes=[('ps', 6), 'mean'])
                    P.op('dve', lambda e: e.tensor_tensor(out=msq[:], in0=mean[:], in1=mean[:], op=ALU.mult), reads=['mean'], writes=['msq'])
                    P.op('dve', lambda e: e.scalar_tensor_tensor(out=msq[:], in0=kb.bank(7), scalar=1.0 / D, in1=msq[:], op0=ALU.mult, op1=ALU.subtract), writes=[('ps', 7), 'msq'])
                    P.op('act', lambda e: e.activation(out=rstd[:], in_=msq[:], func=AF.Ln, bias=kb.eps_t[:]), reads=['msq', 'eps_t'], writes=['rstd'])
                    P.op('act', lambda e: e.activation(out=rstd[:], in_=rstd[:], func=AF.Exp, scale=-0.5), writes=['rstd'])
                    for fc in range(8):
                        r = kb.nextrot('tt', 2)
                        P.op('dve', lambda e, fc=fc, r=r: e.tensor_tensor(out=tt_[r][:], in0=cT[:, fc, :], in1=mean[:], op=ALU.subtract), reads=[('cT', fc), 'mean'], writes=[('tt', r)])
                        P.op('dve', lambda e, r=r: e.tensor_tensor(out=tt_[r][:], in0=tt_[r][:], in1=rstd[:], op=ALU.mult), reads=['rstd'], writes=[('tt', r)])
                        P.op('act', lambda e, fc=fc, r=r, tg=tg: e.activation(out=vT[:, fc, tg * 512:(tg + 1) * 512], in_=tt_[r][:], func=AF.Silu, scale=vecs[:, 3, fc:fc + 1], bias=vecs[:, 4, fc:fc + 1]),
                             reads=[('tt', r), 'vecs'], writes=[('vT', tg)])
        with kb.phase():
            xtmp = kb.sb("xtmp", [128, 8, 512], F32)
            stage = [kb.sb("stage%d" % i, [128, 1024], F32) for i in range(2)]
            ostage = [kb.sb("ostage%d" % i, [128, 1024], F32) for i in range(2)]
            wo = kb.sb("wo", [128, 8, 1024], BF16)
            kb.load_w(wo, 'wo', wpw2_d, 0, KC, 0, 1024)
            for g in range(4):
                kb.load_xT(tsl(xh_d, 128 + g * 512, 128 + (g + 1) * 512), 512, xtmp, 'xtmp', stage)
                resid_proj(kb, vT, 'vT', g * 512, 512, wo, 'wo', xtmp, 'xtmp', 0, mL['G1'], mL['key'], bG=bG)
                kb.store_x(xtmp, 'xtmp', 512, tsl(mid_d, g * 512, (g + 1) * 512), ostage)
    mlp_tail(kb, mid_d, out_d, NTOK, mL, w1_d, w2_d)
    if own:
        P.close()
    return nc


def prep_l2(inp, x2, b, q):
    d = common_inputs(inp, 2, b)
    if x2 is not None:
        xh = np.zeros((2304, D), np.float32)
        lo = q * 2048 - 128
        hi = lo + 2304
        s0, s1 = max(lo, 0), min(hi, 8192)
        xh[s0 - lo:s1 - lo] = x2[b][s0:s1]
        d['xh'] = xh
    w = inp['cv_w_pw1'][0]
    d['wpw1'] = np.ascontiguousarray(np.stack([np.concatenate([w[:, c * 128:(c + 1) * 128], w[:, 1024 + c * 128:1024 + (c + 1) * 128]], axis=1) for c in range(8)]))
    bp = inp['cv_b_pw1'][0]
    d['vecs'] = np.ascontiguousarray(np.stack([fm(bp[:1024]), fm(bp[1024:]), fm(inp['cv_b_dw'][0]), fm(inp['cv_ln_g'][0]), fm(inp['cv_ln_b'][0]), fm(inp['cv_b_pw2'][0])], axis=1))
    d['wdw'] = np.ascontiguousarray(inp['cv_w_dw'][0].T.reshape(8, 128, 31).transpose(1, 0, 2))
    m = np.ones((128, 2), np.float32)
    if q == 0:
        m[:, 0] = 0.0
    if q == 3:
        m[:, 1] = 0.0
    d['mask'] = m
    d['wpw2'] = np.ascontiguousarray(inp['cv_w_pw2'][0])
    return d


def l3_pq_loop(kb, io, hT, csd, pq_d):
    P = kb.P
    pqs = [kb.sb("pqs%d" % i, [128, 2048], BF16) for i in range(2)]
    for tt in range(16):
        ob = tt % 2
        for grp in range(4):
            pb = kb.nextrot('projbank', 4)
            for kl in range(2):
                kc = grp * 2 + kl
                P.op('pe', lambda e, kc=kc, kl=kl, tt=tt, pb=pb: e.matmul(kb.bank(pb), lhsT=hT[:, kc, tt * 128:(tt + 1) * 128], rhs=csd[:, kl, :], start=(kl == 0), stop=(kl == 1)),
                     reads=[(('csd'), kl), ('hT', tt // 4)], writes=[('ps', pb)])
            if grp % 2 == 0:
                P.op('act', lambda e, ob=ob, grp=grp, pb=pb: e.activation(out=pqs[ob][:, grp * 512:(grp + 1) * 512], in_=kb.bank(pb), func=AF.Copy), writes=[('ps', pb), ('pqs', ob, grp)])
            else:
                P.op('dve', lambda e, ob=ob, grp=grp, pb=pb: e.tensor_copy(out=pqs[ob][:, grp * 512:(grp + 1) * 512], in_=kb.bank(pb)), writes=[('ps', pb), ('pqs', ob, grp)])
        P.dma('sp', lambda e, ob=ob, tt=tt: e.dma_start(out=pq_d[tt * 128:(tt + 1) * 128, :], in_=pqs[ob][:]), reads=[('pqs', ob, g_) for g_ in range(4)], writes=[('pqd', tt)])
        if io.get('pqg') is not None and tt % 2 == 1:
            c = tt // 2
            pqg = io['pqg']
            P.coll(lambda e, c=c, pqg=pqg: e.collective_compute("AllGather", ALU.bypass, replica_groups=RG, ins=[pq_d[c * 256:(c + 1) * 256, :].opt()], outs=[pqg[c * 1024:(c + 1) * 1024, :].opt()]),
                   reads=[('pqd', tt - 1), ('pqd', tt)], writes=[('pqg', c)])


def l3_dft_loops(kb, io, pq_d, cn_d, sn_d, zT, chunk_keys=False):
    P = kb.P
    pqb = [kb.sb("pqb%d" % i, [128, 2048], BF16) for i in range(3)]
    tb = [kb.sb("tb%d" % i, [128, 2, 512], BF16) for i in range(3)]
    tokmap = io.get('tokmap') or (lambda nt: nt * 128)
    for kg in range(4):
        for nt in range(64):
            tk = tokmap(nt)
            r = kb.nextrot('pqb', 3)
            P.dma('sp', lambda e, r=r, nt=nt: e.dma_start(out=pqb[r][:], in_=pq_d[nt * 128:(nt + 1) * 128, :]), reads=([('pqg', nt // 8)] if chunk_keys else []), writes=[('pqb', r)])
            P.dma('sp', lambda e, r=r, tk=tk, kg=kg: e.dma_start(out=tb[r][:, 0, :], in_=cn_d[tk:tk + 128, kg * 512:(kg + 1) * 512]), writes=[('tb', r, 0)])
            P.dma('sp', lambda e, r=r, tk=tk, kg=kg: e.dma_start(out=tb[r][:, 1, :], in_=sn_d[tk:tk + 128, kg * 512:(kg + 1) * 512]), writes=[('tb', r, 1)])
            for fz in range(8):
                grp, jh = fz // 2, fz % 2
                P.op('pe', lambda e, r=r, fz=fz, grp=grp, jh=jh, nt=nt: e.matmul(kb.bank(fz), lhsT=pqb[r][:, grp * 512 + jh * 128:grp * 512 + jh * 128 + 128], rhs=tb[r][:, 0, :],
                                                                              start=(nt == 0), stop=False),
                     reads=[('pqb', r), ('tb', r, 0)], writes=[('ps', fz)])
                P.op('pe', lambda e, r=r, fz=fz, grp=grp, jh=jh, nt=nt: e.matmul(kb.bank(fz), lhsT=pqb[r][:, grp * 512 + 256 + jh * 128:grp * 512 + 256 + jh * 128 + 128], rhs=tb[r][:, 1, :],
                                                                              start=False, stop=(nt == 63)),
                     reads=[('pqb', r), ('tb', r, 1)], writes=[('ps', fz)])
        for fz in range(8):
            if fz % 2 == 0:
                P.op('act', lambda e, fz=fz, kg=kg: e.activation(out=zT[:, fz, kg * 512:(kg + 1) * 512], in_=kb.bank(fz), func=AF.Copy), writes=[('ps', fz), ('zT', kg)])
            else:
                P.op('dve', lambda e, fz=fz, kg=kg: e.tensor_copy(out=zT[:, fz, kg * 512:(kg + 1) * 512], in_=kb.bank(fz)), writes=[('ps', fz), ('zT', kg)])


def build_l3_fused(kb, io):
    P, nc = kb.P, kb.nc
    NTOK = 2048
    x_d = io["x"]; out_d = io["out"]; mid_d = io["mid"]; pqo = io["pq"]; pqg = io["pqg"]
    kb.pfx = 'l3a_'
    cv_d = kb.din("cv", [128, 8, 2]); adaw_d = kb.din("adaw", [D, kb.ncol_ada()]); adab_d = kb.din("adab", [2, kb.ncol_ada()])
    g1_d = kb.din("g1", [128, 8]); g2_d = kb.din("g2", [128, 8]); csd_d = kb.din("csd", [256, 512])
    kb.pfx = 'l3b_'
    cn_d = kb.din("cn", [8192, NTOK], BF16); sn_d = kb.din("sn", [8192, NTOK], BF16)
    ftw_d = kb.din("ftw", [D, D]); vecs_d = kb.din("vecs", [128, 2, 8])
    w1_d = kb.din("w1", [D, DFF]); w2_d = kb.din("w2", [DFF, D])
    modsT = kb.sb("modsT", [128, 48, 2], F32)
    mvL = kb.sb("mvL", [128, 6, 8], F32)
    vecs = kb.sb("vecs", [128, 2, 8], F32)
    bG = kb.sb("bG", [128, 8], F32)
    zeros = kb.sb("zeros", [128, 8], F32)
    with kb.phase():
        kb.mods(cv_d, adaw_d, adab_d, modsT)
        mL = kb.mod_vectors(modsT, g1_d, g2_d, 0, mvL, "L")
        P.dma('sp', lambda e: e.dma_start(out=vecs[:], in_=vecs_d), writes=['vecs'])
        P.op('dve', lambda e: e.tensor_tensor(out=bG[:], in0=vecs[:, 0, :], in1=mL['G1'], op=ALU.mult), reads=['vecs', mL['key']], writes=['bG'])
        P.op('pool', lambda e: e.memset(zeros[:], 0.0), writes=['zeros'])
    with kb.phase():
        zT = kb.sb("zT", [128, 8, NTOK], BF16)
        with kb.phase():
            hT = kb.sb("hT", [128, 8, NTOK], BF16)
            csd = kb.sb("csd", [128, 2, 512], BF16)
            kb.load_w(csd, 'csd', csd_d, 0, 2, 0, 512)
            with kb.phase():
                xtmp = kb.sb("xtmp", [128, 8, 512], F32)
                stage = [kb.sb("stage%d" % i, [128, 1024], F32) for i in range(2)]
                scr = norm_scratch(kb)
                for g in range(4):
                    kb.load_xT(tsl(x_d, g * 512, (g + 1) * 512), 512, xtmp, 'xtmp', stage)
                    kb.norm_mod(xtmp, 'xtmp', 512, mL['A1'], mL['B1'], mL['key'], hT, 'hT', scr, t0=0, ht0=g * 512)
            with kb.phase():
                l3_pq_loop(kb, dict(pqg=pqg), hT, csd, pqo)
                l3_dft_loops(kb, dict(tokmap=pq_tokmap), pqg, cn_d, sn_d, zT, chunk_keys=True)
        with kb.phase():
            xtmp = kb.sb("xtmp", [128, 8, 512], F32)
            stage = [kb.sb("stage%d" % i, [128, 1024], F32) for i in range(2)]
            ostage = [kb.sb("ostage%d" % i, [128, 1024], F32) for i in range(2)]
            wo = kb.sb("wo", [128, 8, 1024], BF16)
            kb.load_w(wo, 'wo', ftw_d, 0, KC, 0, 1024)
            for g in range(4):
                kb.load_xT(tsl(x_d, g * 512, (g + 1) * 512), 512, xtmp, 'xtmp', stage)
                resid_proj(kb, zT, 'zT', g * 512, 512, wo, 'wo', xtmp, 'xtmp', 0, mL['G1'], mL['key'], bG=bG)
                kb.store_x(xtmp, 'xtmp', 512, tsl(mid_d, g * 512, (g + 1) * 512), ostage)
    mlp_tail(kb, mid_d, out_d, NTOK, mL, w1_d, w2_d, final=(vecs[:, 1, :], zeros[:], 'vecs'))


def build_l3a(kb=None, io=None):
    own = kb is None
    if own:
        kb = KB()
    io = io or {}
    kb.pfx = '' if own else 'l3a_'
    P, nc = kb.P, kb.nc
    NTOK = 2048
    x_d = io.get("x") or kb.din("x", [NTOK, D])
    cv_d = kb.din("cv", [128, 8, 2]); adaw_d = kb.din("adaw", [D, kb.ncol_ada()]); adab_d = kb.din("adab", [2, kb.ncol_ada()])
    g1_d = kb.din("g1", [128, 8]); g2_d = kb.din("g2", [128, 8])
    csd_d = kb.din("csd", [256, 512])
    pq_d = io.get("pq") or kb.dout("pq", [NTOK, 2048], BF16)
    modsT = kb.sb("modsT", [128, 48, 2], F32)
    mvL = kb.sb("mvL", [128, 6, 8], F32)
    with kb.phase():
        kb.mods(cv_d, adaw_d, adab_d, modsT)
        mL = kb.mod_vectors(modsT, g1_d, g2_d, 0, mvL, "L")
    io['mL_out'] = mL
    with kb.phase():
        hT = kb.sb("hT", [128, 8, NTOK], BF16)
        csd = kb.sb("csd", [128, 2, 512], BF16)
        kb.load_w(csd, 'csd', csd_d, 0, 2, 0, 512)
        with kb.phase():
            xtmp = kb.sb("xtmp", [128, 8, 512], F32)
            stage = [kb.sb("stage%d" % i, [128, 1024], F32) for i in range(2)]
            scr = norm_scratch(kb)
            for g in range(4):
                kb.load_xT(tsl(x_d, g * 512, (g + 1) * 512), 512, xtmp, 'xtmp', stage)
                kb.norm_mod(xtmp, 'xtmp', 512, mL['A1'], mL['B1'], mL['key'], hT, 'hT', scr, t0=0, ht0=g * 512)
        with kb.phase():
            l3_pq_loop(kb, io, hT, csd, pq_d)
    if own:
        P.close()
    return nc


def build_l3b(kb=None, io=None):
    own = kb is None
    if own:
        kb = KB()
    io = io or {}
    kb.pfx = '' if own else 'l3b_'
    P, nc = kb.P, kb.nc
    NTOK = 2048
    x_d = io.get("x") or kb.din("x", [NTOK, D])
    cv_d = kb.din("cv", [128, 8, 2]); adaw_d = kb.din("adaw", [D, kb.ncol_ada()]); adab_d = kb.din("adab", [2, kb.ncol_ada()])
    g1_d = kb.din("g1", [128, 8]); g2_d = kb.din("g2", [128, 8])
    pq_d = io.get("pq") or kb.din("pq", [8192, 2048], BF16)
    cn_d = kb.din("cn", [8192, NTOK], BF16); sn_d = kb.din("sn", [8192, NTOK], BF16)
    ftw_d = kb.din("ftw", [D, D]); vecs_d = kb.din("vecs", [128, 2, 8])
    w1_d = kb.din("w1", [D, DFF]); w2_d = kb.din("w2", [DFF, D])
    out_d = io.get("out") or kb.dout("out", [NTOK, D])
    mid_d = io.get("mid") or out_d
    modsT = kb.sb("modsT", [128, 48, 2], F32)
    mvL = kb.sb("mvL", [128, 6, 8], F32)
    vecs = kb.sb("vecs", [128, 2, 8], F32)
    bG = kb.sb("bG", [128, 8], F32)
    zeros = kb.sb("zeros", [128, 8], F32)
    with kb.phase():
        if io.get('mL') is not None:
            mL = io['mL']
        else:
            kb.mods(cv_d, adaw_d, adab_d, modsT)
            mL = kb.mod_vectors(modsT, g1_d, g2_d, 0, mvL, "L")
        P.dma('sp', lambda e: e.dma_start(out=vecs[:], in_=vecs_d), writes=['vecs'])
        P.op('dve', lambda e: e.tensor_tensor(out=bG[:], in0=vecs[:, 0, :], in1=mL['G1'], op=ALU.mult), reads=['vecs', mL['key']], writes=['bG'])
        P.op('pool', lambda e: e.memset(zeros[:], 0.0), writes=['zeros'])
    with kb.phase():
        zT = kb.sb("zT", [128, 8, NTOK], BF16)
        with kb.phase():
            l3_dft_loops(kb, io, pq_d, cn_d, sn_d, zT)
        with kb.phase():
            xtmp = kb.sb("xtmp", [128, 8, 512], F32)
            stage = [kb.sb("stage%d" % i, [128, 1024], F32) for i in range(2)]
            ostage = [kb.sb("ostage%d" % i, [128, 1024], F32) for i in range(2)]
            wo = kb.sb("wo", [128, 8, 1024], BF16)
            kb.load_w(wo, 'wo', ftw_d, 0, KC, 0, 1024)
            for g in range(4):
                kb.load_xT(tsl(x_d, g * 512, (g + 1) * 512), 512, xtmp, 'xtmp', stage)
                resid_proj(kb, zT, 'zT', g * 512, 512, wo, 'wo', xtmp, 'xtmp', 0, mL['G1'], mL['key'], bG=bG)
                kb.store_x(xtmp, 'xtmp', 512, tsl(mid_d, g * 512, (g + 1) * 512), ostage)
    mlp_tail(kb, mid_d, out_d, NTOK, mL, w1_d, w2_d, final=(vecs[:, 1, :], zeros[:], 'vecs'))
    if own:
        P.close()
    return nc


def prep_l3a(inp, x3, b, q):
    d = common_inputs(inp, 3, b)
    for k in ('w1', 'w2'):
        d.pop(k)
    if x3 is not None:
        d['x'] = np.ascontiguousarray(x3[b][q * 2048:(q + 1) * 2048])
    dd = np.arange(256)[:, None].astype(np.int64)
    jj = np.arange(256)[None, :].astype(np.int64)
    ang = 2.0 * np.pi * ((dd * jj) % 256).astype(np.float64) / 256.0
    d['csd'] = np.ascontiguousarray(np.concatenate([np.cos(ang) / 16.0, np.sin(ang) / 16.0], axis=1).astype(np.float32))
    return d


_DFT_CACHE = {}


def seq_dft_tables(q):
    if q not in _DFT_CACHE:
        import ml_dtypes
        n = np.arange(8192, dtype=np.int64)[:, None]
        k = np.arange(q * 2048, (q + 1) * 2048, dtype=np.int64)[None, :]
        ang = 2.0 * np.pi * ((n * k) % 8192).astype(np.float64) / 8192.0
        s = 1.0 / np.sqrt(8192.0)
        _DFT_CACHE[q] = (np.ascontiguousarray((np.cos(ang) * s).astype(np.float32).astype(ml_dtypes.bfloat16)),
                         np.ascontiguousarray((-np.sin(ang) * s).astype(np.float32).astype(ml_dtypes.bfloat16)))
    return _DFT_CACHE[q]


def prep_l3b(inp, x3, pq_b, b, q):
    d = common_inputs(inp, 3, b)
    if x3 is not None:
        d['x'] = np.ascontiguousarray(x3[b][q * 2048:(q + 1) * 2048])
        d['pq'] = pq_b
    d['cn'], d['sn'] = seq_dft_tables(q)
    d['ftw'] = np.ascontiguousarray(inp['ft_w'][0])
    d['vecs'] = np.ascontiguousarray(np.stack([fm(inp['ft_b'][0]), fm(inp['final_g'])], axis=1))
    return d


CORES = [(b, q) for b in range(2) for q in range(4)]


def _run(nc, maps):
    res = run_bass_kernel_spmd(nc, maps, core_ids=list(range(8)))
    return res.results


def kernel_unfused(**inputs):
    inp = {k: np.asarray(v) for k, v in inputs.items()}
    r = _run(build_l0(), [prep_l0(inp, b, q) for b, q in CORES])
    x1 = np.stack([np.concatenate([r[b * 4 + q]["out"] for q in range(4)], axis=0) for b in range(2)])
    hctx1 = np.stack([r[b * 4]["hctx"] for b in range(2)])
    r = _run(build_l1(), [prep_l1(inp, x1, hctx1, b, q) for b, q in CORES])
    x2 = np.stack([np.concatenate([r[b * 4 + q]["out"] for q in range(4)], axis=0) for b in range(2)])
    r = _run(build_l2(), [prep_l2(inp, x2, b, q) for b, q in CORES])
    x3 = np.stack([np.concatenate([r[b * 4 + q]["out"] for q in range(4)], axis=0) for b in range(2)])
    r = _run(build_l3a(), [prep_l3a(inp, x3, b, q) for b, q in CORES])
    pq = [np.ascontiguousarray(np.concatenate([r[b * 4 + q]["pq"] for q in range(4)], axis=0)) for b in range(2)]
    r = _run(build_l3b(), [prep_l3b(inp, x3, pq[b], b, q) for b, q in CORES])
    out = np.stack([np.concatenate([r[b * 4 + q]["out"] for q in range(4)], axis=0) for b in range(2)])
    return out.astype(np.float32)


def halo_parts(kb, src, dst, H, sel, tag):
    P, nc = kb.P, kb.nc
    bF = kb.dint("bounceF" + tag, [1024, H]); bL = kb.dint("bounceL" + tag, [1024, H])
    gF = kb.dint("gathF" + tag, [4096, H]); gL = kb.dint("gathL" + tag, [4096, H])

    def part1():
        P.dma('pool', lambda e: e.dma_start(out=bF.rearrange("(p k) h -> p k h", k=8), in_=src.ap[:, :, src.t0:src.t0 + H]), writes=['bF'])
        P.dma('pool', lambda e: e.dma_start(out=bL.rearrange("(p k) h -> p k h", k=8), in_=src.ap[:, :, src.t0 + 2048 - H:src.t0 + 2048]), writes=['bL'])
        P.coll([lambda e: e.collective_compute("AllGather", ALU.bypass, replica_groups=RG, ins=[bF.opt()], outs=[gF.opt()]),
                lambda e: e.collective_compute("AllGather", ALU.bypass, replica_groups=RG, ins=[bL.opt()], outs=[gL.opt()])], reads=['bF', 'bL'], writes=['gF', 'gL'])

    def part2():
        cand = [kb.sb("cand%d" % i, [128, 4, 8, H], F32) for i in range(2)]
        acc = [kb.sb("hacc%d" % i, [128, 8, H], F32) for i in range(2)]
        for side in range(2):
            gsrc, gkey = (gL, 'gL') if side == 0 else (gF, 'gF')
            srcv = gsrc.rearrange("(r p k) h -> p r k h", r=4, k=8)
            for r in range(4):
                P.dma('sp', lambda e, side=side, srcv=srcv, r=r: e.dma_start(out=cand[side][:, r, :, :], in_=srcv[:, r, :, :]), reads=[gkey], writes=[('cand', side, r)])
            P.op('dve', lambda e, side=side: e.tensor_scalar(out=acc[side][:], in0=cand[side][:, 0, :, :], scalar1=sel[:, side * 4:side * 4 + 1], scalar2=None, op0=ALU.mult),
                 reads=[('cand', side, 0), 'sel'], writes=[('hacc', side)])
            for r in range(1, 4):
                P.op('dve', lambda e, side=side, r=r: e.scalar_tensor_tensor(out=acc[side][:], in0=cand[side][:, r, :, :], scalar=sel[:, side * 4 + r:side * 4 + r + 1], in1=acc[side][:],
                                                                          op0=ALU.mult, op1=ALU.add),
                     reads=[('cand', side, r), 'sel'], writes=[('hacc', side)])
            d0 = 0 if side == 0 else H + 2048
            P.dma('sp', lambda e, side=side, d0=d0: e.dma_start(out=dst.ap[:, :, d0:d0 + H], in_=acc[side][:]), reads=[('hacc', side)], writes=[('dsth', side)])
    return part1, part2


def pq_tokmap(nt):
    c, r, half = nt // 8, (nt % 8) // 2, nt % 2
    return r * 2048 + c * 256 + half * 128


def build_fused():
    kb = KB()
    P, nc = kb.P, kb.nc
    kb.split_mods = True
    xb_d = kb.din("xb", [2048, D]); ctx_d = kb.din("ctx", [256, D]); sel_d = kb.din("sel", [128, 8])
    out_d = kb.dout("out", [2048, D])
    sel = kb.sb("sel", [128, 8], F32)
    P.dma('sp', lambda e: e.dma_start(out=sel[:], in_=sel_d), writes=['sel'])

    def fmt(name, n):
        return FM(kb.dint(name, [128, 8, n]))
    mid = fmt("mid", 2048)
    hc1 = fmt("hc1", 256)
    xh1 = fmt("xh1", 2560)
    xa = FM(xh1.ap, 256)
    cmid = fmt("cmid", 256)
    build_l0(kb, dict(xb=xb_d, ctx=ctx_d, out=xa, hctx=hc1, mid=mid, kv_gather=True, cmid=cmid))
    p1, p2 = halo_parts(kb, xa, xh1, 256, sel, "1")
    xh2 = fmt("xh2", 2304)
    xb2 = FM(xh2.ap, 128)
    build_l1(kb, dict(xh=xh1, ctx=hc1, out=xb2, mid=mid, pre1=p1, pre2=p2))
    p1, p2 = halo_parts(kb, xb2, xh2, 128, sel, "2")
    xc = fmt("xc", 2048)
    build_l2(kb, dict(xh=xh2, out=xc, mid=mid, pre1=p1, pre2=p2))
    pqo = kb.dint("pqo", [2048, 2048], BF16); pqg = kb.dint("pqg", [8192, 2048], BF16)
    build_l3_fused(kb, dict(x=xc, pq=pqo, pqg=pqg, out=out_d, mid=mid))
    P.close()
    return nc


def prep_fused(inp, b, q):
    d0 = prep_l0(inp, b, q)
    out = {k: d0[k] for k in ('ident', 'cv', 'xb', 'ctx')}
    sel = np.zeros((128, 8), np.float32)
    if q > 0:
        sel[:, q - 1] = 1.0
    if q < 3:
        sel[:, 4 + q + 1] = 1.0
    out['sel'] = sel
    out['xb'] = np.ascontiguousarray(out['xb'][:2048])
    for pfx, dd in (('l0_', d0), ('l1_', prep_l1(inp, None, None, b, q)), ('l2_', prep_l2(inp, None, b, q)),
                    ('l3a_', prep_l3a(inp, None, b, q)), ('l3b_', prep_l3b(inp, None, None, b, q))):
        for k, v in dd.items():
            if k in ('ident', 'cv', 'xb', 'ctx'):
                continue
            if pfx == 'l3b_' and k in ('adaw', 'adab', 'g1', 'g2'):
                continue
            if k in ('adaw', 'adab'):
                v = np.ascontiguousarray(v[:, q * 1536:(q + 1) * 1536])
            if pfx == 'l0_' and k in ('cos', 'sin'):
                v = np.ascontiguousarray(v[:4])
            out[pfx + k] = v
    return out


def kernel(**inputs):
    inp = {k: np.asarray(v) for k, v in inputs.items()}
    r = _run(build_fused(), [prep_fused(inp, b, q) for b, q in CORES])
    out = np.stack([np.concatenate([r[b * 4 + q]["out"] for q in range(4)], axis=0) for b in range(2)])
    return out.astype(np.float32)
```
